# Optimizing a Trainium2 kernel written in Bass

```python
import jax
import jax.numpy as jnp
from jax import lax
import numpy as np

D_MODEL = 2048
BATCH = 2
SEQ = 8192
DEPTH = 2

GRID_W = 64
CTX_LEN = 256
BLOCK = 128
WINDOW = 128
ROPE_BASE = 10000.0
ROPE_DIM = 64
EPS = 1e-6
N_MOD = 9
FFN_DIM = 5632

A_HEADS = 16
A_KV_HEADS = 2
A_HEAD_DIM = 64
A_Q_W = A_HEADS * A_HEAD_DIM
A_KV_W = A_KV_HEADS * A_HEAD_DIM
B_GROUPS = 8
B_GROUP_DIM = 128
B_CHUNK = 128
B_W = B_GROUPS * B_GROUP_DIM
AB_SPLIT = (A_Q_W, A_Q_W + A_KV_W, A_Q_W + 2 * A_KV_W, A_Q_W + 2 * A_KV_W + B_W)
AB_IN = A_Q_W + 2 * A_KV_W + 2 * B_W
AB_OUT = A_Q_W + B_W

C_HEADS = 8
C_NOPE = 128
C_ROPE = ROPE_DIM
C_V = 128
C_Q_RANK = 768
C_KV_RANK = 512
C_W = C_HEADS * C_V
D_W = 1024
D_CONV = 3
CD_SPLIT = (C_Q_RANK, C_Q_RANK + C_KV_RANK, C_Q_RANK + C_KV_RANK + C_ROPE,
            C_Q_RANK + C_KV_RANK + C_ROPE + D_W, C_Q_RANK + C_KV_RANK + C_ROPE + 2 * D_W)
CD_IN = C_Q_RANK + C_KV_RANK + C_ROPE + 3 * D_W
CD_OUT = C_W + D_W

N_AB = (DEPTH + 1) // 2
N_CD = DEPTH // 2

kernel_name = "hybrid_dit_window_gmlp_mla_shortconv"


def rmsnorm(x, g):
    xf = x.astype(jnp.float32)
    y = xf * lax.rsqrt(jnp.mean(xf * xf, axis=-1, keepdims=True) + EPS)
    return (y * g.astype(jnp.float32)).astype(x.dtype)


def modulate(x, g, shift, scale):
    return rmsnorm(x, g) * (1 + scale) + shift


def swiglu(h, w1, w3, w2):
    return (jax.nn.silu(h @ w1) * (h @ w3)) @ w2


def mod_chunks(cond, w, b):
    m = jax.nn.silu(cond) @ w + b
    m = m.reshape(m.shape[0], N_MOD, 1, D_MODEL)
    return [m[:, k] for k in range(N_MOD)]


def axial_angles(n):
    rows = n // GRID_W
    t = jnp.arange(rows * GRID_W)
    row = (t // GRID_W).astype(jnp.float32)
    col = (t % GRID_W).astype(jnp.float32)
    axis_dim = ROPE_DIM // 2
    inv_freq = ROPE_BASE ** (-jnp.arange(0, axis_dim, 2, dtype=jnp.float32) / axis_dim)
    return row[:, None] * inv_freq, col[:, None] * inv_freq


def rope_1d(x, ang):
    xf = x.astype(jnp.float32)
    x1, x2 = jnp.split(xf, 2, axis=-1)
    cos, sin = jnp.cos(ang), jnp.sin(ang)
    return jnp.concatenate([x1 * cos - x2 * sin, x1 * sin + x2 * cos], axis=-1).astype(x.dtype)


def axial_rope(x, ang_r, ang_c):
    xr, xc = jnp.split(x, 2, axis=-1)
    return jnp.concatenate([rope_1d(xr, ang_r), rope_1d(xc, ang_c)], axis=-1)


def window_attention(q, k, v, kc, vc, sink):
    B, n, H, dh = q.shape
    KV = k.shape[2]
    G = H // KV
    nblk = n // BLOCK
    pad = ((0, 0), (BLOCK, BLOCK), (0, 0), (0, 0))
    kp, vp = jnp.pad(k, pad), jnp.pad(v, pad)
    qb = jnp.moveaxis(q.reshape(B, nblk, BLOCK, KV, G, dh), 1, 0)
    sink_kg = sink.reshape(KV, G).astype(jnp.float32)[None, :, :, None, None]
    scale = dh ** -0.5
    offs_q = jnp.arange(BLOCK)
    offs_k = jnp.arange(3 * BLOCK)

    def one_block(args):
        qi, bi = args
        kw = lax.dynamic_slice_in_dim(kp, bi * BLOCK, 3 * BLOCK, axis=1)
        vw = lax.dynamic_slice_in_dim(vp, bi * BLOCK, 3 * BLOCK, axis=1)
        q_pos = bi * BLOCK + offs_q
        k_pos = bi * BLOCK - BLOCK + offs_k
        valid = ((jnp.abs(k_pos[None, :] - q_pos[:, None]) <= WINDOW)
                 & (k_pos >= 0)[None, :] & (k_pos < n)[None, :])
        s_lat = jnp.einsum("bqkgd,bskd->bkgqs", qi, kw).astype(jnp.float32) * scale
        s_lat = jnp.where(valid, s_lat, -jnp.inf)
        s_ctx = jnp.einsum("bqkgd,bskd->bkgqs", qi, kc).astype(jnp.float32) * scale
        s_sink = jnp.broadcast_to(sink_kg, s_ctx.shape[:-1] + (1,))
        p = jax.nn.softmax(jnp.concatenate([s_lat, s_ctx, s_sink], axis=-1), axis=-1).astype(v.dtype)
        o = (jnp.einsum("bkgqs,bskd->bqkgd", p[..., :3 * BLOCK], vw)
             + jnp.einsum("bkgqs,bskd->bqkgd", p[..., 3 * BLOCK:-1], vc))
        return o.reshape(B, BLOCK, H * dh)

    o = lax.map(one_block, (qb, jnp.arange(nblk)))
    return jnp.moveaxis(o, 0, 1).reshape(B, n, H * dh)


def dense_attention_sink(q, k, v, sink):
    B, L, H, dh = q.shape
    KV = k.shape[2]
    G = H // KV
    qg = q.reshape(B, L, KV, G, dh)
    s = jnp.einsum("bqkgd,bskd->bkgqs", qg, k).astype(jnp.float32) * dh ** -0.5
    s_sink = jnp.broadcast_to(sink.reshape(KV, G).astype(jnp.float32)[None, :, :, None, None],
                              s.shape[:-1] + (1,))
    p = jax.nn.softmax(jnp.concatenate([s, s_sink], axis=-1), axis=-1)[..., :-1].astype(v.dtype)
    return jnp.einsum("bkgqs,bskd->bqkgd", p, v).reshape(B, L, H * dh)


def chunk_gmlp(u, v, ws, bias):
    B, T, _ = u.shape
    u = jax.nn.gelu(u)
    v = jax.nn.gelu(v).reshape(B, T // B_CHUNK, B_CHUNK, B_GROUPS, B_GROUP_DIM)
    mixed = jnp.einsum("gpq,bcqgd->bcpgd", ws, v) + bias.T[None, None, :, :, None]
    return u * mixed.reshape(B, T, B_W)


def short_conv(x, w):
    xp = jnp.pad(x, ((0, 0), (1, 1), (0, 0)))
    return w[0] * xp[:, :-2] + w[1] * xp[:, 1:-1] + w[2] * xp[:, 2:]


def mla_queries(cq, g, w_uq):
    B, T, _ = cq.shape
    q = (rmsnorm(cq, g) @ w_uq).reshape(B, T, C_HEADS, C_NOPE + C_ROPE)
    return q[..., :C_NOPE], q[..., C_NOPE:]


def mla_keys_values(ckv, g, w_ukv):
    B, T, _ = ckv.shape
    kv = (rmsnorm(ckv, g) @ w_ukv).reshape(B, T, C_HEADS, C_NOPE + C_V)
    return kv[..., :C_NOPE], kv[..., C_NOPE:]


def mla_attention(q_nope, q_rope, k_nope, k_rope, v):
    B, n, H, _ = q_nope.shape
    nblk = n // BLOCK
    qn = jnp.moveaxis(q_nope.reshape(B, nblk, BLOCK, H, C_NOPE), 1, 0)
    qr = jnp.moveaxis(q_rope.reshape(B, nblk, BLOCK, H, C_ROPE), 1, 0)
    scale = (C_NOPE + C_ROPE) ** -0.5

    def one_block(args):
        qn_i, qr_i = args
        s = jnp.einsum("bqhd,bshd->bhqs", qn_i, k_nope) + jnp.einsum("bqhd,bsd->bhqs", qr_i, k_rope)
        p = jax.nn.softmax(s.astype(jnp.float32) * scale, axis=-1).astype(v.dtype)
        return jnp.einsum("bhqs,bshd->bqhd", p, v)

    o = lax.map(one_block, (qn, qr))
    return jnp.moveaxis(o, 0, 1).reshape(B, n, H * C_V)


def mixer_ab(h, hc, w_in, w_out, sink, ws, bias, ang_r, ang_c, ctx_out):
    B, n, _ = h.shape
    L = hc.shape[1]
    q, k, v, bu, bv = jnp.split(h @ w_in, AB_SPLIT, axis=-1)
    q = axial_rope(q.reshape(B, n, A_HEADS, A_HEAD_DIM), ang_r[:, None], ang_c[:, None])
    k = axial_rope(k.reshape(B, n, A_KV_HEADS, A_HEAD_DIM), ang_r[:, None], ang_c[:, None])
    v = v.reshape(B, n, A_KV_HEADS, A_HEAD_DIM)
    if ctx_out:
        qc, kc, vc, buc, bvc = jnp.split(hc @ w_in, AB_SPLIT, axis=-1)
    else:
        kc, vc = jnp.split(hc @ w_in[:, A_Q_W:A_Q_W + 2 * A_KV_W], 2, axis=-1)
    kc = kc.reshape(B, L, A_KV_HEADS, A_HEAD_DIM)
    vc = vc.reshape(B, L, A_KV_HEADS, A_HEAD_DIM)
    o_a = window_attention(q, k, v, kc, vc, sink)
    o_b = chunk_gmlp(bu, bv, ws, bias)
    y = jnp.concatenate([o_a, o_b], axis=-1) @ w_out
    if not ctx_out:
        return y, None
    oc_a = dense_attention_sink(qc.reshape(B, L, A_HEADS, A_HEAD_DIM), kc, vc, sink)
    oc_b = chunk_gmlp(buc, bvc, ws, bias)
    return y, jnp.concatenate([oc_a, oc_b], axis=-1) @ w_out


def mixer_cd(h, hc, w_in, w_out, q_norm, kv_norm, w_uq, w_ukv, conv_w, ang_r, ang_c, ctx_out):
    cq, ckv, kr, db, dc, dx = jnp.split(h @ w_in, CD_SPLIT, axis=-1)
    q_nope, q_rope = mla_queries(cq, q_norm, w_uq)
    q_rope = axial_rope(q_rope, ang_r[:, None], ang_c[:, None])
    k_nope, v = mla_keys_values(ckv, kv_norm, w_ukv)
    k_rope = axial_rope(kr, ang_r, ang_c)
    if ctx_out:
        cqc, ckvc, krc, dbc, dcc, dxc = jnp.split(hc @ w_in, CD_SPLIT, axis=-1)
    else:
        ckvc, krc = jnp.split(hc @ w_in[:, C_Q_RANK:C_Q_RANK + C_KV_RANK + C_ROPE], (C_KV_RANK,), axis=-1)
    k_nope_c, v_c = mla_keys_values(ckvc, kv_norm, w_ukv)
    o_c = mla_attention(q_nope, q_rope,
                        jnp.concatenate([k_nope, k_nope_c], axis=1),
                        jnp.concatenate([k_rope, krc], axis=1),
                        jnp.concatenate([v, v_c], axis=1))
    o_d = db * short_conv(dc * dx, conv_w)
    y = jnp.concatenate([o_c, o_d], axis=-1) @ w_out
    if not ctx_out:
        return y, None
    qn_c, qr_c = mla_queries(cqc, q_norm, w_uq)
    oc_c = mla_attention(qn_c, qr_c, k_nope_c, krc, v_c)
    oc_d = dbc * short_conv(dcc * dxc, conv_w)
    return y, jnp.concatenate([oc_c, oc_d], axis=-1) @ w_out


def setup_inputs(seed: int = 0) -> dict:
    key = jax.random.key(seed)
    ks = jax.random.split(key, 23)

    def nrm(k, shape, scale):
        return jax.random.normal(k, shape, jnp.float32) * scale

    return {
        "x": nrm(ks[0], (BATCH, SEQ, D_MODEL), 1.0),
        "c": nrm(ks[1], (BATCH, D_MODEL), 1.0),
        "ctx": nrm(ks[2], (BATCH, CTX_LEN, D_MODEL), 1.0),
        "c_ctx": nrm(ks[3], (D_MODEL,), 1.0),
        "mod_w": nrm(ks[4], (DEPTH, D_MODEL, N_MOD * D_MODEL), 0.5 * D_MODEL ** -0.5),
        "mod_b": nrm(ks[5], (DEPTH, N_MOD * D_MODEL), 0.02),
        "norm_g": 1.0 + nrm(ks[6], (DEPTH, 3, D_MODEL), 0.02),
        "ffn_w1": nrm(ks[7], (DEPTH, 2, D_MODEL, FFN_DIM), D_MODEL ** -0.5),
        "ffn_w3": nrm(ks[8], (DEPTH, 2, D_MODEL, FFN_DIM), D_MODEL ** -0.5),
        "ffn_w2": nrm(ks[9], (DEPTH, 2, FFN_DIM, D_MODEL), FFN_DIM ** -0.5),
        "ab_w_in": nrm(ks[10], (N_AB, D_MODEL, AB_IN), D_MODEL ** -0.5),
        "ab_w_out": nrm(ks[11], (N_AB, AB_OUT, D_MODEL), AB_OUT ** -0.5),
        "a_sink": nrm(ks[12], (N_AB, A_HEADS), 0.5),
        "b_ws": nrm(ks[13], (N_AB, B_GROUPS, B_CHUNK, B_CHUNK), B_CHUNK ** -0.5),
        "b_bias": nrm(ks[14], (N_AB, B_GROUPS, B_CHUNK), 0.02),
        "cd_w_in": nrm(ks[15], (N_CD, D_MODEL, CD_IN), D_MODEL ** -0.5),
        "cd_w_out": nrm(ks[16], (N_CD, CD_OUT, D_MODEL), CD_OUT ** -0.5),
        "c_q_norm": 1.0 + nrm(ks[17], (N_CD, C_Q_RANK), 0.02),
        "c_kv_norm": 1.0 + nrm(ks[18], (N_CD, C_KV_RANK), 0.02),
        "c_w_uq": nrm(ks[19], (N_CD, C_Q_RANK, C_HEADS * (C_NOPE + C_ROPE)), C_Q_RANK ** -0.5),
        "c_w_ukv": nrm(ks[20], (N_CD, C_KV_RANK, C_HEADS * (C_NOPE + C_V)), C_KV_RANK ** -0.5),
        "d_conv_w": nrm(ks[21], (N_CD, D_CONV, D_W), D_CONV ** -0.5),
        "final_norm": 1.0 + nrm(ks[22], (D_MODEL,), 0.02),
    }


def reference(x, c, ctx, c_ctx, mod_w, mod_b, norm_g, ffn_w1, ffn_w3, ffn_w2, ab_w_in, ab_w_out,
              a_sink, b_ws, b_bias, cd_w_in, cd_w_out, c_q_norm, c_kv_norm, c_w_uq, c_w_ukv,
              d_conv_w, final_norm):
    ang_r, ang_c = axial_angles(x.shape[1])
    xl, xc = x, ctx
    for i in range(DEPTH):
        last = i == DEPTH - 1
        sh1, sc1, g1, sh2, sc2, g2, sh3, sc3, g3 = mod_chunks(c, mod_w[i], mod_b[i])
        ch1, cs1, cg1, ch2, cs2, cg2, ch3, cs3, cg3 = mod_chunks(c_ctx[None], mod_w[i], mod_b[i])
        xl = xl + 0.5 * g1 * swiglu(modulate(xl, norm_g[i, 0], sh1, sc1), ffn_w1[i, 0], ffn_w3[i, 0], ffn_w2[i, 0])
        xc = xc + 0.5 * cg1 * swiglu(modulate(xc, norm_g[i, 0], ch1, cs1), ffn_w1[i, 0], ffn_w3[i, 0], ffn_w2[i, 0])
        hl = modulate(xl, norm_g[i, 1], sh2, sc2)
        hc = modulate(xc, norm_g[i, 1], ch2, cs2)
        j = i // 2
        if i % 2 == 0:
            yl, yc = mixer_ab(hl, hc, ab_w_in[j], ab_w_out[j], a_sink[j], b_ws[j], b_bias[j],
                              ang_r, ang_c, not last)
        else:
            yl, yc = mixer_cd(hl, hc, cd_w_in[j], cd_w_out[j], c_q_norm[j], c_kv_norm[j], c_w_uq[j],
                              c_w_ukv[j], d_conv_w[j], ang_r, ang_c, not last)
        xl = xl + g2 * yl
        xl = xl + 0.5 * g3 * swiglu(modulate(xl, norm_g[i, 2], sh3, sc3), ffn_w1[i, 1], ffn_w3[i, 1], ffn_w2[i, 1])
        if not last:
            xc = xc + cg2 * yc
            xc = xc + 0.5 * cg3 * swiglu(modulate(xc, norm_g[i, 2], ch3, cs3), ffn_w1[i, 1], ffn_w3[i, 1], ffn_w2[i, 1])
    return rmsnorm(xl, final_norm)
```

```python
import math
import types
from contextlib import ExitStack
import numpy as np
import concourse.bass as bass
import concourse.mybir as mybir
from concourse.bass_utils import run_bass_kernel_spmd

F32 = mybir.dt.float32
BF16 = mybir.dt.bfloat16
AF = mybir.ActivationFunctionType
ALU = mybir.AluOpType

D = 2048
FFN = 5632
NFC = FFN // 128
KC = D // 128
SEQ = 8192
NOWN = 2048
NCTX = 256
NT = 2560
HP0, HN0, CT0 = 2048, 2176, 2304
EPS = 1e-6
ENGS = ("pe", "act", "dve", "pool", "sp")
N_SW = 4
DBG_NOPERM = False
ROPE_ADD_ENG = "pool"
DBG_SECT = None
SW_FRESH = False


def _freeze(fn):
    if fn is None or fn.__closure__ is None:
        return fn
    cells = []
    for c in fn.__closure__:
        try:
            cells.append(types.CellType(c.cell_contents))
        except ValueError:
            cells.append(c)
    return types.FunctionType(fn.__code__, fn.__globals__, fn.__name__, fn.__defaults__, tuple(cells))


class Op:
    __slots__ = ("eng", "fn", "deps", "idx", "ms", "dma", "sem", "semval", "msval", "tag")


class Sched:
    def __init__(self, nc, n_dma_sems=24):
        if SW_FRESH:
            n_dma_sems = 8
        self.nc = nc
        self.ops = {e: [] for e in ENGS}
        self.lastw = {}
        self.readers = {}
        self.n_dma_sems = n_dma_sems
        self.dma_rr = 0
        self.dma_count = [0] * n_dma_sems
        self.dma_last = [None] * n_dma_sems
        self.sw_keys = {}
        self.sw_rr = 0
        self.sw_last = {}

    def add(self, eng, fn, reads=(), writes=(), dma=False, tag=None):
        op = Op()
        op.eng, op.fn, op.dma, op.ms, op.tag = eng, _freeze(fn), dma, False, tag
        op.sem = op.semval = op.msval = None
        sw = dma and eng == "pool" and SW_FRESH
        if eng in ("act", "dve", "pool") and not dma:
            extra = [("psr", r[1]) for r in reads if isinstance(r, tuple) and len(r) == 2 and r[0] == "ps"]
            if extra:
                writes = list(writes) + extra
        deps = set()
        for r in reads:
            w = self.lastw.get(r)
            if w is not None:
                deps.add(w)
        for k in writes:
            w = self.lastw.get(k)
            if w is not None:
                deps.add(w)
            for rd in self.readers.get(k, ()):
                deps.add(rd)
        if sw:
            key = len(self.sw_keys)
            self.sw_keys[key] = key
            op.sem, op.semval = ("sw", key), 16
            op.tag = False
            self.sw_last[key] = op
        elif dma:
            if eng == "pool":
                s = self.n_dma_sems - N_SW + self.sw_rr
                self.sw_rr = (self.sw_rr + 1) % N_SW
            else:
                s = self.dma_rr
                self.dma_rr = (self.dma_rr + 1) % (self.n_dma_sems - N_SW)
            if self.dma_last[s] is not None:
                deps.add(self.dma_last[s])
            self.dma_count[s] += 1
            op.sem, op.semval = s, 16 * self.dma_count[s]
            self.dma_last[s] = op
        if eng == "pe":
            deps = {d for d in deps if d.dma or d.eng != "pe"}
        op.deps = deps
        for d in deps:
            if not d.dma:
                d.ms = True
        for r in reads:
            self.readers.setdefault(r, []).append(op)
        for k in writes:
            self.lastw[k] = op
            self.readers[k] = []
        op.idx = len(self.ops[eng])
        self.ops[eng].append(op)
        return op

    def add_cc(self, fn, reads=(), writes=()):
        op = self.add("pool", fn, reads=reads, writes=[("cc_issue",)])
        op.tag = "cc"
        self.n_cc = getattr(self, "n_cc", 0) + 1
        op.semval = self.n_cc
        return op

    def barrier(self):
        lasts = [self.ops[e][-1] for e in ENGS if self.ops[e]]
        lasts = [x for x in lasts if not x.dma and x.fn is not None]
        dmas = [d for d in self.dma_last if d is not None] + list(self.sw_last.values())
        for e in ENGS:
            op = Op()
            op.eng, op.fn, op.dma, op.ms, op.tag = e, None, False, False, "barrier"
            op.sem = op.semval = op.msval = None
            op.deps = set(x for x in lasts if x.eng != e) | set(dmas)
            for d in op.deps:
                if not d.dma:
                    d.ms = True
            op.idx = len(self.ops[e])
            self.ops[e].append(op)
        self.lastw.clear()
        self.readers.clear()

    def emit(self, stack):
        nc = self.nc
        esem = {e: stack.enter_context(nc.semaphore("s_" + e)) for e in ENGS if e != "sp"}
        dsem = [stack.enter_context(nc.semaphore("d_%d" % i)) for i in range(self.n_dma_sems)]
        wsem = [stack.enter_context(nc.semaphore("w_%d" % i)) for i in range(len(self.sw_keys))]
        ccsem = stack.enter_context(nc.semaphore("ccsem"))
        for e in ENGS:
            c = 0
            for op in self.ops[e]:
                if op.ms and not op.dma:
                    c += 1
                    op.msval = c
        ops = self.ops
        final_dma = [(dsem[i], 16 * self.dma_count[i]) for i in range(self.n_dma_sems) if self.dma_count[i]]

        def run(ename, eng):
            known = {}
            for op in ops[ename]:
                for d in op.deps:
                    if d.dma and isinstance(d.sem, tuple):
                        key, sem, val = ("w", d.sem[1], id(d)), wsem[d.sem[1]], 16
                    elif d.dma:
                        key, sem, val = ("d", d.sem), dsem[d.sem], d.semval
                    else:
                        key, sem, val = d.eng, esem[d.eng], d.msval
                    if known.get(key, 0) < val:
                        eng.wait_ge(sem, val)
                        known[key] = val
                if op.fn is None:
                    continue
                if op.dma and isinstance(op.sem, tuple):
                    if op.tag:
                        eng.wait_ge(wsem[op.sem[1]], 16)
                        eng.sem_clear(wsem[op.sem[1]])
                    op.fn(eng).then_inc(wsem[op.sem[1]], 16)
                    continue
                ins = op.fn(eng)
                if op.tag == "cc":
                    ins.then_inc(ccsem, 1)
                    eng.wait_ge(ccsem, op.semval)
                    if op.ms:
                        eng.memset(self.cc_dummy[:], 0.0).then_inc(esem[ename], 1)
                    continue
                if op.dma:
                    ins.then_inc(dsem[op.sem], 16)
                elif op.ms:
                    ins.then_inc(esem[ename], 1)
            if ename == "sp":
                for sem, val in final_dma:
                    eng.wait_ge(sem, val)

        with nc.Block() as block:
            @block.tensor
            def _(e):
                run("pe", e)

            @block.scalar
            def _(e):
                run("act", e)

            @block.vector
            def _(e):
                run("dve", e)

            @block.gpsimd
            def _(e):
                run("pool", e)

            @block.sync
            def _(e):
                run("sp", e)


class K:
    def __init__(self, ext_in, ext_out):
        self.nc = bass.Bass("TRN2", target_bir_lowering=False)
        self.S = Sched(self.nc)
        self.ext_in, self.ext_out = set(ext_in), set(ext_out)
        self.dr = {}
        self.st = ExitStack()
        self.uid = 0
        self.ffn_sel = {(0, 0): 0, (0, 1): 1, (1, 0): 2, (1, 1): 3}

    def dram(self, name, shape, dtype=F32):
        if name in self.dr:
            return self.dr[name]
        kind = "ExternalInput" if name in self.ext_in else ("ExternalOutput" if name in self.ext_out else "Internal")
        t = self.nc.dram_tensor(name, list(shape), dtype, kind=kind).ap()
        self.dr[name] = t
        return t

    def sb(self, name, shape, dtype):
        return self.st.enter_context(self.nc.sbuf_tensor(name, list(shape), dtype))

    def alloc(self, st):
        self.uid += 1
        u = self.uid
        return lambda n, sh, dt: st.enter_context(self.nc.sbuf_tensor("%s_u%d" % (n, u), list(sh), dt))

    def psum(self, name):
        return self.st.enter_context(self.nc.psum_tensor(name, [128, 512], F32))


def mm(S, ps_ap, lhsT, rhs, start, stop, reads, pskey):
    S.add("pe", lambda e: e.matmul(ps_ap, lhsT, rhs, start=start, stop=stop), reads=reads, writes=[pskey])


def phase_setup(k):
    nc, S = k.nc, k.S
    k.ps = [k.psum("ps%d" % i) for i in range(8)]
    k.ones_f = k.sb("ones_f", [128, 128], F32)
    k.ones_b = k.sb("ones_b", [128, 128], BF16)
    S.add("pool", lambda e: e.memset(k.ones_f[:], 1.0), writes=["ones_f"])
    S.add("pool", lambda e: e.memset(k.ones_b[:], 1.0), writes=["ones_b"])
    k.eps_t = k.sb("eps_t", [128, 1], F32)
    S.cc_dummy = k.sb("cc_dummy", [128, 8], F32)
    S.add("pool", lambda e: e.memset(k.eps_t[:], EPS), writes=["eps_t"])


def phase_mod(k, layers=(0, 1)):
    nc, S = k.nc, k.S
    cT = k.dram("cT", [D, 2])
    nl = len(layers)
    mod_w = k.dram("mod_w", [nl, D, 9 * D])
    mod_bT = k.dram("mod_bT", [nl, 128, 144])
    norm_gT = k.dram("norm_gT", [nl, 128, 48])
    k.modT = [k.sb("modT%d" % l, [128, 144, 2], F32) for l in range(2)]
    k.modA = [k.sb("modA%d" % l, [128, 48, 2], F32) for l in range(2)]
    k.modG = [k.sb("modG%d" % l, [128, 48, 2], F32) for l in range(2)]
    with ExitStack() as st:
        cin = st.enter_context(nc.sbuf_tensor("cin_sb", [128, KC, 2], F32))
        scT = st.enter_context(nc.sbuf_tensor("scT", [128, KC, 2], BF16))
        mb = st.enter_context(nc.sbuf_tensor("mb", [128, 144], F32))
        ng = st.enter_context(nc.sbuf_tensor("ng", [128, 48], F32))
        wm = [st.enter_context(nc.sbuf_tensor("wm%d" % i, [128, KC, 512], BF16)) for i in range(2)]
        S.add("sp", lambda e: e.dma_start(out=cin[:], in_=cT.rearrange("(kc p) n -> p kc n", p=128)), writes=["cin"], dma=True)
        S.add("act", lambda e: e.activation(out=scT[:], in_=cin[:], func=AF.Silu), reads=["cin"], writes=["scT"])
        si = 0
        for li, l in enumerate(layers):
            S.add("sp", lambda e, li=li: e.dma_start(out=mb[:], in_=mod_bT[li]), writes=["mb"], dma=True)
            S.add("sp", lambda e, li=li: e.dma_start(out=ng[:], in_=norm_gT[li]), writes=["ng"], dma=True)
            ps = k.ps[l]
            for sl in range(36):
                w = wm[si % 2]
                wkey = "wm%d" % (si % 2)
                si += 1
                S.add("pool", lambda e, w=w, li=li, sl=sl: e.dma_start(
                    out=w[:], in_=mod_w[li, :, sl * 512:(sl + 1) * 512].rearrange("(kc p) n -> p kc n", p=128)),
                    writes=[wkey], dma=True)
                for jj in range(4):
                    j = sl * 4 + jj
                    for kc in range(KC):
                        mm(S, ps[:, 2 * j:2 * j + 2], w[:, kc, jj * 128:(jj + 1) * 128], scT[:, kc, :],
                           kc == 0, kc == KC - 1, [wkey, "scT"], ("ps", l))
            modT = k.modT[l]
            for cnd in range(2):
                S.add("dve", lambda e, modT=modT, ps=ps, cnd=cnd: e.tensor_tensor(
                    out=modT[:, :, cnd], in0=ps[:, 0:288].rearrange("p (j c) -> p j c", c=2)[:, :, cnd], in1=mb[:], op=ALU.add),
                    reads=[("ps", l), "mb"], writes=[("modT", l)])
            for s in range(3):
                for cnd in range(2):
                    S.add("dve", lambda e, l=l, s=s, cnd=cnd, modT=modT: e.scalar_tensor_tensor(
                        out=k.modA[l][:, s * 16:(s + 1) * 16, cnd], in0=modT[:, (3 * s + 1) * 16:(3 * s + 2) * 16, cnd],
                        scalar=1.0, op0=ALU.add, in1=ng[:, s * 16:(s + 1) * 16], op1=ALU.mult),
                        reads=[("modT", l), "ng"], writes=[("modA", l)])
                    S.add("dve", lambda e, l=l, s=s, cnd=cnd, modT=modT: e.tensor_scalar(
                        out=k.modG[l][:, s * 16:(s + 1) * 16, cnd], in0=modT[:, (3 * s + 2) * 16:(3 * s + 3) * 16, cnd],
                        scalar1=(1.0 if s == 1 else 0.5), scalar2=None, op0=ALU.mult),
                        reads=[("modT", l)], writes=[("modG", l)])
        S.barrier()


def load_x_tile(k, xres, x_in, t0, T):
    S = k.S
    for c in range(KC):
        S.add("sp", lambda e, c=c: e.dma_start(out=xres[:, c, :T], in_=x_in[c * 128:(c + 1) * 128, t0:t0 + T]),
              writes=[("xres", c)], dma=True)


def norm_mod(k, xres, hT, T, segs, l, s, tmp, rstd, sqb):
    S = k.S
    ps = k.ps[6]
    for c in range(KC):
        sq = sqb[c % 2]
        S.add("act", lambda e, c=c, sq=sq: e.activation(out=sq[:, :T], in_=xres[:, c, :T], func=AF.Square),
              reads=[("xres", c)], writes=[("sq", c % 2)])
        mm(S, ps[:, :T], k.ones_f[:], sq[:, :T], c == 0, c == KC - 1, [("sq", c % 2), "ones_f"], ("ps", 6))
    S.add("act", lambda e: e.activation(out=rstd[:, :T], in_=ps[:, :T], func=AF.Sqrt, scale=1.0 / D, bias=k.eps_t[:]),
          reads=[("ps", 6), "eps_t"], writes=["rstd"])
    S.add("dve", lambda e: e.reciprocal(out=rstd[:, :T], in_=rstd[:, :T]), reads=["rstd"], writes=["rstd"])
    A, SH = k.modA[l], k.modT[l]
    for c in range(KC):
        tb = tmp[c % 2]
        for (o, n, cnd) in segs:
            S.add("dve", lambda e, c=c, o=o, n=n, cnd=cnd, tb=tb: e.scalar_tensor_tensor(
                out=tb[:, o:o + n], in0=xres[:, c, o:o + n], scalar=A[:, s * 16 + c, cnd:cnd + 1], op0=ALU.mult,
                in1=rstd[:, o:o + n], op1=ALU.mult),
                reads=[("xres", c), "rstd", ("modA", l)], writes=[("tmp", c % 2)])
            S.add("act", lambda e, c=c, o=o, n=n, cnd=cnd, tb=tb: e.activation(
                out=hT[:, c, o:o + n], in_=tb[:, o:o + n], func=AF.Identity,
                bias=SH[:, 3 * s * 16 + c, cnd:cnd + 1], scale=1.0),
                reads=[("tmp", c % 2), ("modT", l)], writes=[("hT", c)])


def residual_store(k, ps_ap, pskey, xres, c, T, segs, l, s, x_out, t0):
    S = k.S
    G = k.modG[l]
    for (o, n, cnd) in segs:
        S.add("dve", lambda e, o=o, n=n, cnd=cnd: e.scalar_tensor_tensor(
            out=xres[:, c, o:o + n], in0=ps_ap[:, o:o + n], scalar=G[:, s * 16 + c, cnd:cnd + 1], op0=ALU.mult,
            in1=xres[:, c, o:o + n], op1=ALU.add),
            reads=[pskey, ("xres", c), ("modG", l)], writes=[("xres", c)])
    if x_out is not None:
        S.add("sp", lambda e: e.dma_start(out=x_out[c * 128:(c + 1) * 128, t0:t0 + T], in_=xres[:, c, :T]),
              reads=[("xres", c)], writes=[("xdram", x_out.tensor.name, t0, c)], dma=True)


def phase_ffn(k, l, widx, x_in, x_out, tiles, final_out=None):
    nc, S = k.nc, k.S
    s = 0 if widx == 0 else 2
    nsel = len(k.ffn_sel)
    wi_ = k.ffn_sel[(l, widx)]
    w1 = k.dram("ffn_w1", [nsel, D, FFN])[wi_]
    w3 = k.dram("ffn_w3", [nsel, D, FFN])[wi_]
    w2 = k.dram("ffn_w2", [nsel, FFN, D])[wi_]
    with ExitStack() as st:
        al = k.alloc(st)
        xres = al("xres", [128, KC, 512], F32)
        hT = al("hT", [128, KC, 512], BF16)
        gT = al("gT", [128, NFC, 512], BF16)
        tmp = [al("tmp%d" % i, [128, 512], F32) for i in range(2)]
        sqb = [al("sq%d" % i, [128, 512], F32) for i in range(2)]
        rstd = al("rstd", [128, 512], F32)
        su = [al("su%d" % i, [128, 512], BF16) for i in range(2)]
        wa = [al("wa%d" % i, [128, KC, 256], BF16) for i in range(2)]
        wb = [al("wb%d" % i, [128, KC, 256], BF16) for i in range(2)]
        wc = [al("wc%d" % i, [128, NFC, 256], BF16) for i in range(2)]
        na = nc_ = 0
        for (t0, T, segs) in tiles:
            load_x_tile(k, xres, x_in, t0, T)
            norm_mod(k, xres, hT, T, segs, l, s, tmp, rstd, sqb)
            hreads = [("hT", c) for c in range(KC)]
            for sl in range(NFC // 2):
                a, b = wa[na % 2], wb[na % 2]
                ka, kb = "wa%d" % (na % 2), "wb%d" % (na % 2)
                na += 1
                S.add("pool", lambda e, a=a, sl=sl: e.dma_start(
                    out=a[:], in_=w1[:, sl * 256:(sl + 1) * 256].rearrange("(kc p) n -> p kc n", p=128)),
                    writes=[ka], dma=True)
                S.add("pool", lambda e, b=b, sl=sl: e.dma_start(
                    out=b[:], in_=w3[:, sl * 256:(sl + 1) * 256].rearrange("(kc p) n -> p kc n", p=128)),
                    writes=[kb], dma=True)
                for jj in range(2):
                    fc = sl * 2 + jj
                    pu, pv = k.ps[fc % 2], k.ps[2 + fc % 2]
                    for kc in range(KC):
                        mm(S, pu[:, :T], a[:, kc, jj * 128:(jj + 1) * 128], hT[:, kc, :T], kc == 0, kc == KC - 1,
                           [ka, ("hT", kc)], ("ps", fc % 2))
                    for kc in range(KC):
                        mm(S, pv[:, :T], b[:, kc, jj * 128:(jj + 1) * 128], hT[:, kc, :T], kc == 0, kc == KC - 1,
                           [kb, ("hT", kc)], ("ps", 2 + fc % 2))
                    sut = su[fc % 2]
                    S.add("act", lambda e, pu=pu, sut=sut: e.activation(out=sut[:, :T], in_=pu[:, :T], func=AF.Silu),
                          reads=[("ps", fc % 2)], writes=[("su", fc % 2)])
                    S.add("dve", lambda e, pv=pv, sut=sut, fc=fc: e.tensor_tensor(
                        out=gT[:, fc, :T], in0=pv[:, :T], in1=sut[:, :T], op=ALU.mult),
                        reads=[("ps", 2 + fc % 2), ("su", fc % 2)], writes=[("gT", fc)])
            for ds in range(KC // 2):
                w = wc[nc_ % 2]
                kw = "wc%d" % (nc_ % 2)
                nc_ += 1
                S.add("pool", lambda e, w=w, ds=ds: e.dma_start(
                    out=w[:], in_=w2[:, ds * 256:(ds + 1) * 256].rearrange("(fc p) n -> p fc n", p=128)),
                    writes=[kw], dma=True)
                for jj in range(2):
                    c = ds * 2 + jj
                    py = k.ps[4 + c % 2]
                    for fc in range(NFC):
                        mm(S, py[:, :T], w[:, fc, jj * 128:(jj + 1) * 128], gT[:, fc, :T], fc == 0, fc == NFC - 1,
                           [kw, ("gT", fc)], ("ps", 4 + c % 2))
                    residual_store(k, py, ("ps", 4 + c % 2), xres, c, T, segs, l, s, x_out, t0)
            if final_out is not None:
                final_norm_store(k, xres, T, t0, final_out, tmp, rstd, sqb)
        S.barrier()


def final_norm_store(k, xres, T, t0, out, tmp, rstd, sqb):
    S = k.S
    ps = k.ps[6]
    fn = k.fnT
    for c in range(KC):
        sq = sqb[c % 2]
        S.add("act", lambda e, c=c, sq=sq: e.activation(out=sq[:, :T], in_=xres[:, c, :T], func=AF.Square),
              reads=[("xres", c)], writes=[("sq", c % 2)])
        mm(S, ps[:, :T], k.ones_f[:], sq[:, :T], c == 0, c == KC - 1, [("sq", c % 2), "ones_f"], ("ps", 6))
    S.add("act", lambda e: e.activation(out=rstd[:, :T], in_=ps[:, :T], func=AF.Sqrt, scale=1.0 / D, bias=k.eps_t[:]),
          reads=[("ps", 6), "eps_t"], writes=["rstd"])
    S.add("dve", lambda e: e.reciprocal(out=rstd[:, :T], in_=rstd[:, :T]), reads=["rstd"], writes=["rstd"])
    for c in range(KC):
        S.add("dve", lambda e, c=c: e.scalar_tensor_tensor(
            out=xres[:, c, :T], in0=xres[:, c, :T], scalar=fn[:, c:c + 1], op0=ALU.mult, in1=rstd[:, :T], op1=ALU.mult),
            reads=[("xres", c), "rstd", "fnT"], writes=[("xres", c)])
        S.add("sp", lambda e, c=c: e.dma_start(out=out[c * 128:(c + 1) * 128, t0:t0 + T], in_=xres[:, c, :T]),
              reads=[("xres", c)], writes=[("odram", t0, c)], dma=True)


GC1, GC2 = 0.044715, 1.5957691216057308


def gelu_from_psum(k, ps_ap, pskey, out_ap, outkeys, n, t1, t2, t1k, t2k):
    S = k.S
    S.add("act", lambda e: e.activation(out=t1, in_=ps_ap, func=AF.Square), reads=[pskey], writes=[t1k])
    S.add("dve", lambda e: e.tensor_scalar(out=t1, in0=t1, scalar1=GC1, scalar2=1.0, op0=ALU.mult, op1=ALU.add),
          reads=[t1k], writes=[t1k])
    S.add("dve", lambda e: e.tensor_tensor(out=t1, in0=ps_ap, in1=t1, op=ALU.mult), reads=[pskey, t1k], writes=[t1k])
    S.add("act", lambda e: e.activation(out=t2, in_=t1, func=AF.Sigmoid, scale=GC2), reads=[t1k], writes=[t2k])
    S.add("dve", lambda e: e.tensor_tensor(out=out_ap, in0=ps_ap, in1=t2, op=ALU.mult), reads=[pskey, t2k], writes=outkeys)


def rope_from_psum(k, ps_ap, pskey, ps2, ps2key, cos, sin, T, out_ap, outkeys, qraw, t1, t2, keys):
    S = k.S
    qk, t1k, t2k = keys
    S.add("act", lambda e: e.activation(out=qraw[:, :T], in_=ps_ap, func=AF.Identity), reads=[pskey], writes=[qk])
    S.add("dve", lambda e: e.tensor_tensor(out=t1[:, :T], in0=ps_ap, in1=cos, op=ALU.mult), reads=[pskey, "rope"], writes=[t1k])
    mm(S, ps2[:, :T], (k.ones_b if DBG_NOPERM else k.permM)[:], qraw[:, :T], True, True, [qk, "permM"], ps2key)
    S.add("dve", lambda e: e.tensor_tensor(out=t2[:, :T], in0=ps2[:, :T], in1=sin, op=ALU.mult), reads=[ps2key, "rope"], writes=[t2k])
    S.add(ROPE_ADD_ENG, lambda e: e.tensor_tensor(out=out_ap, in0=t1[:, :T], in1=t2[:, :T], op=ALU.add), reads=[t1k, t2k], writes=outkeys)


def load_rope_consts(k):
    nc, S = k.nc, k.S
    permD = k.dram("permM", [128, 128])
    k.permM = k.sb("permM_sb", [128, 128], BF16)
    S.add("pool", lambda e: e.dma_start(out=k.permM[:], in_=permD), writes=["permM"], dma=True)


def phase_ab_in(k, x_in, tiles):
    nc, S = k.nc, k.S
    l, s = 0, 1
    w_in = k.dram("ab_w_in", [1, D, 3328])[0]
    cosD, sinD = k.dram("cosT", [128, NT]), k.dram("sinT", [128, NT])
    qT = k.dram("qT", [1024, NT], BF16)
    kT = k.dram("kT", [2, 128, NT], BF16)
    vtok = k.dram("vtok", [NT, 128], BF16)
    guT = k.dram("guT", [1024, NT], BF16)
    gvtok = k.dram("gvtok", [NT, 1024], BF16)
    with ExitStack() as st:
        al = k.alloc(st)
        xres = al("xres", [128, KC, 512], F32)
        hT = al("hT", [128, KC, 512], BF16)
        tmp = [al("tmp%d" % i, [128, 512], F32) for i in range(2)]
        sqb = [al("sq%d" % i, [128, 512], F32) for i in range(2)]
        rstd = al("rstd", [128, 512], F32)
        cos, sin = al("cos", [128, 512], F32), al("sin", [128, 512], F32)
        wtm = al("wtm", [128, KC, 1152], BF16)
        kdw = al("kdw", [128, KC, 256], BF16)
        wa = [al("wa%d" % i, [128, KC, 256], BF16) for i in range(2)]
        qraw = [al("qraw%d" % i, [128, 512], BF16) for i in range(2)]
        r1 = [al("r1_%d" % i, [128, 512], F32) for i in range(2)]
        r2 = [al("r2_%d" % i, [128, 512], F32) for i in range(2)]
        ob = [al("ob%d" % i, [128, 512], BF16) for i in range(2)]
        obt = [al("obt%d" % i, [128, 1152], BF16) for i in range(2)]
        for i, (c0, n) in enumerate([(1152, 128), (2304, 512), (2816, 512)]):
            o = [0, 128, 640][i]
            S.add("pool", lambda e, c0=c0, n=n, o=o: e.dma_start(
                out=wtm[:, :, o:o + n], in_=w_in[:, c0:c0 + n].rearrange("(kc p) n -> p kc n", p=128)),
                writes=[("wtm", i)], dma=True)
        for i in range(4):
            c0 = 1024 + 64 * (i // 2)
            S.add("pool", lambda e, c0=c0, i=i: e.dma_start(
                out=kdw[:, :, i * 64:(i + 1) * 64], in_=w_in[:, c0:c0 + 64].rearrange("(kc p) n -> p kc n", p=128)),
                writes=[("kdw", i)], dma=True)
        na = 0
        cnt = 0
        for (t0, T, segs) in tiles:
            load_x_tile(k, xres, x_in, t0, T)
            S.add("sp", lambda e, t0=t0, T=T: e.dma_start(out=cos[:, :T], in_=cosD[:, t0:t0 + T]), writes=["rope"], dma=True)
            S.add("sp", lambda e, t0=t0, T=T: e.dma_start(out=sin[:, :T], in_=sinD[:, t0:t0 + T]), writes=["rope"], dma=True)
            norm_mod(k, xres, hT, T, segs, l, s, tmp, rstd, sqb)
            hreads = [("hT", c) for c in range(KC)]
            for sl in range(8):
                if DBG_SECT is not None and ("q" if sl < 4 else "gu") not in DBG_SECT:
                    continue
                a = wa[na % 2]
                ka = "wa%d" % (na % 2)
                na += 1
                c0 = sl * 256 if sl < 4 else 1280 + (sl - 4) * 256
                S.add("pool", lambda e, a=a, c0=c0: e.dma_start(
                    out=a[:], in_=w_in[:, c0:c0 + 256].rearrange("(kc p) n -> p kc n", p=128)), writes=[ka], dma=True)
                for jj in range(2):
                    ch = (sl % 4) * 2 + jj
                    i2 = cnt % 2
                    cnt += 1
                    pq = k.ps[i2]
                    for kc in range(KC):
                        mm(S, pq[:, :T], a[:, kc, jj * 128:(jj + 1) * 128], hT[:, kc, :T], kc == 0, kc == KC - 1,
                           [ka, ("hT", kc)], ("ps", i2))
                    o_ = ob[i2]
                    if sl < 4:
                        rope_from_psum(k, pq[:, :T], ("ps", i2), k.ps[2 + i2], ("ps", 2 + i2), cos[:, :T], sin[:, :T], T,
                                       o_[:, :T], [("ob", i2)], qraw[i2], r1[i2], r2[i2], (("qraw", i2), ("r1", i2), ("r2", i2)))
                        dst = qT[ch * 128:(ch + 1) * 128, t0:t0 + T]
                    else:
                        gelu_from_psum(k, pq[:, :T], ("ps", i2), o_[:, :T], [("ob", i2)], T, r1[i2][:, :T], r2[i2][:, :T],
                                       ("r1", i2), ("r2", i2))
                        dst = guT[ch * 128:(ch + 1) * 128, t0:t0 + T]
                    S.add("sp", lambda e, dst=dst, o_=o_, T=T: e.dma_start(out=dst, in_=o_[:, :T]),
                          reads=[("ob", i2)], writes=[("dr", dst.tensor.name, t0, ch)], dma=True)
            for kv in range(2):
                if DBG_SECT is not None and "k" not in DBG_SECT:
                    continue
                i2 = cnt % 2
                cnt += 1
                pq = k.ps[i2]
                for kc in range(KC):
                    mm(S, pq[:, :T], kdw[:, kc, kv * 128:(kv + 1) * 128], hT[:, kc, :T], kc == 0, kc == KC - 1,
                       [("kdw", 2 * kv), ("kdw", 2 * kv + 1), ("hT", kc)], ("ps", i2))
                o_ = ob[i2]
                rope_from_psum(k, pq[:, :T], ("ps", i2), k.ps[2 + i2], ("ps", 2 + i2), cos[:, :T], sin[:, :T], T,
                               o_[:, :T], [("ob", i2)], qraw[i2], r1[i2], r2[i2], (("qraw", i2), ("r1", i2), ("r2", i2)))
                dst = kT[kv, :, t0:t0 + T]
                S.add("sp", lambda e, dst=dst, o_=o_, T=T: e.dma_start(out=dst, in_=o_[:, :T]),
                      reads=[("ob", i2)], writes=[("dr", "kT", t0, kv)], dma=True)
            for tb in range(T // 128):
                if DBG_SECT is not None and "tok" not in DBG_SECT:
                    continue
                ot = obt[tb % 2]
                okey = ("obt", tb % 2)
                pv = k.ps[4 + tb % 2]
                for kc in range(KC):
                    mm(S, pv[:, :128], hT[:, kc, tb * 128:(tb + 1) * 128], wtm[:, kc, 0:128], kc == 0, kc == KC - 1,
                       [("wtm", 0), ("hT", kc)], ("ps", 4 + tb % 2))
                S.add("act", lambda e, pv=pv, ot=ot: e.activation(out=ot[:, 0:128], in_=pv[:, :128], func=AF.Identity),
                      reads=[("ps", 4 + tb % 2)], writes=[okey])
                for hf in range(2):
                    pg = k.ps[6 + hf]
                    for kc in range(KC):
                        mm(S, pg[:, :], hT[:, kc, tb * 128:(tb + 1) * 128], wtm[:, kc, 128 + hf * 512:128 + (hf + 1) * 512],
                           kc == 0, kc == KC - 1, [("wtm", 1 + hf), ("hT", kc)], ("ps", 6 + hf))
                    gelu_from_psum(k, pg[:, :], ("ps", 6 + hf), ot[:, 128 + hf * 512:128 + (hf + 1) * 512], [okey], 512,
                                   r1[hf][:, :], r2[hf][:, :], ("r1", hf), ("r2", hf))
                r0 = t0 + tb * 128
                S.add("sp", lambda e, ot=ot, r0=r0: e.dma_start(out=vtok[r0:r0 + 128, :], in_=ot[:, 0:128]),
                      reads=[okey], writes=[("dr", "vtok", r0)], dma=True)
                S.add("sp", lambda e, ot=ot, r0=r0: e.dma_start(out=gvtok[r0:r0 + 128, :], in_=ot[:, 128:1152]),
                      reads=[okey], writes=[("dr", "gvtok", r0)], dma=True)
        S.barrier()


def phase_ab_mix(k):
    nc, S = k.nc, k.S
    qT = k.dram("qT", [1024, NT], BF16)
    kT = k.dram("kT", [2, 128, NT], BF16)
    vtok = k.dram("vtok", [NT, 128], BF16)
    guT = k.dram("guT", [1024, NT], BF16)
    gvtok = k.dram("gvtok", [NT, 1024], BF16)
    oT = k.dram("oT", [2048, NT], BF16)
    masksD = k.dram("masks", [128, 4, 128])
    sinkD = k.dram("sinkT", [128, 8])
    wsD = k.dram("b_wsT", [128, 8, 128])
    bbD = k.dram("b_biasbc", [128, 8, 128])
    scale = 1.0 / 8.0
    with ExitStack() as st:
        al = k.alloc(st)
        kTs = al("kTs", [128, 2, NT], BF16)
        vts = al("vts", [128, 20, 128], BF16)
        masks = al("masks", [128, 4, 128], BF16)
        sink = al("sink", [128, 8], F32)
        esbc = al("esbc", [128, 8, 128], F32)
        wsT = al("wsT", [128, 8, 128], BF16)
        bbc = al("bbc", [128, 8, 128], F32)
        qb_ = [al("qb%d" % i, [128, 8, 128], BF16) for i in range(2)]
        gub = [al("gub%d" % i, [128, 8, 128], BF16) for i in range(2)]
        gvb = [al("gvb%d" % i, [128, 1024], BF16) for i in range(2)]
        pT = [al("pT%d" % i, [128, 2, 4, 128], BF16) for i in range(3)]
        rr = al("rr", [128, 512], F32)
        gt = al("gt", [128, 512], F32)
        ob = [al("oblk%d" % i, [128, 16, 128], BF16) for i in range(2)]
        S.add("sp", lambda e: e.dma_start(out=kTs[:], in_=kT.rearrange("v p t -> p v t")), writes=["kTs"], dma=True)
        S.add("sp", lambda e: e.dma_start(out=vts[:], in_=vtok.rearrange("(b p) d -> p b d", p=128)), writes=["vts"], dma=True)
        S.add("pool", lambda e: e.dma_start(out=masks[:], in_=masksD), writes=["masks"], dma=True)
        S.add("pool", lambda e: e.dma_start(out=wsT[:], in_=wsD), writes=["wsT"], dma=True)
        S.add("sp", lambda e: e.dma_start(out=bbc[:], in_=bbD), writes=["bbc"], dma=True)
        S.add("sp", lambda e: e.dma_start(out=sink[:], in_=sinkD), writes=["sink"], dma=True)
        S.add("act", lambda e: e.activation(out=sink[:], in_=sink[:], func=AF.Exp), reads=["sink"], writes=["sink"])
        S.add("dve", lambda e: e.tensor_copy(out=esbc[:], in_=sink[:].unsqueeze(2).broadcast_to([128, 8, 128])),
              reads=["sink"], writes=["esbc"])
        blocks = list(range(16)) + [18, 19]
        npT = 0
        for bi, b in enumerate(blocks):
            i2 = bi % 2
            c0 = b * 128
            qb, gu, gv, o_ = qb_[i2], gub[i2], gvb[i2], ob[i2]
            S.add("sp", lambda e, qb=qb, c0=c0: e.dma_start(out=qb[:], in_=qT[:, c0:c0 + 128].rearrange("(c p) t -> p c t", p=128)),
                  writes=[("qb", i2)], dma=True)
            S.add("sp", lambda e, gu=gu, c0=c0: e.dma_start(out=gu[:], in_=guT[:, c0:c0 + 128].rearrange("(c p) t -> p c t", p=128)),
                  writes=[("gub", i2)], dma=True)
            S.add("sp", lambda e, gv=gv, c0=c0: e.dma_start(out=gv[:], in_=gvtok[c0:c0 + 128, :]), writes=[("gvb", i2)], dma=True)
            if b < 16:
                kl = [((b - 1) if b > 0 else 16, 0 if b > 0 else 2), (b, None), ((b + 1) if b < 15 else 17, 1 if b < 15 else 3),
                      (18, None), (19, None)]
            else:
                kl = [(18, None), (19, None)]
            for kv in range(2):
                po, pd = k.ps[4], k.ps[5]
                for ji, (kb, mi) in enumerate(kl):
                    pa, pb = k.ps[2 * (ji % 2)], k.ps[2 * (ji % 2) + 1]
                    ka, kb_ = ("ps", 2 * (ji % 2)), ("ps", 2 * (ji % 2) + 1)
                    mm(S, pa[:, :], kTs[0:64, kv, kb * 128:(kb + 1) * 128], qb[0:64, kv * 4:(kv + 1) * 4, :], True, True,
                       ["kTs", ("qb", i2)], ka)
                    mm(S, pb[:, :], kTs[64:128, kv, kb * 128:(kb + 1) * 128], qb[64:128, kv * 4:(kv + 1) * 4, :], True, True,
                       ["kTs", ("qb", i2)], kb_)
                    p = pT[npT % 3]
                    pk = ("pT", npT % 3)
                    npT += 1
                    S.add("act", lambda e, p=p, pa=pa: e.activation(out=p[:, 0, :, :], in_=pa[:, :].rearrange("p (c t) -> p c t", c=4),
                                                                    func=AF.Exp, scale=scale), reads=[ka], writes=[pk])
                    S.add("act", lambda e, p=p, pb=pb: e.activation(out=p[:, 1, :, :], in_=pb[:, :].rearrange("p (c t) -> p c t", c=4),
                                                                    func=AF.Exp, scale=scale), reads=[kb_], writes=[pk])
                    if mi is not None:
                        S.add("pool", lambda e, p=p, mi=mi: e.tensor_tensor(
                            out=p[:].rearrange("p e c t -> p (e c) t"), in0=p[:].rearrange("p e c t -> p (e c) t"),
                            in1=masks[:, mi:mi + 1, :].broadcast_to([128, 8, 128]), op=ALU.mult),
                            reads=[pk, "masks"], writes=[pk])
                    first, last = ji == 0, ji == len(kl) - 1
                    for e_ in range(2):
                        mm(S, po[64 * e_:64 * e_ + 64, :], vts[:, kb, kv * 64:(kv + 1) * 64], p[:, e_, :, :], first, last,
                           ["vts", pk], ("ps", 4))
                        mm(S, pd[64 * e_:64 * e_ + 64, :], k.ones_b[:, 0:64], p[:, e_, :, :], first, last,
                           ["ones_b", pk], ("ps", 5))
                S.add("dve", lambda e, pd=pd, kv=kv: e.tensor_tensor(
                    out=rr[:].rearrange("p (c t) -> p c t", c=4), in0=pd[:, :].rearrange("p (c t) -> p c t", c=4),
                    in1=esbc[:, kv * 4:(kv + 1) * 4, :], op=ALU.add), reads=[("ps", 5), "esbc"], writes=["rr"])
                S.add("dve", lambda e: e.reciprocal(out=rr[:], in_=rr[:]), reads=["rr"], writes=["rr"])
                S.add("dve", lambda e, po=po, kv=kv, o_=o_: e.tensor_tensor(
                    out=o_[:, kv * 4:(kv + 1) * 4, :], in0=po[:, :].rearrange("p (c t) -> p c t", c=4),
                    in1=rr[:].rearrange("p (c t) -> p c t", c=4), op=ALU.mult), reads=[("ps", 4), "rr"], writes=[("oblk", i2)])
            for hf in range(2):
                pg = k.ps[6 + hf]
                for gg in range(4):
                    g = hf * 4 + gg
                    mm(S, pg[:, gg * 128:(gg + 1) * 128], gv[:, g * 128:(g + 1) * 128], wsT[:, g, :], True, True,
                       [("gvb", i2), "wsT"], ("ps", 6 + hf))
                S.add("dve", lambda e, pg=pg, hf=hf: e.tensor_tensor(
                    out=gt[:].rearrange("p (c t) -> p c t", c=4), in0=pg[:, :].rearrange("p (c t) -> p c t", c=4),
                    in1=bbc[:, hf * 4:(hf + 1) * 4, :], op=ALU.add), reads=[("ps", 6 + hf), "bbc"], writes=["gt"])
                S.add("dve", lambda e, hf=hf, o_=o_, gu=gu: e.tensor_tensor(
                    out=o_[:, 8 + hf * 4:8 + (hf + 1) * 4, :], in0=gt[:].rearrange("p (c t) -> p c t", c=4),
                    in1=gu[:, hf * 4:(hf + 1) * 4, :], op=ALU.mult), reads=["gt", ("gub", i2)], writes=[("oblk", i2)])
            S.add("sp", lambda e, o_=o_, c0=c0: e.dma_start(out=oT[:, c0:c0 + 128].rearrange("(c p) t -> p c t", p=128), in_=o_[:]),
                  reads=[("oblk", i2)], writes=[("dr", "oT", c0)], dma=True)
        S.barrier()


def phase_outproj(k, l, wname, oT, x_in, x_out, tiles):
    nc, S = k.nc, k.S
    w = k.dram(wname, [1, D, D])[0]
    with ExitStack() as st:
        al = k.alloc(st)
        xres = al("xres", [128, KC, 512], F32)
        hT = al("hT", [128, KC, 512], BF16)
        wa = [al("wa%d" % i, [128, KC, 256], BF16) for i in range(2)]
        na = 0
        for (t0, T, segs) in tiles:
            load_x_tile(k, xres, x_in, t0, T)
            for c in range(KC):
                S.add("sp", lambda e, c=c, t0=t0, T=T: e.dma_start(out=hT[:, c, :T], in_=oT[c * 128:(c + 1) * 128, t0:t0 + T]),
                      writes=[("hT", c)], dma=True)
            for sl in range(KC // 2):
                a = wa[na % 2]
                ka = "wa%d" % (na % 2)
                na += 1
                S.add("pool", lambda e, a=a, sl=sl: e.dma_start(
                    out=a[:], in_=w[:, sl * 256:(sl + 1) * 256].rearrange("(kc p) n -> p kc n", p=128)), writes=[ka], dma=True)
                for jj in range(2):
                    c = sl * 2 + jj
                    py = k.ps[c % 2]
                    for kc in range(KC):
                        mm(S, py[:, :T], a[:, kc, jj * 128:(jj + 1) * 128], hT[:, kc, :T], kc == 0, kc == KC - 1,
                           [ka, ("hT", kc)], ("ps", c % 2))
                    residual_store(k, py, ("ps", c % 2), xres, c, T, segs, l, 1, x_out, t0)
        S.barrier()


NK = SEQ + NCTX
NKC = NK // 128


def lat_norm(k, src, nch, T, gain, dst, rstd, sqb, pskey_i):
    S = k.S
    ps = k.ps[pskey_i]
    for c in range(nch):
        sq = sqb[c % 2]
        S.add("act", lambda e, c=c, sq=sq: e.activation(out=sq[:, :T], in_=src[:, c, :T], func=AF.Square),
              reads=[("lsrc", c)], writes=[("sq", c % 2)])
        mm(S, ps[:, :T], k.ones_f[:], sq[:, :T], c == 0, c == nch - 1, [("sq", c % 2), "ones_f"], ("ps", pskey_i))
    S.add("act", lambda e: e.activation(out=rstd[:, :T], in_=ps[:, :T], func=AF.Sqrt, scale=1.0 / (nch * 128), bias=k.eps_t[:]),
          reads=[("ps", pskey_i), "eps_t"], writes=["rstd2"])
    S.add("dve", lambda e: e.reciprocal(out=rstd[:, :T], in_=rstd[:, :T]), reads=["rstd2"], writes=["rstd2"])
    for c in range(nch):
        S.add("dve", lambda e, c=c: e.scalar_tensor_tensor(
            out=dst[:, c, :T], in0=src[:, c, :T], scalar=gain[:, c:c + 1], op0=ALU.mult, in1=rstd[:, :T], op1=ALU.mult),
            reads=[("lsrc", c), "rstd2", "gains"], writes=[("ldst", c)])


def phase_cd_in(k, x_in, tiles):
    nc, S = k.nc, k.S
    l, s = 1, 1
    w_in = k.dram("cd_w_in", [1, D, 4416])[0]
    w_uq = k.dram("c_w_uq", [1, 768, 1536])[0]
    cosD, sinD = k.dram("cosT", [128, NT]), k.dram("sinT", [128, NT])
    qgD, kgD = k.dram("c_qnT", [128, 6]), k.dram("c_kvnT", [128, 4])
    qnT = k.dram("qnT", [1024, NOWN], BF16)
    qrT = k.dram("qrT", [512, NOWN], BF16)
    xin = [k.dram("xch_in%d" % i, [128, NOWN], BF16) for i in range(5)]
    ctxkv = k.dram("ctxkv", [640, NCTX], BF16)
    zb = k.dram("zb_in", [2, 1024])
    dbT = k.dram("dbT", [1024, NOWN], BF16)
    zT = k.dram("zT", [1024, NOWN])
    with ExitStack() as st:
        al = k.alloc(st)
        xres = al("xres", [128, KC, 512], F32)
        hT = al("hT", [128, KC, 512], BF16)
        tmp = [al("tmp%d" % i, [128, 512], F32) for i in range(2)]
        sqb = [al("sq%d" % i, [128, 512], F32) for i in range(2)]
        rstd = al("rstd", [128, 512], F32)
        rstd2 = al("rstd2", [128, 512], F32)
        cos, sin = al("cos", [128, 512], F32), al("sin", [128, 512], F32)
        lat = al("lat", [128, 6, 512], F32)
        latn = al("latn", [128, 6, 512], BF16)
        qg, kg = al("qg", [128, 6], F32), al("kg", [128, 4], F32)
        wuq = al("wuq", [128, 6, 1536], BF16)
        wqr = al("wqr", [128, 6, 4, 128], BF16)
        krw = al("krw", [128, KC, 128], BF16)
        wa = [al("wa%d" % i, [128, KC, 256], BF16) for i in range(2)]
        qraw = [al("qraw%d" % i, [128, 512], BF16) for i in range(2)]
        r1 = [al("r1_%d" % i, [128, 512], F32) for i in range(2)]
        r2 = [al("r2_%d" % i, [128, 512], F32) for i in range(2)]
        ob = [al("ob%d" % i, [128, 512], BF16) for i in range(2)]
        zo = [al("zo%d" % i, [128, 512], F32) for i in range(2)]
        dcs = [al("dcs%d" % i, [128, 512], F32) for i in range(2)]
        S.add("sp", lambda e: e.dma_start(out=qg[:], in_=qgD), writes=["gains"], dma=True)
        S.add("sp", lambda e: e.dma_start(out=kg[:], in_=kgD), writes=["gains"], dma=True)
        for i in range(3):
            S.add("pool", lambda e, i=i: e.dma_start(
                out=wuq[:, :, i * 512:(i + 1) * 512], in_=w_uq[:, i * 512:(i + 1) * 512].rearrange("(kc p) n -> p kc n", p=128)),
                writes=[("wuq", i)], dma=True)
        for h in range(8):
            S.add("pool", lambda e, h=h: e.dma_start(
                out=wqr[:, :, h // 2, (h % 2) * 64:(h % 2) * 64 + 64],
                in_=w_uq[:, h * 192 + 128:h * 192 + 192].rearrange("(kc p) n -> p kc n", p=128)),
                writes=[("wqr", h)], dma=True)
        for i in range(2):
            S.add("pool", lambda e, i=i: e.dma_start(
                out=krw[:, :, i * 64:(i + 1) * 64], in_=w_in[:, 1280:1344].rearrange("(kc p) n -> p kc n", p=128)),
                writes=[("krw", i)], dma=True)
        wuq_r = [("wuq", i) for i in range(3)]
        wqr_r = [("wqr", h) for h in range(8)]
        na = 0
        cnt = 0
        for (t0, T, segs) in tiles:
            own = t0 < NOWN
            oc0 = t0 if own else 0
            kvd = (lambda i: xin[i]) if own else (lambda i: ctxkv[i * 128:(i + 1) * 128, :])
            load_x_tile(k, xres, x_in, t0, T)
            S.add("sp", lambda e, t0=t0, T=T: e.dma_start(out=cos[:, :T], in_=cosD[:, t0:t0 + T]), writes=["rope"], dma=True)
            S.add("sp", lambda e, t0=t0, T=T: e.dma_start(out=sin[:, :T], in_=sinD[:, t0:t0 + T]), writes=["rope"], dma=True)
            norm_mod(k, xres, hT, T, segs, l, s, tmp, rstd, sqb)

            def proj_chunk(a, ka, jj):
                nonlocal cnt
                i2 = cnt % 2
                cnt += 1
                pq = k.ps[i2]
                for kc in range(KC):
                    mm(S, pq[:, :T], a[:, kc, jj * 128:(jj + 1) * 128], hT[:, kc, :T], kc == 0, kc == KC - 1,
                       [ka, ("hT", kc)] if not isinstance(ka, list) else ka + [("hT", kc)], ("ps", i2))
                return pq, i2

            def slab(c0):
                nonlocal na
                a = wa[na % 2]
                ka = "wa%d" % (na % 2)
                na += 1
                S.add("pool", lambda e: e.dma_start(
                    out=a[:], in_=w_in[:, c0:c0 + 256].rearrange("(kc p) n -> p kc n", p=128)), writes=[ka], dma=True)
                return a, ka

            if own:
                for sl in range(3):
                    a, ka = slab(sl * 256)
                    for jj in range(2):
                        c = sl * 2 + jj
                        pq, i2 = proj_chunk(a, ka, jj)
                        S.add("act", lambda e, pq=pq, c=c: e.activation(out=lat[:, c, :T], in_=pq[:, :T], func=AF.Identity),
                              reads=[("ps", i2)], writes=[("lsrc", c)])
                lat_norm(k, lat, 6, T, qg, latn, rstd2, sqb, 6)
                lr = [("ldst", c) for c in range(6)]
                for h in range(8):
                    i2 = cnt % 2
                    cnt += 1
                    pq = k.ps[i2]
                    for kc in range(6):
                        mm(S, pq[:, :T], wuq[:, kc, h * 192:h * 192 + 128], latn[:, kc, :T], kc == 0, kc == 5,
                           wuq_r + [("ldst", kc)], ("ps", i2))
                    o_ = ob[i2]
                    S.add("act", lambda e, pq=pq, o_=o_: e.activation(out=o_[:, :T], in_=pq[:, :T], func=AF.Identity),
                          reads=[("ps", i2)], writes=[("ob", i2)])
                    S.add("sp", lambda e, o_=o_, h=h: e.dma_start(out=qnT[h * 128:(h + 1) * 128, t0:t0 + T], in_=o_[:, :T]),
                          reads=[("ob", i2)], writes=[("dr", "qnT", t0, h)], dma=True)
                for j in range(4):
                    i2 = cnt % 2
                    cnt += 1
                    pq = k.ps[i2]
                    for kc in range(6):
                        mm(S, pq[:, :T], wqr[:, kc, j, :], latn[:, kc, :T], kc == 0, kc == 5, wqr_r + [("ldst", kc)], ("ps", i2))
                    o_ = ob[i2]
                    rope_from_psum(k, pq[:, :T], ("ps", i2), k.ps[2 + i2], ("ps", 2 + i2), cos[:, :T], sin[:, :T], T,
                                   o_[:, :T], [("ob", i2)], qraw[i2], r1[i2], r2[i2], (("qraw", i2), ("r1", i2), ("r2", i2)))
                    S.add("sp", lambda e, o_=o_, j=j: e.dma_start(out=qrT[j * 128:(j + 1) * 128, t0:t0 + T], in_=o_[:, :T]),
                          reads=[("ob", i2)], writes=[("dr", "qrT", t0, j)], dma=True)
            for sl in range(2):
                a, ka = slab(768 + sl * 256)
                for jj in range(2):
                    c = sl * 2 + jj
                    pq, i2 = proj_chunk(a, ka, jj)
                    S.add("act", lambda e, pq=pq, c=c: e.activation(out=lat[:, c, :T], in_=pq[:, :T], func=AF.Identity),
                          reads=[("ps", i2)], writes=[("lsrc", c)])
            lat_norm(k, lat, 4, T, kg, latn, rstd2, sqb, 6)
            for c in range(4):
                S.add("sp", lambda e, c=c: e.dma_start(out=kvd(c)[:, oc0:oc0 + T], in_=latn[:, c, :T]),
                      reads=[("ldst", c)], writes=[("dr", "ckvn", t0, c)], dma=True)
            pq, i2 = proj_chunk(krw, [("krw", 0), ("krw", 1)], 0)
            o_ = ob[i2]
            rope_from_psum(k, pq[:, :T], ("ps", i2), k.ps[2 + i2], ("ps", 2 + i2), cos[:, :T], sin[:, :T], T,
                           o_[:, :T], [("ob", i2)], qraw[i2], r1[i2], r2[i2], (("qraw", i2), ("r1", i2), ("r2", i2)))
            S.add("sp", lambda e, o_=o_: e.dma_start(out=kvd(4)[:, oc0:oc0 + T], in_=o_[:, :T]),
                  reads=[("ob", i2)], writes=[("dr", "krT", t0)], dma=True)
            if own:
                for sl in range(4):
                    a, ka = slab(1344 + sl * 256)
                    for jj in range(2):
                        c = sl * 2 + jj
                        pq, i2 = proj_chunk(a, ka, jj)
                        o_ = ob[i2]
                        S.add("act", lambda e, pq=pq, o_=o_: e.activation(out=o_[:, :T], in_=pq[:, :T], func=AF.Identity),
                              reads=[("ps", i2)], writes=[("ob", i2)])
                        S.add("sp", lambda e, o_=o_, c=c: e.dma_start(out=dbT[c * 128:(c + 1) * 128, t0:t0 + T], in_=o_[:, :T]),
                              reads=[("ob", i2)], writes=[("dr", "dbT", t0, c)], dma=True)
                for sl in range(4):
                    a, ka = slab(2368 + sl * 256)
                    a2, ka2 = slab(3392 + sl * 256)
                    for jj in range(2):
                        c = sl * 2 + jj
                        pq, i2 = proj_chunk(a, ka, jj)
                        dc_ = dcs[i2]
                        S.add("act", lambda e, pq=pq, dc_=dc_: e.activation(out=dc_[:, :T], in_=pq[:, :T], func=AF.Identity),
                              reads=[("ps", i2)], writes=[("dcs", i2)])
                        pq2, j2 = proj_chunk(a2, ka2, jj)
                        z_ = zo[i2]
                        S.add("dve", lambda e, pq2=pq2, dc_=dc_, z_=z_: e.tensor_tensor(
                            out=z_[:, :T], in0=pq2[:, :T], in1=dc_[:, :T], op=ALU.mult),
                            reads=[("ps", j2), ("dcs", i2)], writes=[("zo", i2)])
                        S.add("sp", lambda e, z_=z_, c=c: e.dma_start(out=zT[c * 128:(c + 1) * 128, t0:t0 + T], in_=z_[:, :T]),
                              reads=[("zo", i2)], writes=[("dr", "zT", t0, c)], dma=True)
                        for (tt, which, col) in ((0, 0, 0), (NOWN - 512, 1, 511)):
                            if t0 == tt:
                                S.add("sp", lambda e, z_=z_, c=c, which=which, col=col: e.dma_start(
                                    out=zb[which:which + 1, :].rearrange("a (p c) -> p (a c)", c=8)[:, c:c + 1],
                                    in_=z_[:, col:col + 1], allow_slow_non_contiguous=True),
                                    reads=[("zo", i2)], writes=[("dr", "zb", which, c)], dma=True)
        S.barrier()


def phase_exchange(k):
    nc, S = k.nc, k.S
    xin = [k.dram("xch_in%d" % i, [128, NOWN], BF16) for i in range(5)]
    xall = [k.dram("xch_all%d" % i, [512, NOWN], BF16) for i in range(5)]
    zb = k.dram("zb_in", [2, 1024])
    zall = k.dram("zb_all", [8, 1024])
    groups = [[0, 1, 2, 3], [4, 5, 6, 7]]
    for i in range(5):
        S.add_cc(lambda e, i=i: e.collective_compute("AllGather", ALU.bypass, replica_groups=groups, ins=[xin[i].opt()], outs=[xall[i].opt()]))
    S.add_cc(lambda e: e.collective_compute("AllGather", ALU.bypass, replica_groups=groups, ins=[zb.opt()], outs=[zall.opt()]))
    S.barrier()
    with ExitStack() as st:
        al = k.alloc(st)
        zsel = al("zsel", [128, 8, 8], F32)
        S.add("sp", lambda e: e.dma_start(out=zsel[:], in_=zall.rearrange("j (p c) -> p j c", c=8)), reads=["zall"], writes=["zsel"], dma=True)
        for w in range(2):
            for j in range(8):
                if j == 0:
                    S.add("dve", lambda e, w=w, j=j: e.tensor_scalar(out=k.zpn[:, w, :], in0=zsel[:, j, :], scalar1=k.selT[:, w, j:j + 1],
                                                                     scalar2=None, op0=ALU.mult), reads=["zsel", "selT"], writes=["zpn"])
                else:
                    S.add("dve", lambda e, w=w, j=j: e.scalar_tensor_tensor(out=k.zpn[:, w, :], in0=zsel[:, j, :], scalar=k.selT[:, w, j:j + 1],
                                                                            op0=ALU.mult, in1=k.zpn[:, w, :], op1=ALU.add),
                          reads=["zsel", "selT", "zpn"], writes=["zpn"])
        S.barrier()


def phase_mla(k):
    nc, S = k.nc, k.S
    xall = [k.dram("xch_all%d" % i, [512, NOWN], BF16) for i in range(5)]
    ctxkv = k.dram("ctxkv", [640, NCTX], BF16)
    w_ukv = k.dram("c_w_ukv", [1, 512, 2048])[0]
    qnT = k.dram("qnT", [1024, NOWN], BF16)
    qrT = k.dram("qrT", [512, NOWN], BF16)
    oT = k.dram("oT2", [1024, NOWN], BF16)
    scale = 192.0 ** -0.5
    with ExitStack() as st:
        al = k.alloc(st)
        ckv = al("ckv", [128, 4, NK], BF16)
        krd = al("krd", [128, NK], BF16)
        wkv = al("wkv", [128, 4, 2048], BF16)
        knT = al("knT", [128, NK], BF16)
        vh = al("vh", [128, NKC, 128], BF16)
        qn = [al("qn%d" % i, [128, 512], BF16) for i in range(2)]
        qr = [al("qr%d" % i, [128, 512], BF16) for i in range(2)]
        pT = [al("pT%d" % i, [128, 512], BF16) for i in range(3)]
        rr = al("rr", [128, 512], F32)
        oo = [al("oo%d" % i, [128, 512], BF16) for i in range(2)]
        for c in range(4):
            for r in range(4):
                S.add("sp", lambda e, c=c, r=r: e.dma_start(out=ckv[:, c, r * NOWN:(r + 1) * NOWN],
                                                            in_=xall[c][r * 128:(r + 1) * 128, :]),
                      writes=[("ckv", c, r)], dma=True)
            S.add("sp", lambda e, c=c: e.dma_start(out=ckv[:, c, SEQ:NK], in_=ctxkv[c * 128:(c + 1) * 128, :]), writes=[("ckv", c, 4)], dma=True)
        for r in range(4):
            S.add("sp", lambda e, r=r: e.dma_start(out=krd[:, r * NOWN:(r + 1) * NOWN], in_=xall[4][r * 128:(r + 1) * 128, :]),
                  writes=[("krd", r)], dma=True)
        S.add("sp", lambda e: e.dma_start(out=krd[:, SEQ:NK], in_=ctxkv[512:640, :]), writes=[("krd", 4)], dma=True)
        for i in range(4):
            S.add("pool", lambda e, i=i: e.dma_start(
                out=wkv[:, :, i * 512:(i + 1) * 512], in_=w_ukv[:, i * 512:(i + 1) * 512].rearrange("(kc p) n -> p kc n", p=128)),
                writes=[("wkv", i)], dma=True)
        ckr = [("ckv", c, r) for c in range(4) for r in range(5)]
        krr = [("krd", r) for r in range(5)]
        npT = 0
        nq = 0
        for h in range(8):
            wr = [("wkv", h // 2)]
            for kg in range(17):
                n = 512 if kg < 16 else 256
                pk_ = k.ps[4 + kg % 2]
                for kc in range(4):
                    mm(S, pk_[:, :n], wkv[:, kc, h * 256:h * 256 + 128], ckv[:, kc, kg * 512:kg * 512 + n], kc == 0, kc == 3,
                       wr + ckr, ("ps", 4 + kg % 2))
                S.add("act", lambda e, pk_=pk_, kg=kg, n=n: e.activation(out=knT[:, kg * 512:kg * 512 + n], in_=pk_[:, :n], func=AF.Identity),
                      reads=[("ps", 4 + kg % 2)], writes=["knT"])
            for g4 in range(17):
                nb = 4 if g4 < 16 else 2
                pv_ = k.ps[6 + g4 % 2]
                for bb in range(nb):
                    kb = g4 * 4 + bb
                    for kc in range(4):
                        mm(S, pv_[:, bb * 128:(bb + 1) * 128], ckv[:, kc, kb * 128:(kb + 1) * 128],
                           wkv[:, kc, h * 256 + 128:h * 256 + 256], kc == 0, kc == 3, wr + ckr, ("ps", 6 + g4 % 2))
                S.add("dve", lambda e, pv_=pv_, g4=g4, nb=nb: e.tensor_copy(
                    out=vh[:, g4 * 4:g4 * 4 + nb, :], in_=pv_[:, :nb * 128].rearrange("p (b d) -> p b d", d=128)),
                    reads=[("ps", 6 + g4 % 2)], writes=["vh"])
            e2 = h % 2
            for qgi in range(4):
                i2 = nq % 2
                nq += 1
                q0 = qgi * 512
                S.add("sp", lambda e, i2=i2, q0=q0, h=h: e.dma_start(out=qn[i2][:], in_=qnT[h * 128:(h + 1) * 128, q0:q0 + 512]),
                      writes=[("qn", i2)], dma=True)
                S.add("sp", lambda e, i2=i2, q0=q0, h=h: e.dma_start(out=qr[i2][:], in_=qrT[(h // 2) * 128:(h // 2 + 1) * 128, q0:q0 + 512]),
                      writes=[("qr", i2)], dma=True)
                po, pd = k.ps[2], k.ps[3]
                for kc in range(NKC):
                    ps_ = k.ps[kc % 2]
                    mm(S, ps_[:, :], knT[:, kc * 128:(kc + 1) * 128], qn[i2][:], True, False, ["knT", ("qn", i2)], ("ps", kc % 2))
                    mm(S, ps_[:, :], krd[64 * e2:64 * e2 + 64, kc * 128:(kc + 1) * 128], qr[i2][64 * e2:64 * e2 + 64, :], False, True,
                       krr + [("qr", i2)], ("ps", kc % 2))
                    p = pT[npT % 3]
                    pk = ("pT", npT % 3)
                    npT += 1
                    S.add("act", lambda e, p=p, ps_=ps_: e.activation(out=p[:], in_=ps_[:, :], func=AF.Exp, scale=scale),
                          reads=[("ps", kc % 2)], writes=[pk])
                    mm(S, po[:, :], vh[:, kc, :], p[:], kc == 0, kc == NKC - 1, ["vh", pk], ("ps", 2))
                    mm(S, pd[:, :], k.ones_b[:], p[:], kc == 0, kc == NKC - 1, ["ones_b", pk], ("ps", 3))
                S.add("dve", lambda e, pd=pd: e.reciprocal(out=rr[:], in_=pd[:, :]), reads=[("ps", 3)], writes=["rr"])
                o_ = oo[i2]
                S.add("dve", lambda e, po=po, o_=o_: e.tensor_tensor(out=o_[:], in0=po[:, :], in1=rr[:], op=ALU.mult),
                      reads=[("ps", 2), "rr"], writes=[("oo", i2)])
                S.add("sp", lambda e, o_=o_, h=h, q0=q0: e.dma_start(out=oT[h * 128:(h + 1) * 128, q0:q0 + 512], in_=o_[:]),
                      reads=[("oo", i2)], writes=[("dr", "oT2", h, q0)], dma=True)
        S.barrier()


def phase_cd_out(k, x_in, x_out):
    nc, S = k.nc, k.S
    l = 1
    w = k.dram("cd_w_out", [1, D, D])[0]
    oT = k.dram("oT2", [1024, NOWN], BF16)
    dbT = k.dram("dbT", [1024, NOWN], BF16)
    zT = k.dram("zT", [1024, NOWN])
    cwD = k.dram("convT", [128, 24])
    with ExitStack() as st:
        al = k.alloc(st)
        xres = al("xres", [128, KC, 512], F32)
        hT = al("hT", [128, KC, 512], BF16)
        ze = al("ze", [128, 8, 514], F32)
        dbs = al("dbs", [128, 8, 512], BF16)
        cw = al("cw", [128, 24], F32)
        ct = [al("ct%d" % i, [128, 512], F32) for i in range(2)]
        wa = [al("wa%d" % i, [128, KC, 256], BF16) for i in range(2)]
        S.add("sp", lambda e: e.dma_start(out=cw[:], in_=cwD), writes=["cw"], dma=True)
        na = 0
        for (t0, T, segs) in OWN_TILES:
            load_x_tile(k, xres, x_in, t0, T)
            for c in range(8):
                S.add("sp", lambda e, c=c, t0=t0: e.dma_start(out=hT[:, c, :], in_=oT[c * 128:(c + 1) * 128, t0:t0 + 512]),
                      writes=[("hT", c)], dma=True)
            S.add("sp", lambda e, t0=t0: e.dma_start(out=ze[:, :, 1:513], in_=zT[:, t0:t0 + 512].rearrange("(c p) t -> p c t", p=128)),
                  writes=[("ze", 1)], dma=True)
            if t0 > 0:
                S.add("sp", lambda e, t0=t0: e.dma_start(out=ze[:, :, 0:1], in_=zT[:, t0 - 1:t0].rearrange("(c p) t -> p c t", p=128),
                                                         allow_slow_non_contiguous=True), writes=[("ze", 0)], dma=True)
            else:
                S.add("dve", lambda e: e.tensor_copy(out=ze[:, :, 0], in_=k.zpn[:, 0, :]), reads=["zpn"], writes=[("ze", 0)])
            if t0 + 512 < NOWN:
                S.add("sp", lambda e, t0=t0: e.dma_start(out=ze[:, :, 513:514], in_=zT[:, t0 + 512:t0 + 513].rearrange("(c p) t -> p c t", p=128),
                                                         allow_slow_non_contiguous=True), writes=[("ze", 2)], dma=True)
            else:
                S.add("dve", lambda e: e.tensor_copy(out=ze[:, :, 513], in_=k.zpn[:, 1, :]), reads=["zpn"], writes=[("ze", 2)])
            S.add("sp", lambda e, t0=t0: e.dma_start(out=dbs[:], in_=dbT[:, t0:t0 + 512].rearrange("(c p) t -> p c t", p=128)),
                  writes=["dbs"], dma=True)
            zr = [("ze", i) for i in range(3)]
            for c in range(8):
                t_ = ct[c % 2]
                tk = ("ct", c % 2)
                S.add("dve", lambda e, c=c, t_=t_: e.tensor_scalar(out=t_[:], in0=ze[:, c, 0:512], scalar1=cw[:, c:c + 1], scalar2=None,
                                                                   op0=ALU.mult), reads=zr + ["cw"], writes=[tk])
                S.add("dve", lambda e, c=c, t_=t_: e.scalar_tensor_tensor(out=t_[:], in0=ze[:, c, 1:513], scalar=cw[:, 8 + c:9 + c],
                                                                          op0=ALU.mult, in1=t_[:], op1=ALU.add), reads=zr + ["cw", tk], writes=[tk])
                S.add("dve", lambda e, c=c, t_=t_: e.scalar_tensor_tensor(out=t_[:], in0=ze[:, c, 2:514], scalar=cw[:, 16 + c:17 + c],
                                                                          op0=ALU.mult, in1=t_[:], op1=ALU.add), reads=zr + ["cw", tk], writes=[tk])
                S.add("dve", lambda e, c=c, t_=t_: e.tensor_tensor(out=hT[:, 8 + c, :], in0=t_[:], in1=dbs[:, c, :], op=ALU.mult),
                      reads=[tk, "dbs"], writes=[("hT", 8 + c)])
            for sl in range(KC // 2):
                a = wa[na % 2]
                ka = "wa%d" % (na % 2)
                na += 1
                S.add("pool", lambda e, a=a, sl=sl: e.dma_start(
                    out=a[:], in_=w[:, sl * 256:(sl + 1) * 256].rearrange("(kc p) n -> p kc n", p=128)), writes=[ka], dma=True)
                for jj in range(2):
                    c = sl * 2 + jj
                    py = k.ps[c % 2]
                    for kc in range(KC):
                        mm(S, py[:, :T], a[:, kc, jj * 128:(jj + 1) * 128], hT[:, kc, :T], kc == 0, kc == KC - 1,
                           [ka, ("hT", kc)], ("ps", c % 2))
                    residual_store(k, py, ("ps", c % 2), xres, c, T, segs, l, 1, x_out, t0)
        S.barrier()


OWN_TILES = [(i * 512, 512, [(0, 512, 0)]) for i in range(4)]
MISC_TILE = (2048, 512, [(0, 256, 0), (256, 256, 1)])
CTX_TILE = (CT0, 256, [(0, 256, 1)])


def fm(v):
    v = np.asarray(v)
    lead = v.shape[:-1]
    n = v.shape[-1] // 128
    r = v.reshape(lead + (n, 128))
    r = np.moveaxis(r, -1, 0)
    return np.ascontiguousarray(r.reshape(128, -1))


def prep_core(inp, core):
    b, q = core // 4, core % 4
    p0 = q * NOWN
    x = inp["x"]
    xT = np.zeros((D, NT), np.float32)
    xT[:, 0:NOWN] = x[b, p0:p0 + NOWN].T
    if q > 0:
        xT[:, HP0:HP0 + 128] = x[b, p0 - 128:p0].T
    if q < 3:
        xT[:, HN0:HN0 + 128] = x[b, p0 + NOWN:p0 + NOWN + 128].T
    xT[:, CT0:CT0 + NCTX] = inp["ctx"][b].T
    m = {"xT": xT}
    m["cT"] = np.ascontiguousarray(np.stack([inp["c"][b], inp["c_ctx"]], axis=1))
    pos = np.zeros(NT, np.int64)
    pos[0:NOWN] = p0 + np.arange(NOWN)
    pos[HP0:HP0 + 128] = p0 - 128 + np.arange(128)
    pos[HN0:HN0 + 128] = p0 + NOWN + np.arange(128)
    pos = np.clip(pos, 0, SEQ - 1)
    row = (pos // 64).astype(np.float32)
    col = (pos % 64).astype(np.float32)
    inv = (10000.0 ** (-np.arange(0, 32, 2, dtype=np.float32) / 32)).astype(np.float32)
    ang = np.zeros((64, NT), np.float32)
    ang[0:16] = (row[None, :] * inv[:, None]).astype(np.float32)
    ang[16:32] = ang[0:16]
    ang[32:48] = (col[None, :] * inv[:, None]).astype(np.float32)
    ang[48:64] = ang[32:48]
    cosT = np.cos(ang).astype(np.float32)
    sinT = np.sin(ang).astype(np.float32)
    cosT[:, CT0:] = 1.0
    sinT[:, CT0:] = 0.0
    m["cosT"] = np.ascontiguousarray(np.concatenate([cosT, cosT], 0))
    m["sinT"] = np.ascontiguousarray(np.concatenate([sinT, sinT], 0))
    j = np.arange(128)[:, None]
    i = np.arange(128)[None, :]
    mk = np.zeros((128, 4, 128), np.float32)
    mk[:, 0, :] = (j >= i)
    mk[:, 1, :] = (j <= i)
    mk[:, 2, :] = (j >= i) * (1.0 if q > 0 else 0.0)
    mk[:, 3, :] = (j <= i) * (1.0 if q < 3 else 0.0)
    m["masks"] = mk
    sel = np.zeros((128, 2, 8), np.float32)
    if q > 0:
        sel[:, 0, 2 * (q - 1) + 1] = 1.0
    if q < 3:
        sel[:, 1, 2 * (q + 1)] = 1.0
    m["selT"] = sel
    return m


def prep_shared(inp):
    m = {}
    m["mod_w"] = inp["mod_w"]
    m["mod_bT"] = np.stack([fm(inp["mod_b"][l]) for l in range(2)])
    m["norm_gT"] = np.stack([fm(inp["norm_g"][l]) for l in range(2)])
    for n in ("ffn_w1", "ffn_w3", "ffn_w2", "ab_w_in", "ab_w_out", "cd_w_in", "cd_w_out", "c_w_uq", "c_w_ukv"):
        m[n] = inp[n]
    pm = np.zeros((128, 128), np.float32)
    for mm_ in range(128):
        if mm_ % 32 < 16:
            pm[mm_ + 16, mm_] = -1.0
        else:
            pm[mm_ - 16, mm_] = 1.0
    m["permM"] = pm
    sk = inp["a_sink"][0]
    m["sinkT"] = np.ascontiguousarray(np.stack([np.repeat(sk[2 * c:2 * c + 2], 64) for c in range(8)], axis=1))
    m["b_wsT"] = np.ascontiguousarray(np.transpose(inp["b_ws"][0], (2, 0, 1)))
    m["b_biasbc"] = np.ascontiguousarray(np.broadcast_to(inp["b_bias"][0][None], (128, 8, 128)))
    m["c_qnT"] = fm(inp["c_q_norm"][0])
    m["c_kvnT"] = fm(inp["c_kv_norm"][0])
    m["convT"] = fm(inp["d_conv_w"][0])
    m["fnT"] = fm(inp["final_norm"])
    return m


def load_final_consts(k):
    fnD = k.dram("fnT", [128, 16])
    k.fnT = k.sb("fnT_sb", [128, 16], F32)
    k.S.add("sp", lambda e: e.dma_start(out=k.fnT[:], in_=fnD), writes=["fnT"], dma=True)
    selD = k.dram("selT", [128, 2, 8])
    k.selT = k.sb("selT_sb", [128, 2, 8], F32)
    k.zpn = k.sb("zpn_sb", [128, 2, 8], F32)
    k.S.add("sp", lambda e: e.dma_start(out=k.selT[:], in_=selD), writes=["selT"], dma=True)


INS = ["xT", "cT", "mod_w", "mod_bT", "norm_gT", "ffn_w1", "ffn_w3", "ffn_w2", "ab_w_in", "ab_w_out", "cosT", "sinT",
       "permM", "masks", "sinkT", "b_wsT", "b_biasbc", "cd_w_in", "c_w_uq", "c_qnT", "c_kvnT",
       "cd_w_out", "c_w_ukv", "convT", "fnT", "selT"]
OUTS = ["outT"]


def build():
    k = K(INS, OUTS)
    xT = k.dram("xT", [D, NT])
    x1, x2, x3 = k.dram("x1", [D, NT]), k.dram("x2", [D, NT]), k.dram("x3", [D, NT])
    x4, x5 = k.dram("x4", [D, NT]), k.dram("x5", [D, NOWN])
    outT = k.dram("outT", [D, NOWN])
    phase_setup(k)
    load_rope_consts(k)
    load_final_consts(k)
    phase_mod(k, (0, 1))
    phase_ffn(k, 0, 0, xT, x1, OWN_TILES + [MISC_TILE])
    phase_ab_in(k, x1, OWN_TILES + [MISC_TILE])
    phase_ab_mix(k)
    phase_outproj(k, 0, "ab_w_out", k.dram("oT", [2048, NT], BF16), x1, x2, OWN_TILES + [CTX_TILE])
    phase_ffn(k, 0, 1, x2, x3, OWN_TILES + [CTX_TILE])
    phase_ffn(k, 1, 0, x3, x4, OWN_TILES + [CTX_TILE])
    phase_cd_in(k, x4, OWN_TILES + [CTX_TILE])
    phase_exchange(k)
    phase_mla(k)
    phase_cd_out(k, x4, x5)
    phase_ffn(k, 1, 1, x5, None, OWN_TILES, final_out=outT)
    k.S.emit(k.st)
    k.st.close()
    return k


def kernel(**inp):
    inp = {kk: np.asarray(v) for kk, v in inp.items()}
    n = 8
    sh = prep_shared(inp)
    for nm in ("ffn_w1", "ffn_w3", "ffn_w2"):
        a = inp[nm]
        sh[nm] = a.reshape((4,) + a.shape[2:])
    k = build()
    maps = []
    for c in range(n):
        m = dict(sh)
        m.update(prep_core(inp, c))
        maps.append({kk: m[kk] for kk in INS})
    res = run_bass_kernel_spmd(k.nc, maps, core_ids=list(range(n))).results
    out = np.zeros((2, SEQ, D), np.float32)
    for c in range(n):
        b, q = c // 4, c % 4
        out[b, q * NOWN:(q + 1) * NOWN, :] = np.asarray(res[c]["outT"]).T
    return out
```

```python
import math
import types
from contextlib import ExitStack
import numpy as np
import concourse.bass as bass
import concourse.mybir as mybir
from concourse.bass_utils import run_bass_kernel_spmd

F32 = mybir.dt.float32
BF16 = mybir.dt.bfloat16
AF = mybir.ActivationFunctionType
ALU = mybir.AluOpType

D = 2048
FFN = 5632
NFC = FFN // 128
KC = D // 128
SEQ = 8192
NOWN = 2048
NCTX = 256
NT = 2560
HP0, HN0, CT0 = 2048, 2176, 2304
EPS = 1e-6
ENGS = ("pe", "act", "dve", "pool", "sp")
N_SW = 4
DBG_NOPERM = False
ROPE_ADD_ENG = "pool"
DBG_SECT = None
SW_FRESH = False


def _freeze(fn):
    if fn is None or fn.__closure__ is None:
        return fn
    cells = []
    for c in fn.__closure__:
        try:
            cells.append(types.CellType(c.cell_contents))
        except ValueError:
            cells.append(c)
    return types.FunctionType(fn.__code__, fn.__globals__, fn.__name__, fn.__defaults__, tuple(cells))


class Op:
    __slots__ = ("eng", "fn", "deps", "idx", "ms", "dma", "sem", "semval", "msval", "tag")


class Sched:
    def __init__(self, nc, n_dma_sems=24):
        if SW_FRESH:
            n_dma_sems = 8
        self.nc = nc
        self.ops = {e: [] for e in ENGS}
        self.lastw = {}
        self.readers = {}
        self.n_dma_sems = n_dma_sems
        self.dma_rr = 0
        self.dma_count = [0] * n_dma_sems
        self.dma_last = [None] * n_dma_sems
        self.sw_keys = {}
        self.sw_rr = 0
        self.sw_last = {}

    def add(self, eng, fn, reads=(), writes=(), dma=False, tag=None):
        op = Op()
        op.eng, op.fn, op.dma, op.ms, op.tag = eng, _freeze(fn), dma, False, tag
        op.sem = op.semval = op.msval = None
        sw = dma and eng == "pool" and SW_FRESH
        if eng in ("act", "dve", "pool") and not dma:
            extra = [("psr", r[1]) for r in reads if isinstance(r, tuple) and len(r) == 2 and r[0] == "ps"]
            if extra:
                writes = list(writes) + extra
        deps = set()
        for r in reads:
            w = self.lastw.get(r)
            if w is not None:
                deps.add(w)
        for k in writes:
            w = self.lastw.get(k)
            if w is not None:
                deps.add(w)
            for rd in self.readers.get(k, ()):
                deps.add(rd)
        if sw:
            key = len(self.sw_keys)
            self.sw_keys[key] = key
            op.sem, op.semval = ("sw", key), 16
            op.tag = False
            self.sw_last[key] = op
        elif dma:
            if eng == "pool":
                s = self.n_dma_sems - N_SW + self.sw_rr
                self.sw_rr = (self.sw_rr + 1) % N_SW
            else:
                s = self.dma_rr
                self.dma_rr = (self.dma_rr + 1) % (self.n_dma_sems - N_SW)
            if self.dma_last[s] is not None:
                deps.add(self.dma_last[s])
            self.dma_count[s] += 1
            op.sem, op.semval = s, 16 * self.dma_count[s]
            self.dma_last[s] = op
        if eng == "pe":
            deps = {d for d in deps if d.dma or d.eng != "pe"}
        op.deps = deps
        for d in deps:
            if not d.dma:
                d.ms = True
        for r in reads:
            self.readers.setdefault(r, []).append(op)
        for k in writes:
            self.lastw[k] = op
            self.readers[k] = []
        op.idx = len(self.ops[eng])
        self.ops[eng].append(op)
        return op

    def add_cc(self, fn, reads=(), writes=()):
        op = self.add("pool", fn, reads=reads, writes=[("cc_issue",)])
        op.tag = "cc"
        self.n_cc = getattr(self, "n_cc", 0) + 1
        op.semval = self.n_cc
        return op

    def barrier(self):
        lasts = [self.ops[e][-1] for e in ENGS if self.ops[e]]
        lasts = [x for x in lasts if not x.dma and x.fn is not None]
        dmas = [d for d in self.dma_last if d is not None] + list(self.sw_last.values())
        for e in ENGS:
            op = Op()
            op.eng, op.fn, op.dma, op.ms, op.tag = e, None, False, False, "barrier"
            op.sem = op.semval = op.msval = None
            op.deps = set(x for x in lasts if x.eng != e) | set(dmas)
            for d in op.deps:
                if not d.dma:
                    d.ms = True
            op.idx = len(self.ops[e])
            self.ops[e].append(op)
        self.lastw.clear()
        self.readers.clear()

    def emit(self, stack):
        nc = self.nc
        esem = {e: stack.enter_context(nc.semaphore("s_" + e)) for e in ENGS if e != "sp"}
        dsem = [stack.enter_context(nc.semaphore("d_%d" % i)) for i in range(self.n_dma_sems)]
        wsem = [stack.enter_context(nc.semaphore("w_%d" % i)) for i in range(len(self.sw_keys))]
        ccsem = stack.enter_context(nc.semaphore("ccsem"))
        for e in ENGS:
            c = 0
            for op in self.ops[e]:
                if op.ms and not op.dma:
                    c += 1
                    op.msval = c
        ops = self.ops
        final_dma = [(dsem[i], 16 * self.dma_count[i]) for i in range(self.n_dma_sems) if self.dma_count[i]]

        def run(ename, eng):
            known = {}
            for op in ops[ename]:
                for d in op.deps:
                    if d.dma and isinstance(d.sem, tuple):
                        key, sem, val = ("w", d.sem[1], id(d)), wsem[d.sem[1]], 16
                    elif d.dma:
                        key, sem, val = ("d", d.sem), dsem[d.sem], d.semval
                    else:
                        key, sem, val = d.eng, esem[d.eng], d.msval
                    if known.get(key, 0) < val:
                        eng.wait_ge(sem, val)
                        known[key] = val
                if op.fn is None:
                    continue
                if op.dma and isinstance(op.sem, tuple):
                    if op.tag:
                        eng.wait_ge(wsem[op.sem[1]], 16)
                        eng.sem_clear(wsem[op.sem[1]])
                    op.fn(eng).then_inc(wsem[op.sem[1]], 16)
                    continue
                ins = op.fn(eng)
                if op.tag == "cc":
                    ins.then_inc(ccsem, 1)
                    eng.wait_ge(ccsem, op.semval)
                    if op.ms:
                        eng.memset(self.cc_dummy[:], 0.0).then_inc(esem[ename], 1)
                    continue
                if op.dma:
                    ins.then_inc(dsem[op.sem], 16)
                elif op.ms:
                    ins.then_inc(esem[ename], 1)
            if ename == "sp":
                for sem, val in final_dma:
                    eng.wait_ge(sem, val)

        with nc.Block() as block:
            @block.tensor
            def _(e):
                run("pe", e)

            @block.scalar
            def _(e):
                run("act", e)

            @block.vector
            def _(e):
                run("dve", e)

            @block.gpsimd
            def _(e):
                run("pool", e)

            @block.sync
            def _(e):
                run("sp", e)


class K:
    def __init__(self, ext_in, ext_out):
        self.nc = bass.Bass("TRN2", target_bir_lowering=False)
        self.S = Sched(self.nc)
        self.ext_in, self.ext_out = set(ext_in), set(ext_out)
        self.dr = {}
        self.st = ExitStack()
        self.uid = 0
        self.ffn_sel = {(0, 0): 0, (0, 1): 1, (1, 0): 2, (1, 1): 3}

    def dram(self, name, shape, dtype=F32):
        if name in self.dr:
            return self.dr[name]
        kind = "ExternalInput" if name in self.ext_in else ("ExternalOutput" if name in self.ext_out else "Internal")
        t = self.nc.dram_tensor(name, list(shape), dtype, kind=kind).ap()
        self.dr[name] = t
        return t

    def sb(self, name, shape, dtype):
        return self.st.enter_context(self.nc.sbuf_tensor(name, list(shape), dtype))

    def alloc(self, st):
        self.uid += 1
        u = self.uid
        return lambda n, sh, dt: st.enter_context(self.nc.sbuf_tensor("%s_u%d" % (n, u), list(sh), dt))

    def psum(self, name):
        return self.st.enter_context(self.nc.psum_tensor(name, [128, 512], F32))


def mm(S, ps_ap, lhsT, rhs, start, stop, reads, pskey):
    S.add("pe", lambda e: e.matmul(ps_ap, lhsT, rhs, start=start, stop=stop), reads=reads, writes=[pskey])


def phase_setup(k):
    nc, S = k.nc, k.S
    k.ps = [k.psum("ps%d" % i) for i in range(8)]
    k.ones_f = k.sb("ones_f", [128, 128], F32)
    k.ones_b = k.sb("ones_b", [128, 128], BF16)
    S.add("pool", lambda e: e.memset(k.ones_f[:], 1.0), writes=["ones_f"])
    S.add("pool", lambda e: e.memset(k.ones_b[:], 1.0), writes=["ones_b"])
    k.eps_t = k.sb("eps_t", [128, 1], F32)
    S.cc_dummy = k.sb("cc_dummy", [128, 8], F32)
    S.add("pool", lambda e: e.memset(k.eps_t[:], EPS), writes=["eps_t"])


def phase_mod(k, layers=(0, 1)):
    nc, S = k.nc, k.S
    cT = k.dram("cT", [D, 2])
    nl = len(layers)
    mod_w = k.dram("mod_w", [nl, D, 9 * D])
    mod_bT = k.dram("mod_bT", [nl, 128, 144])
    norm_gT = k.dram("norm_gT", [nl, 128, 48])
    k.modT = [k.sb("modT%d" % l, [128, 144, 2], F32) for l in range(2)]
    k.modA = [k.sb("modA%d" % l, [128, 48, 2], F32) for l in range(2)]
    k.modG = [k.sb("modG%d" % l, [128, 48, 2], F32) for l in range(2)]
    with ExitStack() as st:
        cin = st.enter_context(nc.sbuf_tensor("cin_sb", [128, KC, 2], F32))
        scT = st.enter_context(nc.sbuf_tensor("scT", [128, KC, 2], BF16))
        mb = st.enter_context(nc.sbuf_tensor("mb", [128, 144], F32))
        ng = st.enter_context(nc.sbuf_tensor("ng", [128, 48], F32))
        wm = [st.enter_context(nc.sbuf_tensor("wm%d" % i, [128, KC, 512], BF16)) for i in range(2)]
        S.add("sp", lambda e: e.dma_start(out=cin[:], in_=cT.rearrange("(kc p) n -> p kc n", p=128)), writes=["cin"], dma=True)
        S.add("act", lambda e: e.activation(out=scT[:], in_=cin[:], func=AF.Silu), reads=["cin"], writes=["scT"])
        si = 0
        for li, l in enumerate(layers):
            S.add("sp", lambda e, li=li: e.dma_start(out=mb[:], in_=mod_bT[li]), writes=["mb"], dma=True)
            S.add("sp", lambda e, li=li: e.dma_start(out=ng[:], in_=norm_gT[li]), writes=["ng"], dma=True)
            ps = k.ps[l]
            for sl in range(36):
                w = wm[si % 2]
                wkey = "wm%d" % (si % 2)
                si += 1
                S.add("pool", lambda e, w=w, li=li, sl=sl: e.dma_start(
                    out=w[:], in_=mod_w[li, :, sl * 512:(sl + 1) * 512].rearrange("(kc p) n -> p kc n", p=128)),
                    writes=[wkey], dma=True)
                for jj in range(4):
                    j = sl * 4 + jj
                    for kc in range(KC):
                        mm(S, ps[:, 2 * j:2 * j + 2], w[:, kc, jj * 128:(jj + 1) * 128], scT[:, kc, :],
                           kc == 0, kc == KC - 1, [wkey, "scT"], ("ps", l))
            modT = k.modT[l]
            for cnd in range(2):
                S.add("dve", lambda e, modT=modT, ps=ps, cnd=cnd: e.tensor_tensor(
                    out=modT[:, :, cnd], in0=ps[:, 0:288].rearrange("p (j c) -> p j c", c=2)[:, :, cnd], in1=mb[:], op=ALU.add),
                    reads=[("ps", l), "mb"], writes=[("modT", l)])
            for s in range(3):
                for cnd in range(2):
                    S.add("dve", lambda e, l=l, s=s, cnd=cnd, modT=modT: e.scalar_tensor_tensor(
                        out=k.modA[l][:, s * 16:(s + 1) * 16, cnd], in0=modT[:, (3 * s + 1) * 16:(3 * s + 2) * 16, cnd],
                        scalar=1.0, op0=ALU.add, in1=ng[:, s * 16:(s + 1) * 16], op1=ALU.mult),
                        reads=[("modT", l), "ng"], writes=[("modA", l)])
                    S.add("dve", lambda e, l=l, s=s, cnd=cnd, modT=modT: e.tensor_scalar(
                        out=k.modG[l][:, s * 16:(s + 1) * 16, cnd], in0=modT[:, (3 * s + 2) * 16:(3 * s + 3) * 16, cnd],
                        scalar1=(1.0 if s == 1 else 0.5), scalar2=None, op0=ALU.mult),
                        reads=[("modT", l)], writes=[("modG", l)])
        S.barrier()


def load_x_tile(k, xres, x_in, t0, T):
    S = k.S
    for c in range(KC):
        S.add("sp", lambda e, c=c: e.dma_start(out=xres[:, c, :T], in_=x_in[c * 128:(c + 1) * 128, t0:t0 + T]),
              writes=[("xres", c)], dma=True)


def norm_mod(k, xres, hT, T, segs, l, s, tmp, rstd, sqb):
    S = k.S
    ps = k.ps[6]
    for c in range(KC):
        sq = sqb[c % 2]
        S.add("act", lambda e, c=c, sq=sq: e.activation(out=sq[:, :T], in_=xres[:, c, :T], func=AF.Square),
              reads=[("xres", c)], writes=[("sq", c % 2)])
        mm(S, ps[:, :T], k.ones_f[:], sq[:, :T], c == 0, c == KC - 1, [("sq", c % 2), "ones_f"], ("ps", 6))
    S.add("act", lambda e: e.activation(out=rstd[:, :T], in_=ps[:, :T], func=AF.Sqrt, scale=1.0 / D, bias=k.eps_t[:]),
          reads=[("ps", 6), "eps_t"], writes=["rstd"])
    S.add("dve", lambda e: e.reciprocal(out=rstd[:, :T], in_=rstd[:, :T]), reads=["rstd"], writes=["rstd"])
    A, SH = k.modA[l], k.modT[l]
    for c in range(KC):
        tb = tmp[c % 2]
        for (o, n, cnd) in segs:
            S.add("dve", lambda e, c=c, o=o, n=n, cnd=cnd, tb=tb: e.scalar_tensor_tensor(
                out=tb[:, o:o + n], in0=xres[:, c, o:o + n], scalar=A[:, s * 16 + c, cnd:cnd + 1], op0=ALU.mult,
                in1=rstd[:, o:o + n], op1=ALU.mult),
                reads=[("xres", c), "rstd", ("modA", l)], writes=[("tmp", c % 2)])
            S.add("act", lambda e, c=c, o=o, n=n, cnd=cnd, tb=tb: e.activation(
                out=hT[:, c, o:o + n], in_=tb[:, o:o + n], func=AF.Identity,
                bias=SH[:, 3 * s * 16 + c, cnd:cnd + 1], scale=1.0),
                reads=[("tmp", c % 2), ("modT", l)], writes=[("hT", c)])


def residual_store(k, ps_ap, pskey, xres, c, T, segs, l, s, x_out, t0):
    S = k.S
    G = k.modG[l]
    for (o, n, cnd) in segs:
        S.add("dve", lambda e, o=o, n=n, cnd=cnd: e.scalar_tensor_tensor(
            out=xres[:, c, o:o + n], in0=ps_ap[:, o:o + n], scalar=G[:, s * 16 + c, cnd:cnd + 1], op0=ALU.mult,
            in1=xres[:, c, o:o + n], op1=ALU.add),
            reads=[pskey, ("xres", c), ("modG", l)], writes=[("xres", c)])
    if x_out is not None:
        S.add("sp", lambda e: e.dma_start(out=x_out[c * 128:(c + 1) * 128, t0:t0 + T], in_=xres[:, c, :T]),
              reads=[("xres", c)], writes=[("xdram", x_out.tensor.name, t0, c)], dma=True)


def phase_ffn(k, l, widx, x_in, x_out, tiles, final_out=None):
    nc, S = k.nc, k.S
    s = 0 if widx == 0 else 2
    nsel = len(k.ffn_sel)
    wi_ = k.ffn_sel[(l, widx)]
    w1 = k.dram("ffn_w1", [nsel, D, FFN])[wi_]
    w3 = k.dram("ffn_w3", [nsel, D, FFN])[wi_]
    w2 = k.dram("ffn_w2", [nsel, FFN, D])[wi_]
    with ExitStack() as st:
        al = k.alloc(st)
        xres = al("xres", [128, KC, 512], F32)
        hT = al("hT", [128, KC, 512], BF16)
        gT = al("gT", [128, NFC, 512], BF16)
        tmp = [al("tmp%d" % i, [128, 512], F32) for i in range(2)]
        sqb = [al("sq%d" % i, [128, 512], F32) for i in range(2)]
        rstd = al("rstd", [128, 512], F32)
        su = [al("su%d" % i, [128, 512], BF16) for i in range(2)]
        wa = [al("wa%d" % i, [128, KC, 256], BF16) for i in range(2)]
        wb = [al("wb%d" % i, [128, KC, 256], BF16) for i in range(2)]
        wc = [al("wc%d" % i, [128, NFC, 256], BF16) for i in range(2)]
        na = nc_ = 0
        for (t0, T, segs) in tiles:
            load_x_tile(k, xres, x_in, t0, T)
            norm_mod(k, xres, hT, T, segs, l, s, tmp, rstd, sqb)
            hreads = [("hT", c) for c in range(KC)]
            for sl in range(NFC // 2):
                a, b = wa[na % 2], wb[na % 2]
                ka, kb = "wa%d" % (na % 2), "wb%d" % (na % 2)
                na += 1
                S.add("pool", lambda e, a=a, sl=sl: e.dma_start(
                    out=a[:], in_=w1[:, sl * 256:(sl + 1) * 256].rearrange("(kc p) n -> p kc n", p=128)),
                    writes=[ka], dma=True)
                S.add("pool", lambda e, b=b, sl=sl: e.dma_start(
                    out=b[:], in_=w3[:, sl * 256:(sl + 1) * 256].rearrange("(kc p) n -> p kc n", p=128)),
                    writes=[kb], dma=True)
                for jj in range(2):
                    fc = sl * 2 + jj
                    pu, pv = k.ps[fc % 2], k.ps[2 + fc % 2]
                    for kc in range(KC):
                        mm(S, pu[:, :T], a[:, kc, jj * 128:(jj + 1) * 128], hT[:, kc, :T], kc == 0, kc == KC - 1,
                           [ka, ("hT", kc)], ("ps", fc % 2))
                    for kc in range(KC):
                        mm(S, pv[:, :T], b[:, kc, jj * 128:(jj + 1) * 128], hT[:, kc, :T], kc == 0, kc == KC - 1,
                           [kb, ("hT", kc)], ("ps", 2 + fc % 2))
                    sut = su[fc % 2]
                    S.add("act", lambda e, pu=pu, sut=sut: e.activation(out=sut[:, :T], in_=pu[:, :T], func=AF.Silu),
                          reads=[("ps", fc % 2)], writes=[("su", fc % 2)])
                    S.add("dve", lambda e, pv=pv, sut=sut, fc=fc: e.tensor_tensor(
                        out=gT[:, fc, :T], in0=pv[:, :T], in1=sut[:, :T], op=ALU.mult),
                        reads=[("ps", 2 + fc % 2), ("su", fc % 2)], writes=[("gT", fc)])
            for ds in range(KC // 2):
                w = wc[nc_ % 2]
                kw = "wc%d" % (nc_ % 2)
                nc_ += 1
                S.add("pool", lambda e, w=w, ds=ds: e.dma_start(
                    out=w[:], in_=w2[:, ds * 256:(ds + 1) * 256].rearrange("(fc p) n -> p fc n", p=128)),
                    writes=[kw], dma=True)
                for jj in range(2):
                    c = ds * 2 + jj
                    py = k.ps[4 + c % 2]
                    for fc in range(NFC):
                        mm(S, py[:, :T], w[:, fc, jj * 128:(jj + 1) * 128], gT[:, fc, :T], fc == 0, fc == NFC - 1,
                           [kw, ("gT", fc)], ("ps", 4 + c % 2))
                    residual_store(k, py, ("ps", 4 + c % 2), xres, c, T, segs, l, s, x_out, t0)
            if final_out is not None:
                final_norm_store(k, xres, T, t0, final_out, tmp, rstd, sqb)
        S.barrier()


def final_norm_store(k, xres, T, t0, out, tmp, rstd, sqb):
    S = k.S
    ps = k.ps[6]
    fn = k.fnT
    for c in range(KC):
        sq = sqb[c % 2]
        S.add("act", lambda e, c=c, sq=sq: e.activation(out=sq[:, :T], in_=xres[:, c, :T], func=AF.Square),
              reads=[("xres", c)], writes=[("sq", c % 2)])
        mm(S, ps[:, :T], k.ones_f[:], sq[:, :T], c == 0, c == KC - 1, [("sq", c % 2), "ones_f"], ("ps", 6))
    S.add("act", lambda e: e.activation(out=rstd[:, :T], in_=ps[:, :T], func=AF.Sqrt, scale=1.0 / D, bias=k.eps_t[:]),
          reads=[("ps", 6), "eps_t"], writes=["rstd"])
    S.add("dve", lambda e: e.reciprocal(out=rstd[:, :T], in_=rstd[:, :T]), reads=["rstd"], writes=["rstd"])
    for c in range(KC):
        S.add("dve", lambda e, c=c: e.scalar_tensor_tensor(
            out=xres[:, c, :T], in0=xres[:, c, :T], scalar=fn[:, c:c + 1], op0=ALU.mult, in1=rstd[:, :T], op1=ALU.mult),
            reads=[("xres", c), "rstd", "fnT"], writes=[("xres", c)])
        S.add("sp", lambda e, c=c: e.dma_start(out=out[c * 128:(c + 1) * 128, t0:t0 + T], in_=xres[:, c, :T]),
              reads=[("xres", c)], writes=[("odram", t0, c)], dma=True)


GC1, GC2 = 0.044715, 1.5957691216057308


def gelu_from_psum(k, ps_ap, pskey, out_ap, outkeys, n, t1, t2, t1k, t2k):
    S = k.S
    S.add("act", lambda e: e.activation(out=t1, in_=ps_ap, func=AF.Square), reads=[pskey], writes=[t1k])
    S.add("dve", lambda e: e.tensor_scalar(out=t1, in0=t1, scalar1=GC1, scalar2=1.0, op0=ALU.mult, op1=ALU.add),
          reads=[t1k], writes=[t1k])
    S.add("dve", lambda e: e.tensor_tensor(out=t1, in0=ps_ap, in1=t1, op=ALU.mult), reads=[pskey, t1k], writes=[t1k])
    S.add("act", lambda e: e.activation(out=t2, in_=t1, func=AF.Sigmoid, scale=GC2), reads=[t1k], writes=[t2k])
    S.add("dve", lambda e: e.tensor_tensor(out=out_ap, in0=ps_ap, in1=t2, op=ALU.mult), reads=[pskey, t2k], writes=outkeys)


def rope_from_psum(k, ps_ap, pskey, ps2, ps2key, cos, sin, T, out_ap, outkeys, qraw, t1, t2, keys):
    S = k.S
    qk, t1k, t2k = keys
    S.add("act", lambda e: e.activation(out=qraw[:, :T], in_=ps_ap, func=AF.Identity), reads=[pskey], writes=[qk])
    S.add("dve", lambda e: e.tensor_tensor(out=t1[:, :T], in0=ps_ap, in1=cos, op=ALU.mult), reads=[pskey, "rope"], writes=[t1k])
    mm(S, ps2[:, :T], (k.ones_b if DBG_NOPERM else k.permM)[:], qraw[:, :T], True, True, [qk, "permM"], ps2key)
    S.add("dve", lambda e: e.tensor_tensor(out=t2[:, :T], in0=ps2[:, :T], in1=sin, op=ALU.mult), reads=[ps2key, "rope"], writes=[t2k])
    S.add(ROPE_ADD_ENG, lambda e: e.tensor_tensor(out=out_ap, in0=t1[:, :T], in1=t2[:, :T], op=ALU.add), reads=[t1k, t2k], writes=outkeys)


def load_rope_consts(k):
    nc, S = k.nc, k.S
    permD = k.dram("permM", [128, 128])
    k.permM = k.sb("permM_sb", [128, 128], BF16)
    S.add("pool", lambda e: e.dma_start(out=k.permM[:], in_=permD), writes=["permM"], dma=True)


def phase_ab_in(k, x_in, tiles):
    nc, S = k.nc, k.S
    l, s = 0, 1
    w_in = k.dram("ab_w_in", [1, D, 3328])[0]
    cosD, sinD = k.dram("cosT", [128, NT]), k.dram("sinT", [128, NT])
    qT = k.dram("qT", [1024, NT], BF16)
    kT = k.dram("kT", [2, 128, NT], BF16)
    vtok = k.dram("vtok", [NT, 128], BF16)
    guT = k.dram("guT", [1024, NT], BF16)
    gvtok = k.dram("gvtok", [NT, 1024], BF16)
    with ExitStack() as st:
        al = k.alloc(st)
        xres = al("xres", [128, KC, 512], F32)
        hT = al("hT", [128, KC, 512], BF16)
        tmp = [al("tmp%d" % i, [128, 512], F32) for i in range(2)]
        sqb = [al("sq%d" % i, [128, 512], F32) for i in range(2)]
        rstd = al("rstd", [128, 512], F32)
        cos, sin = al("cos", [128, 512], F32), al("sin", [128, 512], F32)
        wtm = al("wtm", [128, KC, 1152], BF16)
        kdw = al("kdw", [128, KC, 256], BF16)
        wa = [al("wa%d" % i, [128, KC, 256], BF16) for i in range(2)]
        qraw = [al("qraw%d" % i, [128, 512], BF16) for i in range(2)]
        r1 = [al("r1_%d" % i, [128, 512], F32) for i in range(2)]
        r2 = [al("r2_%d" % i, [128, 512], F32) for i in range(2)]
        ob = [al("ob%d" % i, [128, 512], BF16) for i in range(2)]
        obt = [al("obt%d" % i, [128, 1152], BF16) for i in range(2)]
        for i, (c0, n) in enumerate([(1152, 128), (2304, 512), (2816, 512)]):
            o = [0, 128, 640][i]
            S.add("pool", lambda e, c0=c0, n=n, o=o: e.dma_start(
                out=wtm[:, :, o:o + n], in_=w_in[:, c0:c0 + n].rearrange("(kc p) n -> p kc n", p=128)),
                writes=[("wtm", i)], dma=True)
        for i in range(4):
            c0 = 1024 + 64 * (i // 2)
            S.add("pool", lambda e, c0=c0, i=i: e.dma_start(
                out=kdw[:, :, i * 64:(i + 1) * 64], in_=w_in[:, c0:c0 + 64].rearrange("(kc p) n -> p kc n", p=128)),
                writes=[("kdw", i)], dma=True)
        na = 0
        cnt = 0
        for (t0, T, segs) in tiles:
            load_x_tile(k, xres, x_in, t0, T)
            S.add("sp", lambda e, t0=t0, T=T: e.dma_start(out=cos[:, :T], in_=cosD[:, t0:t0 + T]), writes=["rope"], dma=True)
            S.add("sp", lambda e, t0=t0, T=T: e.dma_start(out=sin[:, :T], in_=sinD[:, t0:t0 + T]), writes=["rope"], dma=True)
            norm_mod(k, xres, hT, T, segs, l, s, tmp, rstd, sqb)
            hreads = [("hT", c) for c in range(KC)]
            for sl in range(8):
                if DBG_SECT is not None and ("q" if sl < 4 else "gu") not in DBG_SECT:
                    continue
                a = wa[na % 2]
                ka = "wa%d" % (na % 2)
                na += 1
                c0 = sl * 256 if sl < 4 else 1280 + (sl - 4) * 256
                S.add("pool", lambda e, a=a, c0=c0: e.dma_start(
                    out=a[:], in_=w_in[:, c0:c0 + 256].rearrange("(kc p) n -> p kc n", p=128)), writes=[ka], dma=True)
                for jj in range(2):
                    ch = (sl % 4) * 2 + jj
                    i2 = cnt % 2
                    cnt += 1
                    pq = k.ps[i2]
                    for kc in range(KC):
                        mm(S, pq[:, :T], a[:, kc, jj * 128:(jj + 1) * 128], hT[:, kc, :T], kc == 0, kc == KC - 1,
                           [ka, ("hT", kc)], ("ps", i2))
                    o_ = ob[i2]
                    if sl < 4:
                        rope_from_psum(k, pq[:, :T], ("ps", i2), k.ps[2 + i2], ("ps", 2 + i2), cos[:, :T], sin[:, :T], T,
                                       o_[:, :T], [("ob", i2)], qraw[i2], r1[i2], r2[i2], (("qraw", i2), ("r1", i2), ("r2", i2)))
                        dst = qT[ch * 128:(ch + 1) * 128, t0:t0 + T]
                    else:
                        gelu_from_psum(k, pq[:, :T], ("ps", i2), o_[:, :T], [("ob", i2)], T, r1[i2][:, :T], r2[i2][:, :T],
                                       ("r1", i2), ("r2", i2))
                        dst = guT[ch * 128:(ch + 1) * 128, t0:t0 + T]
                    S.add("sp", lambda e, dst=dst, o_=o_, T=T: e.dma_start(out=dst, in_=o_[:, :T]),
                          reads=[("ob", i2)], writes=[("dr", dst.tensor.name, t0, ch)], dma=True)
            for kv in range(2):
                if DBG_SECT is not None and "k" not in DBG_SECT:
                    continue
                i2 = cnt % 2
                cnt += 1
                pq = k.ps[i2]
                for kc in range(KC):
                    mm(S, pq[:, :T], kdw[:, kc, kv * 128:(kv + 1) * 128], hT[:, kc, :T], kc == 0, kc == KC - 1,
                       [("kdw", 2 * kv), ("kdw", 2 * kv + 1), ("hT", kc)], ("ps", i2))
                o_ = ob[i2]
                rope_from_psum(k, pq[:, :T], ("ps", i2), k.ps[2 + i2], ("ps", 2 + i2), cos[:, :T], sin[:, :T], T,
                               o_[:, :T], [("ob", i2)], qraw[i2], r1[i2], r2[i2], (("qraw", i2), ("r1", i2), ("r2", i2)))
                dst = kT[kv, :, t0:t0 + T]
                S.add("sp", lambda e, dst=dst, o_=o_, T=T: e.dma_start(out=dst, in_=o_[:, :T]),
                      reads=[("ob", i2)], writes=[("dr", "kT", t0, kv)], dma=True)
            for tb in range(T // 128):
                if DBG_SECT is not None and "tok" not in DBG_SECT:
                    continue
                ot = obt[tb % 2]
                okey = ("obt", tb % 2)
                pv = k.ps[4 + tb % 2]
                for kc in range(KC):
                    mm(S, pv[:, :128], hT[:, kc, tb * 128:(tb + 1) * 128], wtm[:, kc, 0:128], kc == 0, kc == KC - 1,
                       [("wtm", 0), ("hT", kc)], ("ps", 4 + tb % 2))
                S.add("act", lambda e, pv=pv, ot=ot: e.activation(out=ot[:, 0:128], in_=pv[:, :128], func=AF.Identity),
                      reads=[("ps", 4 + tb % 2)], writes=[okey])
                for hf in range(2):
                    pg = k.ps[6 + hf]
                    for kc in range(KC):
                        mm(S, pg[:, :], hT[:, kc, tb * 128:(tb + 1) * 128], wtm[:, kc, 128 + hf * 512:128 + (hf + 1) * 512],
                           kc == 0, kc == KC - 1, [("wtm", 1 + hf), ("hT", kc)], ("ps", 6 + hf))
                    gelu_from_psum(k, pg[:, :], ("ps", 6 + hf), ot[:, 128 + hf * 512:128 + (hf + 1) * 512], [okey], 512,
                                   r1[hf][:, :], r2[hf][:, :], ("r1", hf), ("r2", hf))
                r0 = t0 + tb * 128
                S.add("sp", lambda e, ot=ot, r0=r0: e.dma_start(out=vtok[r0:r0 + 128, :], in_=ot[:, 0:128]),
                      reads=[okey], writes=[("dr", "vtok", r0)], dma=True)
                S.add("sp", lambda e, ot=ot, r0=r0: e.dma_start(out=gvtok[r0:r0 + 128, :], in_=ot[:, 128:1152]),
                      reads=[okey], writes=[("dr", "gvtok", r0)], dma=True)
        S.barrier()


def phase_ab_mix(k):
    nc, S = k.nc, k.S
    qT = k.dram("qT", [1024, NT], BF16)
    kT = k.dram("kT", [2, 128, NT], BF16)
    vtok = k.dram("vtok", [NT, 128], BF16)
    guT = k.dram("guT", [1024, NT], BF16)
    gvtok = k.dram("gvtok", [NT, 1024], BF16)
    oT = k.dram("oT", [2048, NT], BF16)
    masksD = k.dram("masks", [128, 4, 128])
    sinkD = k.dram("sinkT", [128, 8])
    wsD = k.dram("b_wsT", [128, 8, 128])
    bbD = k.dram("b_biasbc", [128, 8, 128])
    scale = 1.0 / 8.0
    with ExitStack() as st:
        al = k.alloc(st)
        kTs = al("kTs", [128, 2, NT], BF16)
        vts = al("vts", [128, 20, 128], BF16)
        masks = al("masks", [128, 4, 128], BF16)
        sink = al("sink", [128, 8], F32)
        esbc = al("esbc", [128, 8, 128], F32)
        wsT = al("wsT", [128, 8, 128], BF16)
        bbc = al("bbc", [128, 8, 128], F32)
        qb_ = [al("qb%d" % i, [128, 8, 128], BF16) for i in range(2)]
        gub = [al("gub%d" % i, [128, 8, 128], BF16) for i in range(2)]
        gvb = [al("gvb%d" % i, [128, 1024], BF16) for i in range(2)]
        pT = [al("pT%d" % i, [128, 2, 4, 128], BF16) for i in range(3)]
        rr = al("rr", [128, 512], F32)
        gt = al("gt", [128, 512], F32)
        ob = [al("oblk%d" % i, [128, 16, 128], BF16) for i in range(2)]
        S.add("sp", lambda e: e.dma_start(out=kTs[:], in_=kT.rearrange("v p t -> p v t")), writes=["kTs"], dma=True)
        S.add("sp", lambda e: e.dma_start(out=vts[:], in_=vtok.rearrange("(b p) d -> p b d", p=128)), writes=["vts"], dma=True)
        S.add("pool", lambda e: e.dma_start(out=masks[:], in_=masksD), writes=["masks"], dma=True)
        S.add("pool", lambda e: e.dma_start(out=wsT[:], in_=wsD), writes=["wsT"], dma=True)
        S.add("sp", lambda e: e.dma_start(out=bbc[:], in_=bbD), writes=["bbc"], dma=True)
        S.add("sp", lambda e: e.dma_start(out=sink[:], in_=sinkD), writes=["sink"], dma=True)
        S.add("act", lambda e: e.activation(out=sink[:], in_=sink[:], func=AF.Exp), reads=["sink"], writes=["sink"])
        S.add("dve", lambda e: e.tensor_copy(out=esbc[:], in_=sink[:].unsqueeze(2).broadcast_to([128, 8, 128])),
              reads=["sink"], writes=["esbc"])
        blocks = list(range(16)) + [18, 19]
        npT = 0
        for bi, b in enumerate(blocks):
            i2 = bi % 2
            c0 = b * 128
            qb, gu, gv, o_ = qb_[i2], gub[i2], gvb[i2], ob[i2]
            S.add("sp", lambda e, qb=qb, c0=c0: e.dma_start(out=qb[:], in_=qT[:, c0:c0 + 128].rearrange("(c p) t -> p c t", p=128)),
                  writes=[("qb", i2)], dma=True)
            S.add("sp", lambda e, gu=gu, c0=c0: e.dma_start(out=gu[:], in_=guT[:, c0:c0 + 128].rearrange("(c p) t -> p c t", p=128)),
                  writes=[("gub", i2)], dma=True)
            S.add("sp", lambda e, gv=gv, c0=c0: e.dma_start(out=gv[:], in_=gvtok[c0:c0 + 128, :]), writes=[("gvb", i2)], dma=True)
            if b < 16:
                kl = [((b - 1) if b > 0 else 16, 0 if b > 0 else 2), (b, None), ((b + 1) if b < 15 else 17, 1 if b < 15 else 3),
                      (18, None), (19, None)]
            else:
                kl = [(18, None), (19, None)]
            for kv in range(2):
                po, pd = k.ps[4], k.ps[5]
                for ji, (kb, mi) in enumerate(kl):
                    pa, pb = k.ps[2 * (ji % 2)], k.ps[2 * (ji % 2) + 1]
                    ka, kb_ = ("ps", 2 * (ji % 2)), ("ps", 2 * (ji % 2) + 1)
                    mm(S, pa[:, :], kTs[0:64, kv, kb * 128:(kb + 1) * 128], qb[0:64, kv * 4:(kv + 1) * 4, :], True, True,
                       ["kTs", ("qb", i2)], ka)
                    mm(S, pb[:, :], kTs[64:128, kv, kb * 128:(kb + 1) * 128], qb[64:128, kv * 4:(kv + 1) * 4, :], True, True,
                       ["kTs", ("qb", i2)], kb_)
                    p = pT[npT % 3]
                    pk = ("pT", npT % 3)
                    npT += 1
                    S.add("act", lambda e, p=p, pa=pa: e.activation(out=p[:, 0, :, :], in_=pa[:, :].rearrange("p (c t) -> p c t", c=4),
                                                                    func=AF.Exp, scale=scale), reads=[ka], writes=[pk])
                    S.add("act", lambda e, p=p, pb=pb: e.activation(out=p[:, 1, :, :], in_=pb[:, :].rearrange("p (c t) -> p c t", c=4),
                                                                    func=AF.Exp, scale=scale), reads=[kb_], writes=[pk])
                    if mi is not None:
                        S.add("pool", lambda e, p=p, mi=mi: e.tensor_tensor(
                            out=p[:].rearrange("p e c t -> p (e c) t"), in0=p[:].rearrange("p e c t -> p (e c) t"),
                            in1=masks[:, mi:mi + 1, :].broadcast_to([128, 8, 128]), op=ALU.mult),
                            reads=[pk, "masks"], writes=[pk])
                    first, last = ji == 0, ji == len(kl) - 1
                    for e_ in range(2):
                        mm(S, po[64 * e_:64 * e_ + 64, :], vts[:, kb, kv * 64:(kv + 1) * 64], p[:, e_, :, :], first, last,
                           ["vts", pk], ("ps", 4))
                        mm(S, pd[64 * e_:64 * e_ + 64, :], k.ones_b[:, 0:64], p[:, e_, :, :], first, last,
                           ["ones_b", pk], ("ps", 5))
                S.add("dve", lambda e, pd=pd, kv=kv: e.tensor_tensor(
                    out=rr[:].rearrange("p (c t) -> p c t", c=4), in0=pd[:, :].rearrange("p (c t) -> p c t", c=4),
                    in1=esbc[:, kv * 4:(kv + 1) * 4, :], op=ALU.add), reads=[("ps", 5), "esbc"], writes=["rr"])
                S.add("dve", lambda e: e.reciprocal(out=rr[:], in_=rr[:]), reads=["rr"], writes=["rr"])
                S.add("dve", lambda e, po=po, kv=kv, o_=o_: e.tensor_tensor(
                    out=o_[:, kv * 4:(kv + 1) * 4, :], in0=po[:, :].rearrange("p (c t) -> p c t", c=4),
                    in1=rr[:].rearrange("p (c t) -> p c t", c=4), op=ALU.mult), reads=[("ps", 4), "rr"], writes=[("oblk", i2)])
            for hf in range(2):
                pg = k.ps[6 + hf]
                for gg in range(4):
                    g = hf * 4 + gg
                    mm(S, pg[:, gg * 128:(gg + 1) * 128], gv[:, g * 128:(g + 1) * 128], wsT[:, g, :], True, True,
                       [("gvb", i2), "wsT"], ("ps", 6 + hf))
                S.add("dve", lambda e, pg=pg, hf=hf: e.tensor_tensor(
                    out=gt[:].rearrange("p (c t) -> p c t", c=4), in0=pg[:, :].rearrange("p (c t) -> p c t", c=4),
                    in1=bbc[:, hf * 4:(hf + 1) * 4, :], op=ALU.add), reads=[("ps", 6 + hf), "bbc"], writes=["gt"])
                S.add("dve", lambda e, hf=hf, o_=o_, gu=gu: e.tensor_tensor(
                    out=o_[:, 8 + hf * 4:8 + (hf + 1) * 4, :], in0=gt[:].rearrange("p (c t) -> p c t", c=4),
                    in1=gu[:, hf * 4:(hf + 1) * 4, :], op=ALU.mult), reads=["gt", ("gub", i2)], writes=[("oblk", i2)])
            S.add("sp", lambda e, o_=o_, c0=c0: e.dma_start(out=oT[:, c0:c0 + 128].rearrange("(c p) t -> p c t", p=128), in_=o_[:]),
                  reads=[("oblk", i2)], writes=[("dr", "oT", c0)], dma=True)
        S.barrier()


def phase_outproj(k, l, wname, oT, x_in, x_out, tiles):
    nc, S = k.nc, k.S
    w = k.dram(wname, [1, D, D])[0]
    with ExitStack() as st:
        al = k.alloc(st)
        xres = al("xres", [128, KC, 512], F32)
        hT = al("hT", [128, KC, 512], BF16)
        wa = [al("wa%d" % i, [128, KC, 256], BF16) for i in range(2)]
        na = 0
        for (t0, T, segs) in tiles:
            load_x_tile(k, xres, x_in, t0, T)
            for c in range(KC):
                S.add("sp", lambda e, c=c, t0=t0, T=T: e.dma_start(out=hT[:, c, :T], in_=oT[c * 128:(c + 1) * 128, t0:t0 + T]),
                      writes=[("hT", c)], dma=True)
            for sl in range(KC // 2):
                a = wa[na % 2]
                ka = "wa%d" % (na % 2)
                na += 1
                S.add("pool", lambda e, a=a, sl=sl: e.dma_start(
                    out=a[:], in_=w[:, sl * 256:(sl + 1) * 256].rearrange("(kc p) n -> p kc n", p=128)), writes=[ka], dma=True)
                for jj in range(2):
                    c = sl * 2 + jj
                    py = k.ps[c % 2]
                    for kc in range(KC):
                        mm(S, py[:, :T], a[:, kc, jj * 128:(jj + 1) * 128], hT[:, kc, :T], kc == 0, kc == KC - 1,
                           [ka, ("hT", kc)], ("ps", c % 2))
                    residual_store(k, py, ("ps", c % 2), xres, c, T, segs, l, 1, x_out, t0)
        S.barrier()


NK = SEQ + NCTX
NKC = NK // 128


def lat_norm(k, src, nch, T, gain, dst, rstd, sqb, pskey_i):
    S = k.S
    ps = k.ps[pskey_i]
    for c in range(nch):
        sq = sqb[c % 2]
        S.add("act", lambda e, c=c, sq=sq: e.activation(out=sq[:, :T], in_=src[:, c, :T], func=AF.Square),
              reads=[("lsrc", c)], writes=[("sq", c % 2)])
        mm(S, ps[:, :T], k.ones_f[:], sq[:, :T], c == 0, c == nch - 1, [("sq", c % 2), "ones_f"], ("ps", pskey_i))
    S.add("act", lambda e: e.activation(out=rstd[:, :T], in_=ps[:, :T], func=AF.Sqrt, scale=1.0 / (nch * 128), bias=k.eps_t[:]),
          reads=[("ps", pskey_i), "eps_t"], writes=["rstd2"])
    S.add("dve", lambda e: e.reciprocal(out=rstd[:, :T], in_=rstd[:, :T]), reads=["rstd2"], writes=["rstd2"])
    for c in range(nch):
        S.add("dve", lambda e, c=c: e.scalar_tensor_tensor(
            out=dst[:, c, :T], in0=src[:, c, :T], scalar=gain[:, c:c + 1], op0=ALU.mult, in1=rstd[:, :T], op1=ALU.mult),
            reads=[("lsrc", c), "rstd2", "gains"], writes=[("ldst", c)])


def phase_cd_in(k, x_in, tiles):
    nc, S = k.nc, k.S
    l, s = 1, 1
    w_in = k.dram("cd_w_in", [1, D, 4416])[0]
    w_uq = k.dram("c_w_uq", [1, 768, 1536])[0]
    cosD, sinD = k.dram("cosT", [128, NT]), k.dram("sinT", [128, NT])
    qgD, kgD = k.dram("c_qnT", [128, 6]), k.dram("c_kvnT", [128, 4])
    qnT = k.dram("qnT", [1024, NOWN], BF16)
    qrT = k.dram("qrT", [512, NOWN], BF16)
    xin = [k.dram("xch_in%d" % i, [128, NOWN], BF16) for i in range(5)]
    ctxkv = k.dram("ctxkv", [640, NCTX], BF16)
    zb = k.dram("zb_in", [2, 1024])
    dbT = k.dram("dbT", [1024, NOWN], BF16)
    zT = k.dram("zT", [1024, NOWN])
    with ExitStack() as st:
        al = k.alloc(st)
        xres = al("xres", [128, KC, 512], F32)
        hT = al("hT", [128, KC, 512], BF16)
        tmp = [al("tmp%d" % i, [128, 512], F32) for i in range(2)]
        sqb = [al("sq%d" % i, [128, 512], F32) for i in range(2)]
        rstd = al("rstd", [128, 512], F32)
        rstd2 = al("rstd2", [128, 512], F32)
        cos, sin = al("cos", [128, 512], F32), al("sin", [128, 512], F32)
        lat = al("lat", [128, 6, 512], F32)
        latn = al("latn", [128, 6, 512], BF16)
        qg, kg = al("qg", [128, 6], F32), al("kg", [128, 4], F32)
        wuq = al("wuq", [128, 6, 1536], BF16)
        wqr = al("wqr", [128, 6, 4, 128], BF16)
        krw = al("krw", [128, KC, 128], BF16)
        wa = [al("wa%d" % i, [128, KC, 256], BF16) for i in range(2)]
        qraw = [al("qraw%d" % i, [128, 512], BF16) for i in range(2)]
        r1 = [al("r1_%d" % i, [128, 512], F32) for i in range(2)]
        r2 = [al("r2_%d" % i, [128, 512], F32) for i in range(2)]
        ob = [al("ob%d" % i, [128, 512], BF16) for i in range(2)]
        zo = [al("zo%d" % i, [128, 512], F32) for i in range(2)]
        dcs = [al("dcs%d" % i, [128, 512], F32) for i in range(2)]
        S.add("sp", lambda e: e.dma_start(out=qg[:], in_=qgD), writes=["gains"], dma=True)
        S.add("sp", lambda e: e.dma_start(out=kg[:], in_=kgD), writes=["gains"], dma=True)
        for i in range(3):
            S.add("pool", lambda e, i=i: e.dma_start(
                out=wuq[:, :, i * 512:(i + 1) * 512], in_=w_uq[:, i * 512:(i + 1) * 512].rearrange("(kc p) n -> p kc n", p=128)),
                writes=[("wuq", i)], dma=True)
        for h in range(8):
            S.add("pool", lambda e, h=h: e.dma_start(
                out=wqr[:, :, h // 2, (h % 2) * 64:(h % 2) * 64 + 64],
                in_=w_uq[:, h * 192 + 128:h * 192 + 192].rearrange("(kc p) n -> p kc n", p=128)),
                writes=[("wqr", h)], dma=True)
        for i in range(2):
            S.add("pool", lambda e, i=i: e.dma_start(
                out=krw[:, :, i * 64:(i + 1) * 64], in_=w_in[:, 1280:1344].rearrange("(kc p) n -> p kc n", p=128)),
                writes=[("krw", i)], dma=True)
        wuq_r = [("wuq", i) for i in range(3)]
        wqr_r = [("wqr", h) for h in range(8)]
        na = 0
        cnt = 0
        for (t0, T, segs) in tiles:
            own = t0 < NOWN
            oc0 = t0 if own else 0
            kvd = (lambda i: xin[i]) if own else (lambda i: ctxkv[i * 128:(i + 1) * 128, :])
            load_x_tile(k, xres, x_in, t0, T)
            S.add("sp", lambda e, t0=t0, T=T: e.dma_start(out=cos[:, :T], in_=cosD[:, t0:t0 + T]), writes=["rope"], dma=True)
            S.add("sp", lambda e, t0=t0, T=T: e.dma_start(out=sin[:, :T], in_=sinD[:, t0:t0 + T]), writes=["rope"], dma=True)
            norm_mod(k, xres, hT, T, segs, l, s, tmp, rstd, sqb)

            def proj_chunk(a, ka, jj):
                nonlocal cnt
                i2 = cnt % 2
                cnt += 1
                pq = k.ps[i2]
                for kc in range(KC):
                    mm(S, pq[:, :T], a[:, kc, jj * 128:(jj + 1) * 128], hT[:, kc, :T], kc == 0, kc == KC - 1,
                       [ka, ("hT", kc)] if not isinstance(ka, list) else ka + [("hT", kc)], ("ps", i2))
                return pq, i2

            def slab(c0):
                nonlocal na
                a = wa[na % 2]
                ka = "wa%d" % (na % 2)
                na += 1
                S.add("pool", lambda e: e.dma_start(
                    out=a[:], in_=w_in[:, c0:c0 + 256].rearrange("(kc p) n -> p kc n", p=128)), writes=[ka], dma=True)
                return a, ka

            if own:
                for sl in range(3):
                    a, ka = slab(sl * 256)
                    for jj in range(2):
                        c = sl * 2 + jj
                        pq, i2 = proj_chunk(a, ka, jj)
                        S.add("act", lambda e, pq=pq, c=c: e.activation(out=lat[:, c, :T], in_=pq[:, :T], func=AF.Identity),
                              reads=[("ps", i2)], writes=[("lsrc", c)])
                lat_norm(k, lat, 6, T, qg, latn, rstd2, sqb, 6)
                lr = [("ldst", c) for c in range(6)]
                for h in range(8):
                    i2 = cnt % 2
                    cnt += 1
                    pq = k.ps[i2]
                    for kc in range(6):
                        mm(S, pq[:, :T], wuq[:, kc, h * 192:h * 192 + 128], latn[:, kc, :T], kc == 0, kc == 5,
                           wuq_r + [("ldst", kc)], ("ps", i2))
                    o_ = ob[i2]
                    S.add("act", lambda e, pq=pq, o_=o_: e.activation(out=o_[:, :T], in_=pq[:, :T], func=AF.Identity),
                          reads=[("ps", i2)], writes=[("ob", i2)])
                    S.add("sp", lambda e, o_=o_, h=h: e.dma_start(out=qnT[h * 128:(h + 1) * 128, t0:t0 + T], in_=o_[:, :T]),
                          reads=[("ob", i2)], writes=[("dr", "qnT", t0, h)], dma=True)
                for j in range(4):
                    i2 = cnt % 2
                    cnt += 1
                    pq = k.ps[i2]
                    for kc in range(6):
                        mm(S, pq[:, :T], wqr[:, kc, j, :], latn[:, kc, :T], kc == 0, kc == 5, wqr_r + [("ldst", kc)], ("ps", i2))
                    o_ = ob[i2]
                    rope_from_psum(k, pq[:, :T], ("ps", i2), k.ps[2 + i2], ("ps", 2 + i2), cos[:, :T], sin[:, :T], T,
                                   o_[:, :T], [("ob", i2)], qraw[i2], r1[i2], r2[i2], (("qraw", i2), ("r1", i2), ("r2", i2)))
                    S.add("sp", lambda e, o_=o_, j=j: e.dma_start(out=qrT[j * 128:(j + 1) * 128, t0:t0 + T], in_=o_[:, :T]),
                          reads=[("ob", i2)], writes=[("dr", "qrT", t0, j)], dma=True)
            for sl in range(2):
                a, ka = slab(768 + sl * 256)
                for jj in range(2):
                    c = sl * 2 + jj
                    pq, i2 = proj_chunk(a, ka, jj)
                    S.add("act", lambda e, pq=pq, c=c: e.activation(out=lat[:, c, :T], in_=pq[:, :T], func=AF.Identity),
                          reads=[("ps", i2)], writes=[("lsrc", c)])
            lat_norm(k, lat, 4, T, kg, latn, rstd2, sqb, 6)
            for c in range(4):
                S.add("sp", lambda e, c=c: e.dma_start(out=kvd(c)[:, oc0:oc0 + T], in_=latn[:, c, :T]),
                      reads=[("ldst", c)], writes=[("dr", "ckvn", t0, c)], dma=True)
            pq, i2 = proj_chunk(krw, [("krw", 0), ("krw", 1)], 0)
            o_ = ob[i2]
            rope_from_psum(k, pq[:, :T], ("ps", i2), k.ps[2 + i2], ("ps", 2 + i2), cos[:, :T], sin[:, :T], T,
                           o_[:, :T], [("ob", i2)], qraw[i2], r1[i2], r2[i2], (("qraw", i2), ("r1", i2), ("r2", i2)))
            S.add("sp", lambda e, o_=o_: e.dma_start(out=kvd(4)[:, oc0:oc0 + T], in_=o_[:, :T]),
                  reads=[("ob", i2)], writes=[("dr", "krT", t0)], dma=True)
            if own:
                for sl in range(4):
                    a, ka = slab(1344 + sl * 256)
                    for jj in range(2):
                        c = sl * 2 + jj
                        pq, i2 = proj_chunk(a, ka, jj)
                        o_ = ob[i2]
                        S.add("act", lambda e, pq=pq, o_=o_: e.activation(out=o_[:, :T], in_=pq[:, :T], func=AF.Identity),
                              reads=[("ps", i2)], writes=[("ob", i2)])
                        S.add("sp", lambda e, o_=o_, c=c: e.dma_start(out=dbT[c * 128:(c + 1) * 128, t0:t0 + T], in_=o_[:, :T]),
                              reads=[("ob", i2)], writes=[("dr", "dbT", t0, c)], dma=True)
                for sl in range(4):
                    a, ka = slab(2368 + sl * 256)
                    a2, ka2 = slab(3392 + sl * 256)
                    for jj in range(2):
                        c = sl * 2 + jj
                        pq, i2 = proj_chunk(a, ka, jj)
                        dc_ = dcs[i2]
                        S.add("act", lambda e, pq=pq, dc_=dc_: e.activation(out=dc_[:, :T], in_=pq[:, :T], func=AF.Identity),
                              reads=[("ps", i2)], writes=[("dcs", i2)])
                        pq2, j2 = proj_chunk(a2, ka2, jj)
                        z_ = zo[i2]
                        S.add("dve", lambda e, pq2=pq2, dc_=dc_, z_=z_: e.tensor_tensor(
                            out=z_[:, :T], in0=pq2[:, :T], in1=dc_[:, :T], op=ALU.mult),
                            reads=[("ps", j2), ("dcs", i2)], writes=[("zo", i2)])
                        S.add("sp", lambda e, z_=z_, c=c: e.dma_start(out=zT[c * 128:(c + 1) * 128, t0:t0 + T], in_=z_[:, :T]),
                              reads=[("zo", i2)], writes=[("dr", "zT", t0, c)], dma=True)
                        for (tt, which, col) in ((0, 0, 0), (NOWN - 512, 1, 511)):
                            if t0 == tt:
                                S.add("sp", lambda e, z_=z_, c=c, which=which, col=col: e.dma_start(
                                    out=zb[which:which + 1, :].rearrange("a (p c) -> p (a c)", c=8)[:, c:c + 1],
                                    in_=z_[:, col:col + 1], allow_slow_non_contiguous=True),
                                    reads=[("zo", i2)], writes=[("dr", "zb", which, c)], dma=True)
        S.barrier()


def phase_exchange(k):
    nc, S = k.nc, k.S
    xin = [k.dram("xch_in%d" % i, [128, NOWN], BF16) for i in range(5)]
    xall = [k.dram("xch_all%d" % i, [512, NOWN], BF16) for i in range(5)]
    zb = k.dram("zb_in", [2, 1024])
    zall = k.dram("zb_all", [8, 1024])
    groups = [[0, 1, 2, 3], [4, 5, 6, 7]]
    for i in range(5):
        S.add_cc(lambda e, i=i: e.collective_compute("AllGather", ALU.bypass, replica_groups=groups, ins=[xin[i].opt()], outs=[xall[i].opt()]))
    S.add_cc(lambda e: e.collective_compute("AllGather", ALU.bypass, replica_groups=groups, ins=[zb.opt()], outs=[zall.opt()]))
    S.barrier()
    with ExitStack() as st:
        al = k.alloc(st)
        zsel = al("zsel", [128, 8, 8], F32)
        S.add("sp", lambda e: e.dma_start(out=zsel[:], in_=zall.rearrange("j (p c) -> p j c", c=8)), reads=["zall"], writes=["zsel"], dma=True)
        for w in range(2):
            for j in range(8):
                if j == 0:
                    S.add("dve", lambda e, w=w, j=j: e.tensor_scalar(out=k.zpn[:, w, :], in0=zsel[:, j, :], scalar1=k.selT[:, w, j:j + 1],
                                                                     scalar2=None, op0=ALU.mult), reads=["zsel", "selT"], writes=["zpn"])
                else:
                    S.add("dve", lambda e, w=w, j=j: e.scalar_tensor_tensor(out=k.zpn[:, w, :], in0=zsel[:, j, :], scalar=k.selT[:, w, j:j + 1],
                                                                            op0=ALU.mult, in1=k.zpn[:, w, :], op1=ALU.add),
                          reads=["zsel", "selT", "zpn"], writes=["zpn"])
        S.barrier()


def phase_mla(k):
    nc, S = k.nc, k.S
    xall = [k.dram("xch_all%d" % i, [512, NOWN], BF16) for i in range(5)]
    ctxkv = k.dram("ctxkv", [640, NCTX], BF16)
    w_ukv = k.dram("c_w_ukv", [1, 512, 2048])[0]
    qnT = k.dram("qnT", [1024, NOWN], BF16)
    qrT = k.dram("qrT", [512, NOWN], BF16)
    oT = k.dram("oT2", [1024, NOWN], BF16)
    scale = 192.0 ** -0.5
    with ExitStack() as st:
        al = k.alloc(st)
        ckv = al("ckv", [128, 4, NK], BF16)
        krd = al("krd", [128, NK], BF16)
        wkv = al("wkv", [128, 4, 2048], BF16)
        knT = al("knT", [128, NK], BF16)
        vh = al("vh", [128, NKC, 128], BF16)
        qn = [al("qn%d" % i, [128, 512], BF16) for i in range(2)]
        qr = [al("qr%d" % i, [128, 512], BF16) for i in range(2)]
        pT = [al("pT%d" % i, [128, 512], BF16) for i in range(6)]
        rr = al("rr", [128, 512], F32)
        oo = [al("oo%d" % i, [128, 512], BF16) for i in range(2)]
        for c in range(4):
            for r in range(4):
                S.add("sp", lambda e, c=c, r=r: e.dma_start(out=ckv[:, c, r * NOWN:(r + 1) * NOWN],
                                                            in_=xall[c][r * 128:(r + 1) * 128, :]),
                      writes=[("ckv", c, r)], dma=True)
            S.add("sp", lambda e, c=c: e.dma_start(out=ckv[:, c, SEQ:NK], in_=ctxkv[c * 128:(c + 1) * 128, :]), writes=[("ckv", c, 4)], dma=True)
        for r in range(4):
            S.add("sp", lambda e, r=r: e.dma_start(out=krd[:, r * NOWN:(r + 1) * NOWN], in_=xall[4][r * 128:(r + 1) * 128, :]),
                  writes=[("krd", r)], dma=True)
        S.add("sp", lambda e: e.dma_start(out=krd[:, SEQ:NK], in_=ctxkv[512:640, :]), writes=[("krd", 4)], dma=True)
        for i in range(4):
            S.add("pool", lambda e, i=i: e.dma_start(
                out=wkv[:, :, i * 512:(i + 1) * 512], in_=w_ukv[:, i * 512:(i + 1) * 512].rearrange("(kc p) n -> p kc n", p=128)),
                writes=[("wkv", i)], dma=True)
        ckr = [("ckv", c, r) for c in range(4) for r in range(5)]
        krr = [("krd", r) for r in range(5)]
        npT = 0
        nq = 0
        for h in range(8):
            wr = [("wkv", h // 2)]
            for kg in range(17):
                n = 512 if kg < 16 else 256
                pk_ = k.ps[6 + kg % 2]
                for kc in range(4):
                    mm(S, pk_[:, :n], wkv[:, kc, h * 256:h * 256 + 128], ckv[:, kc, kg * 512:kg * 512 + n], kc == 0, kc == 3,
                       wr + ckr, ("ps", 6 + kg % 2))
                S.add("act", lambda e, pk_=pk_, kg=kg, n=n: e.activation(out=knT[:, kg * 512:kg * 512 + n], in_=pk_[:, :n], func=AF.Identity),
                      reads=[("ps", 6 + kg % 2)], writes=["knT"])
            for g4 in range(17):
                nb = 4 if g4 < 16 else 2
                pv_ = k.ps[6 + g4 % 2]
                for bb in range(nb):
                    kb = g4 * 4 + bb
                    for kc in range(4):
                        mm(S, pv_[:, bb * 128:(bb + 1) * 128], ckv[:, kc, kb * 128:(kb + 1) * 128],
                           wkv[:, kc, h * 256 + 128:h * 256 + 256], kc == 0, kc == 3, wr + ckr, ("ps", 6 + g4 % 2))
                S.add("dve", lambda e, pv_=pv_, g4=g4, nb=nb: e.tensor_copy(
                    out=vh[:, g4 * 4:g4 * 4 + nb, :], in_=pv_[:, :nb * 128].rearrange("p (b d) -> p b d", d=128)),
                    reads=[("ps", 6 + g4 % 2)], writes=["vh"])
            e2 = h % 2
            for qgi in range(4):
                i2 = nq % 2
                nq += 1
                q0 = qgi * 512
                S.add("sp", lambda e, i2=i2, q0=q0, h=h: e.dma_start(out=qn[i2][:], in_=qnT[h * 128:(h + 1) * 128, q0:q0 + 512]),
                      writes=[("qn", i2)], dma=True)
                S.add("sp", lambda e, i2=i2, q0=q0, h=h: e.dma_start(out=qr[i2][:], in_=qrT[(h // 2) * 128:(h // 2 + 1) * 128, q0:q0 + 512]),
                      writes=[("qr", i2)], dma=True)
                po, pd = k.ps[4], k.ps[5]
                LOOK = 3
                pend = {}
                for kc in range(NKC + LOOK):
                    if kc < NKC:
                        bi = kc % 4
                        ps_ = k.ps[bi]
                        mm(S, ps_[:, :], knT[:, kc * 128:(kc + 1) * 128], qn[i2][:], True, False, ["knT", ("qn", i2)], ("ps", bi))
                        mm(S, ps_[:, :], krd[64 * e2:64 * e2 + 64, kc * 128:(kc + 1) * 128], qr[i2][64 * e2:64 * e2 + 64, :], False, True,
                           krr + [("qr", i2)], ("ps", bi))
                        p = pT[npT % 6]
                        pk = ("pT", npT % 6)
                        npT += 1
                        S.add("act", lambda e, p=p, ps_=ps_: e.activation(out=p[:], in_=ps_[:, :], func=AF.Exp, scale=scale),
                              reads=[("ps", bi)], writes=[pk])
                        pend[kc] = (p, pk)
                    j = kc - LOOK
                    if j >= 0:
                        p, pk = pend.pop(j)
                        mm(S, po[:, :], vh[:, j, :], p[:], j == 0, j == NKC - 1, ["vh", pk], ("ps", 4))
                        mm(S, pd[:, :], k.ones_b[:], p[:], j == 0, j == NKC - 1, ["ones_b", pk], ("ps", 5))
                S.add("dve", lambda e, pd=pd: e.reciprocal(out=rr[:], in_=pd[:, :]), reads=[("ps", 5)], writes=["rr"])
                o_ = oo[i2]
                S.add("dve", lambda e, po=po, o_=o_: e.tensor_tensor(out=o_[:], in0=po[:, :], in1=rr[:], op=ALU.mult),
                      reads=[("ps", 4), "rr"], writes=[("oo", i2)])
                S.add("sp", lambda e, o_=o_, h=h, q0=q0: e.dma_start(out=oT[h * 128:(h + 1) * 128, q0:q0 + 512], in_=o_[:]),
                      reads=[("oo", i2)], writes=[("dr", "oT2", h, q0)], dma=True)
        S.barrier()


def phase_cd_out(k, x_in, x_out):
    nc, S = k.nc, k.S
    l = 1
    w = k.dram("cd_w_out", [1, D, D])[0]
    oT = k.dram("oT2", [1024, NOWN], BF16)
    dbT = k.dram("dbT", [1024, NOWN], BF16)
    zT = k.dram("zT", [1024, NOWN])
    cwD = k.dram("convT", [128, 24])
    with ExitStack() as st:
        al = k.alloc(st)
        xres = al("xres", [128, KC, 512], F32)
        hT = al("hT", [128, KC, 512], BF16)
        ze = al("ze", [128, 8, 514], F32)
        dbs = al("dbs", [128, 8, 512], BF16)
        cw = al("cw", [128, 24], F32)
        ct = [al("ct%d" % i, [128, 512], F32) for i in range(2)]
        wa = [al("wa%d" % i, [128, KC, 256], BF16) for i in range(2)]
        S.add("sp", lambda e: e.dma_start(out=cw[:], in_=cwD), writes=["cw"], dma=True)
        na = 0
        for (t0, T, segs) in OWN_TILES:
            load_x_tile(k, xres, x_in, t0, T)
            for c in range(8):
                S.add("sp", lambda e, c=c, t0=t0: e.dma_start(out=hT[:, c, :], in_=oT[c * 128:(c + 1) * 128, t0:t0 + 512]),
                      writes=[("hT", c)], dma=True)
            S.add("sp", lambda e, t0=t0: e.dma_start(out=ze[:, :, 1:513], in_=zT[:, t0:t0 + 512].rearrange("(c p) t -> p c t", p=128)),
                  writes=[("ze", 1)], dma=True)
            if t0 > 0:
                S.add("sp", lambda e, t0=t0: e.dma_start(out=ze[:, :, 0:1], in_=zT[:, t0 - 1:t0].rearrange("(c p) t -> p c t", p=128),
                                                         allow_slow_non_contiguous=True), writes=[("ze", 0)], dma=True)
            else:
                S.add("dve", lambda e: e.tensor_copy(out=ze[:, :, 0], in_=k.zpn[:, 0, :]), reads=["zpn"], writes=[("ze", 0)])
            if t0 + 512 < NOWN:
                S.add("sp", lambda e, t0=t0: e.dma_start(out=ze[:, :, 513:514], in_=zT[:, t0 + 512:t0 + 513].rearrange("(c p) t -> p c t", p=128),
                                                         allow_slow_non_contiguous=True), writes=[("ze", 2)], dma=True)
            else:
                S.add("dve", lambda e: e.tensor_copy(out=ze[:, :, 513], in_=k.zpn[:, 1, :]), reads=["zpn"], writes=[("ze", 2)])
            S.add("sp", lambda e, t0=t0: e.dma_start(out=dbs[:], in_=dbT[:, t0:t0 + 512].rearrange("(c p) t -> p c t", p=128)),
                  writes=["dbs"], dma=True)
            zr = [("ze", i) for i in range(3)]
            for c in range(8):
                t_ = ct[c % 2]
                tk = ("ct", c % 2)
                S.add("dve", lambda e, c=c, t_=t_: e.tensor_scalar(out=t_[:], in0=ze[:, c, 0:512], scalar1=cw[:, c:c + 1], scalar2=None,
                                                                   op0=ALU.mult), reads=zr + ["cw"], writes=[tk])
                S.add("dve", lambda e, c=c, t_=t_: e.scalar_tensor_tensor(out=t_[:], in0=ze[:, c, 1:513], scalar=cw[:, 8 + c:9 + c],
                                                                          op0=ALU.mult, in1=t_[:], op1=ALU.add), reads=zr + ["cw", tk], writes=[tk])
                S.add("dve", lambda e, c=c, t_=t_: e.scalar_tensor_tensor(out=t_[:], in0=ze[:, c, 2:514], scalar=cw[:, 16 + c:17 + c],
                                                                          op0=ALU.mult, in1=t_[:], op1=ALU.add), reads=zr + ["cw", tk], writes=[tk])
                S.add("dve", lambda e, c=c, t_=t_: e.tensor_tensor(out=hT[:, 8 + c, :], in0=t_[:], in1=dbs[:, c, :], op=ALU.mult),
                      reads=[tk, "dbs"], writes=[("hT", 8 + c)])
            for sl in range(KC // 2):
                a = wa[na % 2]
                ka = "wa%d" % (na % 2)
                na += 1
                S.add("pool", lambda e, a=a, sl=sl: e.dma_start(
                    out=a[:], in_=w[:, sl * 256:(sl + 1) * 256].rearrange("(kc p) n -> p kc n", p=128)), writes=[ka], dma=True)
                for jj in range(2):
                    c = sl * 2 + jj
                    py = k.ps[c % 2]
                    for kc in range(KC):
                        mm(S, py[:, :T], a[:, kc, jj * 128:(jj + 1) * 128], hT[:, kc, :T], kc == 0, kc == KC - 1,
                           [ka, ("hT", kc)], ("ps", c % 2))
                    residual_store(k, py, ("ps", c % 2), xres, c, T, segs, l, 1, x_out, t0)
        S.barrier()


OWN_TILES = [(i * 512, 512, [(0, 512, 0)]) for i in range(4)]
MISC_TILE = (2048, 512, [(0, 256, 0), (256, 256, 1)])
CTX_TILE = (CT0, 256, [(0, 256, 1)])


def fm(v):
    v = np.asarray(v)
    lead = v.shape[:-1]
    n = v.shape[-1] // 128
    r = v.reshape(lead + (n, 128))
    r = np.moveaxis(r, -1, 0)
    return np.ascontiguousarray(r.reshape(128, -1))


def prep_core(inp, core):
    b, q = core // 4, core % 4
    p0 = q * NOWN
    x = inp["x"]
    xT = np.zeros((D, NT), np.float32)
    xT[:, 0:NOWN] = x[b, p0:p0 + NOWN].T
    if q > 0:
        xT[:, HP0:HP0 + 128] = x[b, p0 - 128:p0].T
    if q < 3:
        xT[:, HN0:HN0 + 128] = x[b, p0 + NOWN:p0 + NOWN + 128].T
    xT[:, CT0:CT0 + NCTX] = inp["ctx"][b].T
    m = {"xT": xT}
    m["cT"] = np.ascontiguousarray(np.stack([inp["c"][b], inp["c_ctx"]], axis=1))
    pos = np.zeros(NT, np.int64)
    pos[0:NOWN] = p0 + np.arange(NOWN)
    pos[HP0:HP0 + 128] = p0 - 128 + np.arange(128)
    pos[HN0:HN0 + 128] = p0 + NOWN + np.arange(128)
    pos = np.clip(pos, 0, SEQ - 1)
    row = (pos // 64).astype(np.float32)
    col = (pos % 64).astype(np.float32)
    inv = (10000.0 ** (-np.arange(0, 32, 2, dtype=np.float32) / 32)).astype(np.float32)
    ang = np.zeros((64, NT), np.float32)
    ang[0:16] = (row[None, :] * inv[:, None]).astype(np.float32)
    ang[16:32] = ang[0:16]
    ang[32:48] = (col[None, :] * inv[:, None]).astype(np.float32)
    ang[48:64] = ang[32:48]
    cosT = np.cos(ang).astype(np.float32)
    sinT = np.sin(ang).astype(np.float32)
    cosT[:, CT0:] = 1.0
    sinT[:, CT0:] = 0.0
    m["cosT"] = np.ascontiguousarray(np.concatenate([cosT, cosT], 0))
    m["sinT"] = np.ascontiguousarray(np.concatenate([sinT, sinT], 0))
    j = np.arange(128)[:, None]
    i = np.arange(128)[None, :]
    mk = np.zeros((128, 4, 128), np.float32)
    mk[:, 0, :] = (j >= i)
    mk[:, 1, :] = (j <= i)
    mk[:, 2, :] = (j >= i) * (1.0 if q > 0 else 0.0)
    mk[:, 3, :] = (j <= i) * (1.0 if q < 3 else 0.0)
    m["masks"] = mk
    sel = np.zeros((128, 2, 8), np.float32)
    if q > 0:
        sel[:, 0, 2 * (q - 1) + 1] = 1.0
    if q < 3:
        sel[:, 1, 2 * (q + 1)] = 1.0
    m["selT"] = sel
    return m


def prep_shared(inp):
    m = {}
    m["mod_w"] = inp["mod_w"]
    m["mod_bT"] = np.stack([fm(inp["mod_b"][l]) for l in range(2)])
    m["norm_gT"] = np.stack([fm(inp["norm_g"][l]) for l in range(2)])
    for n in ("ffn_w1", "ffn_w3", "ffn_w2", "ab_w_in", "ab_w_out", "cd_w_in", "cd_w_out", "c_w_uq", "c_w_ukv"):
        m[n] = inp[n]
    pm = np.zeros((128, 128), np.float32)
    for mm_ in range(128):
        if mm_ % 32 < 16:
            pm[mm_ + 16, mm_] = -1.0
        else:
            pm[mm_ - 16, mm_] = 1.0
    m["permM"] = pm
    sk = inp["a_sink"][0]
    m["sinkT"] = np.ascontiguousarray(np.stack([np.repeat(sk[2 * c:2 * c + 2], 64) for c in range(8)], axis=1))
    m["b_wsT"] = np.ascontiguousarray(np.transpose(inp["b_ws"][0], (2, 0, 1)))
    m["b_biasbc"] = np.ascontiguousarray(np.broadcast_to(inp["b_bias"][0][None], (128, 8, 128)))
    m["c_qnT"] = fm(inp["c_q_norm"][0])
    m["c_kvnT"] = fm(inp["c_kv_norm"][0])
    m["convT"] = fm(inp["d_conv_w"][0])
    m["fnT"] = fm(inp["final_norm"])
    return m


def load_final_consts(k):
    fnD = k.dram("fnT", [128, 16])
    k.fnT = k.sb("fnT_sb", [128, 16], F32)
    k.S.add("sp", lambda e: e.dma_start(out=k.fnT[:], in_=fnD), writes=["fnT"], dma=True)
    selD = k.dram("selT", [128, 2, 8])
    k.selT = k.sb("selT_sb", [128, 2, 8], F32)
    k.zpn = k.sb("zpn_sb", [128, 2, 8], F32)
    k.S.add("sp", lambda e: e.dma_start(out=k.selT[:], in_=selD), writes=["selT"], dma=True)


INS = ["xT", "cT", "mod_w", "mod_bT", "norm_gT", "ffn_w1", "ffn_w3", "ffn_w2", "ab_w_in", "ab_w_out", "cosT", "sinT",
       "permM", "masks", "sinkT", "b_wsT", "b_biasbc", "cd_w_in", "c_w_uq", "c_qnT", "c_kvnT",
       "cd_w_out", "c_w_ukv", "convT", "fnT", "selT"]
OUTS = ["outT"]


def build():
    k = K(INS, OUTS)
    xT = k.dram("xT", [D, NT])
    x1, x2, x3 = k.dram("x1", [D, NT]), k.dram("x2", [D, NT]), k.dram("x3", [D, NT])
    x4, x5 = k.dram("x4", [D, NT]), k.dram("x5", [D, NOWN])
    outT = k.dram("outT", [D, NOWN])
    phase_setup(k)
    load_rope_consts(k)
    load_final_consts(k)
    phase_mod(k, (0, 1))
    phase_ffn(k, 0, 0, xT, x1, OWN_TILES + [MISC_TILE])
    phase_ab_in(k, x1, OWN_TILES + [MISC_TILE])
    phase_ab_mix(k)
    phase_outproj(k, 0, "ab_w_out", k.dram("oT", [2048, NT], BF16), x1, x2, OWN_TILES + [CTX_TILE])
    phase_ffn(k, 0, 1, x2, x3, OWN_TILES + [CTX_TILE])
    phase_ffn(k, 1, 0, x3, x4, OWN_TILES + [CTX_TILE])
    phase_cd_in(k, x4, OWN_TILES + [CTX_TILE])
    phase_exchange(k)
    phase_mla(k)
    phase_cd_out(k, x4, x5)
    phase_ffn(k, 1, 1, x5, None, OWN_TILES, final_out=outT)
    k.S.emit(k.st)
    k.st.close()
    return k


def kernel(**inp):
    inp = {kk: np.asarray(v) for kk, v in inp.items()}
    n = 8
    sh = prep_shared(inp)
    for nm in ("ffn_w1", "ffn_w3", "ffn_w2"):
        a = inp[nm]
        sh[nm] = a.reshape((4,) + a.shape[2:])
    k = build()
    maps = []
    for c in range(n):
        m = dict(sh)
        m.update(prep_core(inp, c))
        maps.append({kk: m[kk] for kk in INS})
    res = run_bass_kernel_spmd(k.nc, maps, core_ids=list(range(n))).results
    out = np.zeros((2, SEQ, D), np.float32)
    for c in range(n):
        b, q = c // 4, c % 4
        out[b, q * NOWN:(q + 1) * NOWN, :] = np.asarray(res[c]["outT"]).T
    return out
```

```python
import math
import types
from contextlib import ExitStack
import numpy as np
import concourse.bass as bass
import concourse.mybir as mybir
from concourse.bass_utils import run_bass_kernel_spmd

F32 = mybir.dt.float32
BF16 = mybir.dt.bfloat16
AF = mybir.ActivationFunctionType
ALU = mybir.AluOpType

D = 2048
FFN = 5632
NFC = FFN // 128
KC = D // 128
SEQ = 8192
NOWN = 2048
NCTX = 256
NT = 2560
HP0, HN0, CT0 = 2048, 2176, 2304
EPS = 1e-6
ENGS = ("pe", "act", "dve", "pool", "sp")
N_SW = 4
DBG_NOPERM = False
ROPE_ADD_ENG = "pool"
DBG_SECT = None
SW_FRESH = False


def _freeze(fn):
    if fn is None or fn.__closure__ is None:
        return fn
    cells = []
    for c in fn.__closure__:
        try:
            cells.append(types.CellType(c.cell_contents))
        except ValueError:
            cells.append(c)
    return types.FunctionType(fn.__code__, fn.__globals__, fn.__name__, fn.__defaults__, tuple(cells))


class Op:
    __slots__ = ("eng", "fn", "deps", "idx", "ms", "dma", "sem", "semval", "msval", "tag")


class Sched:
    def __init__(self, nc, n_dma_sems=24):
        if SW_FRESH:
            n_dma_sems = 8
        self.nc = nc
        self.ops = {e: [] for e in ENGS}
        self.lastw = {}
        self.readers = {}
        self.n_dma_sems = n_dma_sems
        self.dma_rr = 0
        self.dma_count = [0] * n_dma_sems
        self.dma_last = [None] * n_dma_sems
        self.sw_keys = {}
        self.sw_rr = 0
        self.sw_last = {}

    def add(self, eng, fn, reads=(), writes=(), dma=False, tag=None):
        op = Op()
        op.eng, op.fn, op.dma, op.ms, op.tag = eng, _freeze(fn), dma, False, tag
        op.sem = op.semval = op.msval = None
        sw = dma and eng == "pool" and SW_FRESH
        if eng in ("act", "dve", "pool") and not dma:
            extra = [("psr", r[1]) for r in reads if isinstance(r, tuple) and len(r) == 2 and r[0] == "ps"]
            if extra:
                writes = list(writes) + extra
        deps = set()
        for r in reads:
            w = self.lastw.get(r)
            if w is not None:
                deps.add(w)
        for k in writes:
            w = self.lastw.get(k)
            if w is not None:
                deps.add(w)
            for rd in self.readers.get(k, ()):
                deps.add(rd)
        if sw:
            key = len(self.sw_keys)
            self.sw_keys[key] = key
            op.sem, op.semval = ("sw", key), 16
            op.tag = False
            self.sw_last[key] = op
        elif dma:
            if eng == "pool":
                s = self.n_dma_sems - N_SW + self.sw_rr
                self.sw_rr = (self.sw_rr + 1) % N_SW
            else:
                s = self.dma_rr
                self.dma_rr = (self.dma_rr + 1) % (self.n_dma_sems - N_SW)
            if self.dma_last[s] is not None:
                deps.add(self.dma_last[s])
            self.dma_count[s] += 1
            op.sem, op.semval = s, 16 * self.dma_count[s]
            self.dma_last[s] = op
        if eng == "pe":
            deps = {d for d in deps if d.dma or d.eng != "pe"}
        op.deps = deps
        for d in deps:
            if not d.dma:
                d.ms = True
        for r in reads:
            self.readers.setdefault(r, []).append(op)
        for k in writes:
            self.lastw[k] = op
            self.readers[k] = []
        op.idx = len(self.ops[eng])
        self.ops[eng].append(op)
        return op

    def add_cc(self, fn, reads=(), writes=()):
        op = self.add("pool", fn, reads=reads, writes=[("cc_issue",)])
        op.tag = "cc"
        self.n_cc = getattr(self, "n_cc", 0) + 1
        op.semval = self.n_cc
        return op

    def barrier(self):
        lasts = [self.ops[e][-1] for e in ENGS if self.ops[e]]
        lasts = [x for x in lasts if not x.dma and x.fn is not None]
        dmas = [d for d in self.dma_last if d is not None] + list(self.sw_last.values())
        for e in ENGS:
            op = Op()
            op.eng, op.fn, op.dma, op.ms, op.tag = e, None, False, False, "barrier"
            op.sem = op.semval = op.msval = None
            op.deps = set(x for x in lasts if x.eng != e) | set(dmas)
            for d in op.deps:
                if not d.dma:
                    d.ms = True
            op.idx = len(self.ops[e])
            self.ops[e].append(op)
        self.lastw.clear()
        self.readers.clear()

    def emit(self, stack):
        nc = self.nc
        esem = {e: stack.enter_context(nc.semaphore("s_" + e)) for e in ENGS if e != "sp"}
        dsem = [stack.enter_context(nc.semaphore("d_%d" % i)) for i in range(self.n_dma_sems)]
        wsem = [stack.enter_context(nc.semaphore("w_%d" % i)) for i in range(len(self.sw_keys))]
        ccsem = stack.enter_context(nc.semaphore("ccsem"))
        for e in ENGS:
            c = 0
            for op in self.ops[e]:
                if op.ms and not op.dma:
                    c += 1
                    op.msval = c
        ops = self.ops
        final_dma = [(dsem[i], 16 * self.dma_count[i]) for i in range(self.n_dma_sems) if self.dma_count[i]]

        def run(ename, eng):
            known = {}
            for op in ops[ename]:
                for d in op.deps:
                    if d.dma and isinstance(d.sem, tuple):
                        key, sem, val = ("w", d.sem[1], id(d)), wsem[d.sem[1]], 16
                    elif d.dma:
                        key, sem, val = ("d", d.sem), dsem[d.sem], d.semval
                    else:
                        key, sem, val = d.eng, esem[d.eng], d.msval
                    if known.get(key, 0) < val:
                        eng.wait_ge(sem, val)
                        known[key] = val
                if op.fn is None:
                    continue
                if op.dma and isinstance(op.sem, tuple):
                    if op.tag:
                        eng.wait_ge(wsem[op.sem[1]], 16)
                        eng.sem_clear(wsem[op.sem[1]])
                    op.fn(eng).then_inc(wsem[op.sem[1]], 16)
                    continue
                ins = op.fn(eng)
                if op.tag == "cc":
                    ins.then_inc(ccsem, 1)
                    eng.wait_ge(ccsem, op.semval)
                    if op.ms:
                        eng.memset(self.cc_dummy[:], 0.0).then_inc(esem[ename], 1)
                    continue
                if op.dma:
                    ins.then_inc(dsem[op.sem], 16)
                elif op.ms:
                    ins.then_inc(esem[ename], 1)
            if ename == "sp":
                for sem, val in final_dma:
                    eng.wait_ge(sem, val)

        with nc.Block() as block:
            @block.tensor
            def _(e):
                run("pe", e)

            @block.scalar
            def _(e):
                run("act", e)

            @block.vector
            def _(e):
                run("dve", e)

            @block.gpsimd
            def _(e):
                run("pool", e)

            @block.sync
            def _(e):
                run("sp", e)


class K:
    def __init__(self, ext_in, ext_out):
        self.nc = bass.Bass("TRN2", target_bir_lowering=False)
        self.S = Sched(self.nc)
        self.ext_in, self.ext_out = set(ext_in), set(ext_out)
        self.dr = {}
        self.st = ExitStack()
        self.uid = 0
        self.ffn_sel = {(0, 0): 0, (0, 1): 1, (1, 0): 2, (1, 1): 3}

    def dram(self, name, shape, dtype=F32):
        if name in self.dr:
            return self.dr[name]
        kind = "ExternalInput" if name in self.ext_in else ("ExternalOutput" if name in self.ext_out else "Internal")
        t = self.nc.dram_tensor(name, list(shape), dtype, kind=kind).ap()
        self.dr[name] = t
        return t

    def sb(self, name, shape, dtype):
        return self.st.enter_context(self.nc.sbuf_tensor(name, list(shape), dtype))

    def alloc(self, st):
        self.uid += 1
        u = self.uid
        return lambda n, sh, dt: st.enter_context(self.nc.sbuf_tensor("%s_u%d" % (n, u), list(sh), dt))

    def psum(self, name):
        return self.st.enter_context(self.nc.psum_tensor(name, [128, 512], F32))


def mm(S, ps_ap, lhsT, rhs, start, stop, reads, pskey):
    S.add("pe", lambda e: e.matmul(ps_ap, lhsT, rhs, start=start, stop=stop), reads=reads, writes=[pskey])


def phase_setup(k):
    nc, S = k.nc, k.S
    k.ps = [k.psum("ps%d" % i) for i in range(8)]
    k.ones_f = k.sb("ones_f", [128, 128], F32)
    k.ones_b = k.sb("ones_b", [128, 128], BF16)
    S.add("pool", lambda e: e.memset(k.ones_f[:], 1.0), writes=["ones_f"])
    S.add("pool", lambda e: e.memset(k.ones_b[:], 1.0), writes=["ones_b"])
    k.eps_t = k.sb("eps_t", [128, 1], F32)
    S.cc_dummy = k.sb("cc_dummy", [128, 8], F32)
    S.add("pool", lambda e: e.memset(k.eps_t[:], EPS), writes=["eps_t"])


def phase_mod(k, layers=(0, 1)):
    nc, S = k.nc, k.S
    cT = k.dram("cT", [D, 2])
    nl = len(layers)
    mod_w = k.dram("mod_w", [nl, D, 9 * D])
    mod_bT = k.dram("mod_bT", [nl, 128, 144])
    norm_gT = k.dram("norm_gT", [nl, 128, 48])
    k.modT = [k.sb("modT%d" % l, [128, 144, 2], F32) for l in range(2)]
    k.modA = [k.sb("modA%d" % l, [128, 48, 2], F32) for l in range(2)]
    k.modG = [k.sb("modG%d" % l, [128, 48, 2], F32) for l in range(2)]
    with ExitStack() as st:
        cin = st.enter_context(nc.sbuf_tensor("cin_sb", [128, KC, 2], F32))
        scT = st.enter_context(nc.sbuf_tensor("scT", [128, KC, 2], BF16))
        mb = st.enter_context(nc.sbuf_tensor("mb", [128, 144], F32))
        ng = st.enter_context(nc.sbuf_tensor("ng", [128, 48], F32))
        wm = [st.enter_context(nc.sbuf_tensor("wm%d" % i, [128, KC, 512], BF16)) for i in range(2)]
        S.add("sp", lambda e: e.dma_start(out=cin[:], in_=cT.rearrange("(kc p) n -> p kc n", p=128)), writes=["cin"], dma=True)
        S.add("act", lambda e: e.activation(out=scT[:], in_=cin[:], func=AF.Silu), reads=["cin"], writes=["scT"])
        si = 0
        for li, l in enumerate(layers):
            S.add("sp", lambda e, li=li: e.dma_start(out=mb[:], in_=mod_bT[li]), writes=["mb"], dma=True)
            S.add("sp", lambda e, li=li: e.dma_start(out=ng[:], in_=norm_gT[li]), writes=["ng"], dma=True)
            ps = k.ps[l]
            for sl in range(36):
                w = wm[si % 2]
                wkey = "wm%d" % (si % 2)
                si += 1
                S.add("pool", lambda e, w=w, li=li, sl=sl: e.dma_start(
                    out=w[:], in_=mod_w[li, :, sl * 512:(sl + 1) * 512].rearrange("(kc p) n -> p kc n", p=128)),
                    writes=[wkey], dma=True)
                for jj in range(4):
                    j = sl * 4 + jj
                    for kc in range(KC):
                        mm(S, ps[:, 2 * j:2 * j + 2], w[:, kc, jj * 128:(jj + 1) * 128], scT[:, kc, :],
                           kc == 0, kc == KC - 1, [wkey, "scT"], ("ps", l))
            modT = k.modT[l]
            for cnd in range(2):
                S.add("dve", lambda e, modT=modT, ps=ps, cnd=cnd: e.tensor_tensor(
                    out=modT[:, :, cnd], in0=ps[:, 0:288].rearrange("p (j c) -> p j c", c=2)[:, :, cnd], in1=mb[:], op=ALU.add),
                    reads=[("ps", l), "mb"], writes=[("modT", l)])
            for s in range(3):
                for cnd in range(2):
                    S.add("dve", lambda e, l=l, s=s, cnd=cnd, modT=modT: e.scalar_tensor_tensor(
                        out=k.modA[l][:, s * 16:(s + 1) * 16, cnd], in0=modT[:, (3 * s + 1) * 16:(3 * s + 2) * 16, cnd],
                        scalar=1.0, op0=ALU.add, in1=ng[:, s * 16:(s + 1) * 16], op1=ALU.mult),
                        reads=[("modT", l), "ng"], writes=[("modA", l)])
                    S.add("dve", lambda e, l=l, s=s, cnd=cnd, modT=modT: e.tensor_scalar(
                        out=k.modG[l][:, s * 16:(s + 1) * 16, cnd], in0=modT[:, (3 * s + 2) * 16:(3 * s + 3) * 16, cnd],
                        scalar1=(1.0 if s == 1 else 0.5), scalar2=None, op0=ALU.mult),
                        reads=[("modT", l)], writes=[("modG", l)])
        S.barrier()


def load_x_tile(k, xres, x_in, t0, T):
    S = k.S
    for c in range(KC):
        S.add("sp", lambda e, c=c: e.dma_start(out=xres[:, c, :T], in_=x_in[c * 128:(c + 1) * 128, t0:t0 + T]),
              writes=[("xres", c)], dma=True)


def norm_mod(k, xres, hT, T, segs, l, s, tmp, rstd, sqb):
    S = k.S
    ps = k.ps[6]
    for c in range(KC):
        sq = sqb[c % 2]
        S.add("act", lambda e, c=c, sq=sq: e.activation(out=sq[:, :T], in_=xres[:, c, :T], func=AF.Square),
              reads=[("xres", c)], writes=[("sq", c % 2)])
        mm(S, ps[:, :T], k.ones_b[:], sq[:, :T], c == 0, c == KC - 1, [("sq", c % 2), "ones_b"], ("ps", 6))
    S.add("act", lambda e: e.activation(out=rstd[:, :T], in_=ps[:, :T], func=AF.Sqrt, scale=1.0 / D, bias=k.eps_t[:]),
          reads=[("ps", 6), "eps_t"], writes=["rstd"])
    S.add("dve", lambda e: e.reciprocal(out=rstd[:, :T], in_=rstd[:, :T]), reads=["rstd"], writes=["rstd"])
    A, SH = k.modA[l], k.modT[l]
    for c in range(KC):
        tb = tmp[c % 2]
        for (o, n, cnd) in segs:
            S.add("dve", lambda e, c=c, o=o, n=n, cnd=cnd, tb=tb: e.scalar_tensor_tensor(
                out=tb[:, o:o + n], in0=xres[:, c, o:o + n], scalar=A[:, s * 16 + c, cnd:cnd + 1], op0=ALU.mult,
                in1=rstd[:, o:o + n], op1=ALU.mult),
                reads=[("xres", c), "rstd", ("modA", l)], writes=[("tmp", c % 2)])
            S.add("act", lambda e, c=c, o=o, n=n, cnd=cnd, tb=tb: e.activation(
                out=hT[:, c, o:o + n], in_=tb[:, o:o + n], func=AF.Identity,
                bias=SH[:, 3 * s * 16 + c, cnd:cnd + 1], scale=1.0),
                reads=[("tmp", c % 2), ("modT", l)], writes=[("hT", c)])


def residual_store(k, ps_ap, pskey, xres, c, T, segs, l, s, x_out, t0):
    S = k.S
    G = k.modG[l]
    for (o, n, cnd) in segs:
        S.add("dve", lambda e, o=o, n=n, cnd=cnd: e.scalar_tensor_tensor(
            out=xres[:, c, o:o + n], in0=ps_ap[:, o:o + n], scalar=G[:, s * 16 + c, cnd:cnd + 1], op0=ALU.mult,
            in1=xres[:, c, o:o + n], op1=ALU.add),
            reads=[pskey, ("xres", c), ("modG", l)], writes=[("xres", c)])
    if x_out is not None:
        S.add("sp", lambda e: e.dma_start(out=x_out[c * 128:(c + 1) * 128, t0:t0 + T], in_=xres[:, c, :T]),
              reads=[("xres", c)], writes=[("xdram", x_out.tensor.name, t0, c)], dma=True)


def phase_ffn(k, l, widx, x_in, x_out, tiles, final_out=None):
    nc, S = k.nc, k.S
    s = 0 if widx == 0 else 2
    nsel = len(k.ffn_sel)
    wi_ = k.ffn_sel[(l, widx)]
    w1 = k.dram("ffn_w1", [nsel, D, FFN])[wi_]
    w3 = k.dram("ffn_w3", [nsel, D, FFN])[wi_]
    w2 = k.dram("ffn_w2", [nsel, FFN, D])[wi_]
    with ExitStack() as st:
        al = k.alloc(st)
        xres = al("xres", [128, KC, 512], F32)
        hT = al("hT", [128, KC, 512], BF16)
        gT = al("gT", [128, NFC, 512], BF16)
        tmp = [al("tmp%d" % i, [128, 512], F32) for i in range(2)]
        sqb = [al("sq%d" % i, [128, 512], BF16) for i in range(2)]
        rstd = al("rstd", [128, 512], F32)
        su = [al("su%d" % i, [128, 512], BF16) for i in range(2)]
        wa = [al("wa%d" % i, [128, KC, 256], BF16) for i in range(2)]
        wb = [al("wb%d" % i, [128, KC, 256], BF16) for i in range(2)]
        wc = [al("wc%d" % i, [128, NFC, 256], BF16) for i in range(2)]
        na = nc_ = 0
        for (t0, T, segs) in tiles:
            load_x_tile(k, xres, x_in, t0, T)
            norm_mod(k, xres, hT, T, segs, l, s, tmp, rstd, sqb)
            hreads = [("hT", c) for c in range(KC)]
            for sl in range(NFC // 2):
                a, b = wa[na % 2], wb[na % 2]
                ka, kb = "wa%d" % (na % 2), "wb%d" % (na % 2)
                na += 1
                S.add("pool", lambda e, a=a, sl=sl: e.dma_start(
                    out=a[:], in_=w1[:, sl * 256:(sl + 1) * 256].rearrange("(kc p) n -> p kc n", p=128)),
                    writes=[ka], dma=True)
                S.add("pool", lambda e, b=b, sl=sl: e.dma_start(
                    out=b[:], in_=w3[:, sl * 256:(sl + 1) * 256].rearrange("(kc p) n -> p kc n", p=128)),
                    writes=[kb], dma=True)
                for jj in range(2):
                    fc = sl * 2 + jj
                    pu, pv = k.ps[fc % 2], k.ps[2 + fc % 2]
                    for kc in range(KC):
                        mm(S, pu[:, :T], a[:, kc, jj * 128:(jj + 1) * 128], hT[:, kc, :T], kc == 0, kc == KC - 1,
                           [ka, ("hT", kc)], ("ps", fc % 2))
                    for kc in range(KC):
                        mm(S, pv[:, :T], b[:, kc, jj * 128:(jj + 1) * 128], hT[:, kc, :T], kc == 0, kc == KC - 1,
                           [kb, ("hT", kc)], ("ps", 2 + fc % 2))
                    sut = su[fc % 2]
                    S.add("act", lambda e, pu=pu, sut=sut: e.activation(out=sut[:, :T], in_=pu[:, :T], func=AF.Silu),
                          reads=[("ps", fc % 2)], writes=[("su", fc % 2)])
                    S.add("dve", lambda e, pv=pv, sut=sut, fc=fc: e.tensor_tensor(
                        out=gT[:, fc, :T], in0=pv[:, :T], in1=sut[:, :T], op=ALU.mult),
                        reads=[("ps", 2 + fc % 2), ("su", fc % 2)], writes=[("gT", fc)])
            for ds in range(KC // 2):
                w = wc[nc_ % 2]
                kw = "wc%d" % (nc_ % 2)
                nc_ += 1
                S.add("pool", lambda e, w=w, ds=ds: e.dma_start(
                    out=w[:], in_=w2[:, ds * 256:(ds + 1) * 256].rearrange("(fc p) n -> p fc n", p=128)),
                    writes=[kw], dma=True)
                for jj in range(2):
                    c = ds * 2 + jj
                    py = k.ps[4 + c % 2]
                    for fc in range(NFC):
                        mm(S, py[:, :T], w[:, fc, jj * 128:(jj + 1) * 128], gT[:, fc, :T], fc == 0, fc == NFC - 1,
                           [kw, ("gT", fc)], ("ps", 4 + c % 2))
                    residual_store(k, py, ("ps", 4 + c % 2), xres, c, T, segs, l, s, x_out, t0)
            if final_out is not None:
                final_norm_store(k, xres, T, t0, final_out, tmp, rstd, sqb)
        S.barrier()


def final_norm_store(k, xres, T, t0, out, tmp, rstd, sqb):
    S = k.S
    ps = k.ps[6]
    fn = k.fnT
    for c in range(KC):
        sq = sqb[c % 2]
        S.add("act", lambda e, c=c, sq=sq: e.activation(out=sq[:, :T], in_=xres[:, c, :T], func=AF.Square),
              reads=[("xres", c)], writes=[("sq", c % 2)])
        mm(S, ps[:, :T], k.ones_b[:], sq[:, :T], c == 0, c == KC - 1, [("sq", c % 2), "ones_b"], ("ps", 6))
    S.add("act", lambda e: e.activation(out=rstd[:, :T], in_=ps[:, :T], func=AF.Sqrt, scale=1.0 / D, bias=k.eps_t[:]),
          reads=[("ps", 6), "eps_t"], writes=["rstd"])
    S.add("dve", lambda e: e.reciprocal(out=rstd[:, :T], in_=rstd[:, :T]), reads=["rstd"], writes=["rstd"])
    for c in range(KC):
        S.add("dve", lambda e, c=c: e.scalar_tensor_tensor(
            out=xres[:, c, :T], in0=xres[:, c, :T], scalar=fn[:, c:c + 1], op0=ALU.mult, in1=rstd[:, :T], op1=ALU.mult),
            reads=[("xres", c), "rstd", "fnT"], writes=[("xres", c)])
        S.add("sp", lambda e, c=c: e.dma_start(out=out[c * 128:(c + 1) * 128, t0:t0 + T], in_=xres[:, c, :T]),
              reads=[("xres", c)], writes=[("odram", t0, c)], dma=True)


GC1, GC2 = 0.044715, 1.5957691216057308


def gelu_from_psum(k, ps_ap, pskey, out_ap, outkeys, n, t1, t2, t1k, t2k):
    S = k.S
    S.add("act", lambda e: e.activation(out=t1, in_=ps_ap, func=AF.Square), reads=[pskey], writes=[t1k])
    S.add("dve", lambda e: e.tensor_scalar(out=t1, in0=t1, scalar1=GC1, scalar2=1.0, op0=ALU.mult, op1=ALU.add),
          reads=[t1k], writes=[t1k])
    S.add("dve", lambda e: e.tensor_tensor(out=t1, in0=ps_ap, in1=t1, op=ALU.mult), reads=[pskey, t1k], writes=[t1k])
    S.add("act", lambda e: e.activation(out=t2, in_=t1, func=AF.Sigmoid, scale=GC2), reads=[t1k], writes=[t2k])
    S.add("dve", lambda e: e.tensor_tensor(out=out_ap, in0=ps_ap, in1=t2, op=ALU.mult), reads=[pskey, t2k], writes=outkeys)


def rope_from_psum(k, ps_ap, pskey, ps2, ps2key, cos, sin, T, out_ap, outkeys, qraw, t1, t2, keys):
    S = k.S
    qk, t1k, t2k = keys
    S.add("act", lambda e: e.activation(out=qraw[:, :T], in_=ps_ap, func=AF.Identity), reads=[pskey], writes=[qk])
    S.add("dve", lambda e: e.tensor_tensor(out=t1[:, :T], in0=ps_ap, in1=cos, op=ALU.mult), reads=[pskey, "rope"], writes=[t1k])
    mm(S, ps2[:, :T], (k.ones_b if DBG_NOPERM else k.permM)[:], qraw[:, :T], True, True, [qk, "permM"], ps2key)
    S.add("dve", lambda e: e.tensor_tensor(out=t2[:, :T], in0=ps2[:, :T], in1=sin, op=ALU.mult), reads=[ps2key, "rope"], writes=[t2k])
    S.add(ROPE_ADD_ENG, lambda e: e.tensor_tensor(out=out_ap, in0=t1[:, :T], in1=t2[:, :T], op=ALU.add), reads=[t1k, t2k], writes=outkeys)


def load_rope_consts(k):
    nc, S = k.nc, k.S
    permD = k.dram("permM", [128, 128])
    k.permM = k.sb("permM_sb", [128, 128], BF16)
    S.add("pool", lambda e: e.dma_start(out=k.permM[:], in_=permD), writes=["permM"], dma=True)


def phase_ab_in(k, x_in, tiles):
    nc, S = k.nc, k.S
    l, s = 0, 1
    w_in = k.dram("ab_w_in", [1, D, 3328])[0]
    cosD, sinD = k.dram("cosT", [128, NT]), k.dram("sinT", [128, NT])
    qT = k.dram("qT", [1024, NT], BF16)
    kT = k.dram("kT", [2, 128, NT], BF16)
    vtok = k.dram("vtok", [NT, 128], BF16)
    guT = k.dram("guT", [1024, NT], BF16)
    gvtok = k.dram("gvtok", [NT, 1024], BF16)
    with ExitStack() as st:
        al = k.alloc(st)
        xres = al("xres", [128, KC, 512], F32)
        hT = al("hT", [128, KC, 512], BF16)
        tmp = [al("tmp%d" % i, [128, 512], F32) for i in range(2)]
        sqb = [al("sq%d" % i, [128, 512], BF16) for i in range(2)]
        rstd = al("rstd", [128, 512], F32)
        cos, sin = al("cos", [128, 512], F32), al("sin", [128, 512], F32)
        wtm = al("wtm", [128, KC, 1152], BF16)
        kdw = al("kdw", [128, KC, 256], BF16)
        wa = [al("wa%d" % i, [128, KC, 256], BF16) for i in range(2)]
        qraw = [al("qraw%d" % i, [128, 512], BF16) for i in range(2)]
        r1 = [al("r1_%d" % i, [128, 512], F32) for i in range(2)]
        r2 = [al("r2_%d" % i, [128, 512], F32) for i in range(2)]
        ob = [al("ob%d" % i, [128, 512], BF16) for i in range(2)]
        obt = [al("obt%d" % i, [128, 1152], BF16) for i in range(2)]
        for i, (c0, n) in enumerate([(1152, 128), (2304, 512), (2816, 512)]):
            o = [0, 128, 640][i]
            S.add("pool", lambda e, c0=c0, n=n, o=o: e.dma_start(
                out=wtm[:, :, o:o + n], in_=w_in[:, c0:c0 + n].rearrange("(kc p) n -> p kc n", p=128)),
                writes=[("wtm", i)], dma=True)
        for i in range(4):
            c0 = 1024 + 64 * (i // 2)
            S.add("pool", lambda e, c0=c0, i=i: e.dma_start(
                out=kdw[:, :, i * 64:(i + 1) * 64], in_=w_in[:, c0:c0 + 64].rearrange("(kc p) n -> p kc n", p=128)),
                writes=[("kdw", i)], dma=True)
        na = 0
        cnt = 0
        for (t0, T, segs) in tiles:
            load_x_tile(k, xres, x_in, t0, T)
            S.add("sp", lambda e, t0=t0, T=T: e.dma_start(out=cos[:, :T], in_=cosD[:, t0:t0 + T]), writes=["rope"], dma=True)
            S.add("sp", lambda e, t0=t0, T=T: e.dma_start(out=sin[:, :T], in_=sinD[:, t0:t0 + T]), writes=["rope"], dma=True)
            norm_mod(k, xres, hT, T, segs, l, s, tmp, rstd, sqb)
            hreads = [("hT", c) for c in range(KC)]
            for sl in range(8):
                if DBG_SECT is not None and ("q" if sl < 4 else "gu") not in DBG_SECT:
                    continue
                a = wa[na % 2]
                ka = "wa%d" % (na % 2)
                na += 1
                c0 = sl * 256 if sl < 4 else 1280 + (sl - 4) * 256
                S.add("pool", lambda e, a=a, c0=c0: e.dma_start(
                    out=a[:], in_=w_in[:, c0:c0 + 256].rearrange("(kc p) n -> p kc n", p=128)), writes=[ka], dma=True)
                for jj in range(2):
                    ch = (sl % 4) * 2 + jj
                    i2 = cnt % 2
                    cnt += 1
                    pq = k.ps[i2]
                    for kc in range(KC):
                        mm(S, pq[:, :T], a[:, kc, jj * 128:(jj + 1) * 128], hT[:, kc, :T], kc == 0, kc == KC - 1,
                           [ka, ("hT", kc)], ("ps", i2))
                    o_ = ob[i2]
                    if sl < 4:
                        rope_from_psum(k, pq[:, :T], ("ps", i2), k.ps[2 + i2], ("ps", 2 + i2), cos[:, :T], sin[:, :T], T,
                                       o_[:, :T], [("ob", i2)], qraw[i2], r1[i2], r2[i2], (("qraw", i2), ("r1", i2), ("r2", i2)))
                        dst = qT[ch * 128:(ch + 1) * 128, t0:t0 + T]
                    else:
                        gelu_from_psum(k, pq[:, :T], ("ps", i2), o_[:, :T], [("ob", i2)], T, r1[i2][:, :T], r2[i2][:, :T],
                                       ("r1", i2), ("r2", i2))
                        dst = guT[ch * 128:(ch + 1) * 128, t0:t0 + T]
                    S.add("sp", lambda e, dst=dst, o_=o_, T=T: e.dma_start(out=dst, in_=o_[:, :T]),
                          reads=[("ob", i2)], writes=[("dr", dst.tensor.name, t0, ch)], dma=True)
            for kv in range(2):
                if DBG_SECT is not None and "k" not in DBG_SECT:
                    continue
                i2 = cnt % 2
                cnt += 1
                pq = k.ps[i2]
                for kc in range(KC):
                    mm(S, pq[:, :T], kdw[:, kc, kv * 128:(kv + 1) * 128], hT[:, kc, :T], kc == 0, kc == KC - 1,
                       [("kdw", 2 * kv), ("kdw", 2 * kv + 1), ("hT", kc)], ("ps", i2))
                o_ = ob[i2]
                rope_from_psum(k, pq[:, :T], ("ps", i2), k.ps[2 + i2], ("ps", 2 + i2), cos[:, :T], sin[:, :T], T,
                               o_[:, :T], [("ob", i2)], qraw[i2], r1[i2], r2[i2], (("qraw", i2), ("r1", i2), ("r2", i2)))
                dst = kT[kv, :, t0:t0 + T]
                S.add("sp", lambda e, dst=dst, o_=o_, T=T: e.dma_start(out=dst, in_=o_[:, :T]),
                      reads=[("ob", i2)], writes=[("dr", "kT", t0, kv)], dma=True)
            for tb in range(T // 128):
                if DBG_SECT is not None and "tok" not in DBG_SECT:
                    continue
                ot = obt[tb % 2]
                okey = ("obt", tb % 2)
                pv = k.ps[4 + tb % 2]
                for kc in range(KC):
                    mm(S, pv[:, :128], hT[:, kc, tb * 128:(tb + 1) * 128], wtm[:, kc, 0:128], kc == 0, kc == KC - 1,
                       [("wtm", 0), ("hT", kc)], ("ps", 4 + tb % 2))
                S.add("act", lambda e, pv=pv, ot=ot: e.activation(out=ot[:, 0:128], in_=pv[:, :128], func=AF.Identity),
                      reads=[("ps", 4 + tb % 2)], writes=[okey])
                for hf in range(2):
                    pg = k.ps[6 + hf]
                    for kc in range(KC):
                        mm(S, pg[:, :], hT[:, kc, tb * 128:(tb + 1) * 128], wtm[:, kc, 128 + hf * 512:128 + (hf + 1) * 512],
                           kc == 0, kc == KC - 1, [("wtm", 1 + hf), ("hT", kc)], ("ps", 6 + hf))
                    gelu_from_psum(k, pg[:, :], ("ps", 6 + hf), ot[:, 128 + hf * 512:128 + (hf + 1) * 512], [okey], 512,
                                   r1[hf][:, :], r2[hf][:, :], ("r1", hf), ("r2", hf))
                r0 = t0 + tb * 128
                S.add("sp", lambda e, ot=ot, r0=r0: e.dma_start(out=vtok[r0:r0 + 128, :], in_=ot[:, 0:128]),
                      reads=[okey], writes=[("dr", "vtok", r0)], dma=True)
                S.add("sp", lambda e, ot=ot, r0=r0: e.dma_start(out=gvtok[r0:r0 + 128, :], in_=ot[:, 128:1152]),
                      reads=[okey], writes=[("dr", "gvtok", r0)], dma=True)
        S.barrier()


def phase_ab_mix(k):
    nc, S = k.nc, k.S
    qT = k.dram("qT", [1024, NT], BF16)
    kT = k.dram("kT", [2, 128, NT], BF16)
    vtok = k.dram("vtok", [NT, 128], BF16)
    guT = k.dram("guT", [1024, NT], BF16)
    gvtok = k.dram("gvtok", [NT, 1024], BF16)
    oT = k.dram("oT", [2048, NT], BF16)
    masksD = k.dram("masks", [128, 4, 128])
    sinkD = k.dram("sinkT", [128, 8])
    wsD = k.dram("b_wsT", [128, 8, 128])
    bbD = k.dram("b_biasbc", [128, 8, 128])
    scale = 1.0 / 8.0
    with ExitStack() as st:
        al = k.alloc(st)
        kTs = al("kTs", [128, 2, NT], BF16)
        vts = al("vts", [128, 20, 128], BF16)
        masks = al("masks", [128, 4, 128], BF16)
        sink = al("sink", [128, 8], F32)
        esbc = al("esbc", [128, 8, 128], F32)
        wsT = al("wsT", [128, 8, 128], BF16)
        bbc = al("bbc", [128, 8, 128], F32)
        qb_ = [al("qb%d" % i, [128, 8, 128], BF16) for i in range(2)]
        gub = [al("gub%d" % i, [128, 8, 128], BF16) for i in range(2)]
        gvb = [al("gvb%d" % i, [128, 1024], BF16) for i in range(2)]
        pT = [al("pT%d" % i, [128, 2, 4, 128], BF16) for i in range(3)]
        rr = al("rr", [128, 512], F32)
        gt = al("gt", [128, 512], F32)
        ob = [al("oblk%d" % i, [128, 16, 128], BF16) for i in range(2)]
        S.add("sp", lambda e: e.dma_start(out=kTs[:], in_=kT.rearrange("v p t -> p v t")), writes=["kTs"], dma=True)
        S.add("sp", lambda e: e.dma_start(out=vts[:], in_=vtok.rearrange("(b p) d -> p b d", p=128)), writes=["vts"], dma=True)
        S.add("pool", lambda e: e.dma_start(out=masks[:], in_=masksD), writes=["masks"], dma=True)
        S.add("pool", lambda e: e.dma_start(out=wsT[:], in_=wsD), writes=["wsT"], dma=True)
        S.add("sp", lambda e: e.dma_start(out=bbc[:], in_=bbD), writes=["bbc"], dma=True)
        S.add("sp", lambda e: e.dma_start(out=sink[:], in_=sinkD), writes=["sink"], dma=True)
        S.add("act", lambda e: e.activation(out=sink[:], in_=sink[:], func=AF.Exp), reads=["sink"], writes=["sink"])
        S.add("dve", lambda e: e.tensor_copy(out=esbc[:], in_=sink[:].unsqueeze(2).broadcast_to([128, 8, 128])),
              reads=["sink"], writes=["esbc"])
        blocks = list(range(16)) + [18, 19]
        npT = 0
        for bi, b in enumerate(blocks):
            i2 = bi % 2
            c0 = b * 128
            qb, gu, gv, o_ = qb_[i2], gub[i2], gvb[i2], ob[i2]
            S.add("sp", lambda e, qb=qb, c0=c0: e.dma_start(out=qb[:], in_=qT[:, c0:c0 + 128].rearrange("(c p) t -> p c t", p=128)),
                  writes=[("qb", i2)], dma=True)
            S.add("sp", lambda e, gu=gu, c0=c0: e.dma_start(out=gu[:], in_=guT[:, c0:c0 + 128].rearrange("(c p) t -> p c t", p=128)),
                  writes=[("gub", i2)], dma=True)
            S.add("sp", lambda e, gv=gv, c0=c0: e.dma_start(out=gv[:], in_=gvtok[c0:c0 + 128, :]), writes=[("gvb", i2)], dma=True)
            if b < 16:
                kl = [((b - 1) if b > 0 else 16, 0 if b > 0 else 2), (b, None), ((b + 1) if b < 15 else 17, 1 if b < 15 else 3),
                      (18, None), (19, None)]
            else:
                kl = [(18, None), (19, None)]
            for kv in range(2):
                po, pd = k.ps[4], k.ps[5]
                pend = {}
                for ji in range(len(kl) + 1):
                    if ji < len(kl):
                        kb, mi = kl[ji]
                        pa, pb = k.ps[2 * (ji % 2)], k.ps[2 * (ji % 2) + 1]
                        ka, kb_ = ("ps", 2 * (ji % 2)), ("ps", 2 * (ji % 2) + 1)
                        mm(S, pa[:, :], kTs[0:64, kv, kb * 128:(kb + 1) * 128], qb[0:64, kv * 4:(kv + 1) * 4, :], True, True,
                           ["kTs", ("qb", i2)], ka)
                        mm(S, pb[:, :], kTs[64:128, kv, kb * 128:(kb + 1) * 128], qb[64:128, kv * 4:(kv + 1) * 4, :], True, True,
                           ["kTs", ("qb", i2)], kb_)
                        p = pT[npT % 3]
                        pk = ("pT", npT % 3)
                        npT += 1
                        S.add("act", lambda e, p=p, pa=pa: e.activation(out=p[:, 0, :, :], in_=pa[:, :].rearrange("p (c t) -> p c t", c=4),
                                                                        func=AF.Exp, scale=scale), reads=[ka], writes=[pk])
                        S.add("act", lambda e, p=p, pb=pb: e.activation(out=p[:, 1, :, :], in_=pb[:, :].rearrange("p (c t) -> p c t", c=4),
                                                                        func=AF.Exp, scale=scale), reads=[kb_], writes=[pk])
                        if mi is not None:
                            S.add("pool", lambda e, p=p, mi=mi: e.tensor_tensor(
                                out=p[:].rearrange("p e c t -> p (e c) t"), in0=p[:].rearrange("p e c t -> p (e c) t"),
                                in1=masks[:, mi:mi + 1, :].broadcast_to([128, 8, 128]), op=ALU.mult),
                                reads=[pk, "masks"], writes=[pk])
                        pend[ji] = (p, pk, kb)
                    jj_ = ji - 1
                    if jj_ >= 0:
                        p, pk, kb = pend.pop(jj_)
                        first, last = jj_ == 0, jj_ == len(kl) - 1
                        for e_ in range(2):
                            mm(S, po[64 * e_:64 * e_ + 64, :], vts[:, kb, kv * 64:(kv + 1) * 64], p[:, e_, :, :], first, last,
                               ["vts", pk], ("ps", 4))
                            mm(S, pd[64 * e_:64 * e_ + 64, :], k.ones_b[:, 0:64], p[:, e_, :, :], first, last,
                               ["ones_b", pk], ("ps", 5))
                S.add("dve", lambda e, pd=pd, kv=kv: e.tensor_tensor(
                    out=rr[:].rearrange("p (c t) -> p c t", c=4), in0=pd[:, :].rearrange("p (c t) -> p c t", c=4),
                    in1=esbc[:, kv * 4:(kv + 1) * 4, :], op=ALU.add), reads=[("ps", 5), "esbc"], writes=["rr"])
                S.add("dve", lambda e: e.reciprocal(out=rr[:], in_=rr[:]), reads=["rr"], writes=["rr"])
                S.add("dve", lambda e, po=po, kv=kv, o_=o_: e.tensor_tensor(
                    out=o_[:, kv * 4:(kv + 1) * 4, :], in0=po[:, :].rearrange("p (c t) -> p c t", c=4),
                    in1=rr[:].rearrange("p (c t) -> p c t", c=4), op=ALU.mult), reads=[("ps", 4), "rr"], writes=[("oblk", i2)])
            for hf in range(2):
                pg = k.ps[6 + hf]
                for gg in range(4):
                    g = hf * 4 + gg
                    mm(S, pg[:, gg * 128:(gg + 1) * 128], gv[:, g * 128:(g + 1) * 128], wsT[:, g, :], True, True,
                       [("gvb", i2), "wsT"], ("ps", 6 + hf))
                S.add("dve", lambda e, pg=pg, hf=hf: e.tensor_tensor(
                    out=gt[:].rearrange("p (c t) -> p c t", c=4), in0=pg[:, :].rearrange("p (c t) -> p c t", c=4),
                    in1=bbc[:, hf * 4:(hf + 1) * 4, :], op=ALU.add), reads=[("ps", 6 + hf), "bbc"], writes=["gt"])
                S.add("dve", lambda e, hf=hf, o_=o_, gu=gu: e.tensor_tensor(
                    out=o_[:, 8 + hf * 4:8 + (hf + 1) * 4, :], in0=gt[:].rearrange("p (c t) -> p c t", c=4),
                    in1=gu[:, hf * 4:(hf + 1) * 4, :], op=ALU.mult), reads=["gt", ("gub", i2)], writes=[("oblk", i2)])
            S.add("sp", lambda e, o_=o_, c0=c0: e.dma_start(out=oT[:, c0:c0 + 128].rearrange("(c p) t -> p c t", p=128), in_=o_[:]),
                  reads=[("oblk", i2)], writes=[("dr", "oT", c0)], dma=True)
        S.barrier()


def phase_outproj(k, l, wname, oT, x_in, x_out, tiles):
    nc, S = k.nc, k.S
    w = k.dram(wname, [1, D, D])[0]
    with ExitStack() as st:
        al = k.alloc(st)
        xres = al("xres", [128, KC, 512], F32)
        hT = al("hT", [128, KC, 512], BF16)
        wa = [al("wa%d" % i, [128, KC, 256], BF16) for i in range(2)]
        na = 0
        for (t0, T, segs) in tiles:
            load_x_tile(k, xres, x_in, t0, T)
            for c in range(KC):
                S.add("sp", lambda e, c=c, t0=t0, T=T: e.dma_start(out=hT[:, c, :T], in_=oT[c * 128:(c + 1) * 128, t0:t0 + T]),
                      writes=[("hT", c)], dma=True)
            for sl in range(KC // 2):
                a = wa[na % 2]
                ka = "wa%d" % (na % 2)
                na += 1
                S.add("pool", lambda e, a=a, sl=sl: e.dma_start(
                    out=a[:], in_=w[:, sl * 256:(sl + 1) * 256].rearrange("(kc p) n -> p kc n", p=128)), writes=[ka], dma=True)
                for jj in range(2):
                    c = sl * 2 + jj
                    py = k.ps[c % 2]
                    for kc in range(KC):
                        mm(S, py[:, :T], a[:, kc, jj * 128:(jj + 1) * 128], hT[:, kc, :T], kc == 0, kc == KC - 1,
                           [ka, ("hT", kc)], ("ps", c % 2))
                    residual_store(k, py, ("ps", c % 2), xres, c, T, segs, l, 1, x_out, t0)
        S.barrier()


NK = SEQ + NCTX
NKC = NK // 128


def lat_norm(k, src, nch, T, gain, dst, rstd, sqb, pskey_i):
    S = k.S
    ps = k.ps[pskey_i]
    for c in range(nch):
        sq = sqb[c % 2]
        S.add("act", lambda e, c=c, sq=sq: e.activation(out=sq[:, :T], in_=src[:, c, :T], func=AF.Square),
              reads=[("lsrc", c)], writes=[("sq", c % 2)])
        mm(S, ps[:, :T], k.ones_b[:], sq[:, :T], c == 0, c == nch - 1, [("sq", c % 2), "ones_b"], ("ps", pskey_i))
    S.add("act", lambda e: e.activation(out=rstd[:, :T], in_=ps[:, :T], func=AF.Sqrt, scale=1.0 / (nch * 128), bias=k.eps_t[:]),
          reads=[("ps", pskey_i), "eps_t"], writes=["rstd2"])
    S.add("dve", lambda e: e.reciprocal(out=rstd[:, :T], in_=rstd[:, :T]), reads=["rstd2"], writes=["rstd2"])
    for c in range(nch):
        S.add("dve", lambda e, c=c: e.scalar_tensor_tensor(
            out=dst[:, c, :T], in0=src[:, c, :T], scalar=gain[:, c:c + 1], op0=ALU.mult, in1=rstd[:, :T], op1=ALU.mult),
            reads=[("lsrc", c), "rstd2", "gains"], writes=[("ldst", c)])


def phase_cd_in(k, x_in, tiles):
    nc, S = k.nc, k.S
    l, s = 1, 1
    w_in = k.dram("cd_w_in", [1, D, 4416])[0]
    w_uq = k.dram("c_w_uq", [1, 768, 1536])[0]
    cosD, sinD = k.dram("cosT", [128, NT]), k.dram("sinT", [128, NT])
    qgD, kgD = k.dram("c_qnT", [128, 6]), k.dram("c_kvnT", [128, 4])
    qnT = k.dram("qnT", [1024, NOWN], BF16)
    qrT = k.dram("qrT", [512, NOWN], BF16)
    xin = [k.dram("xch_in%d" % i, [128, NOWN], BF16) for i in range(5)]
    ctxkv = k.dram("ctxkv", [640, NCTX], BF16)
    zb = k.dram("zb_in", [2, 1024])
    dbT = k.dram("dbT", [1024, NOWN], BF16)
    zT = k.dram("zT", [1024, NOWN])
    with ExitStack() as st:
        al = k.alloc(st)
        xres = al("xres", [128, KC, 512], F32)
        hT = al("hT", [128, KC, 512], BF16)
        tmp = [al("tmp%d" % i, [128, 512], F32) for i in range(2)]
        sqb = [al("sq%d" % i, [128, 512], BF16) for i in range(2)]
        rstd = al("rstd", [128, 512], F32)
        rstd2 = al("rstd2", [128, 512], F32)
        cos, sin = al("cos", [128, 512], F32), al("sin", [128, 512], F32)
        lat = al("lat", [128, 6, 512], F32)
        latn = al("latn", [128, 6, 512], BF16)
        qg, kg = al("qg", [128, 6], F32), al("kg", [128, 4], F32)
        wuq = al("wuq", [128, 6, 1536], BF16)
        wqr = al("wqr", [128, 6, 4, 128], BF16)
        krw = al("krw", [128, KC, 128], BF16)
        wa = [al("wa%d" % i, [128, KC, 256], BF16) for i in range(2)]
        qraw = [al("qraw%d" % i, [128, 512], BF16) for i in range(2)]
        r1 = [al("r1_%d" % i, [128, 512], F32) for i in range(2)]
        r2 = [al("r2_%d" % i, [128, 512], F32) for i in range(2)]
        ob = [al("ob%d" % i, [128, 512], BF16) for i in range(2)]
        zo = [al("zo%d" % i, [128, 512], F32) for i in range(2)]
        dcs = [al("dcs%d" % i, [128, 512], F32) for i in range(2)]
        S.add("sp", lambda e: e.dma_start(out=qg[:], in_=qgD), writes=["gains"], dma=True)
        S.add("sp", lambda e: e.dma_start(out=kg[:], in_=kgD), writes=["gains"], dma=True)
        for i in range(3):
            S.add("pool", lambda e, i=i: e.dma_start(
                out=wuq[:, :, i * 512:(i + 1) * 512], in_=w_uq[:, i * 512:(i + 1) * 512].rearrange("(kc p) n -> p kc n", p=128)),
                writes=[("wuq", i)], dma=True)
        for h in range(8):
            S.add("pool", lambda e, h=h: e.dma_start(
                out=wqr[:, :, h // 2, (h % 2) * 64:(h % 2) * 64 + 64],
                in_=w_uq[:, h * 192 + 128:h * 192 + 192].rearrange("(kc p) n -> p kc n", p=128)),
                writes=[("wqr", h)], dma=True)
        for i in range(2):
            S.add("pool", lambda e, i=i: e.dma_start(
                out=krw[:, :, i * 64:(i + 1) * 64], in_=w_in[:, 1280:1344].rearrange("(kc p) n -> p kc n", p=128)),
                writes=[("krw", i)], dma=True)
        wuq_r = [("wuq", i) for i in range(3)]
        wqr_r = [("wqr", h) for h in range(8)]
        na = 0
        cnt = 0
        for (t0, T, segs) in tiles:
            own = t0 < NOWN
            oc0 = t0 if own else 0
            kvd = (lambda i: xin[i]) if own else (lambda i: ctxkv[i * 128:(i + 1) * 128, :])
            load_x_tile(k, xres, x_in, t0, T)
            S.add("sp", lambda e, t0=t0, T=T: e.dma_start(out=cos[:, :T], in_=cosD[:, t0:t0 + T]), writes=["rope"], dma=True)
            S.add("sp", lambda e, t0=t0, T=T: e.dma_start(out=sin[:, :T], in_=sinD[:, t0:t0 + T]), writes=["rope"], dma=True)
            norm_mod(k, xres, hT, T, segs, l, s, tmp, rstd, sqb)

            def proj_chunk(a, ka, jj):
                nonlocal cnt
                i2 = cnt % 2
                cnt += 1
                pq = k.ps[i2]
                for kc in range(KC):
                    mm(S, pq[:, :T], a[:, kc, jj * 128:(jj + 1) * 128], hT[:, kc, :T], kc == 0, kc == KC - 1,
                       [ka, ("hT", kc)] if not isinstance(ka, list) else ka + [("hT", kc)], ("ps", i2))
                return pq, i2

            def slab(c0):
                nonlocal na
                a = wa[na % 2]
                ka = "wa%d" % (na % 2)
                na += 1
                S.add("pool", lambda e: e.dma_start(
                    out=a[:], in_=w_in[:, c0:c0 + 256].rearrange("(kc p) n -> p kc n", p=128)), writes=[ka], dma=True)
                return a, ka

            if own:
                for sl in range(3):
                    a, ka = slab(sl * 256)
                    for jj in range(2):
                        c = sl * 2 + jj
                        pq, i2 = proj_chunk(a, ka, jj)
                        S.add("act", lambda e, pq=pq, c=c: e.activation(out=lat[:, c, :T], in_=pq[:, :T], func=AF.Identity),
                              reads=[("ps", i2)], writes=[("lsrc", c)])
                lat_norm(k, lat, 6, T, qg, latn, rstd2, sqb, 6)
                lr = [("ldst", c) for c in range(6)]
                for h in range(8):
                    i2 = cnt % 2
                    cnt += 1
                    pq = k.ps[i2]
                    for kc in range(6):
                        mm(S, pq[:, :T], wuq[:, kc, h * 192:h * 192 + 128], latn[:, kc, :T], kc == 0, kc == 5,
                           wuq_r + [("ldst", kc)], ("ps", i2))
                    o_ = ob[i2]
                    S.add("act", lambda e, pq=pq, o_=o_: e.activation(out=o_[:, :T], in_=pq[:, :T], func=AF.Identity),
                          reads=[("ps", i2)], writes=[("ob", i2)])
                    S.add("sp", lambda e, o_=o_, h=h: e.dma_start(out=qnT[h * 128:(h + 1) * 128, t0:t0 + T], in_=o_[:, :T]),
                          reads=[("ob", i2)], writes=[("dr", "qnT", t0, h)], dma=True)
                for j in range(4):
                    i2 = cnt % 2
                    cnt += 1
                    pq = k.ps[i2]
                    for kc in range(6):
                        mm(S, pq[:, :T], wqr[:, kc, j, :], latn[:, kc, :T], kc == 0, kc == 5, wqr_r + [("ldst", kc)], ("ps", i2))
                    o_ = ob[i2]
                    rope_from_psum(k, pq[:, :T], ("ps", i2), k.ps[2 + i2], ("ps", 2 + i2), cos[:, :T], sin[:, :T], T,
                                   o_[:, :T], [("ob", i2)], qraw[i2], r1[i2], r2[i2], (("qraw", i2), ("r1", i2), ("r2", i2)))
                    S.add("sp", lambda e, o_=o_, j=j: e.dma_start(out=qrT[j * 128:(j + 1) * 128, t0:t0 + T], in_=o_[:, :T]),
                          reads=[("ob", i2)], writes=[("dr", "qrT", t0, j)], dma=True)
            for sl in range(2):
                a, ka = slab(768 + sl * 256)
                for jj in range(2):
                    c = sl * 2 + jj
                    pq, i2 = proj_chunk(a, ka, jj)
                    S.add("act", lambda e, pq=pq, c=c: e.activation(out=lat[:, c, :T], in_=pq[:, :T], func=AF.Identity),
                          reads=[("ps", i2)], writes=[("lsrc", c)])
            lat_norm(k, lat, 4, T, kg, latn, rstd2, sqb, 6)
            for c in range(4):
                S.add("sp", lambda e, c=c: e.dma_start(out=kvd(c)[:, oc0:oc0 + T], in_=latn[:, c, :T]),
                      reads=[("ldst", c)], writes=[("dr", "ckvn", t0, c)], dma=True)
            pq, i2 = proj_chunk(krw, [("krw", 0), ("krw", 1)], 0)
            o_ = ob[i2]
            rope_from_psum(k, pq[:, :T], ("ps", i2), k.ps[2 + i2], ("ps", 2 + i2), cos[:, :T], sin[:, :T], T,
                           o_[:, :T], [("ob", i2)], qraw[i2], r1[i2], r2[i2], (("qraw", i2), ("r1", i2), ("r2", i2)))
            S.add("sp", lambda e, o_=o_: e.dma_start(out=kvd(4)[:, oc0:oc0 + T], in_=o_[:, :T]),
                  reads=[("ob", i2)], writes=[("dr", "krT", t0)], dma=True)
            if own:
                for sl in range(4):
                    a, ka = slab(1344 + sl * 256)
                    for jj in range(2):
                        c = sl * 2 + jj
                        pq, i2 = proj_chunk(a, ka, jj)
                        o_ = ob[i2]
                        S.add("act", lambda e, pq=pq, o_=o_: e.activation(out=o_[:, :T], in_=pq[:, :T], func=AF.Identity),
                              reads=[("ps", i2)], writes=[("ob", i2)])
                        S.add("sp", lambda e, o_=o_, c=c: e.dma_start(out=dbT[c * 128:(c + 1) * 128, t0:t0 + T], in_=o_[:, :T]),
                              reads=[("ob", i2)], writes=[("dr", "dbT", t0, c)], dma=True)
                for sl in range(4):
                    a, ka = slab(2368 + sl * 256)
                    a2, ka2 = slab(3392 + sl * 256)
                    for jj in range(2):
                        c = sl * 2 + jj
                        pq, i2 = proj_chunk(a, ka, jj)
                        dc_ = dcs[i2]
                        S.add("act", lambda e, pq=pq, dc_=dc_: e.activation(out=dc_[:, :T], in_=pq[:, :T], func=AF.Identity),
                              reads=[("ps", i2)], writes=[("dcs", i2)])
                        pq2, j2 = proj_chunk(a2, ka2, jj)
                        z_ = zo[i2]
                        S.add("dve", lambda e, pq2=pq2, dc_=dc_, z_=z_: e.tensor_tensor(
                            out=z_[:, :T], in0=pq2[:, :T], in1=dc_[:, :T], op=ALU.mult),
                            reads=[("ps", j2), ("dcs", i2)], writes=[("zo", i2)])
                        S.add("sp", lambda e, z_=z_, c=c: e.dma_start(out=zT[c * 128:(c + 1) * 128, t0:t0 + T], in_=z_[:, :T]),
                              reads=[("zo", i2)], writes=[("dr", "zT", t0, c)], dma=True)
                        for (tt, which, col) in ((0, 0, 0), (NOWN - 512, 1, 511)):
                            if t0 == tt:
                                S.add("sp", lambda e, z_=z_, c=c, which=which, col=col: e.dma_start(
                                    out=zb[which:which + 1, :].rearrange("a (p c) -> p (a c)", c=8)[:, c:c + 1],
                                    in_=z_[:, col:col + 1], allow_slow_non_contiguous=True),
                                    reads=[("zo", i2)], writes=[("dr", "zb", which, c)], dma=True)
        S.barrier()


def phase_exchange(k):
    nc, S = k.nc, k.S
    xin = [k.dram("xch_in%d" % i, [128, NOWN], BF16) for i in range(5)]
    xall = [k.dram("xch_all%d" % i, [512, NOWN], BF16) for i in range(5)]
    zb = k.dram("zb_in", [2, 1024])
    zall = k.dram("zb_all", [8, 1024])
    groups = [[0, 1, 2, 3], [4, 5, 6, 7]]
    for i in range(5):
        S.add_cc(lambda e, i=i: e.collective_compute("AllGather", ALU.bypass, replica_groups=groups, ins=[xin[i].opt()], outs=[xall[i].opt()]))
    S.add_cc(lambda e: e.collective_compute("AllGather", ALU.bypass, replica_groups=groups, ins=[zb.opt()], outs=[zall.opt()]))
    S.barrier()
    with ExitStack() as st:
        al = k.alloc(st)
        zsel = al("zsel", [128, 8, 8], F32)
        S.add("sp", lambda e: e.dma_start(out=zsel[:], in_=zall.rearrange("j (p c) -> p j c", c=8)), reads=["zall"], writes=["zsel"], dma=True)
        for w in range(2):
            for j in range(8):
                if j == 0:
                    S.add("dve", lambda e, w=w, j=j: e.tensor_scalar(out=k.zpn[:, w, :], in0=zsel[:, j, :], scalar1=k.selT[:, w, j:j + 1],
                                                                     scalar2=None, op0=ALU.mult), reads=["zsel", "selT"], writes=["zpn"])
                else:
                    S.add("dve", lambda e, w=w, j=j: e.scalar_tensor_tensor(out=k.zpn[:, w, :], in0=zsel[:, j, :], scalar=k.selT[:, w, j:j + 1],
                                                                            op0=ALU.mult, in1=k.zpn[:, w, :], op1=ALU.add),
                          reads=["zsel", "selT", "zpn"], writes=["zpn"])
        S.barrier()


def phase_mla(k):
    nc, S = k.nc, k.S
    xall = [k.dram("xch_all%d" % i, [512, NOWN], BF16) for i in range(5)]
    ctxkv = k.dram("ctxkv", [640, NCTX], BF16)
    w_ukv = k.dram("c_w_ukv", [1, 512, 2048])[0]
    qnT = k.dram("qnT", [1024, NOWN], BF16)
    qrT = k.dram("qrT", [512, NOWN], BF16)
    oT = k.dram("oT2", [1024, NOWN], BF16)
    scale = 192.0 ** -0.5
    with ExitStack() as st:
        al = k.alloc(st)
        ckv = al("ckv", [128, 4, NK], BF16)
        krd = al("krd", [128, NK], BF16)
        wkv = al("wkv", [128, 4, 2048], BF16)
        knT = al("knT", [128, NK], BF16)
        vh = al("vh", [128, NKC, 128], BF16)
        qn = [al("qn%d" % i, [128, 512], BF16) for i in range(2)]
        qr = [al("qr%d" % i, [128, 512], BF16) for i in range(2)]
        pT = [al("pT%d" % i, [128, 512], BF16) for i in range(6)]
        rr = al("rr", [128, 512], F32)
        dacc = [al("dacc%d" % i, [128, 512], F32) for i in range(2)]
        oo = [al("oo%d" % i, [128, 512], BF16) for i in range(2)]
        for c in range(4):
            for r in range(4):
                S.add("sp", lambda e, c=c, r=r: e.dma_start(out=ckv[:, c, r * NOWN:(r + 1) * NOWN],
                                                            in_=xall[c][r * 128:(r + 1) * 128, :]),
                      writes=[("ckv", c, r)], dma=True)
            S.add("sp", lambda e, c=c: e.dma_start(out=ckv[:, c, SEQ:NK], in_=ctxkv[c * 128:(c + 1) * 128, :]), writes=[("ckv", c, 4)], dma=True)
        for r in range(4):
            S.add("sp", lambda e, r=r: e.dma_start(out=krd[:, r * NOWN:(r + 1) * NOWN], in_=xall[4][r * 128:(r + 1) * 128, :]),
                  writes=[("krd", r)], dma=True)
        S.add("sp", lambda e: e.dma_start(out=krd[:, SEQ:NK], in_=ctxkv[512:640, :]), writes=[("krd", 4)], dma=True)
        for i in range(4):
            S.add("pool", lambda e, i=i: e.dma_start(
                out=wkv[:, :, i * 512:(i + 1) * 512], in_=w_ukv[:, i * 512:(i + 1) * 512].rearrange("(kc p) n -> p kc n", p=128)),
                writes=[("wkv", i)], dma=True)
        ckr = [("ckv", c, r) for c in range(4) for r in range(5)]
        krr = [("krd", r) for r in range(5)]
        npT = 0
        nq = 0
        for h in range(8):
            wr = [("wkv", h // 2)]
            for kg in range(17):
                n = 512 if kg < 16 else 256
                pk_ = k.ps[6 + kg % 2]
                for kc in range(4):
                    mm(S, pk_[:, :n], wkv[:, kc, h * 256:h * 256 + 128], ckv[:, kc, kg * 512:kg * 512 + n], kc == 0, kc == 3,
                       wr + ckr, ("ps", 6 + kg % 2))
                S.add("act", lambda e, pk_=pk_, kg=kg, n=n: e.activation(out=knT[:, kg * 512:kg * 512 + n], in_=pk_[:, :n], func=AF.Identity),
                      reads=[("ps", 6 + kg % 2)], writes=["knT"])
            for g4 in range(17):
                nb = 4 if g4 < 16 else 2
                pv_ = k.ps[6 + g4 % 2]
                for bb in range(nb):
                    kb = g4 * 4 + bb
                    for kc in range(4):
                        mm(S, pv_[:, bb * 128:(bb + 1) * 128], ckv[:, kc, kb * 128:(kb + 1) * 128],
                           wkv[:, kc, h * 256 + 128:h * 256 + 256], kc == 0, kc == 3, wr + ckr, ("ps", 6 + g4 % 2))
                S.add("dve", lambda e, pv_=pv_, g4=g4, nb=nb: e.tensor_copy(
                    out=vh[:, g4 * 4:g4 * 4 + nb, :], in_=pv_[:, :nb * 128].rearrange("p (b d) -> p b d", d=128)),
                    reads=[("ps", 6 + g4 % 2)], writes=["vh"])
            e2 = h % 2
            for qgi in range(4):
                i2 = nq % 2
                nq += 1
                q0 = qgi * 512
                S.add("sp", lambda e, i2=i2, q0=q0, h=h: e.dma_start(out=qn[i2][:], in_=qnT[h * 128:(h + 1) * 128, q0:q0 + 512]),
                      writes=[("qn", i2)], dma=True)
                S.add("sp", lambda e, i2=i2, q0=q0, h=h: e.dma_start(out=qr[i2][:], in_=qrT[(h // 2) * 128:(h // 2 + 1) * 128, q0:q0 + 512]),
                      writes=[("qr", i2)], dma=True)
                po, pd = k.ps[4], k.ps[5]
                LOOK = 3
                pend = {}
                for kc in range(NKC + LOOK):
                    if kc < NKC:
                        bi = kc % 4
                        ps_ = k.ps[bi]
                        mm(S, ps_[:, :], knT[:, kc * 128:(kc + 1) * 128], qn[i2][:], True, False, ["knT", ("qn", i2)], ("ps", bi))
                        mm(S, ps_[:, :], krd[64 * e2:64 * e2 + 64, kc * 128:(kc + 1) * 128], qr[i2][64 * e2:64 * e2 + 64, :], False, True,
                           krr + [("qr", i2)], ("ps", bi))
                        p = pT[npT % 6]
                        pk = ("pT", npT % 6)
                        npT += 1
                        S.add("act", lambda e, p=p, ps_=ps_: e.activation(out=p[:], in_=ps_[:, :], func=AF.Exp, scale=scale),
                              reads=[("ps", bi)], writes=[pk])
                        pend[kc] = (p, pk)
                    j = kc - LOOK
                    if j >= 0:
                        p, pk = pend.pop(j)
                        mm(S, po[:, :], vh[:, j, :], p[:], j == 0, j == NKC - 1, ["vh", pk], ("ps", 4))
                        da, dk = dacc[j % 2], ("dacc", j % 2)
                        if j < 2:
                            S.add("dve", lambda e, da=da, p=p: e.tensor_copy(out=da[:], in_=p[:]), reads=[pk], writes=[dk])
                        else:
                            S.add("dve", lambda e, da=da, p=p: e.tensor_tensor(out=da[:], in0=da[:], in1=p[:], op=ALU.add),
                                  reads=[pk, dk], writes=[dk])
                mm(S, pd[:, :], k.ones_f[:], dacc[0][:], True, False, ["ones_f", ("dacc", 0)], ("ps", 5))
                mm(S, pd[:, :], k.ones_f[:], dacc[1][:], False, True, ["ones_f", ("dacc", 1)], ("ps", 5))
                S.add("dve", lambda e, pd=pd: e.reciprocal(out=rr[:], in_=pd[:, :]), reads=[("ps", 5)], writes=["rr"])
                o_ = oo[i2]
                S.add("dve", lambda e, po=po, o_=o_: e.tensor_tensor(out=o_[:], in0=po[:, :], in1=rr[:], op=ALU.mult),
                      reads=[("ps", 4), "rr"], writes=[("oo", i2)])
                S.add("sp", lambda e, o_=o_, h=h, q0=q0: e.dma_start(out=oT[h * 128:(h + 1) * 128, q0:q0 + 512], in_=o_[:]),
                      reads=[("oo", i2)], writes=[("dr", "oT2", h, q0)], dma=True)
        S.barrier()


def phase_cd_out(k, x_in, x_out):
    nc, S = k.nc, k.S
    l = 1
    w = k.dram("cd_w_out", [1, D, D])[0]
    oT = k.dram("oT2", [1024, NOWN], BF16)
    dbT = k.dram("dbT", [1024, NOWN], BF16)
    zT = k.dram("zT", [1024, NOWN])
    cwD = k.dram("convT", [128, 24])
    with ExitStack() as st:
        al = k.alloc(st)
        xres = al("xres", [128, KC, 512], F32)
        hT = al("hT", [128, KC, 512], BF16)
        ze = al("ze", [128, 8, 514], F32)
        dbs = al("dbs", [128, 8, 512], BF16)
        cw = al("cw", [128, 24], F32)
        ct = [al("ct%d" % i, [128, 512], F32) for i in range(2)]
        wa = [al("wa%d" % i, [128, KC, 256], BF16) for i in range(2)]
        S.add("sp", lambda e: e.dma_start(out=cw[:], in_=cwD), writes=["cw"], dma=True)
        na = 0
        for (t0, T, segs) in OWN_TILES:
            load_x_tile(k, xres, x_in, t0, T)
            for c in range(8):
                S.add("sp", lambda e, c=c, t0=t0: e.dma_start(out=hT[:, c, :], in_=oT[c * 128:(c + 1) * 128, t0:t0 + 512]),
                      writes=[("hT", c)], dma=True)
            S.add("sp", lambda e, t0=t0: e.dma_start(out=ze[:, :, 1:513], in_=zT[:, t0:t0 + 512].rearrange("(c p) t -> p c t", p=128)),
                  writes=[("ze", 1)], dma=True)
            if t0 > 0:
                S.add("sp", lambda e, t0=t0: e.dma_start(out=ze[:, :, 0:1], in_=zT[:, t0 - 1:t0].rearrange("(c p) t -> p c t", p=128),
                                                         allow_slow_non_contiguous=True), writes=[("ze", 0)], dma=True)
            else:
                S.add("dve", lambda e: e.tensor_copy(out=ze[:, :, 0], in_=k.zpn[:, 0, :]), reads=["zpn"], writes=[("ze", 0)])
            if t0 + 512 < NOWN:
                S.add("sp", lambda e, t0=t0: e.dma_start(out=ze[:, :, 513:514], in_=zT[:, t0 + 512:t0 + 513].rearrange("(c p) t -> p c t", p=128),
                                                         allow_slow_non_contiguous=True), writes=[("ze", 2)], dma=True)
            else:
                S.add("dve", lambda e: e.tensor_copy(out=ze[:, :, 513], in_=k.zpn[:, 1, :]), reads=["zpn"], writes=[("ze", 2)])
            S.add("sp", lambda e, t0=t0: e.dma_start(out=dbs[:], in_=dbT[:, t0:t0 + 512].rearrange("(c p) t -> p c t", p=128)),
                  writes=["dbs"], dma=True)
            zr = [("ze", i) for i in range(3)]
            for c in range(8):
                t_ = ct[c % 2]
                tk = ("ct", c % 2)
                S.add("dve", lambda e, c=c, t_=t_: e.tensor_scalar(out=t_[:], in0=ze[:, c, 0:512], scalar1=cw[:, c:c + 1], scalar2=None,
                                                                   op0=ALU.mult), reads=zr + ["cw"], writes=[tk])
                S.add("dve", lambda e, c=c, t_=t_: e.scalar_tensor_tensor(out=t_[:], in0=ze[:, c, 1:513], scalar=cw[:, 8 + c:9 + c],
                                                                          op0=ALU.mult, in1=t_[:], op1=ALU.add), reads=zr + ["cw", tk], writes=[tk])
                S.add("dve", lambda e, c=c, t_=t_: e.scalar_tensor_tensor(out=t_[:], in0=ze[:, c, 2:514], scalar=cw[:, 16 + c:17 + c],
                                                                          op0=ALU.mult, in1=t_[:], op1=ALU.add), reads=zr + ["cw", tk], writes=[tk])
                S.add("dve", lambda e, c=c, t_=t_: e.tensor_tensor(out=hT[:, 8 + c, :], in0=t_[:], in1=dbs[:, c, :], op=ALU.mult),
                      reads=[tk, "dbs"], writes=[("hT", 8 + c)])
            for sl in range(KC // 2):
                a = wa[na % 2]
                ka = "wa%d" % (na % 2)
                na += 1
                S.add("pool", lambda e, a=a, sl=sl: e.dma_start(
                    out=a[:], in_=w[:, sl * 256:(sl + 1) * 256].rearrange("(kc p) n -> p kc n", p=128)), writes=[ka], dma=True)
                for jj in range(2):
                    c = sl * 2 + jj
                    py = k.ps[c % 2]
                    for kc in range(KC):
                        mm(S, py[:, :T], a[:, kc, jj * 128:(jj + 1) * 128], hT[:, kc, :T], kc == 0, kc == KC - 1,
                           [ka, ("hT", kc)], ("ps", c % 2))
                    residual_store(k, py, ("ps", c % 2), xres, c, T, segs, l, 1, x_out, t0)
        S.barrier()


OWN_TILES = [(i * 512, 512, [(0, 512, 0)]) for i in range(4)]
MISC_TILE = (2048, 512, [(0, 256, 0), (256, 256, 1)])
CTX_TILE = (CT0, 256, [(0, 256, 1)])


def fm(v):
    v = np.asarray(v)
    lead = v.shape[:-1]
    n = v.shape[-1] // 128
    r = v.reshape(lead + (n, 128))
    r = np.moveaxis(r, -1, 0)
    return np.ascontiguousarray(r.reshape(128, -1))


def prep_core(inp, core):
    b, q = core // 4, core % 4
    p0 = q * NOWN
    x = inp["x"]
    xT = np.zeros((D, NT), np.float32)
    xT[:, 0:NOWN] = x[b, p0:p0 + NOWN].T
    if q > 0:
        xT[:, HP0:HP0 + 128] = x[b, p0 - 128:p0].T
    if q < 3:
        xT[:, HN0:HN0 + 128] = x[b, p0 + NOWN:p0 + NOWN + 128].T
    xT[:, CT0:CT0 + NCTX] = inp["ctx"][b].T
    m = {"xT": xT}
    m["cT"] = np.ascontiguousarray(np.stack([inp["c"][b], inp["c_ctx"]], axis=1))
    pos = np.zeros(NT, np.int64)
    pos[0:NOWN] = p0 + np.arange(NOWN)
    pos[HP0:HP0 + 128] = p0 - 128 + np.arange(128)
    pos[HN0:HN0 + 128] = p0 + NOWN + np.arange(128)
    pos = np.clip(pos, 0, SEQ - 1)
    row = (pos // 64).astype(np.float32)
    col = (pos % 64).astype(np.float32)
    inv = (10000.0 ** (-np.arange(0, 32, 2, dtype=np.float32) / 32)).astype(np.float32)
    ang = np.zeros((64, NT), np.float32)
    ang[0:16] = (row[None, :] * inv[:, None]).astype(np.float32)
    ang[16:32] = ang[0:16]
    ang[32:48] = (col[None, :] * inv[:, None]).astype(np.float32)
    ang[48:64] = ang[32:48]
    cosT = np.cos(ang).astype(np.float32)
    sinT = np.sin(ang).astype(np.float32)
    cosT[:, CT0:] = 1.0
    sinT[:, CT0:] = 0.0
    m["cosT"] = np.ascontiguousarray(np.concatenate([cosT, cosT], 0))
    m["sinT"] = np.ascontiguousarray(np.concatenate([sinT, sinT], 0))
    j = np.arange(128)[:, None]
    i = np.arange(128)[None, :]
    mk = np.zeros((128, 4, 128), np.float32)
    mk[:, 0, :] = (j >= i)
    mk[:, 1, :] = (j <= i)
    mk[:, 2, :] = (j >= i) * (1.0 if q > 0 else 0.0)
    mk[:, 3, :] = (j <= i) * (1.0 if q < 3 else 0.0)
    m["masks"] = mk
    sel = np.zeros((128, 2, 8), np.float32)
    if q > 0:
        sel[:, 0, 2 * (q - 1) + 1] = 1.0
    if q < 3:
        sel[:, 1, 2 * (q + 1)] = 1.0
    m["selT"] = sel
    return m


def prep_shared(inp):
    m = {}
    m["mod_w"] = inp["mod_w"]
    m["mod_bT"] = np.stack([fm(inp["mod_b"][l]) for l in range(2)])
    m["norm_gT"] = np.stack([fm(inp["norm_g"][l]) for l in range(2)])
    for n in ("ffn_w1", "ffn_w3", "ffn_w2", "ab_w_in", "ab_w_out", "cd_w_in", "cd_w_out", "c_w_uq", "c_w_ukv"):
        m[n] = inp[n]
    pm = np.zeros((128, 128), np.float32)
    for mm_ in range(128):
        if mm_ % 32 < 16:
            pm[mm_ + 16, mm_] = -1.0
        else:
            pm[mm_ - 16, mm_] = 1.0
    m["permM"] = pm
    sk = inp["a_sink"][0]
    m["sinkT"] = np.ascontiguousarray(np.stack([np.repeat(sk[2 * c:2 * c + 2], 64) for c in range(8)], axis=1))
    m["b_wsT"] = np.ascontiguousarray(np.transpose(inp["b_ws"][0], (2, 0, 1)))
    m["b_biasbc"] = np.ascontiguousarray(np.broadcast_to(inp["b_bias"][0][None], (128, 8, 128)))
    m["c_qnT"] = fm(inp["c_q_norm"][0])
    m["c_kvnT"] = fm(inp["c_kv_norm"][0])
    m["convT"] = fm(inp["d_conv_w"][0])
    m["fnT"] = fm(inp["final_norm"])
    return m


def load_final_consts(k):
    fnD = k.dram("fnT", [128, 16])
    k.fnT = k.sb("fnT_sb", [128, 16], F32)
    k.S.add("sp", lambda e: e.dma_start(out=k.fnT[:], in_=fnD), writes=["fnT"], dma=True)
    selD = k.dram("selT", [128, 2, 8])
    k.selT = k.sb("selT_sb", [128, 2, 8], F32)
    k.zpn = k.sb("zpn_sb", [128, 2, 8], F32)
    k.S.add("sp", lambda e: e.dma_start(out=k.selT[:], in_=selD), writes=["selT"], dma=True)


INS = ["xT", "cT", "mod_w", "mod_bT", "norm_gT", "ffn_w1", "ffn_w3", "ffn_w2", "ab_w_in", "ab_w_out", "cosT", "sinT",
       "permM", "masks", "sinkT", "b_wsT", "b_biasbc", "cd_w_in", "c_w_uq", "c_qnT", "c_kvnT",
       "cd_w_out", "c_w_ukv", "convT", "fnT", "selT"]
OUTS = ["outT"]


def build():
    k = K(INS, OUTS)
    xT = k.dram("xT", [D, NT])
    x1, x2, x3 = k.dram("x1", [D, NT]), k.dram("x2", [D, NT]), k.dram("x3", [D, NT])
    x4, x5 = k.dram("x4", [D, NT]), k.dram("x5", [D, NOWN])
    outT = k.dram("outT", [D, NOWN])
    phase_setup(k)
    load_rope_consts(k)
    load_final_consts(k)
    phase_mod(k, (0, 1))
    phase_ffn(k, 0, 0, xT, x1, OWN_TILES + [MISC_TILE])
    phase_ab_in(k, x1, OWN_TILES + [MISC_TILE])
    phase_ab_mix(k)
    phase_outproj(k, 0, "ab_w_out", k.dram("oT", [2048, NT], BF16), x1, x2, OWN_TILES + [CTX_TILE])
    phase_ffn(k, 0, 1, x2, x3, OWN_TILES + [CTX_TILE])
    phase_ffn(k, 1, 0, x3, x4, OWN_TILES + [CTX_TILE])
    phase_cd_in(k, x4, OWN_TILES + [CTX_TILE])
    phase_exchange(k)
    phase_mla(k)
    phase_cd_out(k, x4, x5)
    phase_ffn(k, 1, 1, x5, None, OWN_TILES, final_out=outT)
    k.S.emit(k.st)
    k.st.close()
    return k


def kernel(**inp):
    inp = {kk: np.asarray(v) for kk, v in inp.items()}
    n = 8
    sh = prep_shared(inp)
    for nm in ("ffn_w1", "ffn_w3", "ffn_w2"):
        a = inp[nm]
        sh[nm] = a.reshape((4,) + a.shape[2:])
    k = build()
    maps = []
    for c in range(n):
        m = dict(sh)
        m.update(prep_core(inp, c))
        maps.append({kk: m[kk] for kk in INS})
    res = run_bass_kernel_spmd(k.nc, maps, core_ids=list(range(n))).results
    out = np.zeros((2, SEQ, D), np.float32)
    for c in range(n):
        b, q = c // 4, c % 4
        out[b, q * NOWN:(q + 1) * NOWN, :] = np.asarray(res[c]["outT"]).T
    return out
```

```python
import math
import types
from contextlib import ExitStack
import numpy as np
import concourse.bass as bass
import concourse.mybir as mybir
from concourse.bass_utils import run_bass_kernel_spmd

F32 = mybir.dt.float32
BF16 = mybir.dt.bfloat16
AF = mybir.ActivationFunctionType
ALU = mybir.AluOpType

D = 2048
FFN = 5632
NFC = FFN // 128
KC = D // 128
SEQ = 8192
NOWN = 2048
NCTX = 256
NT = 2560
HP0, HN0, CT0 = 2048, 2176, 2304
EPS = 1e-6
ENGS = ("pe", "act", "dve", "pool", "sp")
N_SW = 4
DBG_NOPERM = False
ROPE_ADD_ENG = "pool"
DBG_SECT = None
SW_FRESH = False


def _freeze(fn):
    if fn is None or fn.__closure__ is None:
        return fn
    cells = []
    for c in fn.__closure__:
        try:
            cells.append(types.CellType(c.cell_contents))
        except ValueError:
            cells.append(c)
    return types.FunctionType(fn.__code__, fn.__globals__, fn.__name__, fn.__defaults__, tuple(cells))


class Op:
    __slots__ = ("eng", "fn", "deps", "idx", "ms", "dma", "sem", "semval", "msval", "tag")


class Sched:
    def __init__(self, nc, n_dma_sems=24):
        if SW_FRESH:
            n_dma_sems = 8
        self.nc = nc
        self.ops = {e: [] for e in ENGS}
        self.lastw = {}
        self.readers = {}
        self.n_dma_sems = n_dma_sems
        self.dma_rr = 0
        self.dma_count = [0] * n_dma_sems
        self.dma_last = [None] * n_dma_sems
        self.sw_keys = {}
        self.sw_rr = 0
        self.sw_last = {}

    def add(self, eng, fn, reads=(), writes=(), dma=False, tag=None):
        op = Op()
        op.eng, op.fn, op.dma, op.ms, op.tag = eng, _freeze(fn), dma, False, tag
        op.sem = op.semval = op.msval = None
        sw = dma and eng == "pool" and SW_FRESH
        if eng in ("act", "dve", "pool") and not dma:
            extra = [("psr", r[1]) for r in reads if isinstance(r, tuple) and len(r) == 2 and r[0] == "ps"]
            if extra:
                writes = list(writes) + extra
        deps = set()
        for r in reads:
            w = self.lastw.get(r)
            if w is not None:
                deps.add(w)
        for k in writes:
            w = self.lastw.get(k)
            if w is not None:
                deps.add(w)
            for rd in self.readers.get(k, ()):
                deps.add(rd)
        if sw:
            key = len(self.sw_keys)
            self.sw_keys[key] = key
            op.sem, op.semval = ("sw", key), 16
            op.tag = False
            self.sw_last[key] = op
        elif dma:
            if eng == "pool":
                s = self.n_dma_sems - N_SW + self.sw_rr
                self.sw_rr = (self.sw_rr + 1) % N_SW
            else:
                s = self.dma_rr
                self.dma_rr = (self.dma_rr + 1) % (self.n_dma_sems - N_SW)
            if self.dma_last[s] is not None:
                deps.add(self.dma_last[s])
            self.dma_count[s] += 1
            op.sem, op.semval = s, 16 * self.dma_count[s]
            self.dma_last[s] = op
        if eng == "pe":
            deps = {d for d in deps if d.dma or d.eng != "pe"}
        op.deps = deps
        for d in deps:
            if not d.dma:
                d.ms = True
        for r in reads:
            self.readers.setdefault(r, []).append(op)
        for k in writes:
            self.lastw[k] = op
            self.readers[k] = []
        op.idx = len(self.ops[eng])
        self.ops[eng].append(op)
        return op

    def add_cc(self, fn, reads=(), writes=()):
        op = self.add("pool", fn, reads=reads, writes=[("cc_issue",)])
        op.tag = "cc"
        self.n_cc = getattr(self, "n_cc", 0) + 1
        op.semval = self.n_cc
        return op

    def barrier(self):
        lasts = [self.ops[e][-1] for e in ENGS if self.ops[e]]
        lasts = [x for x in lasts if not x.dma and x.fn is not None]
        dmas = [d for d in self.dma_last if d is not None] + list(self.sw_last.values())
        for e in ENGS:
            op = Op()
            op.eng, op.fn, op.dma, op.ms, op.tag = e, None, False, False, "barrier"
            op.sem = op.semval = op.msval = None
            op.deps = set(x for x in lasts if x.eng != e) | set(dmas)
            for d in op.deps:
                if not d.dma:
                    d.ms = True
            op.idx = len(self.ops[e])
            self.ops[e].append(op)
        self.lastw.clear()
        self.readers.clear()

    def emit(self, stack):
        nc = self.nc
        esem = {e: stack.enter_context(nc.semaphore("s_" + e)) for e in ENGS if e != "sp"}
        dsem = [stack.enter_context(nc.semaphore("d_%d" % i)) for i in range(self.n_dma_sems)]
        wsem = [stack.enter_context(nc.semaphore("w_%d" % i)) for i in range(len(self.sw_keys))]
        ccsem = stack.enter_context(nc.semaphore("ccsem"))
        for e in ENGS:
            c = 0
            for op in self.ops[e]:
                if op.ms and not op.dma:
                    c += 1
                    op.msval = c
        ops = self.ops
        final_dma = [(dsem[i], 16 * self.dma_count[i]) for i in range(self.n_dma_sems) if self.dma_count[i]]

        def run(ename, eng):
            known = {}
            for op in ops[ename]:
                for d in op.deps:
                    if d.dma and isinstance(d.sem, tuple):
                        key, sem, val = ("w", d.sem[1], id(d)), wsem[d.sem[1]], 16
                    elif d.dma:
                        key, sem, val = ("d", d.sem), dsem[d.sem], d.semval
                    else:
                        key, sem, val = d.eng, esem[d.eng], d.msval
                    if known.get(key, 0) < val:
                        eng.wait_ge(sem, val)
                        known[key] = val
                if op.fn is None:
                    continue
                if op.dma and isinstance(op.sem, tuple):
                    if op.tag:
                        eng.wait_ge(wsem[op.sem[1]], 16)
                        eng.sem_clear(wsem[op.sem[1]])
                    op.fn(eng).then_inc(wsem[op.sem[1]], 16)
                    continue
                ins = op.fn(eng)
                if op.tag == "cc":
                    ins.then_inc(ccsem, 1)
                    eng.wait_ge(ccsem, op.semval)
                    if op.ms:
                        eng.memset(self.cc_dummy[:], 0.0).then_inc(esem[ename], 1)
                    continue
                if op.dma:
                    ins.then_inc(dsem[op.sem], 16)
                elif op.ms:
                    ins.then_inc(esem[ename], 1)
            if ename == "sp":
                for sem, val in final_dma:
                    eng.wait_ge(sem, val)

        with nc.Block() as block:
            @block.tensor
            def _(e):
                run("pe", e)

            @block.scalar
            def _(e):
                run("act", e)

            @block.vector
            def _(e):
                run("dve", e)

            @block.gpsimd
            def _(e):
                run("pool", e)

            @block.sync
            def _(e):
                run("sp", e)


class K:
    def __init__(self, ext_in, ext_out):
        self.nc = bass.Bass("TRN2", target_bir_lowering=False)
        self.S = Sched(self.nc)
        self.ext_in, self.ext_out = set(ext_in), set(ext_out)
        self.dr = {}
        self.st = ExitStack()
        self.uid = 0
        self.ffn_sel = {(0, 0): 0, (0, 1): 1, (1, 0): 2, (1, 1): 3}

    def dram(self, name, shape, dtype=F32):
        if name in self.dr:
            return self.dr[name]
        kind = "ExternalInput" if name in self.ext_in else ("ExternalOutput" if name in self.ext_out else "Internal")
        t = self.nc.dram_tensor(name, list(shape), dtype, kind=kind).ap()
        self.dr[name] = t
        return t

    def sb(self, name, shape, dtype):
        return self.st.enter_context(self.nc.sbuf_tensor(name, list(shape), dtype))

    def alloc(self, st):
        self.uid += 1
        u = self.uid
        return lambda n, sh, dt: st.enter_context(self.nc.sbuf_tensor("%s_u%d" % (n, u), list(sh), dt))

    def psum(self, name):
        return self.st.enter_context(self.nc.psum_tensor(name, [128, 512], F32))


def mm(S, ps_ap, lhsT, rhs, start, stop, reads, pskey):
    S.add("pe", lambda e: e.matmul(ps_ap, lhsT, rhs, start=start, stop=stop), reads=reads, writes=[pskey])


def phase_setup(k):
    nc, S = k.nc, k.S
    k.ps = [k.psum("ps%d" % i) for i in range(8)]
    k.ones_f = k.sb("ones_f", [128, 128], F32)
    k.ones_b = k.sb("ones_b", [128, 128], BF16)
    S.add("pool", lambda e: e.memset(k.ones_f[:], 1.0), writes=["ones_f"])
    S.add("pool", lambda e: e.memset(k.ones_b[:], 1.0), writes=["ones_b"])
    k.eps_t = k.sb("eps_t", [128, 1], F32)
    S.cc_dummy = k.sb("cc_dummy", [128, 8], F32)
    S.add("pool", lambda e: e.memset(k.eps_t[:], EPS), writes=["eps_t"])


def phase_mod(k, layers=(0, 1)):
    nc, S = k.nc, k.S
    cT = k.dram("cT", [D, 2])
    nl = len(layers)
    mod_w = k.dram("mod_w", [nl, D, 9 * D])
    mod_bT = k.dram("mod_bT", [nl, 128, 144])
    norm_gT = k.dram("norm_gT", [nl, 128, 48])
    k.modT = [k.sb("modT%d" % l, [128, 144, 2], F32) for l in range(2)]
    k.modA = [k.sb("modA%d" % l, [128, 48, 2], F32) for l in range(2)]
    k.modG = [k.sb("modG%d" % l, [128, 48, 2], F32) for l in range(2)]
    with ExitStack() as st:
        cin = st.enter_context(nc.sbuf_tensor("cin_sb", [128, KC, 2], F32))
        scT = st.enter_context(nc.sbuf_tensor("scT", [128, KC, 2], BF16))
        mb = st.enter_context(nc.sbuf_tensor("mb", [128, 144], F32))
        ng = st.enter_context(nc.sbuf_tensor("ng", [128, 48], F32))
        wm = [st.enter_context(nc.sbuf_tensor("wm%d" % i, [128, KC, 512], BF16)) for i in range(2)]
        S.add("sp", lambda e: e.dma_start(out=cin[:], in_=cT.rearrange("(kc p) n -> p kc n", p=128)), writes=["cin"], dma=True)
        S.add("act", lambda e: e.activation(out=scT[:], in_=cin[:], func=AF.Silu), reads=["cin"], writes=["scT"])
        si = 0
        for li, l in enumerate(layers):
            S.add("sp", lambda e, li=li: e.dma_start(out=mb[:], in_=mod_bT[li]), writes=["mb"], dma=True)
            S.add("sp", lambda e, li=li: e.dma_start(out=ng[:], in_=norm_gT[li]), writes=["ng"], dma=True)
            ps = k.ps[l]
            for sl in range(36):
                w = wm[si % 2]
                wkey = "wm%d" % (si % 2)
                si += 1
                S.add("pool", lambda e, w=w, li=li, sl=sl: e.dma_start(
                    out=w[:], in_=mod_w[li, :, sl * 512:(sl + 1) * 512].rearrange("(kc p) n -> p kc n", p=128)),
                    writes=[wkey], dma=True)
                for jj in range(4):
                    j = sl * 4 + jj
                    for kc in range(KC):
                        mm(S, ps[:, 2 * j:2 * j + 2], w[:, kc, jj * 128:(jj + 1) * 128], scT[:, kc, :],
                           kc == 0, kc == KC - 1, [wkey, "scT"], ("ps", l))
            modT = k.modT[l]
            for cnd in range(2):
                S.add("dve", lambda e, modT=modT, ps=ps, cnd=cnd: e.tensor_tensor(
                    out=modT[:, :, cnd], in0=ps[:, 0:288].rearrange("p (j c) -> p j c", c=2)[:, :, cnd], in1=mb[:], op=ALU.add),
                    reads=[("ps", l), "mb"], writes=[("modT", l)])
            for s in range(3):
                for cnd in range(2):
                    S.add("dve", lambda e, l=l, s=s, cnd=cnd, modT=modT: e.scalar_tensor_tensor(
                        out=k.modA[l][:, s * 16:(s + 1) * 16, cnd], in0=modT[:, (3 * s + 1) * 16:(3 * s + 2) * 16, cnd],
                        scalar=1.0, op0=ALU.add, in1=ng[:, s * 16:(s + 1) * 16], op1=ALU.mult),
                        reads=[("modT", l), "ng"], writes=[("modA", l)])
                    S.add("dve", lambda e, l=l, s=s, cnd=cnd, modT=modT: e.tensor_scalar(
                        out=k.modG[l][:, s * 16:(s + 1) * 16, cnd], in0=modT[:, (3 * s + 2) * 16:(3 * s + 3) * 16, cnd],
                        scalar1=(1.0 if s == 1 else 0.5), scalar2=None, op0=ALU.mult),
                        reads=[("modT", l)], writes=[("modG", l)])
        S.barrier()


def load_x_tile(k, xres, x_in, t0, T):
    S = k.S
    for c in range(KC):
        S.add("sp", lambda e, c=c: e.dma_start(out=xres[:, c, :T], in_=x_in[c * 128:(c + 1) * 128, t0:t0 + T]),
              writes=[("xres", c)], dma=True)


def norm_mod(k, xres, hT, T, segs, l, s, tmp, rstd, sqb):
    S = k.S
    ps = k.ps[6]
    for c in range(KC):
        sq = sqb[c % 2]
        S.add("act", lambda e, c=c, sq=sq: e.activation(out=sq[:, :T], in_=xres[:, c, :T], func=AF.Square),
              reads=[("xres", c)], writes=[("sq", c % 2)])
        mm(S, ps[:, :T], k.ones_b[:], sq[:, :T], c == 0, c == KC - 1, [("sq", c % 2), "ones_b"], ("ps", 6))
    S.add("act", lambda e: e.activation(out=rstd[:, :T], in_=ps[:, :T], func=AF.Sqrt, scale=1.0 / D, bias=k.eps_t[:]),
          reads=[("ps", 6), "eps_t"], writes=["rstd"])
    S.add("dve", lambda e: e.reciprocal(out=rstd[:, :T], in_=rstd[:, :T]), reads=["rstd"], writes=["rstd"])
    A, SH = k.modA[l], k.modT[l]
    for c in range(KC):
        tb = tmp[c % 2]
        for (o, n, cnd) in segs:
            S.add("dve", lambda e, c=c, o=o, n=n, cnd=cnd, tb=tb: e.scalar_tensor_tensor(
                out=tb[:, o:o + n], in0=xres[:, c, o:o + n], scalar=A[:, s * 16 + c, cnd:cnd + 1], op0=ALU.mult,
                in1=rstd[:, o:o + n], op1=ALU.mult),
                reads=[("xres", c), "rstd", ("modA", l)], writes=[("tmp", c % 2)])
            S.add("act", lambda e, c=c, o=o, n=n, cnd=cnd, tb=tb: e.activation(
                out=hT[:, c, o:o + n], in_=tb[:, o:o + n], func=AF.Identity,
                bias=SH[:, 3 * s * 16 + c, cnd:cnd + 1], scale=1.0),
                reads=[("tmp", c % 2), ("modT", l)], writes=[("hT", c)])


def residual_store(k, ps_ap, pskey, xres, c, T, segs, l, s, x_out, t0):
    S = k.S
    G = k.modG[l]
    for (o, n, cnd) in segs:
        S.add("dve", lambda e, o=o, n=n, cnd=cnd: e.scalar_tensor_tensor(
            out=xres[:, c, o:o + n], in0=ps_ap[:, o:o + n], scalar=G[:, s * 16 + c, cnd:cnd + 1], op0=ALU.mult,
            in1=xres[:, c, o:o + n], op1=ALU.add),
            reads=[pskey, ("xres", c), ("modG", l)], writes=[("xres", c)])
    if x_out is not None:
        S.add("sp", lambda e: e.dma_start(out=x_out[c * 128:(c + 1) * 128, t0:t0 + T], in_=xres[:, c, :T]),
              reads=[("xres", c)], writes=[("xdram", x_out.tensor.name, t0, c)], dma=True)


def phase_ffn(k, l, widx, x_in, x_out, tiles, final_out=None):
    nc, S = k.nc, k.S
    s = 0 if widx == 0 else 2
    nsel = len(k.ffn_sel)
    wi_ = k.ffn_sel[(l, widx)]
    w1 = k.dram("ffn_w1", [nsel, D, FFN])[wi_]
    w3 = k.dram("ffn_w3", [nsel, D, FFN])[wi_]
    w2 = k.dram("ffn_w2", [nsel, FFN, D])[wi_]
    with ExitStack() as st:
        al = k.alloc(st)
        xres = al("xres", [128, KC, 512], F32)
        hT = al("hT", [128, KC, 512], BF16)
        gT = al("gT", [128, NFC, 512], BF16)
        tmp = [al("tmp%d" % i, [128, 512], F32) for i in range(2)]
        sqb = [al("sq%d" % i, [128, 512], BF16) for i in range(2)]
        rstd = al("rstd", [128, 512], F32)
        su = [al("su%d" % i, [128, 512], BF16) for i in range(2)]
        wa = [al("wa%d" % i, [128, KC, 256], BF16) for i in range(2)]
        wb = [al("wb%d" % i, [128, KC, 256], BF16) for i in range(2)]
        wc = [al("wc%d" % i, [128, NFC, 256], BF16) for i in range(2)]
        na = nc_ = 0
        for (t0, T, segs) in tiles:
            load_x_tile(k, xres, x_in, t0, T)
            norm_mod(k, xres, hT, T, segs, l, s, tmp, rstd, sqb)
            hreads = [("hT", c) for c in range(KC)]
            for sl in range(NFC // 2):
                a, b = wa[na % 2], wb[na % 2]
                ka, kb = "wa%d" % (na % 2), "wb%d" % (na % 2)
                na += 1
                S.add("pool", lambda e, a=a, sl=sl: e.dma_start(
                    out=a[:], in_=w1[:, sl * 256:(sl + 1) * 256].rearrange("(kc p) n -> p kc n", p=128)),
                    writes=[ka], dma=True)
                S.add("pool", lambda e, b=b, sl=sl: e.dma_start(
                    out=b[:], in_=w3[:, sl * 256:(sl + 1) * 256].rearrange("(kc p) n -> p kc n", p=128)),
                    writes=[kb], dma=True)
                for jj in range(2):
                    fc = sl * 2 + jj
                    pu, pv = k.ps[fc % 2], k.ps[2 + fc % 2]
                    for kc in range(KC):
                        mm(S, pu[:, :T], a[:, kc, jj * 128:(jj + 1) * 128], hT[:, kc, :T], kc == 0, kc == KC - 1,
                           [ka, ("hT", kc)], ("ps", fc % 2))
                    for kc in range(KC):
                        mm(S, pv[:, :T], b[:, kc, jj * 128:(jj + 1) * 128], hT[:, kc, :T], kc == 0, kc == KC - 1,
                           [kb, ("hT", kc)], ("ps", 2 + fc % 2))
                    sut = su[fc % 2]
                    S.add("act", lambda e, pu=pu, sut=sut: e.activation(out=sut[:, :T], in_=pu[:, :T], func=AF.Silu),
                          reads=[("ps", fc % 2)], writes=[("su", fc % 2)])
                    S.add("dve", lambda e, pv=pv, sut=sut, fc=fc: e.tensor_tensor(
                        out=gT[:, fc, :T], in0=pv[:, :T], in1=sut[:, :T], op=ALU.mult),
                        reads=[("ps", 2 + fc % 2), ("su", fc % 2)], writes=[("gT", fc)])
            for ds in range(KC // 2):
                w = wc[nc_ % 2]
                kw = "wc%d" % (nc_ % 2)
                nc_ += 1
                S.add("pool", lambda e, w=w, ds=ds: e.dma_start(
                    out=w[:], in_=w2[:, ds * 256:(ds + 1) * 256].rearrange("(fc p) n -> p fc n", p=128)),
                    writes=[kw], dma=True)
                for jj in range(2):
                    c = ds * 2 + jj
                    py = k.ps[4 + c % 2]
                    for fc in range(NFC):
                        mm(S, py[:, :T], w[:, fc, jj * 128:(jj + 1) * 128], gT[:, fc, :T], fc == 0, fc == NFC - 1,
                           [kw, ("gT", fc)], ("ps", 4 + c % 2))
                    residual_store(k, py, ("ps", 4 + c % 2), xres, c, T, segs, l, s, x_out, t0)
            if final_out is not None:
                final_norm_store(k, xres, T, t0, final_out, tmp, rstd, sqb)
        S.barrier()


def final_norm_store(k, xres, T, t0, out, tmp, rstd, sqb):
    S = k.S
    ps = k.ps[6]
    fn = k.fnT
    for c in range(KC):
        sq = sqb[c % 2]
        S.add("act", lambda e, c=c, sq=sq: e.activation(out=sq[:, :T], in_=xres[:, c, :T], func=AF.Square),
              reads=[("xres", c)], writes=[("sq", c % 2)])
        mm(S, ps[:, :T], k.ones_b[:], sq[:, :T], c == 0, c == KC - 1, [("sq", c % 2), "ones_b"], ("ps", 6))
    S.add("act", lambda e: e.activation(out=rstd[:, :T], in_=ps[:, :T], func=AF.Sqrt, scale=1.0 / D, bias=k.eps_t[:]),
          reads=[("ps", 6), "eps_t"], writes=["rstd"])
    S.add("dve", lambda e: e.reciprocal(out=rstd[:, :T], in_=rstd[:, :T]), reads=["rstd"], writes=["rstd"])
    for c in range(KC):
        S.add("dve", lambda e, c=c: e.scalar_tensor_tensor(
            out=xres[:, c, :T], in0=xres[:, c, :T], scalar=fn[:, c:c + 1], op0=ALU.mult, in1=rstd[:, :T], op1=ALU.mult),
            reads=[("xres", c), "rstd", "fnT"], writes=[("xres", c)])
        S.add("sp", lambda e, c=c: e.dma_start(out=out[c * 128:(c + 1) * 128, t0:t0 + T], in_=xres[:, c, :T]),
              reads=[("xres", c)], writes=[("odram", t0, c)], dma=True)


GC1, GC2 = 0.044715, 1.5957691216057308


def gelu_from_psum(k, ps_ap, pskey, out_ap, outkeys, n, t1, t2, t1k, t2k):
    S = k.S
    S.add("act", lambda e: e.activation(out=t1, in_=ps_ap, func=AF.Square), reads=[pskey], writes=[t1k])
    S.add("dve", lambda e: e.tensor_scalar(out=t1, in0=t1, scalar1=GC1, scalar2=1.0, op0=ALU.mult, op1=ALU.add),
          reads=[t1k], writes=[t1k])
    S.add("dve", lambda e: e.tensor_tensor(out=t1, in0=ps_ap, in1=t1, op=ALU.mult), reads=[pskey, t1k], writes=[t1k])
    S.add("act", lambda e: e.activation(out=t2, in_=t1, func=AF.Sigmoid, scale=GC2), reads=[t1k], writes=[t2k])
    S.add("dve", lambda e: e.tensor_tensor(out=out_ap, in0=ps_ap, in1=t2, op=ALU.mult), reads=[pskey, t2k], writes=outkeys)


def rope_from_psum(k, ps_ap, pskey, ps2, ps2key, cos, sin, T, out_ap, outkeys, qraw, t1, t2, keys):
    S = k.S
    qk, t1k, t2k = keys
    S.add("act", lambda e: e.activation(out=qraw[:, :T], in_=ps_ap, func=AF.Identity), reads=[pskey], writes=[qk])
    S.add("dve", lambda e: e.tensor_tensor(out=t1[:, :T], in0=ps_ap, in1=cos, op=ALU.mult), reads=[pskey, "rope"], writes=[t1k])
    mm(S, ps2[:, :T], (k.ones_b if DBG_NOPERM else k.permM)[:], qraw[:, :T], True, True, [qk, "permM"], ps2key)
    S.add("dve", lambda e: e.tensor_tensor(out=t2[:, :T], in0=ps2[:, :T], in1=sin, op=ALU.mult), reads=[ps2key, "rope"], writes=[t2k])
    S.add(ROPE_ADD_ENG, lambda e: e.tensor_tensor(out=out_ap, in0=t1[:, :T], in1=t2[:, :T], op=ALU.add), reads=[t1k, t2k], writes=outkeys)


def load_rope_consts(k):
    nc, S = k.nc, k.S
    permD = k.dram("permM", [128, 128])
    k.permM = k.sb("permM_sb", [128, 128], BF16)
    S.add("pool", lambda e: e.dma_start(out=k.permM[:], in_=permD), writes=["permM"], dma=True)


def phase_ab_in(k, x_in, tiles):
    nc, S = k.nc, k.S
    l, s = 0, 1
    w_in = k.dram("ab_w_in", [1, D, 3328])[0]
    cosD, sinD = k.dram("cosT", [128, NT]), k.dram("sinT", [128, NT])
    qT = k.dram("qT", [1024, NT], BF16)
    kT = k.dram("kT", [2, 128, NT], BF16)
    vtok = k.dram("vtok", [NT, 128], BF16)
    guT = k.dram("guT", [1024, NT], BF16)
    gvtok = k.dram("gvtok", [NT, 1024], BF16)
    with ExitStack() as st:
        al = k.alloc(st)
        xres = al("xres", [128, KC, 512], F32)
        hT = al("hT", [128, KC, 512], BF16)
        tmp = [al("tmp%d" % i, [128, 512], F32) for i in range(2)]
        sqb = [al("sq%d" % i, [128, 512], BF16) for i in range(2)]
        rstd = al("rstd", [128, 512], F32)
        cos, sin = al("cos", [128, 512], F32), al("sin", [128, 512], F32)
        wtm = al("wtm", [128, KC, 1152], BF16)
        kdw = al("kdw", [128, KC, 256], BF16)
        wa = [al("wa%d" % i, [128, KC, 256], BF16) for i in range(2)]
        qraw = [al("qraw%d" % i, [128, 512], BF16) for i in range(2)]
        r1 = [al("r1_%d" % i, [128, 512], F32) for i in range(2)]
        r2 = [al("r2_%d" % i, [128, 512], F32) for i in range(2)]
        ob = [al("ob%d" % i, [128, 512], BF16) for i in range(2)]
        obt = [al("obt%d" % i, [128, 1152], BF16) for i in range(2)]
        for i, (c0, n) in enumerate([(1152, 128), (2304, 512), (2816, 512)]):
            o = [0, 128, 640][i]
            S.add("pool", lambda e, c0=c0, n=n, o=o: e.dma_start(
                out=wtm[:, :, o:o + n], in_=w_in[:, c0:c0 + n].rearrange("(kc p) n -> p kc n", p=128)),
                writes=[("wtm", i)], dma=True)
        for i in range(4):
            c0 = 1024 + 64 * (i // 2)
            S.add("pool", lambda e, c0=c0, i=i: e.dma_start(
                out=kdw[:, :, i * 64:(i + 1) * 64], in_=w_in[:, c0:c0 + 64].rearrange("(kc p) n -> p kc n", p=128)),
                writes=[("kdw", i)], dma=True)
        na = 0
        cnt = 0
        for (t0, T, segs) in tiles:
            load_x_tile(k, xres, x_in, t0, T)
            S.add("sp", lambda e, t0=t0, T=T: e.dma_start(out=cos[:, :T], in_=cosD[:, t0:t0 + T]), writes=["rope"], dma=True)
            S.add("sp", lambda e, t0=t0, T=T: e.dma_start(out=sin[:, :T], in_=sinD[:, t0:t0 + T]), writes=["rope"], dma=True)
            norm_mod(k, xres, hT, T, segs, l, s, tmp, rstd, sqb)
            hreads = [("hT", c) for c in range(KC)]
            for sl in range(8):
                if DBG_SECT is not None and ("q" if sl < 4 else "gu") not in DBG_SECT:
                    continue
                a = wa[na % 2]
                ka = "wa%d" % (na % 2)
                na += 1
                c0 = sl * 256 if sl < 4 else 1280 + (sl - 4) * 256
                S.add("pool", lambda e, a=a, c0=c0: e.dma_start(
                    out=a[:], in_=w_in[:, c0:c0 + 256].rearrange("(kc p) n -> p kc n", p=128)), writes=[ka], dma=True)
                for jj in range(2):
                    ch = (sl % 4) * 2 + jj
                    i2 = cnt % 2
                    cnt += 1
                    pq = k.ps[i2]
                    for kc in range(KC):
                        mm(S, pq[:, :T], a[:, kc, jj * 128:(jj + 1) * 128], hT[:, kc, :T], kc == 0, kc == KC - 1,
                           [ka, ("hT", kc)], ("ps", i2))
                    o_ = ob[i2]
                    if sl < 4:
                        rope_from_psum(k, pq[:, :T], ("ps", i2), k.ps[2 + i2], ("ps", 2 + i2), cos[:, :T], sin[:, :T], T,
                                       o_[:, :T], [("ob", i2)], qraw[i2], r1[i2], r2[i2], (("qraw", i2), ("r1", i2), ("r2", i2)))
                        dst = qT[ch * 128:(ch + 1) * 128, t0:t0 + T]
                    else:
                        gelu_from_psum(k, pq[:, :T], ("ps", i2), o_[:, :T], [("ob", i2)], T, r1[i2][:, :T], r2[i2][:, :T],
                                       ("r1", i2), ("r2", i2))
                        dst = guT[ch * 128:(ch + 1) * 128, t0:t0 + T]
                    S.add("sp", lambda e, dst=dst, o_=o_, T=T: e.dma_start(out=dst, in_=o_[:, :T]),
                          reads=[("ob", i2)], writes=[("dr", dst.tensor.name, t0, ch)], dma=True)
            for kv in range(2):
                if DBG_SECT is not None and "k" not in DBG_SECT:
                    continue
                i2 = cnt % 2
                cnt += 1
                pq = k.ps[i2]
                for kc in range(KC):
                    mm(S, pq[:, :T], kdw[:, kc, kv * 128:(kv + 1) * 128], hT[:, kc, :T], kc == 0, kc == KC - 1,
                       [("kdw", 2 * kv), ("kdw", 2 * kv + 1), ("hT", kc)], ("ps", i2))
                o_ = ob[i2]
                rope_from_psum(k, pq[:, :T], ("ps", i2), k.ps[2 + i2], ("ps", 2 + i2), cos[:, :T], sin[:, :T], T,
                               o_[:, :T], [("ob", i2)], qraw[i2], r1[i2], r2[i2], (("qraw", i2), ("r1", i2), ("r2", i2)))
                dst = kT[kv, :, t0:t0 + T]
                S.add("sp", lambda e, dst=dst, o_=o_, T=T: e.dma_start(out=dst, in_=o_[:, :T]),
                      reads=[("ob", i2)], writes=[("dr", "kT", t0, kv)], dma=True)
            for tb in range(T // 128):
                if DBG_SECT is not None and "tok" not in DBG_SECT:
                    continue
                ot = obt[tb % 2]
                okey = ("obt", tb % 2)
                pv = k.ps[4 + tb % 2]
                for kc in range(KC):
                    mm(S, pv[:, :128], hT[:, kc, tb * 128:(tb + 1) * 128], wtm[:, kc, 0:128], kc == 0, kc == KC - 1,
                       [("wtm", 0), ("hT", kc)], ("ps", 4 + tb % 2))
                S.add("act", lambda e, pv=pv, ot=ot: e.activation(out=ot[:, 0:128], in_=pv[:, :128], func=AF.Identity),
                      reads=[("ps", 4 + tb % 2)], writes=[okey])
                for hf in range(2):
                    pg = k.ps[6 + hf]
                    for kc in range(KC):
                        mm(S, pg[:, :], hT[:, kc, tb * 128:(tb + 1) * 128], wtm[:, kc, 128 + hf * 512:128 + (hf + 1) * 512],
                           kc == 0, kc == KC - 1, [("wtm", 1 + hf), ("hT", kc)], ("ps", 6 + hf))
                    gelu_from_psum(k, pg[:, :], ("ps", 6 + hf), ot[:, 128 + hf * 512:128 + (hf + 1) * 512], [okey], 512,
                                   r1[hf][:, :], r2[hf][:, :], ("r1", hf), ("r2", hf))
                r0 = t0 + tb * 128
                S.add("sp", lambda e, ot=ot, r0=r0: e.dma_start(out=vtok[r0:r0 + 128, :], in_=ot[:, 0:128]),
                      reads=[okey], writes=[("dr", "vtok", r0)], dma=True)
                S.add("sp", lambda e, ot=ot, r0=r0: e.dma_start(out=gvtok[r0:r0 + 128, :], in_=ot[:, 128:1152]),
                      reads=[okey], writes=[("dr", "gvtok", r0)], dma=True)
        S.barrier()


def phase_ab_mix(k):
    nc, S = k.nc, k.S
    qT = k.dram("qT", [1024, NT], BF16)
    kT = k.dram("kT", [2, 128, NT], BF16)
    vtok = k.dram("vtok", [NT, 128], BF16)
    guT = k.dram("guT", [1024, NT], BF16)
    gvtok = k.dram("gvtok", [NT, 1024], BF16)
    oT = k.dram("oT", [2048, NT], BF16)
    masksD = k.dram("masks", [128, 4, 128])
    sinkD = k.dram("sinkT", [128, 8])
    wsD = k.dram("b_wsT", [128, 8, 128])
    bbD = k.dram("b_biasbc", [128, 8, 128])
    scale = 1.0 / 8.0
    with ExitStack() as st:
        al = k.alloc(st)
        kTs = al("kTs", [128, 2, NT], BF16)
        vts = al("vts", [128, 20, 128], BF16)
        masks = al("masks", [128, 4, 128], BF16)
        sink = al("sink", [128, 8], F32)
        esbc = al("esbc", [128, 8, 128], F32)
        wsT = al("wsT", [128, 8, 128], BF16)
        bbc = al("bbc", [128, 8, 128], F32)
        qb_ = [al("qb%d" % i, [128, 8, 128], BF16) for i in range(2)]
        gub = [al("gub%d" % i, [128, 8, 128], BF16) for i in range(2)]
        gvb = [al("gvb%d" % i, [128, 1024], BF16) for i in range(2)]
        pT = [al("pT%d" % i, [128, 2, 4, 128], BF16) for i in range(3)]
        rr = al("rr", [128, 512], F32)
        gt = al("gt", [128, 512], F32)
        ob = [al("oblk%d" % i, [128, 16, 128], BF16) for i in range(2)]
        S.add("sp", lambda e: e.dma_start(out=kTs[:], in_=kT.rearrange("v p t -> p v t")), writes=["kTs"], dma=True)
        S.add("sp", lambda e: e.dma_start(out=vts[:], in_=vtok.rearrange("(b p) d -> p b d", p=128)), writes=["vts"], dma=True)
        S.add("pool", lambda e: e.dma_start(out=masks[:], in_=masksD), writes=["masks"], dma=True)
        S.add("pool", lambda e: e.dma_start(out=wsT[:], in_=wsD), writes=["wsT"], dma=True)
        S.add("sp", lambda e: e.dma_start(out=bbc[:], in_=bbD), writes=["bbc"], dma=True)
        S.add("sp", lambda e: e.dma_start(out=sink[:], in_=sinkD), writes=["sink"], dma=True)
        S.add("act", lambda e: e.activation(out=sink[:], in_=sink[:], func=AF.Exp), reads=["sink"], writes=["sink"])
        S.add("dve", lambda e: e.tensor_copy(out=esbc[:], in_=sink[:].unsqueeze(2).broadcast_to([128, 8, 128])),
              reads=["sink"], writes=["esbc"])
        blocks = list(range(16)) + [18, 19]
        npT = 0
        for bi, b in enumerate(blocks):
            i2 = bi % 2
            c0 = b * 128
            qb, gu, gv, o_ = qb_[i2], gub[i2], gvb[i2], ob[i2]
            S.add("sp", lambda e, qb=qb, c0=c0: e.dma_start(out=qb[:], in_=qT[:, c0:c0 + 128].rearrange("(c p) t -> p c t", p=128)),
                  writes=[("qb", i2)], dma=True)
            S.add("sp", lambda e, gu=gu, c0=c0: e.dma_start(out=gu[:], in_=guT[:, c0:c0 + 128].rearrange("(c p) t -> p c t", p=128)),
                  writes=[("gub", i2)], dma=True)
            S.add("sp", lambda e, gv=gv, c0=c0: e.dma_start(out=gv[:], in_=gvtok[c0:c0 + 128, :]), writes=[("gvb", i2)], dma=True)
            if b < 16:
                kl = [((b - 1) if b > 0 else 16, 0 if b > 0 else 2), (b, None), ((b + 1) if b < 15 else 17, 1 if b < 15 else 3),
                      (18, None), (19, None)]
            else:
                kl = [(18, None), (19, None)]
            for kv in range(2):
                po, pd = k.ps[4], k.ps[5]
                pend = {}
                for ji in range(len(kl) + 1):
                    if ji < len(kl):
                        kb, mi = kl[ji]
                        pa, pb = k.ps[2 * (ji % 2)], k.ps[2 * (ji % 2) + 1]
                        ka, kb_ = ("ps", 2 * (ji % 2)), ("ps", 2 * (ji % 2) + 1)
                        mm(S, pa[:, :], kTs[0:64, kv, kb * 128:(kb + 1) * 128], qb[0:64, kv * 4:(kv + 1) * 4, :], True, True,
                           ["kTs", ("qb", i2)], ka)
                        mm(S, pb[:, :], kTs[64:128, kv, kb * 128:(kb + 1) * 128], qb[64:128, kv * 4:(kv + 1) * 4, :], True, True,
                           ["kTs", ("qb", i2)], kb_)
                        p = pT[npT % 3]
                        pk = ("pT", npT % 3)
                        npT += 1
                        S.add("act", lambda e, p=p, pa=pa: e.activation(out=p[:, 0, :, :], in_=pa[:, :].rearrange("p (c t) -> p c t", c=4),
                                                                        func=AF.Exp, scale=scale), reads=[ka], writes=[pk])
                        S.add("act", lambda e, p=p, pb=pb: e.activation(out=p[:, 1, :, :], in_=pb[:, :].rearrange("p (c t) -> p c t", c=4),
                                                                        func=AF.Exp, scale=scale), reads=[kb_], writes=[pk])
                        if mi is not None:
                            S.add("pool", lambda e, p=p, mi=mi: e.tensor_tensor(
                                out=p[:].rearrange("p e c t -> p (e c) t"), in0=p[:].rearrange("p e c t -> p (e c) t"),
                                in1=masks[:, mi:mi + 1, :].broadcast_to([128, 8, 128]), op=ALU.mult),
                                reads=[pk, "masks"], writes=[pk])
                        pend[ji] = (p, pk, kb)
                    jj_ = ji - 1
                    if jj_ >= 0:
                        p, pk, kb = pend.pop(jj_)
                        first, last = jj_ == 0, jj_ == len(kl) - 1
                        for e_ in range(2):
                            mm(S, po[64 * e_:64 * e_ + 64, :], vts[:, kb, kv * 64:(kv + 1) * 64], p[:, e_, :, :], first, last,
                               ["vts", pk], ("ps", 4))
                            mm(S, pd[64 * e_:64 * e_ + 64, :], k.ones_b[:, 0:64], p[:, e_, :, :], first, last,
                               ["ones_b", pk], ("ps", 5))
                S.add("dve", lambda e, pd=pd, kv=kv: e.tensor_tensor(
                    out=rr[:].rearrange("p (c t) -> p c t", c=4), in0=pd[:, :].rearrange("p (c t) -> p c t", c=4),
                    in1=esbc[:, kv * 4:(kv + 1) * 4, :], op=ALU.add), reads=[("ps", 5), "esbc"], writes=["rr"])
                S.add("dve", lambda e: e.reciprocal(out=rr[:], in_=rr[:]), reads=["rr"], writes=["rr"])
                S.add("dve", lambda e, po=po, kv=kv, o_=o_: e.tensor_tensor(
                    out=o_[:, kv * 4:(kv + 1) * 4, :], in0=po[:, :].rearrange("p (c t) -> p c t", c=4),
                    in1=rr[:].rearrange("p (c t) -> p c t", c=4), op=ALU.mult), reads=[("ps", 4), "rr"], writes=[("oblk", i2)])
            for hf in range(2):
                pg = k.ps[6 + hf]
                for gg in range(4):
                    g = hf * 4 + gg
                    mm(S, pg[:, gg * 128:(gg + 1) * 128], gv[:, g * 128:(g + 1) * 128], wsT[:, g, :], True, True,
                       [("gvb", i2), "wsT"], ("ps", 6 + hf))
                S.add("dve", lambda e, pg=pg, hf=hf: e.tensor_tensor(
                    out=gt[:].rearrange("p (c t) -> p c t", c=4), in0=pg[:, :].rearrange("p (c t) -> p c t", c=4),
                    in1=bbc[:, hf * 4:(hf + 1) * 4, :], op=ALU.add), reads=[("ps", 6 + hf), "bbc"], writes=["gt"])
                S.add("dve", lambda e, hf=hf, o_=o_, gu=gu: e.tensor_tensor(
                    out=o_[:, 8 + hf * 4:8 + (hf + 1) * 4, :], in0=gt[:].rearrange("p (c t) -> p c t", c=4),
                    in1=gu[:, hf * 4:(hf + 1) * 4, :], op=ALU.mult), reads=["gt", ("gub", i2)], writes=[("oblk", i2)])
            S.add("sp", lambda e, o_=o_, c0=c0: e.dma_start(out=oT[:, c0:c0 + 128].rearrange("(c p) t -> p c t", p=128), in_=o_[:]),
                  reads=[("oblk", i2)], writes=[("dr", "oT", c0)], dma=True)
        S.barrier()


def phase_outproj(k, l, wname, oT, x_in, x_out, tiles):
    nc, S = k.nc, k.S
    w = k.dram(wname, [1, D, D])[0]
    with ExitStack() as st:
        al = k.alloc(st)
        xres = al("xres", [128, KC, 512], F32)
        hT = al("hT", [128, KC, 512], BF16)
        wa = [al("wa%d" % i, [128, KC, 256], BF16) for i in range(2)]
        na = 0
        for (t0, T, segs) in tiles:
            load_x_tile(k, xres, x_in, t0, T)
            for c in range(KC):
                S.add("sp", lambda e, c=c, t0=t0, T=T: e.dma_start(out=hT[:, c, :T], in_=oT[c * 128:(c + 1) * 128, t0:t0 + T]),
                      writes=[("hT", c)], dma=True)
            for sl in range(KC // 2):
                a = wa[na % 2]
                ka = "wa%d" % (na % 2)
                na += 1
                S.add("pool", lambda e, a=a, sl=sl: e.dma_start(
                    out=a[:], in_=w[:, sl * 256:(sl + 1) * 256].rearrange("(kc p) n -> p kc n", p=128)), writes=[ka], dma=True)
                for jj in range(2):
                    c = sl * 2 + jj
                    py = k.ps[c % 2]
                    for kc in range(KC):
                        mm(S, py[:, :T], a[:, kc, jj * 128:(jj + 1) * 128], hT[:, kc, :T], kc == 0, kc == KC - 1,
                           [ka, ("hT", kc)], ("ps", c % 2))
                    residual_store(k, py, ("ps", c % 2), xres, c, T, segs, l, 1, x_out, t0)
        S.barrier()


NK = SEQ + NCTX
NKC = NK // 128


def lat_norm(k, src, nch, T, gain, dst, rstd, sqb, pskey_i):
    S = k.S
    ps = k.ps[pskey_i]
    for c in range(nch):
        sq = sqb[c % 2]
        S.add("act", lambda e, c=c, sq=sq: e.activation(out=sq[:, :T], in_=src[:, c, :T], func=AF.Square),
              reads=[("lsrc", c)], writes=[("sq", c % 2)])
        mm(S, ps[:, :T], k.ones_b[:], sq[:, :T], c == 0, c == nch - 1, [("sq", c % 2), "ones_b"], ("ps", pskey_i))
    S.add("act", lambda e: e.activation(out=rstd[:, :T], in_=ps[:, :T], func=AF.Sqrt, scale=1.0 / (nch * 128), bias=k.eps_t[:]),
          reads=[("ps", pskey_i), "eps_t"], writes=["rstd2"])
    S.add("dve", lambda e: e.reciprocal(out=rstd[:, :T], in_=rstd[:, :T]), reads=["rstd2"], writes=["rstd2"])
    for c in range(nch):
        S.add("dve", lambda e, c=c: e.scalar_tensor_tensor(
            out=dst[:, c, :T], in0=src[:, c, :T], scalar=gain[:, c:c + 1], op0=ALU.mult, in1=rstd[:, :T], op1=ALU.mult),
            reads=[("lsrc", c), "rstd2", "gains"], writes=[("ldst", c)])


def phase_cd_in(k, x_in, tiles):
    nc, S = k.nc, k.S
    l, s = 1, 1
    w_in = k.dram("cd_w_in", [1, D, 4416])[0]
    w_uq = k.dram("c_w_uq", [1, 768, 1536])[0]
    cosD, sinD = k.dram("cosT", [128, NT]), k.dram("sinT", [128, NT])
    qgD, kgD = k.dram("c_qnT", [128, 6]), k.dram("c_kvnT", [128, 4])
    qnT = k.dram("qnT", [1024, NOWN], BF16)
    qrT = k.dram("qrT", [512, NOWN], BF16)
    xin = [k.dram("xch_in%d" % i, [128, NOWN], BF16) for i in range(5)]
    ctxkv = k.dram("ctxkv", [640, NCTX], BF16)
    zb = k.dram("zb_in", [2, 1024])
    dbT = k.dram("dbT", [1024, NOWN], BF16)
    zT = k.dram("zT", [1024, NOWN])
    with ExitStack() as st:
        al = k.alloc(st)
        xres = al("xres", [128, KC, 512], F32)
        hT = al("hT", [128, KC, 512], BF16)
        tmp = [al("tmp%d" % i, [128, 512], F32) for i in range(2)]
        sqb = [al("sq%d" % i, [128, 512], BF16) for i in range(2)]
        rstd = al("rstd", [128, 512], F32)
        rstd2 = al("rstd2", [128, 512], F32)
        cos, sin = al("cos", [128, 512], F32), al("sin", [128, 512], F32)
        lat = al("lat", [128, 6, 512], F32)
        latn = al("latn", [128, 6, 512], BF16)
        qg, kg = al("qg", [128, 6], F32), al("kg", [128, 4], F32)
        wuq = al("wuq", [128, 6, 1536], BF16)
        wqr = al("wqr", [128, 6, 4, 128], BF16)
        krw = al("krw", [128, KC, 128], BF16)
        wa = [al("wa%d" % i, [128, KC, 256], BF16) for i in range(2)]
        qraw = [al("qraw%d" % i, [128, 512], BF16) for i in range(2)]
        r1 = [al("r1_%d" % i, [128, 512], F32) for i in range(2)]
        r2 = [al("r2_%d" % i, [128, 512], F32) for i in range(2)]
        ob = [al("ob%d" % i, [128, 512], BF16) for i in range(2)]
        zo = [al("zo%d" % i, [128, 512], F32) for i in range(2)]
        dcs = [al("dcs%d" % i, [128, 512], F32) for i in range(2)]
        S.add("sp", lambda e: e.dma_start(out=qg[:], in_=qgD), writes=["gains"], dma=True)
        S.add("sp", lambda e: e.dma_start(out=kg[:], in_=kgD), writes=["gains"], dma=True)
        for i in range(3):
            S.add("pool", lambda e, i=i: e.dma_start(
                out=wuq[:, :, i * 512:(i + 1) * 512], in_=w_uq[:, i * 512:(i + 1) * 512].rearrange("(kc p) n -> p kc n", p=128)),
                writes=[("wuq", i)], dma=True)
        for h in range(8):
            S.add("pool", lambda e, h=h: e.dma_start(
                out=wqr[:, :, h // 2, (h % 2) * 64:(h % 2) * 64 + 64],
                in_=w_uq[:, h * 192 + 128:h * 192 + 192].rearrange("(kc p) n -> p kc n", p=128)),
                writes=[("wqr", h)], dma=True)
        for i in range(2):
            S.add("pool", lambda e, i=i: e.dma_start(
                out=krw[:, :, i * 64:(i + 1) * 64], in_=w_in[:, 1280:1344].rearrange("(kc p) n -> p kc n", p=128)),
                writes=[("krw", i)], dma=True)
        wuq_r = [("wuq", i) for i in range(3)]
        wqr_r = [("wqr", h) for h in range(8)]
        na = 0
        cnt = 0
        for (t0, T, segs) in tiles:
            own = t0 < NOWN
            oc0 = t0 if own else 0
            kvd = (lambda i: xin[i]) if own else (lambda i: ctxkv[i * 128:(i + 1) * 128, :])
            load_x_tile(k, xres, x_in, t0, T)
            S.add("sp", lambda e, t0=t0, T=T: e.dma_start(out=cos[:, :T], in_=cosD[:, t0:t0 + T]), writes=["rope"], dma=True)
            S.add("sp", lambda e, t0=t0, T=T: e.dma_start(out=sin[:, :T], in_=sinD[:, t0:t0 + T]), writes=["rope"], dma=True)
            norm_mod(k, xres, hT, T, segs, l, s, tmp, rstd, sqb)

            def proj_chunk(a, ka, jj):
                nonlocal cnt
                i2 = cnt % 2
                cnt += 1
                pq = k.ps[i2]
                for kc in range(KC):
                    mm(S, pq[:, :T], a[:, kc, jj * 128:(jj + 1) * 128], hT[:, kc, :T], kc == 0, kc == KC - 1,
                       [ka, ("hT", kc)] if not isinstance(ka, list) else ka + [("hT", kc)], ("ps", i2))
                return pq, i2

            def slab(c0):
                nonlocal na
                a = wa[na % 2]
                ka = "wa%d" % (na % 2)
                na += 1
                S.add("pool", lambda e: e.dma_start(
                    out=a[:], in_=w_in[:, c0:c0 + 256].rearrange("(kc p) n -> p kc n", p=128)), writes=[ka], dma=True)
                return a, ka

            if own:
                for sl in range(3):
                    a, ka = slab(sl * 256)
                    for jj in range(2):
                        c = sl * 2 + jj
                        pq, i2 = proj_chunk(a, ka, jj)
                        S.add("act", lambda e, pq=pq, c=c: e.activation(out=lat[:, c, :T], in_=pq[:, :T], func=AF.Identity),
                              reads=[("ps", i2)], writes=[("lsrc", c)])
                lat_norm(k, lat, 6, T, qg, latn, rstd2, sqb, 6)
                lr = [("ldst", c) for c in range(6)]
                for h in range(8):
                    i2 = cnt % 2
                    cnt += 1
                    pq = k.ps[i2]
                    for kc in range(6):
                        mm(S, pq[:, :T], wuq[:, kc, h * 192:h * 192 + 128], latn[:, kc, :T], kc == 0, kc == 5,
                           wuq_r + [("ldst", kc)], ("ps", i2))
                    o_ = ob[i2]
                    S.add("act", lambda e, pq=pq, o_=o_: e.activation(out=o_[:, :T], in_=pq[:, :T], func=AF.Identity),
                          reads=[("ps", i2)], writes=[("ob", i2)])
                    S.add("sp", lambda e, o_=o_, h=h: e.dma_start(out=qnT[h * 128:(h + 1) * 128, t0:t0 + T], in_=o_[:, :T]),
                          reads=[("ob", i2)], writes=[("dr", "qnT", t0, h)], dma=True)
                for j in range(4):
                    i2 = cnt % 2
                    cnt += 1
                    pq = k.ps[i2]
                    for kc in range(6):
                        mm(S, pq[:, :T], wqr[:, kc, j, :], latn[:, kc, :T], kc == 0, kc == 5, wqr_r + [("ldst", kc)], ("ps", i2))
                    o_ = ob[i2]
                    rope_from_psum(k, pq[:, :T], ("ps", i2), k.ps[2 + i2], ("ps", 2 + i2), cos[:, :T], sin[:, :T], T,
                                   o_[:, :T], [("ob", i2)], qraw[i2], r1[i2], r2[i2], (("qraw", i2), ("r1", i2), ("r2", i2)))
                    S.add("sp", lambda e, o_=o_, j=j: e.dma_start(out=qrT[j * 128:(j + 1) * 128, t0:t0 + T], in_=o_[:, :T]),
                          reads=[("ob", i2)], writes=[("dr", "qrT", t0, j)], dma=True)
            for sl in range(2):
                a, ka = slab(768 + sl * 256)
                for jj in range(2):
                    c = sl * 2 + jj
                    pq, i2 = proj_chunk(a, ka, jj)
                    S.add("act", lambda e, pq=pq, c=c: e.activation(out=lat[:, c, :T], in_=pq[:, :T], func=AF.Identity),
                          reads=[("ps", i2)], writes=[("lsrc", c)])
            lat_norm(k, lat, 4, T, kg, latn, rstd2, sqb, 6)
            for c in range(4):
                S.add("sp", lambda e, c=c: e.dma_start(out=kvd(c)[:, oc0:oc0 + T], in_=latn[:, c, :T]),
                      reads=[("ldst", c)], writes=[("dr", "ckvn", t0, c)], dma=True)
            pq, i2 = proj_chunk(krw, [("krw", 0), ("krw", 1)], 0)
            o_ = ob[i2]
            rope_from_psum(k, pq[:, :T], ("ps", i2), k.ps[2 + i2], ("ps", 2 + i2), cos[:, :T], sin[:, :T], T,
                           o_[:, :T], [("ob", i2)], qraw[i2], r1[i2], r2[i2], (("qraw", i2), ("r1", i2), ("r2", i2)))
            S.add("sp", lambda e, o_=o_: e.dma_start(out=kvd(4)[:, oc0:oc0 + T], in_=o_[:, :T]),
                  reads=[("ob", i2)], writes=[("dr", "krT", t0)], dma=True)
            if own:
                for sl in range(4):
                    a, ka = slab(1344 + sl * 256)
                    for jj in range(2):
                        c = sl * 2 + jj
                        pq, i2 = proj_chunk(a, ka, jj)
                        o_ = ob[i2]
                        S.add("act", lambda e, pq=pq, o_=o_: e.activation(out=o_[:, :T], in_=pq[:, :T], func=AF.Identity),
                              reads=[("ps", i2)], writes=[("ob", i2)])
                        S.add("sp", lambda e, o_=o_, c=c: e.dma_start(out=dbT[c * 128:(c + 1) * 128, t0:t0 + T], in_=o_[:, :T]),
                              reads=[("ob", i2)], writes=[("dr", "dbT", t0, c)], dma=True)
                for sl in range(4):
                    a, ka = slab(2368 + sl * 256)
                    a2, ka2 = slab(3392 + sl * 256)
                    for jj in range(2):
                        c = sl * 2 + jj
                        pq, i2 = proj_chunk(a, ka, jj)
                        dc_ = dcs[i2]
                        S.add("act", lambda e, pq=pq, dc_=dc_: e.activation(out=dc_[:, :T], in_=pq[:, :T], func=AF.Identity),
                              reads=[("ps", i2)], writes=[("dcs", i2)])
                        pq2, j2 = proj_chunk(a2, ka2, jj)
                        z_ = zo[i2]
                        S.add("dve", lambda e, pq2=pq2, dc_=dc_, z_=z_: e.tensor_tensor(
                            out=z_[:, :T], in0=pq2[:, :T], in1=dc_[:, :T], op=ALU.mult),
                            reads=[("ps", j2), ("dcs", i2)], writes=[("zo", i2)])
                        S.add("sp", lambda e, z_=z_, c=c: e.dma_start(out=zT[c * 128:(c + 1) * 128, t0:t0 + T], in_=z_[:, :T]),
                              reads=[("zo", i2)], writes=[("dr", "zT", t0, c)], dma=True)
                        for (tt, which, col) in ((0, 0, 0), (NOWN - 512, 1, 511)):
                            if t0 == tt:
                                S.add("sp", lambda e, z_=z_, c=c, which=which, col=col: e.dma_start(
                                    out=zb[which:which + 1, :].rearrange("a (p c) -> p (a c)", c=8)[:, c:c + 1],
                                    in_=z_[:, col:col + 1], allow_slow_non_contiguous=True),
                                    reads=[("zo", i2)], writes=[("dr", "zb", which, c)], dma=True)
        S.barrier()


def phase_exchange(k):
    nc, S = k.nc, k.S
    xin = [k.dram("xch_in%d" % i, [128, NOWN], BF16) for i in range(5)]
    xall = [k.dram("xch_all%d" % i, [512, NOWN], BF16) for i in range(5)]
    zb = k.dram("zb_in", [2, 1024])
    zall = k.dram("zb_all", [8, 1024])
    groups = [[0, 1, 2, 3], [4, 5, 6, 7]]
    for i in range(5):
        S.add_cc(lambda e, i=i: e.collective_compute("AllGather", ALU.bypass, replica_groups=groups, ins=[xin[i].opt()], outs=[xall[i].opt()]))
    S.add_cc(lambda e: e.collective_compute("AllGather", ALU.bypass, replica_groups=groups, ins=[zb.opt()], outs=[zall.opt()]))
    S.barrier()
    with ExitStack() as st:
        al = k.alloc(st)
        zsel = al("zsel", [128, 8, 8], F32)
        S.add("sp", lambda e: e.dma_start(out=zsel[:], in_=zall.rearrange("j (p c) -> p j c", c=8)), reads=["zall"], writes=["zsel"], dma=True)
        for w in range(2):
            for j in range(8):
                if j == 0:
                    S.add("dve", lambda e, w=w, j=j: e.tensor_scalar(out=k.zpn[:, w, :], in0=zsel[:, j, :], scalar1=k.selT[:, w, j:j + 1],
                                                                     scalar2=None, op0=ALU.mult), reads=["zsel", "selT"], writes=["zpn"])
                else:
                    S.add("dve", lambda e, w=w, j=j: e.scalar_tensor_tensor(out=k.zpn[:, w, :], in0=zsel[:, j, :], scalar=k.selT[:, w, j:j + 1],
                                                                            op0=ALU.mult, in1=k.zpn[:, w, :], op1=ALU.add),
                          reads=["zsel", "selT", "zpn"], writes=["zpn"])
        S.barrier()


def phase_mla(k):
    nc, S = k.nc, k.S
    xall = [k.dram("xch_all%d" % i, [512, NOWN], BF16) for i in range(5)]
    ctxkv = k.dram("ctxkv", [640, NCTX], BF16)
    w_ukv = k.dram("c_w_ukv", [1, 512, 2048])[0]
    qnT = k.dram("qnT", [1024, NOWN], BF16)
    qrT = k.dram("qrT", [512, NOWN], BF16)
    oT = k.dram("oT2", [1024, NOWN], BF16)
    scale = 192.0 ** -0.5
    with ExitStack() as st:
        al = k.alloc(st)
        ckv = al("ckv", [128, 4, NK], BF16)
        krd = al("krd", [128, NK], BF16)
        wkv = al("wkv", [128, 4, 2048], BF16)
        knT = al("knT", [128, NK], BF16)
        vh = al("vh", [128, NKC, 128], BF16)
        qn = [al("qn%d" % i, [128, 512], BF16) for i in range(2)]
        qr = [[al("qr%d_%d" % (e_, i), [128, 512], BF16) for i in range(2)] for e_ in range(2)]
        for e_ in range(2):
            for i in range(2):
                S.add("pool", lambda e, e_=e_, i=i: e.memset(qr[e_][i][:], 0.0), writes=[("qr", e_, i)])
        pT = [al("pT%d" % i, [128, 512], BF16) for i in range(6)]
        rr = al("rr", [128, 512], F32)
        dacc = [al("dacc%d" % i, [128, 512], F32) for i in range(2)]
        oo = [al("oo%d" % i, [128, 512], BF16) for i in range(2)]
        for c in range(4):
            for r in range(4):
                S.add("sp", lambda e, c=c, r=r: e.dma_start(out=ckv[:, c, r * NOWN:(r + 1) * NOWN],
                                                            in_=xall[c][r * 128:(r + 1) * 128, :]),
                      writes=[("ckv", c, r)], dma=True)
            S.add("sp", lambda e, c=c: e.dma_start(out=ckv[:, c, SEQ:NK], in_=ctxkv[c * 128:(c + 1) * 128, :]), writes=[("ckv", c, 4)], dma=True)
        for r in range(4):
            S.add("sp", lambda e, r=r: e.dma_start(out=krd[:, r * NOWN:(r + 1) * NOWN], in_=xall[4][r * 128:(r + 1) * 128, :]),
                  writes=[("krd", r)], dma=True)
        S.add("sp", lambda e: e.dma_start(out=krd[:, SEQ:NK], in_=ctxkv[512:640, :]), writes=[("krd", 4)], dma=True)
        for i in range(4):
            S.add("pool", lambda e, i=i: e.dma_start(
                out=wkv[:, :, i * 512:(i + 1) * 512], in_=w_ukv[:, i * 512:(i + 1) * 512].rearrange("(kc p) n -> p kc n", p=128)),
                writes=[("wkv", i)], dma=True)
        ckr = [("ckv", c, r) for c in range(4) for r in range(5)]
        krr = [("krd", r) for r in range(5)]
        npT = 0
        nq = 0
        for h in range(8):
            wr = [("wkv", h // 2)]
            for kg in range(17):
                n = 512 if kg < 16 else 256
                pk_ = k.ps[6 + kg % 2]
                for kc in range(4):
                    mm(S, pk_[:, :n], wkv[:, kc, h * 256:h * 256 + 128], ckv[:, kc, kg * 512:kg * 512 + n], kc == 0, kc == 3,
                       wr + ckr, ("ps", 6 + kg % 2))
                S.add("act", lambda e, pk_=pk_, kg=kg, n=n: e.activation(out=knT[:, kg * 512:kg * 512 + n], in_=pk_[:, :n], func=AF.Identity),
                      reads=[("ps", 6 + kg % 2)], writes=["knT"])
            for g4 in range(17):
                nb = 4 if g4 < 16 else 2
                pv_ = k.ps[6 + g4 % 2]
                for bb in range(nb):
                    kb = g4 * 4 + bb
                    for kc in range(4):
                        mm(S, pv_[:, bb * 128:(bb + 1) * 128], ckv[:, kc, kb * 128:(kb + 1) * 128],
                           wkv[:, kc, h * 256 + 128:h * 256 + 256], kc == 0, kc == 3, wr + ckr, ("ps", 6 + g4 % 2))
                S.add("dve", lambda e, pv_=pv_, g4=g4, nb=nb: e.tensor_copy(
                    out=vh[:, g4 * 4:g4 * 4 + nb, :], in_=pv_[:, :nb * 128].rearrange("p (b d) -> p b d", d=128)),
                    reads=[("ps", 6 + g4 % 2)], writes=["vh"])
            e2 = h % 2
            for qgi in range(4):
                i2 = nq % 2
                nq += 1
                q0 = qgi * 512
                S.add("sp", lambda e, i2=i2, q0=q0, h=h: e.dma_start(out=qn[i2][:], in_=qnT[h * 128:(h + 1) * 128, q0:q0 + 512]),
                      writes=[("qn", i2)], dma=True)
                S.add("sp", lambda e, i2=i2, q0=q0, h=h, e2=e2: e.dma_start(
                    out=qr[e2][i2][64 * e2:64 * e2 + 64, :], in_=qrT[(h // 2) * 128 + 64 * e2:(h // 2) * 128 + 64 * e2 + 64, q0:q0 + 512]),
                    writes=[("qr", e2, i2)], dma=True)
                po, pd = k.ps[4], k.ps[5]
                LOOK = 3
                pend = {}
                for kc in range(NKC + LOOK):
                    if kc < NKC:
                        bi = kc % 4
                        ps_ = k.ps[bi]
                        mm(S, ps_[:, :], knT[:, kc * 128:(kc + 1) * 128], qn[i2][:], True, False, ["knT", ("qn", i2)], ("ps", bi))
                        mm(S, ps_[:, :], krd[:, kc * 128:(kc + 1) * 128], qr[e2][i2][:], False, True,
                           krr + [("qr", e2, i2)], ("ps", bi))
                        p = pT[npT % 6]
                        pk = ("pT", npT % 6)
                        npT += 1
                        S.add("act", lambda e, p=p, ps_=ps_: e.activation(out=p[:], in_=ps_[:, :], func=AF.Exp, scale=scale),
                              reads=[("ps", bi)], writes=[pk])
                        pend[kc] = (p, pk)
                    j = kc - LOOK
                    if j >= 0:
                        p, pk = pend.pop(j)
                        mm(S, po[:, :], vh[:, j, :], p[:], j == 0, j == NKC - 1, ["vh", pk], ("ps", 4))
                        if j % 2 == 1:
                            mm(S, pd[:, :], k.ones_b[:], p[:], j == 1, False, ["ones_b", pk], ("ps", 5))
                        elif j == 0:
                            S.add("dve", lambda e, p=p: e.tensor_copy(out=dacc[0][:], in_=p[:]), reads=[pk], writes=[("dacc", 0)])
                        else:
                            S.add("dve", lambda e, p=p: e.tensor_tensor(out=dacc[0][:], in0=dacc[0][:], in1=p[:], op=ALU.add),
                                  reads=[pk, ("dacc", 0)], writes=[("dacc", 0)])
                mm(S, pd[:, :], k.ones_f[:], dacc[0][:], False, True, ["ones_f", ("dacc", 0)], ("ps", 5))
                S.add("dve", lambda e, pd=pd: e.reciprocal(out=rr[:], in_=pd[:, :]), reads=[("ps", 5)], writes=["rr"])
                o_ = oo[i2]
                S.add("dve", lambda e, po=po, o_=o_: e.tensor_tensor(out=o_[:], in0=po[:, :], in1=rr[:], op=ALU.mult),
                      reads=[("ps", 4), "rr"], writes=[("oo", i2)])
                S.add("sp", lambda e, o_=o_, h=h, q0=q0: e.dma_start(out=oT[h * 128:(h + 1) * 128, q0:q0 + 512], in_=o_[:]),
                      reads=[("oo", i2)], writes=[("dr", "oT2", h, q0)], dma=True)
        S.barrier()


def phase_cd_out(k, x_in, x_out):
    nc, S = k.nc, k.S
    l = 1
    w = k.dram("cd_w_out", [1, D, D])[0]
    oT = k.dram("oT2", [1024, NOWN], BF16)
    dbT = k.dram("dbT", [1024, NOWN], BF16)
    zT = k.dram("zT", [1024, NOWN])
    cwD = k.dram("convT", [128, 24])
    with ExitStack() as st:
        al = k.alloc(st)
        xres = al("xres", [128, KC, 512], F32)
        hT = al("hT", [128, KC, 512], BF16)
        ze = al("ze", [128, 8, 514], F32)
        dbs = al("dbs", [128, 8, 512], BF16)
        cw = al("cw", [128, 24], F32)
        ct = [al("ct%d" % i, [128, 512], F32) for i in range(2)]
        wa = [al("wa%d" % i, [128, KC, 256], BF16) for i in range(2)]
        S.add("sp", lambda e: e.dma_start(out=cw[:], in_=cwD), writes=["cw"], dma=True)
        na = 0
        for (t0, T, segs) in OWN_TILES:
            load_x_tile(k, xres, x_in, t0, T)
            for c in range(8):
                S.add("sp", lambda e, c=c, t0=t0: e.dma_start(out=hT[:, c, :], in_=oT[c * 128:(c + 1) * 128, t0:t0 + 512]),
                      writes=[("hT", c)], dma=True)
            S.add("sp", lambda e, t0=t0: e.dma_start(out=ze[:, :, 1:513], in_=zT[:, t0:t0 + 512].rearrange("(c p) t -> p c t", p=128)),
                  writes=[("ze", 1)], dma=True)
            if t0 > 0:
                S.add("sp", lambda e, t0=t0: e.dma_start(out=ze[:, :, 0:1], in_=zT[:, t0 - 1:t0].rearrange("(c p) t -> p c t", p=128),
                                                         allow_slow_non_contiguous=True), writes=[("ze", 0)], dma=True)
            else:
                S.add("dve", lambda e: e.tensor_copy(out=ze[:, :, 0], in_=k.zpn[:, 0, :]), reads=["zpn"], writes=[("ze", 0)])
            if t0 + 512 < NOWN:
                S.add("sp", lambda e, t0=t0: e.dma_start(out=ze[:, :, 513:514], in_=zT[:, t0 + 512:t0 + 513].rearrange("(c p) t -> p c t", p=128),
                                                         allow_slow_non_contiguous=True), writes=[("ze", 2)], dma=True)
            else:
                S.add("dve", lambda e: e.tensor_copy(out=ze[:, :, 513], in_=k.zpn[:, 1, :]), reads=["zpn"], writes=[("ze", 2)])
            S.add("sp", lambda e, t0=t0: e.dma_start(out=dbs[:], in_=dbT[:, t0:t0 + 512].rearrange("(c p) t -> p c t", p=128)),
                  writes=["dbs"], dma=True)
            zr = [("ze", i) for i in range(3)]
            for c in range(8):
                t_ = ct[c % 2]
                tk = ("ct", c % 2)
                S.add("dve", lambda e, c=c, t_=t_: e.tensor_scalar(out=t_[:], in0=ze[:, c, 0:512], scalar1=cw[:, c:c + 1], scalar2=None,
                                                                   op0=ALU.mult), reads=zr + ["cw"], writes=[tk])
                S.add("dve", lambda e, c=c, t_=t_: e.scalar_tensor_tensor(out=t_[:], in0=ze[:, c, 1:513], scalar=cw[:, 8 + c:9 + c],
                                                                          op0=ALU.mult, in1=t_[:], op1=ALU.add), reads=zr + ["cw", tk], writes=[tk])
                S.add("dve", lambda e, c=c, t_=t_: e.scalar_tensor_tensor(out=t_[:], in0=ze[:, c, 2:514], scalar=cw[:, 16 + c:17 + c],
                                                                          op0=ALU.mult, in1=t_[:], op1=ALU.add), reads=zr + ["cw", tk], writes=[tk])
                S.add("dve", lambda e, c=c, t_=t_: e.tensor_tensor(out=hT[:, 8 + c, :], in0=t_[:], in1=dbs[:, c, :], op=ALU.mult),
                      reads=[tk, "dbs"], writes=[("hT", 8 + c)])
            for sl in range(KC // 2):
                a = wa[na % 2]
                ka = "wa%d" % (na % 2)
                na += 1
                S.add("pool", lambda e, a=a, sl=sl: e.dma_start(
                    out=a[:], in_=w[:, sl * 256:(sl + 1) * 256].rearrange("(kc p) n -> p kc n", p=128)), writes=[ka], dma=True)
                for jj in range(2):
                    c = sl * 2 + jj
                    py = k.ps[c % 2]
                    for kc in range(KC):
                        mm(S, py[:, :T], a[:, kc, jj * 128:(jj + 1) * 128], hT[:, kc, :T], kc == 0, kc == KC - 1,
                           [ka, ("hT", kc)], ("ps", c % 2))
                    residual_store(k, py, ("ps", c % 2), xres, c, T, segs, l, 1, x_out, t0)
        S.barrier()


OWN_TILES = [(i * 512, 512, [(0, 512, 0)]) for i in range(4)]
MISC_TILE = (2048, 512, [(0, 256, 0), (256, 256, 1)])
CTX_TILE = (CT0, 256, [(0, 256, 1)])


def fm(v):
    v = np.asarray(v)
    lead = v.shape[:-1]
    n = v.shape[-1] // 128
    r = v.reshape(lead + (n, 128))
    r = np.moveaxis(r, -1, 0)
    return np.ascontiguousarray(r.reshape(128, -1))


def prep_core(inp, core):
    b, q = core // 4, core % 4
    p0 = q * NOWN
    x = inp["x"]
    xT = np.zeros((D, NT), np.float32)
    xT[:, 0:NOWN] = x[b, p0:p0 + NOWN].T
    if q > 0:
        xT[:, HP0:HP0 + 128] = x[b, p0 - 128:p0].T
    if q < 3:
        xT[:, HN0:HN0 + 128] = x[b, p0 + NOWN:p0 + NOWN + 128].T
    xT[:, CT0:CT0 + NCTX] = inp["ctx"][b].T
    m = {"xT": xT}
    m["cT"] = np.ascontiguousarray(np.stack([inp["c"][b], inp["c_ctx"]], axis=1))
    pos = np.zeros(NT, np.int64)
    pos[0:NOWN] = p0 + np.arange(NOWN)
    pos[HP0:HP0 + 128] = p0 - 128 + np.arange(128)
    pos[HN0:HN0 + 128] = p0 + NOWN + np.arange(128)
    pos = np.clip(pos, 0, SEQ - 1)
    row = (pos // 64).astype(np.float32)
    col = (pos % 64).astype(np.float32)
    inv = (10000.0 ** (-np.arange(0, 32, 2, dtype=np.float32) / 32)).astype(np.float32)
    ang = np.zeros((64, NT), np.float32)
    ang[0:16] = (row[None, :] * inv[:, None]).astype(np.float32)
    ang[16:32] = ang[0:16]
    ang[32:48] = (col[None, :] * inv[:, None]).astype(np.float32)
    ang[48:64] = ang[32:48]
    cosT = np.cos(ang).astype(np.float32)
    sinT = np.sin(ang).astype(np.float32)
    cosT[:, CT0:] = 1.0
    sinT[:, CT0:] = 0.0
    m["cosT"] = np.ascontiguousarray(np.concatenate([cosT, cosT], 0))
    m["sinT"] = np.ascontiguousarray(np.concatenate([sinT, sinT], 0))
    j = np.arange(128)[:, None]
    i = np.arange(128)[None, :]
    mk = np.zeros((128, 4, 128), np.float32)
    mk[:, 0, :] = (j >= i)
    mk[:, 1, :] = (j <= i)
    mk[:, 2, :] = (j >= i) * (1.0 if q > 0 else 0.0)
    mk[:, 3, :] = (j <= i) * (1.0 if q < 3 else 0.0)
    m["masks"] = mk
    sel = np.zeros((128, 2, 8), np.float32)
    if q > 0:
        sel[:, 0, 2 * (q - 1) + 1] = 1.0
    if q < 3:
        sel[:, 1, 2 * (q + 1)] = 1.0
    m["selT"] = sel
    return m


def prep_shared(inp):
    m = {}
    m["mod_w"] = inp["mod_w"]
    m["mod_bT"] = np.stack([fm(inp["mod_b"][l]) for l in range(2)])
    m["norm_gT"] = np.stack([fm(inp["norm_g"][l]) for l in range(2)])
    for n in ("ffn_w1", "ffn_w3", "ffn_w2", "ab_w_in", "ab_w_out", "cd_w_in", "cd_w_out", "c_w_uq", "c_w_ukv"):
        m[n] = inp[n]
    pm = np.zeros((128, 128), np.float32)
    for mm_ in range(128):
        if mm_ % 32 < 16:
            pm[mm_ + 16, mm_] = -1.0
        else:
            pm[mm_ - 16, mm_] = 1.0
    m["permM"] = pm
    sk = inp["a_sink"][0]
    m["sinkT"] = np.ascontiguousarray(np.stack([np.repeat(sk[2 * c:2 * c + 2], 64) for c in range(8)], axis=1))
    m["b_wsT"] = np.ascontiguousarray(np.transpose(inp["b_ws"][0], (2, 0, 1)))
    m["b_biasbc"] = np.ascontiguousarray(np.broadcast_to(inp["b_bias"][0][None], (128, 8, 128)))
    m["c_qnT"] = fm(inp["c_q_norm"][0])
    m["c_kvnT"] = fm(inp["c_kv_norm"][0])
    m["convT"] = fm(inp["d_conv_w"][0])
    m["fnT"] = fm(inp["final_norm"])
    return m


def load_final_consts(k):
    fnD = k.dram("fnT", [128, 16])
    k.fnT = k.sb("fnT_sb", [128, 16], F32)
    k.S.add("sp", lambda e: e.dma_start(out=k.fnT[:], in_=fnD), writes=["fnT"], dma=True)
    selD = k.dram("selT", [128, 2, 8])
    k.selT = k.sb("selT_sb", [128, 2, 8], F32)
    k.zpn = k.sb("zpn_sb", [128, 2, 8], F32)
    k.S.add("sp", lambda e: e.dma_start(out=k.selT[:], in_=selD), writes=["selT"], dma=True)


INS = ["xT", "cT", "mod_w", "mod_bT", "norm_gT", "ffn_w1", "ffn_w3", "ffn_w2", "ab_w_in", "ab_w_out", "cosT", "sinT",
       "permM", "masks", "sinkT", "b_wsT", "b_biasbc", "cd_w_in", "c_w_uq", "c_qnT", "c_kvnT",
       "cd_w_out", "c_w_ukv", "convT", "fnT", "selT"]
OUTS = ["outT"]


def build():
    k = K(INS, OUTS)
    xT = k.dram("xT", [D, NT])
    x1, x2, x3 = k.dram("x1", [D, NT]), k.dram("x2", [D, NT]), k.dram("x3", [D, NT])
    x4, x5 = k.dram("x4", [D, NT]), k.dram("x5", [D, NOWN])
    outT = k.dram("outT", [D, NOWN])
    phase_setup(k)
    load_rope_consts(k)
    load_final_consts(k)
    phase_mod(k, (0, 1))
    phase_ffn(k, 0, 0, xT, x1, OWN_TILES + [MISC_TILE])
    phase_ab_in(k, x1, OWN_TILES + [MISC_TILE])
    phase_ab_mix(k)
    phase_outproj(k, 0, "ab_w_out", k.dram("oT", [2048, NT], BF16), x1, x2, OWN_TILES + [CTX_TILE])
    phase_ffn(k, 0, 1, x2, x3, OWN_TILES + [CTX_TILE])
    phase_ffn(k, 1, 0, x3, x4, OWN_TILES + [CTX_TILE])
    phase_cd_in(k, x4, OWN_TILES + [CTX_TILE])
    phase_exchange(k)
    phase_mla(k)
    phase_cd_out(k, x4, x5)
    phase_ffn(k, 1, 1, x5, None, OWN_TILES, final_out=outT)
    k.S.emit(k.st)
    k.st.close()
    return k


def kernel(**inp):
    inp = {kk: np.asarray(v) for kk, v in inp.items()}
    n = 8
    sh = prep_shared(inp)
    for nm in ("ffn_w1", "ffn_w3", "ffn_w2"):
        a = inp[nm]
        sh[nm] = a.reshape((4,) + a.shape[2:])
    k = build()
    maps = []
    for c in range(n):
        m = dict(sh)
        m.update(prep_core(inp, c))
        maps.append({kk: m[kk] for kk in INS})
    res = run_bass_kernel_spmd(k.nc, maps, core_ids=list(range(n))).results
    out = np.zeros((2, SEQ, D), np.float32)
    for c in range(n):
        b, q = c // 4, c % 4
        out[b, q * NOWN:(q + 1) * NOWN, :] = np.asarray(res[c]["outT"]).T
    return out
```

```python
import math
import types
from contextlib import ExitStack
import numpy as np
import concourse.bass as bass
import concourse.mybir as mybir
from concourse.bass_utils import run_bass_kernel_spmd

F32 = mybir.dt.float32
BF16 = mybir.dt.bfloat16
AF = mybir.ActivationFunctionType
ALU = mybir.AluOpType

D = 2048
FFN = 5632
NFC = FFN // 128
KC = D // 128
SEQ = 8192
NOWN = 2048
NCTX = 256
NT = 2560
HP0, HN0, CT0 = 2048, 2176, 2304
EPS = 1e-6
ENGS = ("pe", "act", "dve", "pool", "sp")
N_SW = 4
DBG_NOPERM = False
ROPE_ADD_ENG = "pool"
DBG_SECT = None
SW_FRESH = False


def _freeze(fn):
    if fn is None or fn.__closure__ is None:
        return fn
    cells = []
    for c in fn.__closure__:
        try:
            cells.append(types.CellType(c.cell_contents))
        except ValueError:
            cells.append(c)
    return types.FunctionType(fn.__code__, fn.__globals__, fn.__name__, fn.__defaults__, tuple(cells))


class Op:
    __slots__ = ("eng", "fn", "deps", "idx", "ms", "dma", "sem", "semval", "msval", "tag")


class Sched:
    def __init__(self, nc, n_dma_sems=24):
        if SW_FRESH:
            n_dma_sems = 8
        self.nc = nc
        self.ops = {e: [] for e in ENGS}
        self.lastw = {}
        self.readers = {}
        self.n_dma_sems = n_dma_sems
        self.dma_rr = 0
        self.dma_count = [0] * n_dma_sems
        self.dma_last = [None] * n_dma_sems
        self.sw_keys = {}
        self.sw_rr = 0
        self.sw_last = {}

    def add(self, eng, fn, reads=(), writes=(), dma=False, tag=None):
        op = Op()
        op.eng, op.fn, op.dma, op.ms, op.tag = eng, _freeze(fn), dma, False, tag
        op.sem = op.semval = op.msval = None
        sw = dma and eng == "pool" and SW_FRESH
        if eng in ("act", "dve", "pool") and not dma:
            extra = [("psr", r[1]) for r in reads if isinstance(r, tuple) and len(r) == 2 and r[0] == "ps"]
            if extra:
                writes = list(writes) + extra
        deps = set()
        for r in reads:
            w = self.lastw.get(r)
            if w is not None:
                deps.add(w)
        for k in writes:
            w = self.lastw.get(k)
            if w is not None:
                deps.add(w)
            for rd in self.readers.get(k, ()):
                deps.add(rd)
        if sw:
            key = len(self.sw_keys)
            self.sw_keys[key] = key
            op.sem, op.semval = ("sw", key), 16
            op.tag = False
            self.sw_last[key] = op
        elif dma:
            if eng == "pool":
                s = self.n_dma_sems - N_SW + self.sw_rr
                self.sw_rr = (self.sw_rr + 1) % N_SW
            else:
                s = self.dma_rr
                self.dma_rr = (self.dma_rr + 1) % (self.n_dma_sems - N_SW)
            if self.dma_last[s] is not None:
                deps.add(self.dma_last[s])
            self.dma_count[s] += 1
            op.sem, op.semval = s, 16 * self.dma_count[s]
            self.dma_last[s] = op
        if eng == "pe":
            deps = {d for d in deps if d.dma or d.eng != "pe"}
        op.deps = deps
        for d in deps:
            if not d.dma:
                d.ms = True
        for r in reads:
            self.readers.setdefault(r, []).append(op)
        for k in writes:
            self.lastw[k] = op
            self.readers[k] = []
        op.idx = len(self.ops[eng])
        self.ops[eng].append(op)
        return op

    def add_cc(self, fn, reads=(), writes=()):
        op = self.add("pool", fn, reads=reads, writes=[("cc_issue",)])
        op.tag = "cc"
        self.n_cc = getattr(self, "n_cc", 0) + 1
        op.semval = self.n_cc
        return op

    def barrier(self):
        lasts = [self.ops[e][-1] for e in ENGS if self.ops[e]]
        lasts = [x for x in lasts if not x.dma and x.fn is not None]
        dmas = [d for d in self.dma_last if d is not None] + list(self.sw_last.values())
        for e in ENGS:
            op = Op()
            op.eng, op.fn, op.dma, op.ms, op.tag = e, None, False, False, "barrier"
            op.sem = op.semval = op.msval = None
            op.deps = set(x for x in lasts if x.eng != e) | set(dmas)
            for d in op.deps:
                if not d.dma:
                    d.ms = True
            op.idx = len(self.ops[e])
            self.ops[e].append(op)
        self.lastw.clear()
        self.readers.clear()

    def emit(self, stack):
        nc = self.nc
        esem = {e: stack.enter_context(nc.semaphore("s_" + e)) for e in ENGS if e != "sp"}
        dsem = [stack.enter_context(nc.semaphore("d_%d" % i)) for i in range(self.n_dma_sems)]
        wsem = [stack.enter_context(nc.semaphore("w_%d" % i)) for i in range(len(self.sw_keys))]
        ccsem = stack.enter_context(nc.semaphore("ccsem"))
        for e in ENGS:
            c = 0
            for op in self.ops[e]:
                if op.ms and not op.dma:
                    c += 1
                    op.msval = c
        ops = self.ops
        final_dma = [(dsem[i], 16 * self.dma_count[i]) for i in range(self.n_dma_sems) if self.dma_count[i]]

        def run(ename, eng):
            known = {}
            for op in ops[ename]:
                for d in op.deps:
                    if d.dma and isinstance(d.sem, tuple):
                        key, sem, val = ("w", d.sem[1], id(d)), wsem[d.sem[1]], 16
                    elif d.dma:
                        key, sem, val = ("d", d.sem), dsem[d.sem], d.semval
                    else:
                        key, sem, val = d.eng, esem[d.eng], d.msval
                    if known.get(key, 0) < val:
                        eng.wait_ge(sem, val)
                        known[key] = val
                if op.fn is None:
                    continue
                if op.dma and isinstance(op.sem, tuple):
                    if op.tag:
                        eng.wait_ge(wsem[op.sem[1]], 16)
                        eng.sem_clear(wsem[op.sem[1]])
                    op.fn(eng).then_inc(wsem[op.sem[1]], 16)
                    continue
                ins = op.fn(eng)
                if op.tag == "cc":
                    ins.then_inc(ccsem, 1)
                    eng.wait_ge(ccsem, op.semval)
                    if op.ms:
                        eng.memset(self.cc_dummy[:], 0.0).then_inc(esem[ename], 1)
                    continue
                if op.dma:
                    ins.then_inc(dsem[op.sem], 16)
                elif op.ms:
                    ins.then_inc(esem[ename], 1)
            if ename == "sp":
                for sem, val in final_dma:
                    eng.wait_ge(sem, val)

        with nc.Block() as block:
            @block.tensor
            def _(e):
                run("pe", e)

            @block.scalar
            def _(e):
                run("act", e)

            @block.vector
            def _(e):
                run("dve", e)

            @block.gpsimd
            def _(e):
                run("pool", e)

            @block.sync
            def _(e):
                run("sp", e)


class K:
    def __init__(self, ext_in, ext_out):
        self.nc = bass.Bass("TRN2", target_bir_lowering=False)
        self.S = Sched(self.nc)
        self.ext_in, self.ext_out = set(ext_in), set(ext_out)
        self.dr = {}
        self.st = ExitStack()
        self.uid = 0
        self.ffn_sel = {(0, 0): 0, (0, 1): 1, (1, 0): 2, (1, 1): 3}

    def dram(self, name, shape, dtype=F32):
        if name in self.dr:
            return self.dr[name]
        kind = "ExternalInput" if name in self.ext_in else ("ExternalOutput" if name in self.ext_out else "Internal")
        t = self.nc.dram_tensor(name, list(shape), dtype, kind=kind).ap()
        self.dr[name] = t
        return t

    def sb(self, name, shape, dtype):
        return self.st.enter_context(self.nc.sbuf_tensor(name, list(shape), dtype))

    def alloc(self, st):
        self.uid += 1
        u = self.uid
        return lambda n, sh, dt: st.enter_context(self.nc.sbuf_tensor("%s_u%d" % (n, u), list(sh), dt))

    def psum(self, name):
        return self.st.enter_context(self.nc.psum_tensor(name, [128, 512], F32))


def mm(S, ps_ap, lhsT, rhs, start, stop, reads, pskey):
    S.add("pe", lambda e: e.matmul(ps_ap, lhsT, rhs, start=start, stop=stop), reads=reads, writes=[pskey])


def phase_setup(k):
    nc, S = k.nc, k.S
    k.ps = [k.psum("ps%d" % i) for i in range(8)]
    k.ones_f = k.sb("ones_f", [128, 128], F32)
    k.ones_b = k.sb("ones_b", [128, 128], BF16)
    S.add("pool", lambda e: e.memset(k.ones_f[:], 1.0), writes=["ones_f"])
    S.add("pool", lambda e: e.memset(k.ones_b[:], 1.0), writes=["ones_b"])
    k.eps_t = k.sb("eps_t", [128, 1], F32)
    S.cc_dummy = k.sb("cc_dummy", [128, 8], F32)
    S.add("pool", lambda e: e.memset(k.eps_t[:], EPS), writes=["eps_t"])


def phase_mod(k, layers=(0, 1)):
    nc, S = k.nc, k.S
    cT = k.dram("cT", [D, 2])
    nl = len(layers)
    mod_w = k.dram("mod_w", [nl, D, 9 * D])
    mod_bT = k.dram("mod_bT", [nl, 128, 144])
    norm_gT = k.dram("norm_gT", [nl, 128, 48])
    k.modT = [k.sb("modT%d" % l, [128, 144, 2], F32) for l in range(2)]
    k.modA = [k.sb("modA%d" % l, [128, 48, 2], F32) for l in range(2)]
    k.modG = [k.sb("modG%d" % l, [128, 48, 2], F32) for l in range(2)]
    with ExitStack() as st:
        cin = st.enter_context(nc.sbuf_tensor("cin_sb", [128, KC, 2], F32))
        scT = st.enter_context(nc.sbuf_tensor("scT", [128, KC, 2], BF16))
        mb = st.enter_context(nc.sbuf_tensor("mb", [128, 144], F32))
        ng = st.enter_context(nc.sbuf_tensor("ng", [128, 48], F32))
        wm = [st.enter_context(nc.sbuf_tensor("wm%d" % i, [128, KC, 512], BF16)) for i in range(2)]
        S.add("sp", lambda e: e.dma_start(out=cin[:], in_=cT.rearrange("(kc p) n -> p kc n", p=128)), writes=["cin"], dma=True)
        S.add("act", lambda e: e.activation(out=scT[:], in_=cin[:], func=AF.Silu), reads=["cin"], writes=["scT"])
        si = 0
        for li, l in enumerate(layers):
            S.add("sp", lambda e, li=li: e.dma_start(out=mb[:], in_=mod_bT[li]), writes=["mb"], dma=True)
            S.add("sp", lambda e, li=li: e.dma_start(out=ng[:], in_=norm_gT[li]), writes=["ng"], dma=True)
            ps = k.ps[l]
            for sl in range(36):
                w = wm[si % 2]
                wkey = "wm%d" % (si % 2)
                si += 1
                S.add("pool", lambda e, w=w, li=li, sl=sl: e.dma_start(
                    out=w[:], in_=mod_w[li, :, sl * 512:(sl + 1) * 512].rearrange("(kc p) n -> p kc n", p=128)),
                    writes=[wkey], dma=True)
                for jj in range(4):
                    j = sl * 4 + jj
                    for kc in range(KC):
                        mm(S, ps[:, 2 * j:2 * j + 2], w[:, kc, jj * 128:(jj + 1) * 128], scT[:, kc, :],
                           kc == 0, kc == KC - 1, [wkey, "scT"], ("ps", l))
            modT = k.modT[l]
            for cnd in range(2):
                S.add("dve", lambda e, modT=modT, ps=ps, cnd=cnd: e.tensor_tensor(
                    out=modT[:, :, cnd], in0=ps[:, 0:288].rearrange("p (j c) -> p j c", c=2)[:, :, cnd], in1=mb[:], op=ALU.add),
                    reads=[("ps", l), "mb"], writes=[("modT", l)])
            for s in range(3):
                for cnd in range(2):
                    S.add("dve", lambda e, l=l, s=s, cnd=cnd, modT=modT: e.scalar_tensor_tensor(
                        out=k.modA[l][:, s * 16:(s + 1) * 16, cnd], in0=modT[:, (3 * s + 1) * 16:(3 * s + 2) * 16, cnd],
                        scalar=1.0, op0=ALU.add, in1=ng[:, s * 16:(s + 1) * 16], op1=ALU.mult),
                        reads=[("modT", l), "ng"], writes=[("modA", l)])
                    S.add("dve", lambda e, l=l, s=s, cnd=cnd, modT=modT: e.tensor_scalar(
                        out=k.modG[l][:, s * 16:(s + 1) * 16, cnd], in0=modT[:, (3 * s + 2) * 16:(3 * s + 3) * 16, cnd],
                        scalar1=(1.0 if s == 1 else 0.5), scalar2=None, op0=ALU.mult),
                        reads=[("modT", l)], writes=[("modG", l)])
        S.barrier()


def load_x_tile(k, xres, x_in, t0, T):
    S = k.S
    for c in range(KC):
        S.add("sp", lambda e, c=c: e.dma_start(out=xres[:, c, :T], in_=x_in[c * 128:(c + 1) * 128, t0:t0 + T]),
              writes=[("xres", c)], dma=True)


def norm_mod(k, xres, hT, T, segs, l, s, tmp, rstd, sqb):
    S = k.S
    ps = k.ps[6]
    for c in range(KC):
        sq = sqb[c % 2]
        S.add("act", lambda e, c=c, sq=sq: e.activation(out=sq[:, :T], in_=xres[:, c, :T], func=AF.Square),
              reads=[("xres", c)], writes=[("sq", c % 2)])
        mm(S, ps[:, :T], k.ones_b[:], sq[:, :T], c == 0, c == KC - 1, [("sq", c % 2), "ones_b"], ("ps", 6))
    S.add("act", lambda e: e.activation(out=rstd[:, :T], in_=ps[:, :T], func=AF.Sqrt, scale=1.0 / D, bias=k.eps_t[:]),
          reads=[("ps", 6), "eps_t"], writes=["rstd"])
    S.add("dve", lambda e: e.reciprocal(out=rstd[:, :T], in_=rstd[:, :T]), reads=["rstd"], writes=["rstd"])
    A, SH = k.modA[l], k.modT[l]
    for c in range(KC):
        tb = tmp[c % 2]
        for (o, n, cnd) in segs:
            S.add("dve", lambda e, c=c, o=o, n=n, cnd=cnd, tb=tb: e.scalar_tensor_tensor(
                out=tb[:, o:o + n], in0=xres[:, c, o:o + n], scalar=A[:, s * 16 + c, cnd:cnd + 1], op0=ALU.mult,
                in1=rstd[:, o:o + n], op1=ALU.mult),
                reads=[("xres", c), "rstd", ("modA", l)], writes=[("tmp", c % 2)])
            S.add("act", lambda e, c=c, o=o, n=n, cnd=cnd, tb=tb: e.activation(
                out=hT[:, c, o:o + n], in_=tb[:, o:o + n], func=AF.Identity,
                bias=SH[:, 3 * s * 16 + c, cnd:cnd + 1], scale=1.0),
                reads=[("tmp", c % 2), ("modT", l)], writes=[("hT", c)])


def residual_store(k, ps_ap, pskey, xres, c, T, segs, l, s, x_out, t0):
    S = k.S
    G = k.modG[l]
    for (o, n, cnd) in segs:
        S.add("dve", lambda e, o=o, n=n, cnd=cnd: e.scalar_tensor_tensor(
            out=xres[:, c, o:o + n], in0=ps_ap[:, o:o + n], scalar=G[:, s * 16 + c, cnd:cnd + 1], op0=ALU.mult,
            in1=xres[:, c, o:o + n], op1=ALU.add),
            reads=[pskey, ("xres", c), ("modG", l)], writes=[("xres", c)])
    if x_out is not None:
        S.add("sp", lambda e: e.dma_start(out=x_out[c * 128:(c + 1) * 128, t0:t0 + T], in_=xres[:, c, :T]),
              reads=[("xres", c)], writes=[("xdram", x_out.tensor.name, t0, c)], dma=True)


def _norm_stats(k, xres, T, rstd, sqb, banks=(6, 7)):
    S = k.S
    groups = [(0, min(T, 512))] + ([(512, T - 512)] if T > 512 else [])
    for c in range(KC):
        sq = sqb[c % 2]
        S.add("act", lambda e, c=c, sq=sq: e.activation(out=sq[:, :T], in_=xres[:, c, :T], func=AF.Square),
              reads=[("xres", c)], writes=[("sq", c % 2)])
        for g, (o, n) in enumerate(groups):
            mm(S, k.ps[banks[g]][:, :n], k.ones_b[:], sq[:, o:o + n], c == 0, c == KC - 1, [("sq", c % 2), "ones_b"], ("ps", banks[g]))
    for g, (o, n) in enumerate(groups):
        S.add("act", lambda e, g=g, o=o, n=n: e.activation(out=rstd[:, o:o + n], in_=k.ps[banks[g]][:, :n], func=AF.Sqrt,
                                                          scale=1.0 / D, bias=k.eps_t[:]),
              reads=[("ps", banks[g]), "eps_t"], writes=["rstd"])
    S.add("dve", lambda e: e.reciprocal(out=rstd[:, :T], in_=rstd[:, :T]), reads=["rstd"], writes=["rstd"])


def phase_ffn(k, l, widx, x_in, x_out, tiles, final_out=None):
    nc, S = k.nc, k.S
    s = 0 if widx == 0 else 2
    tiles = [(([(t[0], t[1])], t[2]) if len(t) == 3 else t) for t in tiles]
    TM = max(sum(n for _, n in parts) for parts, _ in tiles)
    FH = 2 if TM > 512 else 1
    NF = NFC // FH
    SLW = 256
    nsel = len(k.ffn_sel)
    wi_ = k.ffn_sel[(l, widx)]
    w1 = k.dram("ffn_w1", [nsel, D, FFN])[wi_]
    w3 = k.dram("ffn_w3", [nsel, D, FFN])[wi_]
    w2 = k.dram("ffn_w2", [nsel, FFN, D])[wi_]
    UB = ((0, 1), (4, 5))
    VB = ((2, 3), (6, 7))
    YB = ((0, 1), (2, 3))
    with ExitStack() as st:
        al = k.alloc(st)
        xres = al("xres", [128, KC, TM], F32)
        hT = al("hT", [128, KC, TM], BF16)
        gT = al("gT", [128, NF, TM], BF16)
        tmp = [al("tmp%d" % i, [128, TM], F32) for i in range(2)]
        sqb = [al("sq%d" % i, [128, TM], BF16) for i in range(2)]
        rstd = al("rstd", [128, TM], F32)
        su = [al("su%d" % i, [128, TM], BF16) for i in range(2)]
        wa = [al("wa%d" % i, [128, KC, SLW], BF16) for i in range(2)]
        wb = [al("wb%d" % i, [128, KC, SLW], BF16) for i in range(2)]
        wc = [al("wc%d" % i, [128, NF, 256], BF16) for i in range(2)]
        na = nc_ = 0
        A, SH, G = k.modA[l], k.modT[l], k.modG[l]
        for (parts, segs) in tiles:
            T = sum(n for _, n in parts)
            groups = [(0, min(T, 512))] + ([(512, T - 512)] if T > 512 else [])
            offs = []
            o_ = 0
            for (c0, n) in parts:
                offs.append((o_, c0, n))
                o_ += n
            for c in range(KC):
                for (o, c0, n) in offs:
                    S.add("sp", lambda e, c=c, o=o, c0=c0, n=n: e.dma_start(out=xres[:, c, o:o + n], in_=x_in[c * 128:(c + 1) * 128, c0:c0 + n]),
                          writes=[("xres", c)], dma=True)
            _norm_stats(k, xres, T, rstd, sqb)
            for c in range(KC):
                tb = tmp[c % 2]
                for (o, n, cnd) in segs:
                    S.add("dve", lambda e, c=c, o=o, n=n, cnd=cnd, tb=tb: e.scalar_tensor_tensor(
                        out=tb[:, o:o + n], in0=xres[:, c, o:o + n], scalar=A[:, s * 16 + c, cnd:cnd + 1], op0=ALU.mult,
                        in1=rstd[:, o:o + n], op1=ALU.mult),
                        reads=[("xres", c), "rstd", ("modA", l)], writes=[("tmp", c % 2)])
                    S.add("act", lambda e, c=c, o=o, n=n, cnd=cnd, tb=tb: e.activation(
                        out=hT[:, c, o:o + n], in_=tb[:, o:o + n], func=AF.Identity,
                        bias=SH[:, 3 * s * 16 + c, cnd:cnd + 1], scale=1.0),
                        reads=[("tmp", c % 2), ("modT", l)], writes=[("hT", c)])
            for fh in range(FH):
                f0 = fh * NF
                for sl in range(NF * 128 // SLW):
                    a, b = wa[na % 2], wb[na % 2]
                    ka, kb = "wa%d" % (na % 2), "wb%d" % (na % 2)
                    na += 1
                    cc0 = f0 * 128 + sl * SLW
                    S.add("pool", lambda e, a=a, cc0=cc0: e.dma_start(
                        out=a[:], in_=w1[:, cc0:cc0 + SLW].rearrange("(kc p) n -> p kc n", p=128)), writes=[ka], dma=True)
                    S.add("pool", lambda e, b=b, cc0=cc0: e.dma_start(
                        out=b[:], in_=w3[:, cc0:cc0 + SLW].rearrange("(kc p) n -> p kc n", p=128)), writes=[kb], dma=True)
                    for jj in range(SLW // 128):
                        fc = sl * (SLW // 128) + jj
                        ub, vb = UB[fc % 2], VB[fc % 2]
                        for g, (o, n) in enumerate(groups):
                            for kc in range(KC):
                                mm(S, k.ps[ub[g]][:, :n], a[:, kc, jj * 128:(jj + 1) * 128], hT[:, kc, o:o + n], kc == 0, kc == KC - 1,
                                   [ka, ("hT", kc)], ("ps", ub[g]))
                        for g, (o, n) in enumerate(groups):
                            for kc in range(KC):
                                mm(S, k.ps[vb[g]][:, :n], b[:, kc, jj * 128:(jj + 1) * 128], hT[:, kc, o:o + n], kc == 0, kc == KC - 1,
                                   [kb, ("hT", kc)], ("ps", vb[g]))
                        sut = su[fc % 2]
                        for g, (o, n) in enumerate(groups):
                            S.add("act", lambda e, g=g, o=o, n=n, ub=ub, sut=sut: e.activation(
                                out=sut[:, o:o + n], in_=k.ps[ub[g]][:, :n], func=AF.Silu),
                                reads=[("ps", ub[g])], writes=[("su", fc % 2, g)])
                            S.add("dve", lambda e, g=g, o=o, n=n, vb=vb, sut=sut, fc=fc: e.tensor_tensor(
                                out=gT[:, fc, o:o + n], in0=k.ps[vb[g]][:, :n], in1=sut[:, o:o + n], op=ALU.mult),
                                reads=[("ps", vb[g]), ("su", fc % 2, g)], writes=[("gT", fc, g)])
                for ds in range(KC // 2):
                    w = wc[nc_ % 2]
                    kw = "wc%d" % (nc_ % 2)
                    nc_ += 1
                    S.add("pool", lambda e, w=w, ds=ds, f0=f0: e.dma_start(
                        out=w[:], in_=w2[f0 * 128:(f0 + NF) * 128, ds * 256:(ds + 1) * 256].rearrange("(fc p) n -> p fc n", p=128)),
                        writes=[kw], dma=True)
                    for jj in range(2):
                        c = ds * 2 + jj
                        yb = YB[c % 2]
                        for g, (o, n) in enumerate(groups):
                            for fc in range(NF):
                                mm(S, k.ps[yb[g]][:, :n], w[:, fc, jj * 128:(jj + 1) * 128], gT[:, fc, o:o + n], fc == 0, fc == NF - 1,
                                   [kw, ("gT", fc, g)], ("ps", yb[g]))
                        for (o, n, cnd) in segs:
                            g = 0 if o < 512 else 1
                            lo = o - groups[g][0]
                            S.add("dve", lambda e, c=c, o=o, n=n, cnd=cnd, g=g, lo=lo, yb=yb: e.scalar_tensor_tensor(
                                out=xres[:, c, o:o + n], in0=k.ps[yb[g]][:, lo:lo + n], scalar=G[:, s * 16 + c, cnd:cnd + 1], op0=ALU.mult,
                                in1=xres[:, c, o:o + n], op1=ALU.add),
                                reads=[("ps", yb[g]), ("xres", c), ("modG", l)], writes=[("xres", c)])
                        if x_out is not None and fh == FH - 1:
                            for (o, c0, n) in offs:
                                S.add("sp", lambda e, c=c, o=o, c0=c0, n=n: e.dma_start(
                                    out=x_out[c * 128:(c + 1) * 128, c0:c0 + n], in_=xres[:, c, o:o + n]),
                                    reads=[("xres", c)], writes=[("xdram", x_out.tensor.name, c0, c)], dma=True)
            if final_out is not None:
                final_norm_store(k, xres, T, parts[0][0], final_out, tmp, rstd, sqb)
        S.barrier()


def final_norm_store(k, xres, T, t0, out, tmp, rstd, sqb):
    S = k.S
    ps = k.ps[6]
    fn = k.fnT
    for c in range(KC):
        sq = sqb[c % 2]
        S.add("act", lambda e, c=c, sq=sq: e.activation(out=sq[:, :T], in_=xres[:, c, :T], func=AF.Square),
              reads=[("xres", c)], writes=[("sq", c % 2)])
        mm(S, ps[:, :T], k.ones_b[:], sq[:, :T], c == 0, c == KC - 1, [("sq", c % 2), "ones_b"], ("ps", 6))
    S.add("act", lambda e: e.activation(out=rstd[:, :T], in_=ps[:, :T], func=AF.Sqrt, scale=1.0 / D, bias=k.eps_t[:]),
          reads=[("ps", 6), "eps_t"], writes=["rstd"])
    S.add("dve", lambda e: e.reciprocal(out=rstd[:, :T], in_=rstd[:, :T]), reads=["rstd"], writes=["rstd"])
    for c in range(KC):
        S.add("dve", lambda e, c=c: e.scalar_tensor_tensor(
            out=xres[:, c, :T], in0=xres[:, c, :T], scalar=fn[:, c:c + 1], op0=ALU.mult, in1=rstd[:, :T], op1=ALU.mult),
            reads=[("xres", c), "rstd", "fnT"], writes=[("xres", c)])
        S.add("sp", lambda e, c=c: e.dma_start(out=out[c * 128:(c + 1) * 128, t0:t0 + T], in_=xres[:, c, :T]),
              reads=[("xres", c)], writes=[("odram", t0, c)], dma=True)


GC1, GC2 = 0.044715, 1.5957691216057308


def gelu_from_psum(k, ps_ap, pskey, out_ap, outkeys, n, t1, t2, t1k, t2k):
    S = k.S
    S.add("act", lambda e: e.activation(out=t1, in_=ps_ap, func=AF.Square), reads=[pskey], writes=[t1k])
    S.add("dve", lambda e: e.tensor_scalar(out=t1, in0=t1, scalar1=GC1, scalar2=1.0, op0=ALU.mult, op1=ALU.add),
          reads=[t1k], writes=[t1k])
    S.add("dve", lambda e: e.tensor_tensor(out=t1, in0=ps_ap, in1=t1, op=ALU.mult), reads=[pskey, t1k], writes=[t1k])
    S.add("act", lambda e: e.activation(out=t2, in_=t1, func=AF.Sigmoid, scale=GC2), reads=[t1k], writes=[t2k])
    S.add("dve", lambda e: e.tensor_tensor(out=out_ap, in0=ps_ap, in1=t2, op=ALU.mult), reads=[pskey, t2k], writes=outkeys)


def rope_from_psum(k, ps_ap, pskey, ps2, ps2key, cos, sin, T, out_ap, outkeys, qraw, t1, t2, keys):
    S = k.S
    qk, t1k, t2k = keys
    S.add("act", lambda e: e.activation(out=qraw[:, :T], in_=ps_ap, func=AF.Identity), reads=[pskey], writes=[qk])
    S.add("dve", lambda e: e.tensor_tensor(out=t1[:, :T], in0=ps_ap, in1=cos, op=ALU.mult), reads=[pskey, "rope"], writes=[t1k])
    mm(S, ps2[:, :T], (k.ones_b if DBG_NOPERM else k.permM)[:], qraw[:, :T], True, True, [qk, "permM"], ps2key)
    S.add("dve", lambda e: e.tensor_tensor(out=t2[:, :T], in0=ps2[:, :T], in1=sin, op=ALU.mult), reads=[ps2key, "rope"], writes=[t2k])
    S.add(ROPE_ADD_ENG, lambda e: e.tensor_tensor(out=out_ap, in0=t1[:, :T], in1=t2[:, :T], op=ALU.add), reads=[t1k, t2k], writes=outkeys)


def load_rope_consts(k):
    nc, S = k.nc, k.S
    permD = k.dram("permM", [128, 128])
    k.permM = k.sb("permM_sb", [128, 128], BF16)
    S.add("pool", lambda e: e.dma_start(out=k.permM[:], in_=permD), writes=["permM"], dma=True)


def phase_ab_in(k, x_in, tiles):
    nc, S = k.nc, k.S
    l, s = 0, 1
    w_in = k.dram("ab_w_in", [1, D, 3328])[0]
    cosD, sinD = k.dram("cosT", [128, NT]), k.dram("sinT", [128, NT])
    qT = k.dram("qT", [1024, NT], BF16)
    kT = k.dram("kT", [2, 128, NT], BF16)
    vtok = k.dram("vtok", [NT, 128], BF16)
    guT = k.dram("guT", [1024, NT], BF16)
    gvtok = k.dram("gvtok", [NT, 1024], BF16)
    with ExitStack() as st:
        al = k.alloc(st)
        xres = al("xres", [128, KC, 512], F32)
        hT = al("hT", [128, KC, 512], BF16)
        tmp = [al("tmp%d" % i, [128, 512], F32) for i in range(2)]
        sqb = [al("sq%d" % i, [128, 512], BF16) for i in range(2)]
        rstd = al("rstd", [128, 512], F32)
        cos, sin = al("cos", [128, 512], F32), al("sin", [128, 512], F32)
        wtm = al("wtm", [128, KC, 1152], BF16)
        kdw = al("kdw", [128, KC, 256], BF16)
        wa = [al("wa%d" % i, [128, KC, 256], BF16) for i in range(2)]
        qraw = [al("qraw%d" % i, [128, 512], BF16) for i in range(2)]
        r1 = [al("r1_%d" % i, [128, 512], F32) for i in range(2)]
        r2 = [al("r2_%d" % i, [128, 512], F32) for i in range(2)]
        ob = [al("ob%d" % i, [128, 512], BF16) for i in range(2)]
        obt = [al("obt%d" % i, [128, 1152], BF16) for i in range(2)]
        for i, (c0, n) in enumerate([(1152, 128), (2304, 512), (2816, 512)]):
            o = [0, 128, 640][i]
            S.add("pool", lambda e, c0=c0, n=n, o=o: e.dma_start(
                out=wtm[:, :, o:o + n], in_=w_in[:, c0:c0 + n].rearrange("(kc p) n -> p kc n", p=128)),
                writes=[("wtm", i)], dma=True)
        for i in range(4):
            c0 = 1024 + 64 * (i // 2)
            S.add("pool", lambda e, c0=c0, i=i: e.dma_start(
                out=kdw[:, :, i * 64:(i + 1) * 64], in_=w_in[:, c0:c0 + 64].rearrange("(kc p) n -> p kc n", p=128)),
                writes=[("kdw", i)], dma=True)
        na = 0
        cnt = 0
        for (t0, T, segs) in tiles:
            load_x_tile(k, xres, x_in, t0, T)
            S.add("sp", lambda e, t0=t0, T=T: e.dma_start(out=cos[:, :T], in_=cosD[:, t0:t0 + T]), writes=["rope"], dma=True)
            S.add("sp", lambda e, t0=t0, T=T: e.dma_start(out=sin[:, :T], in_=sinD[:, t0:t0 + T]), writes=["rope"], dma=True)
            norm_mod(k, xres, hT, T, segs, l, s, tmp, rstd, sqb)
            hreads = [("hT", c) for c in range(KC)]
            for sl in range(8):
                if DBG_SECT is not None and ("q" if sl < 4 else "gu") not in DBG_SECT:
                    continue
                a = wa[na % 2]
                ka = "wa%d" % (na % 2)
                na += 1
                c0 = sl * 256 if sl < 4 else 1280 + (sl - 4) * 256
                S.add("pool", lambda e, a=a, c0=c0: e.dma_start(
                    out=a[:], in_=w_in[:, c0:c0 + 256].rearrange("(kc p) n -> p kc n", p=128)), writes=[ka], dma=True)
                for jj in range(2):
                    ch = (sl % 4) * 2 + jj
                    i2 = cnt % 2
                    cnt += 1
                    pq = k.ps[i2]
                    for kc in range(KC):
                        mm(S, pq[:, :T], a[:, kc, jj * 128:(jj + 1) * 128], hT[:, kc, :T], kc == 0, kc == KC - 1,
                           [ka, ("hT", kc)], ("ps", i2))
                    o_ = ob[i2]
                    if sl < 4:
                        rope_from_psum(k, pq[:, :T], ("ps", i2), k.ps[2 + i2], ("ps", 2 + i2), cos[:, :T], sin[:, :T], T,
                                       o_[:, :T], [("ob", i2)], qraw[i2], r1[i2], r2[i2], (("qraw", i2), ("r1", i2), ("r2", i2)))
                        dst = qT[ch * 128:(ch + 1) * 128, t0:t0 + T]
                    else:
                        gelu_from_psum(k, pq[:, :T], ("ps", i2), o_[:, :T], [("ob", i2)], T, r1[i2][:, :T], r2[i2][:, :T],
                                       ("r1", i2), ("r2", i2))
                        dst = guT[ch * 128:(ch + 1) * 128, t0:t0 + T]
                    S.add("sp", lambda e, dst=dst, o_=o_, T=T: e.dma_start(out=dst, in_=o_[:, :T]),
                          reads=[("ob", i2)], writes=[("dr", dst.tensor.name, t0, ch)], dma=True)
            for kv in range(2):
                if DBG_SECT is not None and "k" not in DBG_SECT:
                    continue
                i2 = cnt % 2
                cnt += 1
                pq = k.ps[i2]
                for kc in range(KC):
                    mm(S, pq[:, :T], kdw[:, kc, kv * 128:(kv + 1) * 128], hT[:, kc, :T], kc == 0, kc == KC - 1,
                       [("kdw", 2 * kv), ("kdw", 2 * kv + 1), ("hT", kc)], ("ps", i2))
                o_ = ob[i2]
                rope_from_psum(k, pq[:, :T], ("ps", i2), k.ps[2 + i2], ("ps", 2 + i2), cos[:, :T], sin[:, :T], T,
                               o_[:, :T], [("ob", i2)], qraw[i2], r1[i2], r2[i2], (("qraw", i2), ("r1", i2), ("r2", i2)))
                dst = kT[kv, :, t0:t0 + T]
                S.add("sp", lambda e, dst=dst, o_=o_, T=T: e.dma_start(out=dst, in_=o_[:, :T]),
                      reads=[("ob", i2)], writes=[("dr", "kT", t0, kv)], dma=True)
            for tb in range(T // 128):
                if DBG_SECT is not None and "tok" not in DBG_SECT:
                    continue
                ot = obt[tb % 2]
                okey = ("obt", tb % 2)
                pv = k.ps[4 + tb % 2]
                for kc in range(KC):
                    mm(S, pv[:, :128], hT[:, kc, tb * 128:(tb + 1) * 128], wtm[:, kc, 0:128], kc == 0, kc == KC - 1,
                       [("wtm", 0), ("hT", kc)], ("ps", 4 + tb % 2))
                S.add("act", lambda e, pv=pv, ot=ot: e.activation(out=ot[:, 0:128], in_=pv[:, :128], func=AF.Identity),
                      reads=[("ps", 4 + tb % 2)], writes=[okey])
                for hf in range(2):
                    pg = k.ps[6 + hf]
                    for kc in range(KC):
                        mm(S, pg[:, :], hT[:, kc, tb * 128:(tb + 1) * 128], wtm[:, kc, 128 + hf * 512:128 + (hf + 1) * 512],
                           kc == 0, kc == KC - 1, [("wtm", 1 + hf), ("hT", kc)], ("ps", 6 + hf))
                    gelu_from_psum(k, pg[:, :], ("ps", 6 + hf), ot[:, 128 + hf * 512:128 + (hf + 1) * 512], [okey], 512,
                                   r1[hf][:, :], r2[hf][:, :], ("r1", hf), ("r2", hf))
                r0 = t0 + tb * 128
                S.add("sp", lambda e, ot=ot, r0=r0: e.dma_start(out=vtok[r0:r0 + 128, :], in_=ot[:, 0:128]),
                      reads=[okey], writes=[("dr", "vtok", r0)], dma=True)
                S.add("sp", lambda e, ot=ot, r0=r0: e.dma_start(out=gvtok[r0:r0 + 128, :], in_=ot[:, 128:1152]),
                      reads=[okey], writes=[("dr", "gvtok", r0)], dma=True)
        S.barrier()


def phase_ab_mix(k):
    nc, S = k.nc, k.S
    qT = k.dram("qT", [1024, NT], BF16)
    kT = k.dram("kT", [2, 128, NT], BF16)
    vtok = k.dram("vtok", [NT, 128], BF16)
    guT = k.dram("guT", [1024, NT], BF16)
    gvtok = k.dram("gvtok", [NT, 1024], BF16)
    oT = k.dram("oT", [2048, NT], BF16)
    masksD = k.dram("masks", [128, 4, 128])
    sinkD = k.dram("sinkT", [128, 8])
    wsD = k.dram("b_wsT", [128, 8, 128])
    bbD = k.dram("b_biasbc", [128, 8, 128])
    scale = 1.0 / 8.0
    with ExitStack() as st:
        al = k.alloc(st)
        kTs = al("kTs", [128, 2, NT], BF16)
        vts = al("vts", [128, 20, 128], BF16)
        masks = al("masks", [128, 4, 128], BF16)
        sink = al("sink", [128, 8], F32)
        esbc = al("esbc", [128, 8, 128], F32)
        wsT = al("wsT", [128, 8, 128], BF16)
        bbc = al("bbc", [128, 8, 128], F32)
        qb_ = [al("qb%d" % i, [128, 8, 128], BF16) for i in range(2)]
        gub = [al("gub%d" % i, [128, 8, 128], BF16) for i in range(2)]
        gvb = [al("gvb%d" % i, [128, 1024], BF16) for i in range(2)]
        pT = [al("pT%d" % i, [128, 2, 4, 128], BF16) for i in range(3)]
        rr = al("rr", [128, 512], F32)
        gt = al("gt", [128, 512], F32)
        ob = [al("oblk%d" % i, [128, 16, 128], BF16) for i in range(2)]
        S.add("sp", lambda e: e.dma_start(out=kTs[:], in_=kT.rearrange("v p t -> p v t")), writes=["kTs"], dma=True)
        S.add("sp", lambda e: e.dma_start(out=vts[:], in_=vtok.rearrange("(b p) d -> p b d", p=128)), writes=["vts"], dma=True)
        S.add("pool", lambda e: e.dma_start(out=masks[:], in_=masksD), writes=["masks"], dma=True)
        S.add("pool", lambda e: e.dma_start(out=wsT[:], in_=wsD), writes=["wsT"], dma=True)
        S.add("sp", lambda e: e.dma_start(out=bbc[:], in_=bbD), writes=["bbc"], dma=True)
        S.add("sp", lambda e: e.dma_start(out=sink[:], in_=sinkD), writes=["sink"], dma=True)
        S.add("act", lambda e: e.activation(out=sink[:], in_=sink[:], func=AF.Exp), reads=["sink"], writes=["sink"])
        S.add("dve", lambda e: e.tensor_copy(out=esbc[:], in_=sink[:].unsqueeze(2).broadcast_to([128, 8, 128])),
              reads=["sink"], writes=["esbc"])
        blocks = list(range(16)) + [18, 19]
        npT = 0
        for bi, b in enumerate(blocks):
            i2 = bi % 2
            c0 = b * 128
            qb, gu, gv, o_ = qb_[i2], gub[i2], gvb[i2], ob[i2]
            S.add("sp", lambda e, qb=qb, c0=c0: e.dma_start(out=qb[:], in_=qT[:, c0:c0 + 128].rearrange("(c p) t -> p c t", p=128)),
                  writes=[("qb", i2)], dma=True)
            S.add("sp", lambda e, gu=gu, c0=c0: e.dma_start(out=gu[:], in_=guT[:, c0:c0 + 128].rearrange("(c p) t -> p c t", p=128)),
                  writes=[("gub", i2)], dma=True)
            S.add("sp", lambda e, gv=gv, c0=c0: e.dma_start(out=gv[:], in_=gvtok[c0:c0 + 128, :]), writes=[("gvb", i2)], dma=True)
            if b < 16:
                kl = [((b - 1) if b > 0 else 16, 0 if b > 0 else 2), (b, None), ((b + 1) if b < 15 else 17, 1 if b < 15 else 3),
                      (18, None), (19, None)]
            else:
                kl = [(18, None), (19, None)]
            for kv in range(2):
                po, pd = k.ps[4], k.ps[5]
                pend = {}
                for ji in range(len(kl) + 1):
                    if ji < len(kl):
                        kb, mi = kl[ji]
                        pa, pb = k.ps[2 * (ji % 2)], k.ps[2 * (ji % 2) + 1]
                        ka, kb_ = ("ps", 2 * (ji % 2)), ("ps", 2 * (ji % 2) + 1)
                        mm(S, pa[:, :], kTs[0:64, kv, kb * 128:(kb + 1) * 128], qb[0:64, kv * 4:(kv + 1) * 4, :], True, True,
                           ["kTs", ("qb", i2)], ka)
                        mm(S, pb[:, :], kTs[64:128, kv, kb * 128:(kb + 1) * 128], qb[64:128, kv * 4:(kv + 1) * 4, :], True, True,
                           ["kTs", ("qb", i2)], kb_)
                        p = pT[npT % 3]
                        pk = ("pT", npT % 3)
                        npT += 1
                        S.add("act", lambda e, p=p, pa=pa: e.activation(out=p[:, 0, :, :], in_=pa[:, :].rearrange("p (c t) -> p c t", c=4),
                                                                        func=AF.Exp, scale=scale), reads=[ka], writes=[pk])
                        S.add("act", lambda e, p=p, pb=pb: e.activation(out=p[:, 1, :, :], in_=pb[:, :].rearrange("p (c t) -> p c t", c=4),
                                                                        func=AF.Exp, scale=scale), reads=[kb_], writes=[pk])
                        if mi is not None:
                            S.add("pool", lambda e, p=p, mi=mi: e.tensor_tensor(
                                out=p[:].rearrange("p e c t -> p (e c) t"), in0=p[:].rearrange("p e c t -> p (e c) t"),
                                in1=masks[:, mi:mi + 1, :].broadcast_to([128, 8, 128]), op=ALU.mult),
                                reads=[pk, "masks"], writes=[pk])
                        pend[ji] = (p, pk, kb)
                    jj_ = ji - 1
                    if jj_ >= 0:
                        p, pk, kb = pend.pop(jj_)
                        first, last = jj_ == 0, jj_ == len(kl) - 1
                        for e_ in range(2):
                            mm(S, po[64 * e_:64 * e_ + 64, :], vts[:, kb, kv * 64:(kv + 1) * 64], p[:, e_, :, :], first, last,
                               ["vts", pk], ("ps", 4))
                            mm(S, pd[64 * e_:64 * e_ + 64, :], k.ones_b[:, 0:64], p[:, e_, :, :], first, last,
                               ["ones_b", pk], ("ps", 5))
                S.add("dve", lambda e, pd=pd, kv=kv: e.tensor_tensor(
                    out=rr[:].rearrange("p (c t) -> p c t", c=4), in0=pd[:, :].rearrange("p (c t) -> p c t", c=4),
                    in1=esbc[:, kv * 4:(kv + 1) * 4, :], op=ALU.add), reads=[("ps", 5), "esbc"], writes=["rr"])
                S.add("dve", lambda e: e.reciprocal(out=rr[:], in_=rr[:]), reads=["rr"], writes=["rr"])
                S.add("dve", lambda e, po=po, kv=kv, o_=o_: e.tensor_tensor(
                    out=o_[:, kv * 4:(kv + 1) * 4, :], in0=po[:, :].rearrange("p (c t) -> p c t", c=4),
                    in1=rr[:].rearrange("p (c t) -> p c t", c=4), op=ALU.mult), reads=[("ps", 4), "rr"], writes=[("oblk", i2)])
            for hf in range(2):
                pg = k.ps[6 + hf]
                for gg in range(4):
                    g = hf * 4 + gg
                    mm(S, pg[:, gg * 128:(gg + 1) * 128], gv[:, g * 128:(g + 1) * 128], wsT[:, g, :], True, True,
                       [("gvb", i2), "wsT"], ("ps", 6 + hf))
                S.add("dve", lambda e, pg=pg, hf=hf: e.tensor_tensor(
                    out=gt[:].rearrange("p (c t) -> p c t", c=4), in0=pg[:, :].rearrange("p (c t) -> p c t", c=4),
                    in1=bbc[:, hf * 4:(hf + 1) * 4, :], op=ALU.add), reads=[("ps", 6 + hf), "bbc"], writes=["gt"])
                S.add("dve", lambda e, hf=hf, o_=o_, gu=gu: e.tensor_tensor(
                    out=o_[:, 8 + hf * 4:8 + (hf + 1) * 4, :], in0=gt[:].rearrange("p (c t) -> p c t", c=4),
                    in1=gu[:, hf * 4:(hf + 1) * 4, :], op=ALU.mult), reads=["gt", ("gub", i2)], writes=[("oblk", i2)])
            S.add("sp", lambda e, o_=o_, c0=c0: e.dma_start(out=oT[:, c0:c0 + 128].rearrange("(c p) t -> p c t", p=128), in_=o_[:]),
                  reads=[("oblk", i2)], writes=[("dr", "oT", c0)], dma=True)
        S.barrier()


def phase_outproj(k, l, wname, oT, x_in, x_out, tiles):
    nc, S = k.nc, k.S
    w = k.dram(wname, [1, D, D])[0]
    with ExitStack() as st:
        al = k.alloc(st)
        xres = al("xres", [128, KC, 512], F32)
        hT = al("hT", [128, KC, 512], BF16)
        wa = [al("wa%d" % i, [128, KC, 256], BF16) for i in range(2)]
        na = 0
        for (t0, T, segs) in tiles:
            load_x_tile(k, xres, x_in, t0, T)
            for c in range(KC):
                S.add("sp", lambda e, c=c, t0=t0, T=T: e.dma_start(out=hT[:, c, :T], in_=oT[c * 128:(c + 1) * 128, t0:t0 + T]),
                      writes=[("hT", c)], dma=True)
            for sl in range(KC // 2):
                a = wa[na % 2]
                ka = "wa%d" % (na % 2)
                na += 1
                S.add("pool", lambda e, a=a, sl=sl: e.dma_start(
                    out=a[:], in_=w[:, sl * 256:(sl + 1) * 256].rearrange("(kc p) n -> p kc n", p=128)), writes=[ka], dma=True)
                for jj in range(2):
                    c = sl * 2 + jj
                    py = k.ps[c % 2]
                    for kc in range(KC):
                        mm(S, py[:, :T], a[:, kc, jj * 128:(jj + 1) * 128], hT[:, kc, :T], kc == 0, kc == KC - 1,
                           [ka, ("hT", kc)], ("ps", c % 2))
                    residual_store(k, py, ("ps", c % 2), xres, c, T, segs, l, 1, x_out, t0)
        S.barrier()


NK = SEQ + NCTX
NKC = NK // 128


def lat_norm(k, src, nch, T, gain, dst, rstd, sqb, pskey_i):
    S = k.S
    ps = k.ps[pskey_i]
    for c in range(nch):
        sq = sqb[c % 2]
        S.add("act", lambda e, c=c, sq=sq: e.activation(out=sq[:, :T], in_=src[:, c, :T], func=AF.Square),
              reads=[("lsrc", c)], writes=[("sq", c % 2)])
        mm(S, ps[:, :T], k.ones_b[:], sq[:, :T], c == 0, c == nch - 1, [("sq", c % 2), "ones_b"], ("ps", pskey_i))
    S.add("act", lambda e: e.activation(out=rstd[:, :T], in_=ps[:, :T], func=AF.Sqrt, scale=1.0 / (nch * 128), bias=k.eps_t[:]),
          reads=[("ps", pskey_i), "eps_t"], writes=["rstd2"])
    S.add("dve", lambda e: e.reciprocal(out=rstd[:, :T], in_=rstd[:, :T]), reads=["rstd2"], writes=["rstd2"])
    for c in range(nch):
        S.add("dve", lambda e, c=c: e.scalar_tensor_tensor(
            out=dst[:, c, :T], in0=src[:, c, :T], scalar=gain[:, c:c + 1], op0=ALU.mult, in1=rstd[:, :T], op1=ALU.mult),
            reads=[("lsrc", c), "rstd2", "gains"], writes=[("ldst", c)])


def phase_cd_in(k, x_in, tiles):
    nc, S = k.nc, k.S
    l, s = 1, 1
    w_in = k.dram("cd_w_in", [1, D, 4416])[0]
    w_uq = k.dram("c_w_uq", [1, 768, 1536])[0]
    cosD, sinD = k.dram("cosT", [128, NT]), k.dram("sinT", [128, NT])
    qgD, kgD = k.dram("c_qnT", [128, 6]), k.dram("c_kvnT", [128, 4])
    qnT = k.dram("qnT", [1024, NOWN], BF16)
    qrT = k.dram("qrT", [512, NOWN], BF16)
    xin = [k.dram("xch_in%d" % i, [128, NOWN], BF16) for i in range(5)]
    ctxkv = k.dram("ctxkv", [640, NCTX], BF16)
    zb = k.dram("zb_in", [2, 1024])
    dbT = k.dram("dbT", [1024, NOWN], BF16)
    zT = k.dram("zT", [1024, NOWN])
    with ExitStack() as st:
        al = k.alloc(st)
        xres = al("xres", [128, KC, 512], F32)
        hT = al("hT", [128, KC, 512], BF16)
        tmp = [al("tmp%d" % i, [128, 512], F32) for i in range(2)]
        sqb = [al("sq%d" % i, [128, 512], BF16) for i in range(2)]
        rstd = al("rstd", [128, 512], F32)
        rstd2 = al("rstd2", [128, 512], F32)
        cos, sin = al("cos", [128, 512], F32), al("sin", [128, 512], F32)
        lat = al("lat", [128, 6, 512], F32)
        latn = al("latn", [128, 6, 512], BF16)
        qg, kg = al("qg", [128, 6], F32), al("kg", [128, 4], F32)
        wuq = al("wuq", [128, 6, 1536], BF16)
        wqr = al("wqr", [128, 6, 4, 128], BF16)
        krw = al("krw", [128, KC, 128], BF16)
        wa = [al("wa%d" % i, [128, KC, 256], BF16) for i in range(2)]
        qraw = [al("qraw%d" % i, [128, 512], BF16) for i in range(2)]
        r1 = [al("r1_%d" % i, [128, 512], F32) for i in range(2)]
        r2 = [al("r2_%d" % i, [128, 512], F32) for i in range(2)]
        ob = [al("ob%d" % i, [128, 512], BF16) for i in range(2)]
        zo = [al("zo%d" % i, [128, 512], F32) for i in range(2)]
        dcs = [al("dcs%d" % i, [128, 512], F32) for i in range(2)]
        S.add("sp", lambda e: e.dma_start(out=qg[:], in_=qgD), writes=["gains"], dma=True)
        S.add("sp", lambda e: e.dma_start(out=kg[:], in_=kgD), writes=["gains"], dma=True)
        for i in range(3):
            S.add("pool", lambda e, i=i: e.dma_start(
                out=wuq[:, :, i * 512:(i + 1) * 512], in_=w_uq[:, i * 512:(i + 1) * 512].rearrange("(kc p) n -> p kc n", p=128)),
                writes=[("wuq", i)], dma=True)
        for h in range(8):
            S.add("pool", lambda e, h=h: e.dma_start(
                out=wqr[:, :, h // 2, (h % 2) * 64:(h % 2) * 64 + 64],
                in_=w_uq[:, h * 192 + 128:h * 192 + 192].rearrange("(kc p) n -> p kc n", p=128)),
                writes=[("wqr", h)], dma=True)
        for i in range(2):
            S.add("pool", lambda e, i=i: e.dma_start(
                out=krw[:, :, i * 64:(i + 1) * 64], in_=w_in[:, 1280:1344].rearrange("(kc p) n -> p kc n", p=128)),
                writes=[("krw", i)], dma=True)
        wuq_r = [("wuq", i) for i in range(3)]
        wqr_r = [("wqr", h) for h in range(8)]
        na = 0
        cnt = 0
        for (t0, T, segs) in tiles:
            own = t0 < NOWN
            oc0 = t0 if own else 0
            kvd = (lambda i: xin[i]) if own else (lambda i: ctxkv[i * 128:(i + 1) * 128, :])
            load_x_tile(k, xres, x_in, t0, T)
            S.add("sp", lambda e, t0=t0, T=T: e.dma_start(out=cos[:, :T], in_=cosD[:, t0:t0 + T]), writes=["rope"], dma=True)
            S.add("sp", lambda e, t0=t0, T=T: e.dma_start(out=sin[:, :T], in_=sinD[:, t0:t0 + T]), writes=["rope"], dma=True)
            norm_mod(k, xres, hT, T, segs, l, s, tmp, rstd, sqb)

            def proj_chunk(a, ka, jj):
                nonlocal cnt
                i2 = cnt % 2
                cnt += 1
                pq = k.ps[i2]
                for kc in range(KC):
                    mm(S, pq[:, :T], a[:, kc, jj * 128:(jj + 1) * 128], hT[:, kc, :T], kc == 0, kc == KC - 1,
                       [ka, ("hT", kc)] if not isinstance(ka, list) else ka + [("hT", kc)], ("ps", i2))
                return pq, i2

            def slab(c0):
                nonlocal na
                a = wa[na % 2]
                ka = "wa%d" % (na % 2)
                na += 1
                S.add("pool", lambda e: e.dma_start(
                    out=a[:], in_=w_in[:, c0:c0 + 256].rearrange("(kc p) n -> p kc n", p=128)), writes=[ka], dma=True)
                return a, ka

            if own:
                for sl in range(3):
                    a, ka = slab(sl * 256)
                    for jj in range(2):
                        c = sl * 2 + jj
                        pq, i2 = proj_chunk(a, ka, jj)
                        S.add("act", lambda e, pq=pq, c=c: e.activation(out=lat[:, c, :T], in_=pq[:, :T], func=AF.Identity),
                              reads=[("ps", i2)], writes=[("lsrc", c)])
                lat_norm(k, lat, 6, T, qg, latn, rstd2, sqb, 6)
                lr = [("ldst", c) for c in range(6)]
                for h in range(8):
                    i2 = cnt % 2
                    cnt += 1
                    pq = k.ps[i2]
                    for kc in range(6):
                        mm(S, pq[:, :T], wuq[:, kc, h * 192:h * 192 + 128], latn[:, kc, :T], kc == 0, kc == 5,
                           wuq_r + [("ldst", kc)], ("ps", i2))
                    o_ = ob[i2]
                    S.add("act", lambda e, pq=pq, o_=o_: e.activation(out=o_[:, :T], in_=pq[:, :T], func=AF.Identity),
                          reads=[("ps", i2)], writes=[("ob", i2)])
                    S.add("sp", lambda e, o_=o_, h=h: e.dma_start(out=qnT[h * 128:(h + 1) * 128, t0:t0 + T], in_=o_[:, :T]),
                          reads=[("ob", i2)], writes=[("dr", "qnT", t0, h)], dma=True)
                for j in range(4):
                    i2 = cnt % 2
                    cnt += 1
                    pq = k.ps[i2]
                    for kc in range(6):
                        mm(S, pq[:, :T], wqr[:, kc, j, :], latn[:, kc, :T], kc == 0, kc == 5, wqr_r + [("ldst", kc)], ("ps", i2))
                    o_ = ob[i2]
                    rope_from_psum(k, pq[:, :T], ("ps", i2), k.ps[2 + i2], ("ps", 2 + i2), cos[:, :T], sin[:, :T], T,
                                   o_[:, :T], [("ob", i2)], qraw[i2], r1[i2], r2[i2], (("qraw", i2), ("r1", i2), ("r2", i2)))
                    S.add("sp", lambda e, o_=o_, j=j: e.dma_start(out=qrT[j * 128:(j + 1) * 128, t0:t0 + T], in_=o_[:, :T]),
                          reads=[("ob", i2)], writes=[("dr", "qrT", t0, j)], dma=True)
            for sl in range(2):
                a, ka = slab(768 + sl * 256)
                for jj in range(2):
                    c = sl * 2 + jj
                    pq, i2 = proj_chunk(a, ka, jj)
                    S.add("act", lambda e, pq=pq, c=c: e.activation(out=lat[:, c, :T], in_=pq[:, :T], func=AF.Identity),
                          reads=[("ps", i2)], writes=[("lsrc", c)])
            lat_norm(k, lat, 4, T, kg, latn, rstd2, sqb, 6)
            for c in range(4):
                S.add("sp", lambda e, c=c: e.dma_start(out=kvd(c)[:, oc0:oc0 + T], in_=latn[:, c, :T]),
                      reads=[("ldst", c)], writes=[("dr", "ckvn", t0, c)], dma=True)
            pq, i2 = proj_chunk(krw, [("krw", 0), ("krw", 1)], 0)
            o_ = ob[i2]
            rope_from_psum(k, pq[:, :T], ("ps", i2), k.ps[2 + i2], ("ps", 2 + i2), cos[:, :T], sin[:, :T], T,
                           o_[:, :T], [("ob", i2)], qraw[i2], r1[i2], r2[i2], (("qraw", i2), ("r1", i2), ("r2", i2)))
            S.add("sp", lambda e, o_=o_: e.dma_start(out=kvd(4)[:, oc0:oc0 + T], in_=o_[:, :T]),
                  reads=[("ob", i2)], writes=[("dr", "krT", t0)], dma=True)
            if own:
                for sl in range(4):
                    a, ka = slab(1344 + sl * 256)
                    for jj in range(2):
                        c = sl * 2 + jj
                        pq, i2 = proj_chunk(a, ka, jj)
                        o_ = ob[i2]
                        S.add("act", lambda e, pq=pq, o_=o_: e.activation(out=o_[:, :T], in_=pq[:, :T], func=AF.Identity),
                              reads=[("ps", i2)], writes=[("ob", i2)])
                        S.add("sp", lambda e, o_=o_, c=c: e.dma_start(out=dbT[c * 128:(c + 1) * 128, t0:t0 + T], in_=o_[:, :T]),
                              reads=[("ob", i2)], writes=[("dr", "dbT", t0, c)], dma=True)
                for sl in range(4):
                    a, ka = slab(2368 + sl * 256)
                    a2, ka2 = slab(3392 + sl * 256)
                    for jj in range(2):
                        c = sl * 2 + jj
                        pq, i2 = proj_chunk(a, ka, jj)
                        dc_ = dcs[i2]
                        S.add("act", lambda e, pq=pq, dc_=dc_: e.activation(out=dc_[:, :T], in_=pq[:, :T], func=AF.Identity),
                              reads=[("ps", i2)], writes=[("dcs", i2)])
                        pq2, j2 = proj_chunk(a2, ka2, jj)
                        z_ = zo[i2]
                        S.add("dve", lambda e, pq2=pq2, dc_=dc_, z_=z_: e.tensor_tensor(
                            out=z_[:, :T], in0=pq2[:, :T], in1=dc_[:, :T], op=ALU.mult),
                            reads=[("ps", j2), ("dcs", i2)], writes=[("zo", i2)])
                        S.add("sp", lambda e, z_=z_, c=c: e.dma_start(out=zT[c * 128:(c + 1) * 128, t0:t0 + T], in_=z_[:, :T]),
                              reads=[("zo", i2)], writes=[("dr", "zT", t0, c)], dma=True)
                        for (tt, which, col) in ((0, 0, 0), (NOWN - 512, 1, 511)):
                            if t0 == tt:
                                S.add("sp", lambda e, z_=z_, c=c, which=which, col=col: e.dma_start(
                                    out=zb[which:which + 1, :].rearrange("a (p c) -> p (a c)", c=8)[:, c:c + 1],
                                    in_=z_[:, col:col + 1], allow_slow_non_contiguous=True),
                                    reads=[("zo", i2)], writes=[("dr", "zb", which, c)], dma=True)
        S.barrier()


def phase_exchange(k):
    nc, S = k.nc, k.S
    xin = [k.dram("xch_in%d" % i, [128, NOWN], BF16) for i in range(5)]
    xall = [k.dram("xch_all%d" % i, [512, NOWN], BF16) for i in range(5)]
    zb = k.dram("zb_in", [2, 1024])
    zall = k.dram("zb_all", [8, 1024])
    groups = [[0, 1, 2, 3], [4, 5, 6, 7]]
    for i in range(5):
        S.add_cc(lambda e, i=i: e.collective_compute("AllGather", ALU.bypass, replica_groups=groups, ins=[xin[i].opt()], outs=[xall[i].opt()]))
    S.add_cc(lambda e: e.collective_compute("AllGather", ALU.bypass, replica_groups=groups, ins=[zb.opt()], outs=[zall.opt()]))
    S.barrier()
    with ExitStack() as st:
        al = k.alloc(st)
        zsel = al("zsel", [128, 8, 8], F32)
        S.add("sp", lambda e: e.dma_start(out=zsel[:], in_=zall.rearrange("j (p c) -> p j c", c=8)), reads=["zall"], writes=["zsel"], dma=True)
        for w in range(2):
            for j in range(8):
                if j == 0:
                    S.add("dve", lambda e, w=w, j=j: e.tensor_scalar(out=k.zpn[:, w, :], in0=zsel[:, j, :], scalar1=k.selT[:, w, j:j + 1],
                                                                     scalar2=None, op0=ALU.mult), reads=["zsel", "selT"], writes=["zpn"])
                else:
                    S.add("dve", lambda e, w=w, j=j: e.scalar_tensor_tensor(out=k.zpn[:, w, :], in0=zsel[:, j, :], scalar=k.selT[:, w, j:j + 1],
                                                                            op0=ALU.mult, in1=k.zpn[:, w, :], op1=ALU.add),
                          reads=["zsel", "selT", "zpn"], writes=["zpn"])
        S.barrier()


def phase_mla(k):
    nc, S = k.nc, k.S
    xall = [k.dram("xch_all%d" % i, [512, NOWN], BF16) for i in range(5)]
    ctxkv = k.dram("ctxkv", [640, NCTX], BF16)
    w_ukv = k.dram("c_w_ukv", [1, 512, 2048])[0]
    qnT = k.dram("qnT", [1024, NOWN], BF16)
    qrT = k.dram("qrT", [512, NOWN], BF16)
    oT = k.dram("oT2", [1024, NOWN], BF16)
    scale = 192.0 ** -0.5
    with ExitStack() as st:
        al = k.alloc(st)
        ckv = al("ckv", [128, 4, NK], BF16)
        krd = al("krd", [128, NK], BF16)
        wkv = al("wkv", [128, 4, 2048], BF16)
        knT = al("knT", [128, NK], BF16)
        vh = al("vh", [128, NKC, 128], BF16)
        qn = [al("qn%d" % i, [128, 512], BF16) for i in range(2)]
        qr = [[al("qr%d_%d" % (e_, i), [128, 512], BF16) for i in range(2)] for e_ in range(2)]
        for e_ in range(2):
            for i in range(2):
                S.add("pool", lambda e, e_=e_, i=i: e.memset(qr[e_][i][:], 0.0), writes=[("qr", e_, i)])
        pT = [al("pT%d" % i, [128, 512], BF16) for i in range(6)]
        rr = al("rr", [128, 512], F32)
        dacc = [al("dacc%d" % i, [128, 512], F32) for i in range(2)]
        oo = [al("oo%d" % i, [128, 512], BF16) for i in range(2)]
        for c in range(4):
            for r in range(4):
                S.add("sp", lambda e, c=c, r=r: e.dma_start(out=ckv[:, c, r * NOWN:(r + 1) * NOWN],
                                                            in_=xall[c][r * 128:(r + 1) * 128, :]),
                      writes=[("ckv", c, r)], dma=True)
            S.add("sp", lambda e, c=c: e.dma_start(out=ckv[:, c, SEQ:NK], in_=ctxkv[c * 128:(c + 1) * 128, :]), writes=[("ckv", c, 4)], dma=True)
        for r in range(4):
            S.add("sp", lambda e, r=r: e.dma_start(out=krd[:, r * NOWN:(r + 1) * NOWN], in_=xall[4][r * 128:(r + 1) * 128, :]),
                  writes=[("krd", r)], dma=True)
        S.add("sp", lambda e: e.dma_start(out=krd[:, SEQ:NK], in_=ctxkv[512:640, :]), writes=[("krd", 4)], dma=True)
        for i in range(4):
            S.add("pool", lambda e, i=i: e.dma_start(
                out=wkv[:, :, i * 512:(i + 1) * 512], in_=w_ukv[:, i * 512:(i + 1) * 512].rearrange("(kc p) n -> p kc n", p=128)),
                writes=[("wkv", i)], dma=True)
        ckr = [("ckv", c, r) for c in range(4) for r in range(5)]
        krr = [("krd", r) for r in range(5)]
        npT = 0
        nq = 0
        for h in range(8):
            wr = [("wkv", h // 2)]
            for kg in range(17):
                n = 512 if kg < 16 else 256
                pk_ = k.ps[6 + kg % 2]
                for kc in range(4):
                    mm(S, pk_[:, :n], wkv[:, kc, h * 256:h * 256 + 128], ckv[:, kc, kg * 512:kg * 512 + n], kc == 0, kc == 3,
                       wr + ckr, ("ps", 6 + kg % 2))
                S.add("act", lambda e, pk_=pk_, kg=kg, n=n: e.activation(out=knT[:, kg * 512:kg * 512 + n], in_=pk_[:, :n], func=AF.Identity),
                      reads=[("ps", 6 + kg % 2)], writes=["knT"])
            for g4 in range(17):
                nb = 4 if g4 < 16 else 2
                pv_ = k.ps[6 + g4 % 2]
                for bb in range(nb):
                    kb = g4 * 4 + bb
                    for kc in range(4):
                        mm(S, pv_[:, bb * 128:(bb + 1) * 128], ckv[:, kc, kb * 128:(kb + 1) * 128],
                           wkv[:, kc, h * 256 + 128:h * 256 + 256], kc == 0, kc == 3, wr + ckr, ("ps", 6 + g4 % 2))
                S.add("dve", lambda e, pv_=pv_, g4=g4, nb=nb: e.tensor_copy(
                    out=vh[:, g4 * 4:g4 * 4 + nb, :], in_=pv_[:, :nb * 128].rearrange("p (b d) -> p b d", d=128)),
                    reads=[("ps", 6 + g4 % 2)], writes=["vh"])
            e2 = h % 2
            for qgi in range(4):
                i2 = nq % 2
                nq += 1
                q0 = qgi * 512
                S.add("sp", lambda e, i2=i2, q0=q0, h=h: e.dma_start(out=qn[i2][:], in_=qnT[h * 128:(h + 1) * 128, q0:q0 + 512]),
                      writes=[("qn", i2)], dma=True)
                S.add("sp", lambda e, i2=i2, q0=q0, h=h, e2=e2: e.dma_start(
                    out=qr[e2][i2][64 * e2:64 * e2 + 64, :], in_=qrT[(h // 2) * 128 + 64 * e2:(h // 2) * 128 + 64 * e2 + 64, q0:q0 + 512]),
                    writes=[("qr", e2, i2)], dma=True)
                po, pd = k.ps[4], k.ps[5]
                LOOK = 3
                pend = {}
                for kc in range(NKC + LOOK):
                    if kc < NKC:
                        bi = kc % 4
                        ps_ = k.ps[bi]
                        mm(S, ps_[:, :], knT[:, kc * 128:(kc + 1) * 128], qn[i2][:], True, False, ["knT", ("qn", i2)], ("ps", bi))
                        mm(S, ps_[:, :], krd[:, kc * 128:(kc + 1) * 128], qr[e2][i2][:], False, True,
                           krr + [("qr", e2, i2)], ("ps", bi))
                        p = pT[npT % 6]
                        pk = ("pT", npT % 6)
                        npT += 1
                        S.add("act", lambda e, p=p, ps_=ps_: e.activation(out=p[:], in_=ps_[:, :], func=AF.Exp, scale=scale),
                              reads=[("ps", bi)], writes=[pk])
                        pend[kc] = (p, pk)
                    j = kc - LOOK
                    if j >= 0:
                        p, pk = pend.pop(j)
                        mm(S, po[:, :], vh[:, j, :], p[:], j == 0, j == NKC - 1, ["vh", pk], ("ps", 4))
                        if j % 2 == 1:
                            mm(S, pd[:, :], k.ones_b[:], p[:], j == 1, False, ["ones_b", pk], ("ps", 5))
                        elif j == 0:
                            S.add("dve", lambda e, p=p: e.tensor_copy(out=dacc[0][:], in_=p[:]), reads=[pk], writes=[("dacc", 0)])
                        else:
                            S.add("dve", lambda e, p=p: e.tensor_tensor(out=dacc[0][:], in0=dacc[0][:], in1=p[:], op=ALU.add),
                                  reads=[pk, ("dacc", 0)], writes=[("dacc", 0)])
                mm(S, pd[:, :], k.ones_f[:], dacc[0][:], False, True, ["ones_f", ("dacc", 0)], ("ps", 5))
                S.add("dve", lambda e, pd=pd: e.reciprocal(out=rr[:], in_=pd[:, :]), reads=[("ps", 5)], writes=["rr"])
                o_ = oo[i2]
                S.add("dve", lambda e, po=po, o_=o_: e.tensor_tensor(out=o_[:], in0=po[:, :], in1=rr[:], op=ALU.mult),
                      reads=[("ps", 4), "rr"], writes=[("oo", i2)])
                S.add("sp", lambda e, o_=o_, h=h, q0=q0: e.dma_start(out=oT[h * 128:(h + 1) * 128, q0:q0 + 512], in_=o_[:]),
                      reads=[("oo", i2)], writes=[("dr", "oT2", h, q0)], dma=True)
        S.barrier()


def phase_cd_out(k, x_in, x_out):
    nc, S = k.nc, k.S
    l = 1
    w = k.dram("cd_w_out", [1, D, D])[0]
    oT = k.dram("oT2", [1024, NOWN], BF16)
    dbT = k.dram("dbT", [1024, NOWN], BF16)
    zT = k.dram("zT", [1024, NOWN])
    cwD = k.dram("convT", [128, 24])
    with ExitStack() as st:
        al = k.alloc(st)
        xres = al("xres", [128, KC, 512], F32)
        hT = al("hT", [128, KC, 512], BF16)
        ze = al("ze", [128, 8, 514], F32)
        dbs = al("dbs", [128, 8, 512], BF16)
        cw = al("cw", [128, 24], F32)
        ct = [al("ct%d" % i, [128, 512], F32) for i in range(2)]
        wa = [al("wa%d" % i, [128, KC, 256], BF16) for i in range(2)]
        S.add("sp", lambda e: e.dma_start(out=cw[:], in_=cwD), writes=["cw"], dma=True)
        na = 0
        for (t0, T, segs) in OWN_TILES:
            load_x_tile(k, xres, x_in, t0, T)
            for c in range(8):
                S.add("sp", lambda e, c=c, t0=t0: e.dma_start(out=hT[:, c, :], in_=oT[c * 128:(c + 1) * 128, t0:t0 + 512]),
                      writes=[("hT", c)], dma=True)
            S.add("sp", lambda e, t0=t0: e.dma_start(out=ze[:, :, 1:513], in_=zT[:, t0:t0 + 512].rearrange("(c p) t -> p c t", p=128)),
                  writes=[("ze", 1)], dma=True)
            if t0 > 0:
                S.add("sp", lambda e, t0=t0: e.dma_start(out=ze[:, :, 0:1], in_=zT[:, t0 - 1:t0].rearrange("(c p) t -> p c t", p=128),
                                                         allow_slow_non_contiguous=True), writes=[("ze", 0)], dma=True)
            else:
                S.add("dve", lambda e: e.tensor_copy(out=ze[:, :, 0], in_=k.zpn[:, 0, :]), reads=["zpn"], writes=[("ze", 0)])
            if t0 + 512 < NOWN:
                S.add("sp", lambda e, t0=t0: e.dma_start(out=ze[:, :, 513:514], in_=zT[:, t0 + 512:t0 + 513].rearrange("(c p) t -> p c t", p=128),
                                                         allow_slow_non_contiguous=True), writes=[("ze", 2)], dma=True)
            else:
                S.add("dve", lambda e: e.tensor_copy(out=ze[:, :, 513], in_=k.zpn[:, 1, :]), reads=["zpn"], writes=[("ze", 2)])
            S.add("sp", lambda e, t0=t0: e.dma_start(out=dbs[:], in_=dbT[:, t0:t0 + 512].rearrange("(c p) t -> p c t", p=128)),
                  writes=["dbs"], dma=True)
            zr = [("ze", i) for i in range(3)]
            for c in range(8):
                t_ = ct[c % 2]
                tk = ("ct", c % 2)
                S.add("dve", lambda e, c=c, t_=t_: e.tensor_scalar(out=t_[:], in0=ze[:, c, 0:512], scalar1=cw[:, c:c + 1], scalar2=None,
                                                                   op0=ALU.mult), reads=zr + ["cw"], writes=[tk])
                S.add("dve", lambda e, c=c, t_=t_: e.scalar_tensor_tensor(out=t_[:], in0=ze[:, c, 1:513], scalar=cw[:, 8 + c:9 + c],
                                                                          op0=ALU.mult, in1=t_[:], op1=ALU.add), reads=zr + ["cw", tk], writes=[tk])
                S.add("dve", lambda e, c=c, t_=t_: e.scalar_tensor_tensor(out=t_[:], in0=ze[:, c, 2:514], scalar=cw[:, 16 + c:17 + c],
                                                                          op0=ALU.mult, in1=t_[:], op1=ALU.add), reads=zr + ["cw", tk], writes=[tk])
                S.add("dve", lambda e, c=c, t_=t_: e.tensor_tensor(out=hT[:, 8 + c, :], in0=t_[:], in1=dbs[:, c, :], op=ALU.mult),
                      reads=[tk, "dbs"], writes=[("hT", 8 + c)])
            for sl in range(KC // 2):
                a = wa[na % 2]
                ka = "wa%d" % (na % 2)
                na += 1
                S.add("pool", lambda e, a=a, sl=sl: e.dma_start(
                    out=a[:], in_=w[:, sl * 256:(sl + 1) * 256].rearrange("(kc p) n -> p kc n", p=128)), writes=[ka], dma=True)
                for jj in range(2):
                    c = sl * 2 + jj
                    py = k.ps[c % 2]
                    for kc in range(KC):
                        mm(S, py[:, :T], a[:, kc, jj * 128:(jj + 1) * 128], hT[:, kc, :T], kc == 0, kc == KC - 1,
                           [ka, ("hT", kc)], ("ps", c % 2))
                    residual_store(k, py, ("ps", c % 2), xres, c, T, segs, l, 1, x_out, t0)
        S.barrier()


OWN_TILES = [(i * 512, 512, [(0, 512, 0)]) for i in range(4)]
MISC_TILE = (2048, 512, [(0, 256, 0), (256, 256, 1)])
CTX_TILE = (CT0, 256, [(0, 256, 1)])
OWN3_CTX_TILE = ([(1536, 512), (CT0, 256)], [(0, 512, 0), (512, 256, 1)])


def fm(v):
    v = np.asarray(v)
    lead = v.shape[:-1]
    n = v.shape[-1] // 128
    r = v.reshape(lead + (n, 128))
    r = np.moveaxis(r, -1, 0)
    return np.ascontiguousarray(r.reshape(128, -1))


def prep_core(inp, core):
    b, q = core // 4, core % 4
    p0 = q * NOWN
    x = inp["x"]
    xT = np.zeros((D, NT), np.float32)
    xT[:, 0:NOWN] = x[b, p0:p0 + NOWN].T
    if q > 0:
        xT[:, HP0:HP0 + 128] = x[b, p0 - 128:p0].T
    if q < 3:
        xT[:, HN0:HN0 + 128] = x[b, p0 + NOWN:p0 + NOWN + 128].T
    xT[:, CT0:CT0 + NCTX] = inp["ctx"][b].T
    m = {"xT": xT}
    m["cT"] = np.ascontiguousarray(np.stack([inp["c"][b], inp["c_ctx"]], axis=1))
    pos = np.zeros(NT, np.int64)
    pos[0:NOWN] = p0 + np.arange(NOWN)
    pos[HP0:HP0 + 128] = p0 - 128 + np.arange(128)
    pos[HN0:HN0 + 128] = p0 + NOWN + np.arange(128)
    pos = np.clip(pos, 0, SEQ - 1)
    row = (pos // 64).astype(np.float32)
    col = (pos % 64).astype(np.float32)
    inv = (10000.0 ** (-np.arange(0, 32, 2, dtype=np.float32) / 32)).astype(np.float32)
    ang = np.zeros((64, NT), np.float32)
    ang[0:16] = (row[None, :] * inv[:, None]).astype(np.float32)
    ang[16:32] = ang[0:16]
    ang[32:48] = (col[None, :] * inv[:, None]).astype(np.float32)
    ang[48:64] = ang[32:48]
    cosT = np.cos(ang).astype(np.float32)
    sinT = np.sin(ang).astype(np.float32)
    cosT[:, CT0:] = 1.0
    sinT[:, CT0:] = 0.0
    m["cosT"] = np.ascontiguousarray(np.concatenate([cosT, cosT], 0))
    m["sinT"] = np.ascontiguousarray(np.concatenate([sinT, sinT], 0))
    j = np.arange(128)[:, None]
    i = np.arange(128)[None, :]
    mk = np.zeros((128, 4, 128), np.float32)
    mk[:, 0, :] = (j >= i)
    mk[:, 1, :] = (j <= i)
    mk[:, 2, :] = (j >= i) * (1.0 if q > 0 else 0.0)
    mk[:, 3, :] = (j <= i) * (1.0 if q < 3 else 0.0)
    m["masks"] = mk
    sel = np.zeros((128, 2, 8), np.float32)
    if q > 0:
        sel[:, 0, 2 * (q - 1) + 1] = 1.0
    if q < 3:
        sel[:, 1, 2 * (q + 1)] = 1.0
    m["selT"] = sel
    return m


def prep_shared(inp):
    m = {}
    m["mod_w"] = inp["mod_w"]
    m["mod_bT"] = np.stack([fm(inp["mod_b"][l]) for l in range(2)])
    m["norm_gT"] = np.stack([fm(inp["norm_g"][l]) for l in range(2)])
    for n in ("ffn_w1", "ffn_w3", "ffn_w2", "ab_w_in", "ab_w_out", "cd_w_in", "cd_w_out", "c_w_uq", "c_w_ukv"):
        m[n] = inp[n]
    pm = np.zeros((128, 128), np.float32)
    for mm_ in range(128):
        if mm_ % 32 < 16:
            pm[mm_ + 16, mm_] = -1.0
        else:
            pm[mm_ - 16, mm_] = 1.0
    m["permM"] = pm
    sk = inp["a_sink"][0]
    m["sinkT"] = np.ascontiguousarray(np.stack([np.repeat(sk[2 * c:2 * c + 2], 64) for c in range(8)], axis=1))
    m["b_wsT"] = np.ascontiguousarray(np.transpose(inp["b_ws"][0], (2, 0, 1)))
    m["b_biasbc"] = np.ascontiguousarray(np.broadcast_to(inp["b_bias"][0][None], (128, 8, 128)))
    m["c_qnT"] = fm(inp["c_q_norm"][0])
    m["c_kvnT"] = fm(inp["c_kv_norm"][0])
    m["convT"] = fm(inp["d_conv_w"][0])
    m["fnT"] = fm(inp["final_norm"])
    return m


def load_final_consts(k):
    fnD = k.dram("fnT", [128, 16])
    k.fnT = k.sb("fnT_sb", [128, 16], F32)
    k.S.add("sp", lambda e: e.dma_start(out=k.fnT[:], in_=fnD), writes=["fnT"], dma=True)
    selD = k.dram("selT", [128, 2, 8])
    k.selT = k.sb("selT_sb", [128, 2, 8], F32)
    k.zpn = k.sb("zpn_sb", [128, 2, 8], F32)
    k.S.add("sp", lambda e: e.dma_start(out=k.selT[:], in_=selD), writes=["selT"], dma=True)


INS = ["xT", "cT", "mod_w", "mod_bT", "norm_gT", "ffn_w1", "ffn_w3", "ffn_w2", "ab_w_in", "ab_w_out", "cosT", "sinT",
       "permM", "masks", "sinkT", "b_wsT", "b_biasbc", "cd_w_in", "c_w_uq", "c_qnT", "c_kvnT",
       "cd_w_out", "c_w_ukv", "convT", "fnT", "selT"]
OUTS = ["outT"]


def build():
    k = K(INS, OUTS)
    xT = k.dram("xT", [D, NT])
    x1, x2, x3 = k.dram("x1", [D, NT]), k.dram("x2", [D, NT]), k.dram("x3", [D, NT])
    x4, x5 = k.dram("x4", [D, NT]), k.dram("x5", [D, NOWN])
    outT = k.dram("outT", [D, NOWN])
    phase_setup(k)
    load_rope_consts(k)
    load_final_consts(k)
    phase_mod(k, (0, 1))
    phase_ffn(k, 0, 0, xT, x1, OWN_TILES + [MISC_TILE])
    phase_ab_in(k, x1, OWN_TILES + [MISC_TILE])
    phase_ab_mix(k)
    phase_outproj(k, 0, "ab_w_out", k.dram("oT", [2048, NT], BF16), x1, x2, OWN_TILES + [CTX_TILE])
    phase_ffn(k, 0, 1, x2, x3, OWN_TILES[:3] + [OWN3_CTX_TILE])
    phase_ffn(k, 1, 0, x3, x4, OWN_TILES[:3] + [OWN3_CTX_TILE])
    phase_cd_in(k, x4, OWN_TILES + [CTX_TILE])
    phase_exchange(k)
    phase_mla(k)
    phase_cd_out(k, x4, x5)
    phase_ffn(k, 1, 1, x5, None, OWN_TILES, final_out=outT)
    k.S.emit(k.st)
    k.st.close()
    return k


def kernel(**inp):
    inp = {kk: np.asarray(v) for kk, v in inp.items()}
    n = 8
    sh = prep_shared(inp)
    for nm in ("ffn_w1", "ffn_w3", "ffn_w2"):
        a = inp[nm]
        sh[nm] = a.reshape((4,) + a.shape[2:])
    k = build()
    maps = []
    for c in range(n):
        m = dict(sh)
        m.update(prep_core(inp, c))
        maps.append({kk: m[kk] for kk in INS})
    res = run_bass_kernel_spmd(k.nc, maps, core_ids=list(range(n))).results
    out = np.zeros((2, SEQ, D), np.float32)
    for c in range(n):
        b, q = c // 4, c % 4
        out[b, q * NOWN:(q + 1) * NOWN, :] = np.asarray(res[c]["outT"]).T
    return out
```

```python
import math
import types
from contextlib import ExitStack
import numpy as np
import concourse.bass as bass
import concourse.mybir as mybir
from concourse.bass_utils import run_bass_kernel_spmd

F32 = mybir.dt.float32
BF16 = mybir.dt.bfloat16
AF = mybir.ActivationFunctionType
ALU = mybir.AluOpType

D = 2048
FFN = 5632
NFC = FFN // 128
KC = D // 128
SEQ = 8192
NOWN = 2048
NCTX = 256
NT = 2560
HP0, HN0, CT0 = 2048, 2176, 2304
EPS = 1e-6
ENGS = ("pe", "act", "dve", "pool", "sp")
N_SW = 4
DBG_NOPERM = False
ROPE_ADD_ENG = "pool"
DBG_SECT = None
SW_FRESH = False


def _freeze(fn):
    if fn is None or fn.__closure__ is None:
        return fn
    cells = []
    for c in fn.__closure__:
        try:
            cells.append(types.CellType(c.cell_contents))
        except ValueError:
            cells.append(c)
    return types.FunctionType(fn.__code__, fn.__globals__, fn.__name__, fn.__defaults__, tuple(cells))


class Op:
    __slots__ = ("eng", "fn", "deps", "idx", "ms", "dma", "sem", "semval", "msval", "tag")


class Sched:
    def __init__(self, nc, n_dma_sems=24):
        if SW_FRESH:
            n_dma_sems = 8
        self.nc = nc
        self.ops = {e: [] for e in ENGS}
        self.lastw = {}
        self.readers = {}
        self.n_dma_sems = n_dma_sems
        self.dma_rr = 0
        self.dma_count = [0] * n_dma_sems
        self.dma_last = [None] * n_dma_sems
        self.sw_keys = {}
        self.sw_rr = 0
        self.sw_last = {}

    def add(self, eng, fn, reads=(), writes=(), dma=False, tag=None):
        op = Op()
        op.eng, op.fn, op.dma, op.ms, op.tag = eng, _freeze(fn), dma, False, tag
        op.sem = op.semval = op.msval = None
        sw = dma and eng == "pool" and SW_FRESH
        if eng in ("act", "dve", "pool") and not dma:
            extra = [("psr", r[1]) for r in reads if isinstance(r, tuple) and len(r) == 2 and r[0] == "ps"]
            if extra:
                writes = list(writes) + extra
        deps = set()
        for r in reads:
            w = self.lastw.get(r)
            if w is not None:
                deps.add(w)
        for k in writes:
            w = self.lastw.get(k)
            if w is not None:
                deps.add(w)
            for rd in self.readers.get(k, ()):
                deps.add(rd)
        if sw:
            key = len(self.sw_keys)
            self.sw_keys[key] = key
            op.sem, op.semval = ("sw", key), 16
            op.tag = False
            self.sw_last[key] = op
        elif dma:
            if eng == "pool":
                s = self.n_dma_sems - N_SW + self.sw_rr
                self.sw_rr = (self.sw_rr + 1) % N_SW
            else:
                s = self.dma_rr
                self.dma_rr = (self.dma_rr + 1) % (self.n_dma_sems - N_SW)
            if self.dma_last[s] is not None:
                deps.add(self.dma_last[s])
            self.dma_count[s] += 1
            op.sem, op.semval = s, 16 * self.dma_count[s]
            self.dma_last[s] = op
        if eng == "pe":
            deps = {d for d in deps if d.dma or d.eng != "pe"}
        op.deps = deps
        for d in deps:
            if not d.dma:
                d.ms = True
        for r in reads:
            self.readers.setdefault(r, []).append(op)
        for k in writes:
            self.lastw[k] = op
            self.readers[k] = []
        op.idx = len(self.ops[eng])
        self.ops[eng].append(op)
        return op

    def add_cc(self, fn, reads=(), writes=()):
        op = self.add("pool", fn, reads=reads, writes=[("cc_issue",)])
        op.tag = "cc"
        self.n_cc = getattr(self, "n_cc", 0) + 1
        op.semval = self.n_cc
        return op

    def barrier(self):
        lasts = [self.ops[e][-1] for e in ENGS if self.ops[e]]
        lasts = [x for x in lasts if not x.dma and x.fn is not None]
        dmas = [d for d in self.dma_last if d is not None] + list(self.sw_last.values())
        for e in ENGS:
            op = Op()
            op.eng, op.fn, op.dma, op.ms, op.tag = e, None, False, False, "barrier"
            op.sem = op.semval = op.msval = None
            op.deps = set(x for x in lasts if x.eng != e) | set(dmas)
            for d in op.deps:
                if not d.dma:
                    d.ms = True
            op.idx = len(self.ops[e])
            self.ops[e].append(op)
        self.lastw.clear()
        self.readers.clear()

    def emit(self, stack):
        nc = self.nc
        esem = {e: stack.enter_context(nc.semaphore("s_" + e)) for e in ENGS if e != "sp"}
        dsem = [stack.enter_context(nc.semaphore("d_%d" % i)) for i in range(self.n_dma_sems)]
        wsem = [stack.enter_context(nc.semaphore("w_%d" % i)) for i in range(len(self.sw_keys))]
        ccsem = stack.enter_context(nc.semaphore("ccsem"))
        for e in ENGS:
            c = 0
            for op in self.ops[e]:
                if op.ms and not op.dma:
                    c += 1
                    op.msval = c
        ops = self.ops
        final_dma = [(dsem[i], 16 * self.dma_count[i]) for i in range(self.n_dma_sems) if self.dma_count[i]]

        def run(ename, eng):
            known = {}
            for op in ops[ename]:
                for d in op.deps:
                    if d.dma and isinstance(d.sem, tuple):
                        key, sem, val = ("w", d.sem[1], id(d)), wsem[d.sem[1]], 16
                    elif d.dma:
                        key, sem, val = ("d", d.sem), dsem[d.sem], d.semval
                    else:
                        key, sem, val = d.eng, esem[d.eng], d.msval
                    if known.get(key, 0) < val:
                        eng.wait_ge(sem, val)
                        known[key] = val
                if op.fn is None:
                    continue
                if op.dma and isinstance(op.sem, tuple):
                    if op.tag:
                        eng.wait_ge(wsem[op.sem[1]], 16)
                        eng.sem_clear(wsem[op.sem[1]])
                    op.fn(eng).then_inc(wsem[op.sem[1]], 16)
                    continue
                ins = op.fn(eng)
                if op.tag == "cc":
                    ins.then_inc(ccsem, 1)
                    eng.wait_ge(ccsem, op.semval)
                    if op.ms:
                        eng.memset(self.cc_dummy[:], 0.0).then_inc(esem[ename], 1)
                    continue
                if op.dma:
                    ins.then_inc(dsem[op.sem], 16)
                elif op.ms:
                    ins.then_inc(esem[ename], 1)
            if ename == "sp":
                for sem, val in final_dma:
                    eng.wait_ge(sem, val)

        with nc.Block() as block:
            @block.tensor
            def _(e):
                run("pe", e)

            @block.scalar
            def _(e):
                run("act", e)

            @block.vector
            def _(e):
                run("dve", e)

            @block.gpsimd
            def _(e):
                run("pool", e)

            @block.sync
            def _(e):
                run("sp", e)


class K:
    def __init__(self, ext_in, ext_out):
        self.nc = bass.Bass("TRN2", target_bir_lowering=False)
        self.S = Sched(self.nc)
        self.ext_in, self.ext_out = set(ext_in), set(ext_out)
        self.dr = {}
        self.st = ExitStack()
        self.uid = 0
        self.ffn_sel = {(0, 0): 0, (0, 1): 1, (1, 0): 2, (1, 1): 3}

    def dram(self, name, shape, dtype=F32):
        if name in self.dr:
            return self.dr[name]
        kind = "ExternalInput" if name in self.ext_in else ("ExternalOutput" if name in self.ext_out else "Internal")
        t = self.nc.dram_tensor(name, list(shape), dtype, kind=kind).ap()
        self.dr[name] = t
        return t

    def sb(self, name, shape, dtype):
        return self.st.enter_context(self.nc.sbuf_tensor(name, list(shape), dtype))

    def alloc(self, st):
        self.uid += 1
        u = self.uid
        return lambda n, sh, dt: st.enter_context(self.nc.sbuf_tensor("%s_u%d" % (n, u), list(sh), dt))

    def psum(self, name):
        return self.st.enter_context(self.nc.psum_tensor(name, [128, 512], F32))


def mm(S, ps_ap, lhsT, rhs, start, stop, reads, pskey):
    S.add("pe", lambda e: e.matmul(ps_ap, lhsT, rhs, start=start, stop=stop), reads=reads, writes=[pskey])


def phase_setup(k):
    nc, S = k.nc, k.S
    k.ps = [k.psum("ps%d" % i) for i in range(8)]
    k.ones_f = k.sb("ones_f", [128, 128], F32)
    k.ones_b = k.sb("ones_b", [128, 128], BF16)
    S.add("pool", lambda e: e.memset(k.ones_f[:], 1.0), writes=["ones_f"])
    S.add("pool", lambda e: e.memset(k.ones_b[:], 1.0), writes=["ones_b"])
    k.eps_t = k.sb("eps_t", [128, 1], F32)
    S.cc_dummy = k.sb("cc_dummy", [128, 8], F32)
    S.add("pool", lambda e: e.memset(k.eps_t[:], EPS), writes=["eps_t"])


def phase_mod(k, layers=(0, 1)):
    nc, S = k.nc, k.S
    cT = k.dram("cT", [D, 2])
    nl = len(layers)
    mod_w = k.dram("mod_w", [nl, D, 9 * D])
    mod_bT = k.dram("mod_bT", [nl, 128, 144])
    norm_gT = k.dram("norm_gT", [nl, 128, 48])
    k.modT = [k.sb("modT%d" % l, [128, 144, 2], F32) for l in range(2)]
    k.modA = [k.sb("modA%d" % l, [128, 48, 2], F32) for l in range(2)]
    k.modG = [k.sb("modG%d" % l, [128, 48, 2], F32) for l in range(2)]
    with ExitStack() as st:
        cin = st.enter_context(nc.sbuf_tensor("cin_sb", [128, KC, 2], F32))
        scT = st.enter_context(nc.sbuf_tensor("scT", [128, KC, 2], BF16))
        mb = st.enter_context(nc.sbuf_tensor("mb", [128, 144], F32))
        ng = st.enter_context(nc.sbuf_tensor("ng", [128, 48], F32))
        wm = [st.enter_context(nc.sbuf_tensor("wm%d" % i, [128, KC, 512], BF16)) for i in range(2)]
        S.add("sp", lambda e: e.dma_start(out=cin[:], in_=cT.rearrange("(kc p) n -> p kc n", p=128)), writes=["cin"], dma=True)
        S.add("act", lambda e: e.activation(out=scT[:], in_=cin[:], func=AF.Silu), reads=["cin"], writes=["scT"])
        si = 0
        for li, l in enumerate(layers):
            S.add("sp", lambda e, li=li: e.dma_start(out=mb[:], in_=mod_bT[li]), writes=["mb"], dma=True)
            S.add("sp", lambda e, li=li: e.dma_start(out=ng[:], in_=norm_gT[li]), writes=["ng"], dma=True)
            ps = k.ps[l]
            for sl in range(36):
                w = wm[si % 2]
                wkey = "wm%d" % (si % 2)
                si += 1
                S.add("pool", lambda e, w=w, li=li, sl=sl: e.dma_start(
                    out=w[:], in_=mod_w[li, :, sl * 512:(sl + 1) * 512].rearrange("(kc p) n -> p kc n", p=128)),
                    writes=[wkey], dma=True)
                for jj in range(4):
                    j = sl * 4 + jj
                    for kc in range(KC):
                        mm(S, ps[:, 2 * j:2 * j + 2], w[:, kc, jj * 128:(jj + 1) * 128], scT[:, kc, :],
                           kc == 0, kc == KC - 1, [wkey, "scT"], ("ps", l))
            modT = k.modT[l]
            for cnd in range(2):
                S.add("dve", lambda e, modT=modT, ps=ps, cnd=cnd: e.tensor_tensor(
                    out=modT[:, :, cnd], in0=ps[:, 0:288].rearrange("p (j c) -> p j c", c=2)[:, :, cnd], in1=mb[:], op=ALU.add),
                    reads=[("ps", l), "mb"], writes=[("modT", l)])
            for s in range(3):
                for cnd in range(2):
                    S.add("dve", lambda e, l=l, s=s, cnd=cnd, modT=modT: e.scalar_tensor_tensor(
                        out=k.modA[l][:, s * 16:(s + 1) * 16, cnd], in0=modT[:, (3 * s + 1) * 16:(3 * s + 2) * 16, cnd],
                        scalar=1.0, op0=ALU.add, in1=ng[:, s * 16:(s + 1) * 16], op1=ALU.mult),
                        reads=[("modT", l), "ng"], writes=[("modA", l)])
                    S.add("dve", lambda e, l=l, s=s, cnd=cnd, modT=modT: e.tensor_scalar(
                        out=k.modG[l][:, s * 16:(s + 1) * 16, cnd], in0=modT[:, (3 * s + 2) * 16:(3 * s + 3) * 16, cnd],
                        scalar1=(1.0 if s == 1 else 0.5), scalar2=None, op0=ALU.mult),
                        reads=[("modT", l)], writes=[("modG", l)])
        S.barrier()


def load_x_tile(k, xres, x_in, t0, T):
    S = k.S
    for c in range(KC):
        S.add("sp", lambda e, c=c: e.dma_start(out=xres[:, c, :T], in_=x_in[c * 128:(c + 1) * 128, t0:t0 + T]),
              writes=[("xres", c)], dma=True)


def norm_mod(k, xres, hT, T, segs, l, s, tmp, rstd, sqb):
    S = k.S
    ps = k.ps[6]
    for c in range(KC):
        sq = sqb[c % 2]
        S.add("act", lambda e, c=c, sq=sq: e.activation(out=sq[:, :T], in_=xres[:, c, :T], func=AF.Square),
              reads=[("xres", c)], writes=[("sq", c % 2)])
        mm(S, ps[:, :T], k.ones_b[:], sq[:, :T], c == 0, c == KC - 1, [("sq", c % 2), "ones_b"], ("ps", 6))
    S.add("act", lambda e: e.activation(out=rstd[:, :T], in_=ps[:, :T], func=AF.Sqrt, scale=1.0 / D, bias=k.eps_t[:]),
          reads=[("ps", 6), "eps_t"], writes=["rstd"])
    S.add("dve", lambda e: e.reciprocal(out=rstd[:, :T], in_=rstd[:, :T]), reads=["rstd"], writes=["rstd"])
    A, SH = k.modA[l], k.modT[l]
    for c in range(KC):
        tb = tmp[c % 2]
        for (o, n, cnd) in segs:
            S.add("dve", lambda e, c=c, o=o, n=n, cnd=cnd, tb=tb: e.scalar_tensor_tensor(
                out=tb[:, o:o + n], in0=xres[:, c, o:o + n], scalar=A[:, s * 16 + c, cnd:cnd + 1], op0=ALU.mult,
                in1=rstd[:, o:o + n], op1=ALU.mult),
                reads=[("xres", c), "rstd", ("modA", l)], writes=[("tmp", c % 2)])
            S.add("act", lambda e, c=c, o=o, n=n, cnd=cnd, tb=tb: e.activation(
                out=hT[:, c, o:o + n], in_=tb[:, o:o + n], func=AF.Identity,
                bias=SH[:, 3 * s * 16 + c, cnd:cnd + 1], scale=1.0),
                reads=[("tmp", c % 2), ("modT", l)], writes=[("hT", c)])


def residual_store(k, ps_ap, pskey, xres, c, T, segs, l, s, x_out, t0):
    S = k.S
    G = k.modG[l]
    for (o, n, cnd) in segs:
        S.add("dve", lambda e, o=o, n=n, cnd=cnd: e.scalar_tensor_tensor(
            out=xres[:, c, o:o + n], in0=ps_ap[:, o:o + n], scalar=G[:, s * 16 + c, cnd:cnd + 1], op0=ALU.mult,
            in1=xres[:, c, o:o + n], op1=ALU.add),
            reads=[pskey, ("xres", c), ("modG", l)], writes=[("xres", c)])
    if x_out is not None:
        S.add("sp", lambda e: e.dma_start(out=x_out[c * 128:(c + 1) * 128, t0:t0 + T], in_=xres[:, c, :T]),
              reads=[("xres", c)], writes=[("xdram", x_out.tensor.name, t0, c)], dma=True)


def _norm_stats(k, xres, T, rstd, sqb, banks=(6, 7)):
    S = k.S
    groups = [(0, min(T, 512))] + ([(512, T - 512)] if T > 512 else [])
    for c in range(KC):
        sq = sqb[c % 2]
        S.add("act", lambda e, c=c, sq=sq: e.activation(out=sq[:, :T], in_=xres[:, c, :T], func=AF.Square),
              reads=[("xres", c)], writes=[("sq", c % 2)])
        for g, (o, n) in enumerate(groups):
            mm(S, k.ps[banks[g]][:, :n], k.ones_b[:], sq[:, o:o + n], c == 0, c == KC - 1, [("sq", c % 2), "ones_b"], ("ps", banks[g]))
    for g, (o, n) in enumerate(groups):
        S.add("act", lambda e, g=g, o=o, n=n: e.activation(out=rstd[:, o:o + n], in_=k.ps[banks[g]][:, :n], func=AF.Sqrt,
                                                          scale=1.0 / D, bias=k.eps_t[:]),
              reads=[("ps", banks[g]), "eps_t"], writes=["rstd"])
    S.add("dve", lambda e: e.reciprocal(out=rstd[:, :T], in_=rstd[:, :T]), reads=["rstd"], writes=["rstd"])


def phase_ffn(k, l, widx, x_in, x_out, tiles, final_out=None):
    nc, S = k.nc, k.S
    s = 0 if widx == 0 else 2
    tiles = [(([(t[0], t[1])], t[2]) if len(t) == 3 else t) for t in tiles]
    TM = max(sum(n for _, n in parts) for parts, _ in tiles)
    FH = 2 if TM > 512 else 1
    NF = NFC // FH
    SLW = 256
    nsel = len(k.ffn_sel)
    wi_ = k.ffn_sel[(l, widx)]
    w1 = k.dram("ffn_w1", [nsel, D, FFN])[wi_]
    w3 = k.dram("ffn_w3", [nsel, D, FFN])[wi_]
    w2 = k.dram("ffn_w2", [nsel, FFN, D])[wi_]
    UB = ((0, 1), (4, 5))
    VB = ((2, 3), (6, 7))
    YB = ((0, 1), (2, 3))
    with ExitStack() as st:
        al = k.alloc(st)
        xres = al("xres", [128, KC, TM], F32)
        hT = al("hT", [128, KC, TM], BF16)
        gT = al("gT", [128, NF, TM], BF16)
        tmp = [al("tmp%d" % i, [128, TM], F32) for i in range(2)]
        sqb = [al("sq%d" % i, [128, TM], BF16) for i in range(2)]
        rstd = al("rstd", [128, TM], F32)
        su = [al("su%d" % i, [128, TM], BF16) for i in range(2)]
        wa = [al("wa%d" % i, [128, KC, SLW], BF16) for i in range(2)]
        wb = [al("wb%d" % i, [128, KC, SLW], BF16) for i in range(2)]
        wc = [al("wc%d" % i, [128, NF, 256], BF16) for i in range(2)]
        na = nc_ = 0
        A, SH, G = k.modA[l], k.modT[l], k.modG[l]
        for (parts, segs) in tiles:
            T = sum(n for _, n in parts)
            groups = [(0, min(T, 512))] + ([(512, T - 512)] if T > 512 else [])
            offs = []
            o_ = 0
            for (c0, n) in parts:
                offs.append((o_, c0, n))
                o_ += n
            for c in range(KC):
                for (o, c0, n) in offs:
                    S.add("sp", lambda e, c=c, o=o, c0=c0, n=n: e.dma_start(out=xres[:, c, o:o + n], in_=x_in[c * 128:(c + 1) * 128, c0:c0 + n]),
                          writes=[("xres", c)], dma=True)
            _norm_stats(k, xres, T, rstd, sqb)
            for c in range(KC):
                tb = tmp[c % 2]
                for (o, n, cnd) in segs:
                    S.add("dve", lambda e, c=c, o=o, n=n, cnd=cnd, tb=tb: e.scalar_tensor_tensor(
                        out=tb[:, o:o + n], in0=xres[:, c, o:o + n], scalar=A[:, s * 16 + c, cnd:cnd + 1], op0=ALU.mult,
                        in1=rstd[:, o:o + n], op1=ALU.mult),
                        reads=[("xres", c), "rstd", ("modA", l)], writes=[("tmp", c % 2)])
                    S.add("act", lambda e, c=c, o=o, n=n, cnd=cnd, tb=tb: e.activation(
                        out=hT[:, c, o:o + n], in_=tb[:, o:o + n], func=AF.Identity,
                        bias=SH[:, 3 * s * 16 + c, cnd:cnd + 1], scale=1.0),
                        reads=[("tmp", c % 2), ("modT", l)], writes=[("hT", c)])
            for fh in range(FH):
                f0 = fh * NF
                for sl in range(NF * 128 // SLW):
                    a, b = wa[na % 2], wb[na % 2]
                    ka, kb = "wa%d" % (na % 2), "wb%d" % (na % 2)
                    na += 1
                    cc0 = f0 * 128 + sl * SLW
                    S.add("pool", lambda e, a=a, cc0=cc0: e.dma_start(
                        out=a[:], in_=w1[:, cc0:cc0 + SLW].rearrange("(kc p) n -> p kc n", p=128)), writes=[ka], dma=True)
                    S.add("pool", lambda e, b=b, cc0=cc0: e.dma_start(
                        out=b[:], in_=w3[:, cc0:cc0 + SLW].rearrange("(kc p) n -> p kc n", p=128)), writes=[kb], dma=True)
                    for jj in range(SLW // 128):
                        fc = sl * (SLW // 128) + jj
                        ub, vb = UB[fc % 2], VB[fc % 2]
                        for g, (o, n) in enumerate(groups):
                            for kc in range(KC):
                                mm(S, k.ps[ub[g]][:, :n], a[:, kc, jj * 128:(jj + 1) * 128], hT[:, kc, o:o + n], kc == 0, kc == KC - 1,
                                   [ka, ("hT", kc)], ("ps", ub[g]))
                        for g, (o, n) in enumerate(groups):
                            for kc in range(KC):
                                mm(S, k.ps[vb[g]][:, :n], b[:, kc, jj * 128:(jj + 1) * 128], hT[:, kc, o:o + n], kc == 0, kc == KC - 1,
                                   [kb, ("hT", kc)], ("ps", vb[g]))
                        sut = su[fc % 2]
                        for g, (o, n) in enumerate(groups):
                            S.add("act", lambda e, g=g, o=o, n=n, ub=ub, sut=sut: e.activation(
                                out=sut[:, o:o + n], in_=k.ps[ub[g]][:, :n], func=AF.Silu),
                                reads=[("ps", ub[g])], writes=[("su", fc % 2, g)])
                            S.add("dve", lambda e, g=g, o=o, n=n, vb=vb, sut=sut, fc=fc: e.tensor_tensor(
                                out=gT[:, fc, o:o + n], in0=k.ps[vb[g]][:, :n], in1=sut[:, o:o + n], op=ALU.mult),
                                reads=[("ps", vb[g]), ("su", fc % 2, g)], writes=[("gT", fc, g)])
                for ds in range(KC // 2):
                    w = wc[nc_ % 2]
                    kw = "wc%d" % (nc_ % 2)
                    nc_ += 1
                    S.add("pool", lambda e, w=w, ds=ds, f0=f0: e.dma_start(
                        out=w[:], in_=w2[f0 * 128:(f0 + NF) * 128, ds * 256:(ds + 1) * 256].rearrange("(fc p) n -> p fc n", p=128)),
                        writes=[kw], dma=True)
                    for jj in range(2):
                        c = ds * 2 + jj
                        yb = YB[c % 2]
                        for g, (o, n) in enumerate(groups):
                            for fc in range(NF):
                                mm(S, k.ps[yb[g]][:, :n], w[:, fc, jj * 128:(jj + 1) * 128], gT[:, fc, o:o + n], fc == 0, fc == NF - 1,
                                   [kw, ("gT", fc, g)], ("ps", yb[g]))
                        for (o, n, cnd) in segs:
                            g = 0 if o < 512 else 1
                            lo = o - groups[g][0]
                            S.add("dve", lambda e, c=c, o=o, n=n, cnd=cnd, g=g, lo=lo, yb=yb: e.scalar_tensor_tensor(
                                out=xres[:, c, o:o + n], in0=k.ps[yb[g]][:, lo:lo + n], scalar=G[:, s * 16 + c, cnd:cnd + 1], op0=ALU.mult,
                                in1=xres[:, c, o:o + n], op1=ALU.add),
                                reads=[("ps", yb[g]), ("xres", c), ("modG", l)], writes=[("xres", c)])
                        if x_out is not None and fh == FH - 1:
                            for (o, c0, n) in offs:
                                S.add("sp", lambda e, c=c, o=o, c0=c0, n=n: e.dma_start(
                                    out=x_out[c * 128:(c + 1) * 128, c0:c0 + n], in_=xres[:, c, o:o + n]),
                                    reads=[("xres", c)], writes=[("xdram", x_out.tensor.name, c0, c)], dma=True)
            if final_out is not None:
                final_norm_store(k, xres, T, parts[0][0], final_out, tmp, rstd, sqb)
        S.barrier()


def final_norm_store(k, xres, T, t0, out, tmp, rstd, sqb):
    S = k.S
    ps = k.ps[6]
    fn = k.fnT
    for c in range(KC):
        sq = sqb[c % 2]
        S.add("act", lambda e, c=c, sq=sq: e.activation(out=sq[:, :T], in_=xres[:, c, :T], func=AF.Square),
              reads=[("xres", c)], writes=[("sq", c % 2)])
        mm(S, ps[:, :T], k.ones_b[:], sq[:, :T], c == 0, c == KC - 1, [("sq", c % 2), "ones_b"], ("ps", 6))
    S.add("act", lambda e: e.activation(out=rstd[:, :T], in_=ps[:, :T], func=AF.Sqrt, scale=1.0 / D, bias=k.eps_t[:]),
          reads=[("ps", 6), "eps_t"], writes=["rstd"])
    S.add("dve", lambda e: e.reciprocal(out=rstd[:, :T], in_=rstd[:, :T]), reads=["rstd"], writes=["rstd"])
    for c in range(KC):
        S.add("dve", lambda e, c=c: e.scalar_tensor_tensor(
            out=xres[:, c, :T], in0=xres[:, c, :T], scalar=fn[:, c:c + 1], op0=ALU.mult, in1=rstd[:, :T], op1=ALU.mult),
            reads=[("xres", c), "rstd", "fnT"], writes=[("xres", c)])
        S.add("sp", lambda e, c=c: e.dma_start(out=out[c * 128:(c + 1) * 128, t0:t0 + T], in_=xres[:, c, :T]),
              reads=[("xres", c)], writes=[("odram", t0, c)], dma=True)


GC1, GC2 = 0.044715, 1.5957691216057308


def gelu_from_psum(k, ps_ap, pskey, out_ap, outkeys, n, t1, t2, t1k, t2k):
    S = k.S
    S.add("act", lambda e: e.activation(out=t1, in_=ps_ap, func=AF.Square), reads=[pskey], writes=[t1k])
    S.add("dve", lambda e: e.tensor_scalar(out=t1, in0=t1, scalar1=GC1, scalar2=1.0, op0=ALU.mult, op1=ALU.add),
          reads=[t1k], writes=[t1k])
    S.add("dve", lambda e: e.tensor_tensor(out=t1, in0=ps_ap, in1=t1, op=ALU.mult), reads=[pskey, t1k], writes=[t1k])
    S.add("act", lambda e: e.activation(out=t2, in_=t1, func=AF.Sigmoid, scale=GC2), reads=[t1k], writes=[t2k])
    S.add("dve", lambda e: e.tensor_tensor(out=out_ap, in0=ps_ap, in1=t2, op=ALU.mult), reads=[pskey, t2k], writes=outkeys)


def rope_from_psum(k, ps_ap, pskey, ps2, ps2key, cos, sin, T, out_ap, outkeys, qraw, t1, t2, keys):
    S = k.S
    qk, t1k, t2k = keys
    S.add("act", lambda e: e.activation(out=qraw[:, :T], in_=ps_ap, func=AF.Identity), reads=[pskey], writes=[qk])
    S.add("dve", lambda e: e.tensor_tensor(out=t1[:, :T], in0=ps_ap, in1=cos, op=ALU.mult), reads=[pskey, "rope"], writes=[t1k])
    mm(S, ps2[:, :T], (k.ones_b if DBG_NOPERM else k.permM)[:], qraw[:, :T], True, True, [qk, "permM"], ps2key)
    S.add("dve", lambda e: e.tensor_tensor(out=t2[:, :T], in0=ps2[:, :T], in1=sin, op=ALU.mult), reads=[ps2key, "rope"], writes=[t2k])
    S.add(ROPE_ADD_ENG, lambda e: e.tensor_tensor(out=out_ap, in0=t1[:, :T], in1=t2[:, :T], op=ALU.add), reads=[t1k, t2k], writes=outkeys)


def load_rope_consts(k):
    nc, S = k.nc, k.S
    permD = k.dram("permM", [128, 128])
    k.permM = k.sb("permM_sb", [128, 128], BF16)
    S.add("pool", lambda e: e.dma_start(out=k.permM[:], in_=permD), writes=["permM"], dma=True)


def phase_ab_in(k, x_in, tiles):
    nc, S = k.nc, k.S
    l, s = 0, 1
    w_in = k.dram("ab_w_in", [1, D, 3328])[0]
    cosD, sinD = k.dram("cosT", [128, NT]), k.dram("sinT", [128, NT])
    qT = k.dram("qT", [1024, NT], BF16)
    kT = k.dram("kT", [2, 128, NT], BF16)
    vtok = k.dram("vtok", [NT, 128], BF16)
    guT = k.dram("guT", [1024, NT], BF16)
    gvtok = k.dram("gvtok", [NT, 1024], BF16)
    with ExitStack() as st:
        al = k.alloc(st)
        xres = al("xres", [128, KC, 512], F32)
        hT = al("hT", [128, KC, 512], BF16)
        tmp = [al("tmp%d" % i, [128, 512], F32) for i in range(2)]
        sqb = [al("sq%d" % i, [128, 512], BF16) for i in range(2)]
        rstd = al("rstd", [128, 512], F32)
        cos, sin = al("cos", [128, 512], F32), al("sin", [128, 512], F32)
        wtm = al("wtm", [128, KC, 1152], BF16)
        kdw = al("kdw", [128, KC, 256], BF16)
        wa = [al("wa%d" % i, [128, KC, 256], BF16) for i in range(2)]
        qraw = [al("qraw%d" % i, [128, 512], BF16) for i in range(2)]
        r1 = [al("r1_%d" % i, [128, 512], F32) for i in range(2)]
        r2 = [al("r2_%d" % i, [128, 512], F32) for i in range(2)]
        ob = [al("ob%d" % i, [128, 512], BF16) for i in range(2)]
        obt = [al("obt%d" % i, [128, 1152], BF16) for i in range(2)]
        for i, (c0, n) in enumerate([(1152, 128), (2304, 512), (2816, 512)]):
            o = [0, 128, 640][i]
            S.add("pool", lambda e, c0=c0, n=n, o=o: e.dma_start(
                out=wtm[:, :, o:o + n], in_=w_in[:, c0:c0 + n].rearrange("(kc p) n -> p kc n", p=128)),
                writes=[("wtm", i)], dma=True)
        for i in range(4):
            c0 = 1024 + 64 * (i // 2)
            S.add("pool", lambda e, c0=c0, i=i: e.dma_start(
                out=kdw[:, :, i * 64:(i + 1) * 64], in_=w_in[:, c0:c0 + 64].rearrange("(kc p) n -> p kc n", p=128)),
                writes=[("kdw", i)], dma=True)
        na = 0
        cnt = 0
        for (t0, T, segs) in tiles:
            load_x_tile(k, xres, x_in, t0, T)
            S.add("sp", lambda e, t0=t0, T=T: e.dma_start(out=cos[:, :T], in_=cosD[:, t0:t0 + T]), writes=["rope"], dma=True)
            S.add("sp", lambda e, t0=t0, T=T: e.dma_start(out=sin[:, :T], in_=sinD[:, t0:t0 + T]), writes=["rope"], dma=True)
            norm_mod(k, xres, hT, T, segs, l, s, tmp, rstd, sqb)
            hreads = [("hT", c) for c in range(KC)]
            for sl in range(8):
                if DBG_SECT is not None and ("q" if sl < 4 else "gu") not in DBG_SECT:
                    continue
                a = wa[na % 2]
                ka = "wa%d" % (na % 2)
                na += 1
                c0 = sl * 256 if sl < 4 else 1280 + (sl - 4) * 256
                S.add("pool", lambda e, a=a, c0=c0: e.dma_start(
                    out=a[:], in_=w_in[:, c0:c0 + 256].rearrange("(kc p) n -> p kc n", p=128)), writes=[ka], dma=True)
                for jj in range(2):
                    ch = (sl % 4) * 2 + jj
                    i2 = cnt % 2
                    cnt += 1
                    pq = k.ps[i2]
                    for kc in range(KC):
                        mm(S, pq[:, :T], a[:, kc, jj * 128:(jj + 1) * 128], hT[:, kc, :T], kc == 0, kc == KC - 1,
                           [ka, ("hT", kc)], ("ps", i2))
                    o_ = ob[i2]
                    if sl < 4:
                        rope_from_psum(k, pq[:, :T], ("ps", i2), k.ps[2 + i2], ("ps", 2 + i2), cos[:, :T], sin[:, :T], T,
                                       o_[:, :T], [("ob", i2)], qraw[i2], r1[i2], r2[i2], (("qraw", i2), ("r1", i2), ("r2", i2)))
                        dst = qT[ch * 128:(ch + 1) * 128, t0:t0 + T]
                    else:
                        gelu_from_psum(k, pq[:, :T], ("ps", i2), o_[:, :T], [("ob", i2)], T, r1[i2][:, :T], r2[i2][:, :T],
                                       ("r1", i2), ("r2", i2))
                        dst = guT[ch * 128:(ch + 1) * 128, t0:t0 + T]
                    S.add("sp", lambda e, dst=dst, o_=o_, T=T: e.dma_start(out=dst, in_=o_[:, :T]),
                          reads=[("ob", i2)], writes=[("dr", dst.tensor.name, t0, ch)], dma=True)
            for kv in range(2):
                if DBG_SECT is not None and "k" not in DBG_SECT:
                    continue
                i2 = cnt % 2
                cnt += 1
                pq = k.ps[i2]
                for kc in range(KC):
                    mm(S, pq[:, :T], kdw[:, kc, kv * 128:(kv + 1) * 128], hT[:, kc, :T], kc == 0, kc == KC - 1,
                       [("kdw", 2 * kv), ("kdw", 2 * kv + 1), ("hT", kc)], ("ps", i2))
                o_ = ob[i2]
                rope_from_psum(k, pq[:, :T], ("ps", i2), k.ps[2 + i2], ("ps", 2 + i2), cos[:, :T], sin[:, :T], T,
                               o_[:, :T], [("ob", i2)], qraw[i2], r1[i2], r2[i2], (("qraw", i2), ("r1", i2), ("r2", i2)))
                dst = kT[kv, :, t0:t0 + T]
                S.add("sp", lambda e, dst=dst, o_=o_, T=T: e.dma_start(out=dst, in_=o_[:, :T]),
                      reads=[("ob", i2)], writes=[("dr", "kT", t0, kv)], dma=True)
            for tb in range(T // 128):
                if DBG_SECT is not None and "tok" not in DBG_SECT:
                    continue
                ot = obt[tb % 2]
                okey = ("obt", tb % 2)
                pv = k.ps[4 + tb % 2]
                for kc in range(KC):
                    mm(S, pv[:, :128], hT[:, kc, tb * 128:(tb + 1) * 128], wtm[:, kc, 0:128], kc == 0, kc == KC - 1,
                       [("wtm", 0), ("hT", kc)], ("ps", 4 + tb % 2))
                S.add("act", lambda e, pv=pv, ot=ot: e.activation(out=ot[:, 0:128], in_=pv[:, :128], func=AF.Identity),
                      reads=[("ps", 4 + tb % 2)], writes=[okey])
                for hf in range(2):
                    pg = k.ps[6 + hf]
                    for kc in range(KC):
                        mm(S, pg[:, :], hT[:, kc, tb * 128:(tb + 1) * 128], wtm[:, kc, 128 + hf * 512:128 + (hf + 1) * 512],
                           kc == 0, kc == KC - 1, [("wtm", 1 + hf), ("hT", kc)], ("ps", 6 + hf))
                    gelu_from_psum(k, pg[:, :], ("ps", 6 + hf), ot[:, 128 + hf * 512:128 + (hf + 1) * 512], [okey], 512,
                                   r1[hf][:, :], r2[hf][:, :], ("r1", hf), ("r2", hf))
                r0 = t0 + tb * 128
                S.add("sp", lambda e, ot=ot, r0=r0: e.dma_start(out=vtok[r0:r0 + 128, :], in_=ot[:, 0:128]),
                      reads=[okey], writes=[("dr", "vtok", r0)], dma=True)
                S.add("sp", lambda e, ot=ot, r0=r0: e.dma_start(out=gvtok[r0:r0 + 128, :], in_=ot[:, 128:1152]),
                      reads=[okey], writes=[("dr", "gvtok", r0)], dma=True)
        S.barrier()


def phase_ab_mix(k):
    nc, S = k.nc, k.S
    qT = k.dram("qT", [1024, NT], BF16)
    kT = k.dram("kT", [2, 128, NT], BF16)
    vtok = k.dram("vtok", [NT, 128], BF16)
    guT = k.dram("guT", [1024, NT], BF16)
    gvtok = k.dram("gvtok", [NT, 1024], BF16)
    oT = k.dram("oT", [2048, NT], BF16)
    masksD = k.dram("masks", [128, 4, 128])
    sinkD = k.dram("sinkT", [128, 8])
    wsD = k.dram("b_wsT", [128, 8, 128])
    bbD = k.dram("b_biasbc", [128, 8, 128])
    scale = 1.0 / 8.0
    with ExitStack() as st:
        al = k.alloc(st)
        kTs = al("kTs", [128, 2, NT], BF16)
        vts = al("vts", [128, 20, 128], BF16)
        masks = al("masks", [128, 4, 128], BF16)
        sink = al("sink", [128, 8], F32)
        esbc = al("esbc", [128, 8, 128], F32)
        wsT = al("wsT", [128, 8, 128], BF16)
        bbc = al("bbc", [128, 8, 128], F32)
        qb_ = [al("qb%d" % i, [128, 8, 128], BF16) for i in range(2)]
        gub = [al("gub%d" % i, [128, 8, 128], BF16) for i in range(2)]
        gvb = [al("gvb%d" % i, [128, 1024], BF16) for i in range(2)]
        pT = [al("pT%d" % i, [128, 2, 4, 128], BF16) for i in range(3)]
        rr = al("rr", [128, 512], F32)
        gt = al("gt", [128, 512], F32)
        ob = [al("oblk%d" % i, [128, 16, 128], BF16) for i in range(2)]
        S.add("sp", lambda e: e.dma_start(out=kTs[:], in_=kT.rearrange("v p t -> p v t")), writes=["kTs"], dma=True)
        S.add("sp", lambda e: e.dma_start(out=vts[:], in_=vtok.rearrange("(b p) d -> p b d", p=128)), writes=["vts"], dma=True)
        S.add("pool", lambda e: e.dma_start(out=masks[:], in_=masksD), writes=["masks"], dma=True)
        S.add("pool", lambda e: e.dma_start(out=wsT[:], in_=wsD), writes=["wsT"], dma=True)
        S.add("sp", lambda e: e.dma_start(out=bbc[:], in_=bbD), writes=["bbc"], dma=True)
        S.add("sp", lambda e: e.dma_start(out=sink[:], in_=sinkD), writes=["sink"], dma=True)
        S.add("act", lambda e: e.activation(out=sink[:], in_=sink[:], func=AF.Exp), reads=["sink"], writes=["sink"])
        S.add("dve", lambda e: e.tensor_copy(out=esbc[:], in_=sink[:].unsqueeze(2).broadcast_to([128, 8, 128])),
              reads=["sink"], writes=["esbc"])
        blocks = list(range(16)) + [18, 19]
        npT = 0
        for bi, b in enumerate(blocks):
            i2 = bi % 2
            c0 = b * 128
            qb, gu, gv, o_ = qb_[i2], gub[i2], gvb[i2], ob[i2]
            S.add("sp", lambda e, qb=qb, c0=c0: e.dma_start(out=qb[:], in_=qT[:, c0:c0 + 128].rearrange("(c p) t -> p c t", p=128)),
                  writes=[("qb", i2)], dma=True)
            S.add("sp", lambda e, gu=gu, c0=c0: e.dma_start(out=gu[:], in_=guT[:, c0:c0 + 128].rearrange("(c p) t -> p c t", p=128)),
                  writes=[("gub", i2)], dma=True)
            S.add("sp", lambda e, gv=gv, c0=c0: e.dma_start(out=gv[:], in_=gvtok[c0:c0 + 128, :]), writes=[("gvb", i2)], dma=True)
            if b < 16:
                kl = [((b - 1) if b > 0 else 16, 0 if b > 0 else 2), (b, None), ((b + 1) if b < 15 else 17, 1 if b < 15 else 3),
                      (18, None), (19, None)]
            else:
                kl = [(18, None), (19, None)]
            for kv in range(2):
                po, pd = k.ps[4], k.ps[5]
                pend = {}
                for ji in range(len(kl) + 1):
                    if ji < len(kl):
                        kb, mi = kl[ji]
                        pa, pb = k.ps[2 * (ji % 2)], k.ps[2 * (ji % 2) + 1]
                        ka, kb_ = ("ps", 2 * (ji % 2)), ("ps", 2 * (ji % 2) + 1)
                        mm(S, pa[:, :], kTs[0:64, kv, kb * 128:(kb + 1) * 128], qb[0:64, kv * 4:(kv + 1) * 4, :], True, True,
                           ["kTs", ("qb", i2)], ka)
                        mm(S, pb[:, :], kTs[64:128, kv, kb * 128:(kb + 1) * 128], qb[64:128, kv * 4:(kv + 1) * 4, :], True, True,
                           ["kTs", ("qb", i2)], kb_)
                        p = pT[npT % 3]
                        pk = ("pT", npT % 3)
                        npT += 1
                        S.add("act", lambda e, p=p, pa=pa: e.activation(out=p[:, 0, :, :], in_=pa[:, :].rearrange("p (c t) -> p c t", c=4),
                                                                        func=AF.Exp, scale=scale), reads=[ka], writes=[pk])
                        S.add("act", lambda e, p=p, pb=pb: e.activation(out=p[:, 1, :, :], in_=pb[:, :].rearrange("p (c t) -> p c t", c=4),
                                                                        func=AF.Exp, scale=scale), reads=[kb_], writes=[pk])
                        if mi is not None:
                            S.add("dve", lambda e, p=p, mi=mi: e.tensor_tensor(
                                out=p[:].rearrange("p e c t -> p (e c) t"), in0=p[:].rearrange("p e c t -> p (e c) t"),
                                in1=masks[:, mi:mi + 1, :].broadcast_to([128, 8, 128]), op=ALU.mult),
                                reads=[pk, "masks"], writes=[pk])
                        pend[ji] = (p, pk, kb)
                    jj_ = ji - 1
                    if jj_ >= 0:
                        p, pk, kb = pend.pop(jj_)
                        first, last = jj_ == 0, jj_ == len(kl) - 1
                        for e_ in range(2):
                            mm(S, po[64 * e_:64 * e_ + 64, :], vts[:, kb, kv * 64:(kv + 1) * 64], p[:, e_, :, :], first, last,
                               ["vts", pk], ("ps", 4))
                            mm(S, pd[64 * e_:64 * e_ + 64, :], k.ones_b[:, 0:64], p[:, e_, :, :], first, last,
                               ["ones_b", pk], ("ps", 5))
                S.add("dve", lambda e, pd=pd, kv=kv: e.tensor_tensor(
                    out=rr[:].rearrange("p (c t) -> p c t", c=4), in0=pd[:, :].rearrange("p (c t) -> p c t", c=4),
                    in1=esbc[:, kv * 4:(kv + 1) * 4, :], op=ALU.add), reads=[("ps", 5), "esbc"], writes=["rr"])
                S.add("dve", lambda e: e.reciprocal(out=rr[:], in_=rr[:]), reads=["rr"], writes=["rr"])
                S.add("dve", lambda e, po=po, kv=kv, o_=o_: e.tensor_tensor(
                    out=o_[:, kv * 4:(kv + 1) * 4, :], in0=po[:, :].rearrange("p (c t) -> p c t", c=4),
                    in1=rr[:].rearrange("p (c t) -> p c t", c=4), op=ALU.mult), reads=[("ps", 4), "rr"], writes=[("oblk", i2)])
            for hf in range(2):
                pg = k.ps[6 + hf]
                for gg in range(4):
                    g = hf * 4 + gg
                    mm(S, pg[:, gg * 128:(gg + 1) * 128], gv[:, g * 128:(g + 1) * 128], wsT[:, g, :], True, True,
                       [("gvb", i2), "wsT"], ("ps", 6 + hf))
                S.add("dve", lambda e, pg=pg, hf=hf: e.tensor_tensor(
                    out=gt[:].rearrange("p (c t) -> p c t", c=4), in0=pg[:, :].rearrange("p (c t) -> p c t", c=4),
                    in1=bbc[:, hf * 4:(hf + 1) * 4, :], op=ALU.add), reads=[("ps", 6 + hf), "bbc"], writes=["gt"])
                S.add("dve", lambda e, hf=hf, o_=o_, gu=gu: e.tensor_tensor(
                    out=o_[:, 8 + hf * 4:8 + (hf + 1) * 4, :], in0=gt[:].rearrange("p (c t) -> p c t", c=4),
                    in1=gu[:, hf * 4:(hf + 1) * 4, :], op=ALU.mult), reads=["gt", ("gub", i2)], writes=[("oblk", i2)])
            S.add("sp", lambda e, o_=o_, c0=c0: e.dma_start(out=oT[:, c0:c0 + 128].rearrange("(c p) t -> p c t", p=128), in_=o_[:]),
                  reads=[("oblk", i2)], writes=[("dr", "oT", c0)], dma=True)
        S.barrier()


def phase_outproj(k, l, wname, oT, x_in, x_out, tiles):
    nc, S = k.nc, k.S
    w = k.dram(wname, [1, D, D])[0]
    with ExitStack() as st:
        al = k.alloc(st)
        xres = al("xres", [128, KC, 512], F32)
        hT = al("hT", [128, KC, 512], BF16)
        wa = [al("wa%d" % i, [128, KC, 256], BF16) for i in range(2)]
        na = 0
        for (t0, T, segs) in tiles:
            load_x_tile(k, xres, x_in, t0, T)
            for c in range(KC):
                S.add("sp", lambda e, c=c, t0=t0, T=T: e.dma_start(out=hT[:, c, :T], in_=oT[c * 128:(c + 1) * 128, t0:t0 + T]),
                      writes=[("hT", c)], dma=True)
            for sl in range(KC // 2):
                a = wa[na % 2]
                ka = "wa%d" % (na % 2)
                na += 1
                S.add("pool", lambda e, a=a, sl=sl: e.dma_start(
                    out=a[:], in_=w[:, sl * 256:(sl + 1) * 256].rearrange("(kc p) n -> p kc n", p=128)), writes=[ka], dma=True)
                for jj in range(2):
                    c = sl * 2 + jj
                    py = k.ps[c % 2]
                    for kc in range(KC):
                        mm(S, py[:, :T], a[:, kc, jj * 128:(jj + 1) * 128], hT[:, kc, :T], kc == 0, kc == KC - 1,
                           [ka, ("hT", kc)], ("ps", c % 2))
                    residual_store(k, py, ("ps", c % 2), xres, c, T, segs, l, 1, x_out, t0)
        S.barrier()


NK = SEQ + NCTX
NKC = NK // 128


def lat_norm(k, src, nch, T, gain, dst, rstd, sqb, pskey_i):
    S = k.S
    ps = k.ps[pskey_i]
    for c in range(nch):
        sq = sqb[c % 2]
        S.add("act", lambda e, c=c, sq=sq: e.activation(out=sq[:, :T], in_=src[:, c, :T], func=AF.Square),
              reads=[("lsrc", c)], writes=[("sq", c % 2)])
        mm(S, ps[:, :T], k.ones_b[:], sq[:, :T], c == 0, c == nch - 1, [("sq", c % 2), "ones_b"], ("ps", pskey_i))
    S.add("act", lambda e: e.activation(out=rstd[:, :T], in_=ps[:, :T], func=AF.Sqrt, scale=1.0 / (nch * 128), bias=k.eps_t[:]),
          reads=[("ps", pskey_i), "eps_t"], writes=["rstd2"])
    S.add("dve", lambda e: e.reciprocal(out=rstd[:, :T], in_=rstd[:, :T]), reads=["rstd2"], writes=["rstd2"])
    for c in range(nch):
        S.add("dve", lambda e, c=c: e.scalar_tensor_tensor(
            out=dst[:, c, :T], in0=src[:, c, :T], scalar=gain[:, c:c + 1], op0=ALU.mult, in1=rstd[:, :T], op1=ALU.mult),
            reads=[("lsrc", c), "rstd2", "gains"], writes=[("ldst", c)])


def phase_cd_in(k, x_in, tiles):
    nc, S = k.nc, k.S
    l, s = 1, 1
    w_in = k.dram("cd_w_in", [1, D, 4416])[0]
    w_uq = k.dram("c_w_uq", [1, 768, 1536])[0]
    cosD, sinD = k.dram("cosT", [128, NT]), k.dram("sinT", [128, NT])
    qgD, kgD = k.dram("c_qnT", [128, 6]), k.dram("c_kvnT", [128, 4])
    qnT = k.dram("qnT", [1024, NOWN], BF16)
    qrT = k.dram("qrT", [512, NOWN], BF16)
    xin = [k.dram("xch_in%d" % i, [128, NOWN], BF16) for i in range(5)]
    ctxkv = k.dram("ctxkv", [640, NCTX], BF16)
    zb = k.dram("zb_in", [2, 1024])
    dbT = k.dram("dbT", [1024, NOWN], BF16)
    zT = k.dram("zT", [1024, NOWN])
    with ExitStack() as st:
        al = k.alloc(st)
        xres = al("xres", [128, KC, 512], F32)
        hT = al("hT", [128, KC, 512], BF16)
        tmp = [al("tmp%d" % i, [128, 512], F32) for i in range(2)]
        sqb = [al("sq%d" % i, [128, 512], BF16) for i in range(2)]
        rstd = al("rstd", [128, 512], F32)
        rstd2 = al("rstd2", [128, 512], F32)
        cos, sin = al("cos", [128, 512], F32), al("sin", [128, 512], F32)
        lat = al("lat", [128, 6, 512], F32)
        latn = al("latn", [128, 6, 512], BF16)
        qg, kg = al("qg", [128, 6], F32), al("kg", [128, 4], F32)
        wuq = al("wuq", [128, 6, 1536], BF16)
        wqr = al("wqr", [128, 6, 4, 128], BF16)
        krw = al("krw", [128, KC, 128], BF16)
        wa = [al("wa%d" % i, [128, KC, 256], BF16) for i in range(2)]
        qraw = [al("qraw%d" % i, [128, 512], BF16) for i in range(2)]
        r1 = [al("r1_%d" % i, [128, 512], F32) for i in range(2)]
        r2 = [al("r2_%d" % i, [128, 512], F32) for i in range(2)]
        ob = [al("ob%d" % i, [128, 512], BF16) for i in range(2)]
        zo = [al("zo%d" % i, [128, 512], F32) for i in range(2)]
        dcs = [al("dcs%d" % i, [128, 512], F32) for i in range(2)]
        S.add("sp", lambda e: e.dma_start(out=qg[:], in_=qgD), writes=["gains"], dma=True)
        S.add("sp", lambda e: e.dma_start(out=kg[:], in_=kgD), writes=["gains"], dma=True)
        for i in range(3):
            S.add("pool", lambda e, i=i: e.dma_start(
                out=wuq[:, :, i * 512:(i + 1) * 512], in_=w_uq[:, i * 512:(i + 1) * 512].rearrange("(kc p) n -> p kc n", p=128)),
                writes=[("wuq", i)], dma=True)
        for h in range(8):
            S.add("pool", lambda e, h=h: e.dma_start(
                out=wqr[:, :, h // 2, (h % 2) * 64:(h % 2) * 64 + 64],
                in_=w_uq[:, h * 192 + 128:h * 192 + 192].rearrange("(kc p) n -> p kc n", p=128)),
                writes=[("wqr", h)], dma=True)
        for i in range(2):
            S.add("pool", lambda e, i=i: e.dma_start(
                out=krw[:, :, i * 64:(i + 1) * 64], in_=w_in[:, 1280:1344].rearrange("(kc p) n -> p kc n", p=128)),
                writes=[("krw", i)], dma=True)
        wuq_r = [("wuq", i) for i in range(3)]
        wqr_r = [("wqr", h) for h in range(8)]
        na = 0
        cnt = 0
        for (t0, T, segs) in tiles:
            own = t0 < NOWN
            oc0 = t0 if own else 0
            kvd = (lambda i: xin[i]) if own else (lambda i: ctxkv[i * 128:(i + 1) * 128, :])
            load_x_tile(k, xres, x_in, t0, T)
            S.add("sp", lambda e, t0=t0, T=T: e.dma_start(out=cos[:, :T], in_=cosD[:, t0:t0 + T]), writes=["rope"], dma=True)
            S.add("sp", lambda e, t0=t0, T=T: e.dma_start(out=sin[:, :T], in_=sinD[:, t0:t0 + T]), writes=["rope"], dma=True)
            norm_mod(k, xres, hT, T, segs, l, s, tmp, rstd, sqb)

            def proj_chunk(a, ka, jj):
                nonlocal cnt
                i2 = cnt % 2
                cnt += 1
                pq = k.ps[i2]
                for kc in range(KC):
                    mm(S, pq[:, :T], a[:, kc, jj * 128:(jj + 1) * 128], hT[:, kc, :T], kc == 0, kc == KC - 1,
                       [ka, ("hT", kc)] if not isinstance(ka, list) else ka + [("hT", kc)], ("ps", i2))
                return pq, i2

            def slab(c0):
                nonlocal na
                a = wa[na % 2]
                ka = "wa%d" % (na % 2)
                na += 1
                S.add("pool", lambda e: e.dma_start(
                    out=a[:], in_=w_in[:, c0:c0 + 256].rearrange("(kc p) n -> p kc n", p=128)), writes=[ka], dma=True)
                return a, ka

            if own:
                for sl in range(3):
                    a, ka = slab(sl * 256)
                    for jj in range(2):
                        c = sl * 2 + jj
                        pq, i2 = proj_chunk(a, ka, jj)
                        S.add("act", lambda e, pq=pq, c=c: e.activation(out=lat[:, c, :T], in_=pq[:, :T], func=AF.Identity),
                              reads=[("ps", i2)], writes=[("lsrc", c)])
                lat_norm(k, lat, 6, T, qg, latn, rstd2, sqb, 6)
                lr = [("ldst", c) for c in range(6)]
                for h in range(8):
                    i2 = cnt % 2
                    cnt += 1
                    pq = k.ps[i2]
                    for kc in range(6):
                        mm(S, pq[:, :T], wuq[:, kc, h * 192:h * 192 + 128], latn[:, kc, :T], kc == 0, kc == 5,
                           wuq_r + [("ldst", kc)], ("ps", i2))
                    o_ = ob[i2]
                    S.add("act", lambda e, pq=pq, o_=o_: e.activation(out=o_[:, :T], in_=pq[:, :T], func=AF.Identity),
                          reads=[("ps", i2)], writes=[("ob", i2)])
                    S.add("sp", lambda e, o_=o_, h=h: e.dma_start(out=qnT[h * 128:(h + 1) * 128, t0:t0 + T], in_=o_[:, :T]),
                          reads=[("ob", i2)], writes=[("dr", "qnT", t0, h)], dma=True)
                for j in range(4):
                    i2 = cnt % 2
                    cnt += 1
                    pq = k.ps[i2]
                    for kc in range(6):
                        mm(S, pq[:, :T], wqr[:, kc, j, :], latn[:, kc, :T], kc == 0, kc == 5, wqr_r + [("ldst", kc)], ("ps", i2))
                    o_ = ob[i2]
                    rope_from_psum(k, pq[:, :T], ("ps", i2), k.ps[2 + i2], ("ps", 2 + i2), cos[:, :T], sin[:, :T], T,
                                   o_[:, :T], [("ob", i2)], qraw[i2], r1[i2], r2[i2], (("qraw", i2), ("r1", i2), ("r2", i2)))
                    S.add("sp", lambda e, o_=o_, j=j: e.dma_start(out=qrT[j * 128:(j + 1) * 128, t0:t0 + T], in_=o_[:, :T]),
                          reads=[("ob", i2)], writes=[("dr", "qrT", t0, j)], dma=True)
            for sl in range(2):
                a, ka = slab(768 + sl * 256)
                for jj in range(2):
                    c = sl * 2 + jj
                    pq, i2 = proj_chunk(a, ka, jj)
                    S.add("act", lambda e, pq=pq, c=c: e.activation(out=lat[:, c, :T], in_=pq[:, :T], func=AF.Identity),
                          reads=[("ps", i2)], writes=[("lsrc", c)])
            lat_norm(k, lat, 4, T, kg, latn, rstd2, sqb, 6)
            for c in range(4):
                S.add("sp", lambda e, c=c: e.dma_start(out=kvd(c)[:, oc0:oc0 + T], in_=latn[:, c, :T]),
                      reads=[("ldst", c)], writes=[("dr", "ckvn", t0, c)], dma=True)
            pq, i2 = proj_chunk(krw, [("krw", 0), ("krw", 1)], 0)
            o_ = ob[i2]
            rope_from_psum(k, pq[:, :T], ("ps", i2), k.ps[2 + i2], ("ps", 2 + i2), cos[:, :T], sin[:, :T], T,
                           o_[:, :T], [("ob", i2)], qraw[i2], r1[i2], r2[i2], (("qraw", i2), ("r1", i2), ("r2", i2)))
            S.add("sp", lambda e, o_=o_: e.dma_start(out=kvd(4)[:, oc0:oc0 + T], in_=o_[:, :T]),
                  reads=[("ob", i2)], writes=[("dr", "krT", t0)], dma=True)
            if own:
                for sl in range(4):
                    a, ka = slab(1344 + sl * 256)
                    for jj in range(2):
                        c = sl * 2 + jj
                        pq, i2 = proj_chunk(a, ka, jj)
                        o_ = ob[i2]
                        S.add("act", lambda e, pq=pq, o_=o_: e.activation(out=o_[:, :T], in_=pq[:, :T], func=AF.Identity),
                              reads=[("ps", i2)], writes=[("ob", i2)])
                        S.add("sp", lambda e, o_=o_, c=c: e.dma_start(out=dbT[c * 128:(c + 1) * 128, t0:t0 + T], in_=o_[:, :T]),
                              reads=[("ob", i2)], writes=[("dr", "dbT", t0, c)], dma=True)
                for sl in range(4):
                    a, ka = slab(2368 + sl * 256)
                    a2, ka2 = slab(3392 + sl * 256)
                    for jj in range(2):
                        c = sl * 2 + jj
                        pq, i2 = proj_chunk(a, ka, jj)
                        dc_ = dcs[i2]
                        S.add("act", lambda e, pq=pq, dc_=dc_: e.activation(out=dc_[:, :T], in_=pq[:, :T], func=AF.Identity),
                              reads=[("ps", i2)], writes=[("dcs", i2)])
                        pq2, j2 = proj_chunk(a2, ka2, jj)
                        z_ = zo[i2]
                        S.add("dve", lambda e, pq2=pq2, dc_=dc_, z_=z_: e.tensor_tensor(
                            out=z_[:, :T], in0=pq2[:, :T], in1=dc_[:, :T], op=ALU.mult),
                            reads=[("ps", j2), ("dcs", i2)], writes=[("zo", i2)])
                        S.add("sp", lambda e, z_=z_, c=c: e.dma_start(out=zT[c * 128:(c + 1) * 128, t0:t0 + T], in_=z_[:, :T]),
                              reads=[("zo", i2)], writes=[("dr", "zT", t0, c)], dma=True)
                        for (tt, which, col) in ((0, 0, 0), (NOWN - 512, 1, 511)):
                            if t0 == tt:
                                S.add("sp", lambda e, z_=z_, c=c, which=which, col=col: e.dma_start(
                                    out=zb[which:which + 1, :].rearrange("a (p c) -> p (a c)", c=8)[:, c:c + 1],
                                    in_=z_[:, col:col + 1], allow_slow_non_contiguous=True),
                                    reads=[("zo", i2)], writes=[("dr", "zb", which, c)], dma=True)
        S.barrier()


def phase_exchange(k):
    nc, S = k.nc, k.S
    xin = [k.dram("xch_in%d" % i, [128, NOWN], BF16) for i in range(5)]
    xall = [k.dram("xch_all%d" % i, [512, NOWN], BF16) for i in range(5)]
    zb = k.dram("zb_in", [2, 1024])
    zall = k.dram("zb_all", [8, 1024])
    groups = [[0, 1, 2, 3], [4, 5, 6, 7]]
    for i in range(5):
        S.add_cc(lambda e, i=i: e.collective_compute("AllGather", ALU.bypass, replica_groups=groups, ins=[xin[i].opt()], outs=[xall[i].opt()]))
    S.add_cc(lambda e: e.collective_compute("AllGather", ALU.bypass, replica_groups=groups, ins=[zb.opt()], outs=[zall.opt()]))
    S.barrier()
    with ExitStack() as st:
        al = k.alloc(st)
        zsel = al("zsel", [128, 8, 8], F32)
        S.add("sp", lambda e: e.dma_start(out=zsel[:], in_=zall.rearrange("j (p c) -> p j c", c=8)), reads=["zall"], writes=["zsel"], dma=True)
        for w in range(2):
            for j in range(8):
                if j == 0:
                    S.add("dve", lambda e, w=w, j=j: e.tensor_scalar(out=k.zpn[:, w, :], in0=zsel[:, j, :], scalar1=k.selT[:, w, j:j + 1],
                                                                     scalar2=None, op0=ALU.mult), reads=["zsel", "selT"], writes=["zpn"])
                else:
                    S.add("dve", lambda e, w=w, j=j: e.scalar_tensor_tensor(out=k.zpn[:, w, :], in0=zsel[:, j, :], scalar=k.selT[:, w, j:j + 1],
                                                                            op0=ALU.mult, in1=k.zpn[:, w, :], op1=ALU.add),
                          reads=["zsel", "selT", "zpn"], writes=["zpn"])
        S.barrier()


def phase_mla(k):
    nc, S = k.nc, k.S
    xall = [k.dram("xch_all%d" % i, [512, NOWN], BF16) for i in range(5)]
    ctxkv = k.dram("ctxkv", [640, NCTX], BF16)
    w_ukv = k.dram("c_w_ukv", [1, 512, 2048])[0]
    qnT = k.dram("qnT", [1024, NOWN], BF16)
    qrT = k.dram("qrT", [512, NOWN], BF16)
    oT = k.dram("oT2", [1024, NOWN], BF16)
    scale = 192.0 ** -0.5
    with ExitStack() as st:
        al = k.alloc(st)
        ckv = al("ckv", [128, 4, NK], BF16)
        krd = al("krd", [128, NK], BF16)
        wkv = al("wkv", [128, 4, 2048], BF16)
        knT = al("knT", [128, NK], BF16)
        vh = al("vh", [128, NKC, 128], BF16)
        qn = [al("qn%d" % i, [128, 512], BF16) for i in range(2)]
        qr = [[al("qr%d_%d" % (e_, i), [128, 512], BF16) for i in range(2)] for e_ in range(2)]
        for e_ in range(2):
            for i in range(2):
                S.add("pool", lambda e, e_=e_, i=i: e.memset(qr[e_][i][:], 0.0), writes=[("qr", e_, i)])
        pT = [al("pT%d" % i, [128, 512], BF16) for i in range(6)]
        rr = al("rr", [128, 512], F32)
        dacc = [al("dacc%d" % i, [128, 512], F32) for i in range(2)]
        oo = [al("oo%d" % i, [128, 512], BF16) for i in range(2)]
        for c in range(4):
            for r in range(4):
                S.add("sp", lambda e, c=c, r=r: e.dma_start(out=ckv[:, c, r * NOWN:(r + 1) * NOWN],
                                                            in_=xall[c][r * 128:(r + 1) * 128, :]),
                      writes=[("ckv", c, r)], dma=True)
            S.add("sp", lambda e, c=c: e.dma_start(out=ckv[:, c, SEQ:NK], in_=ctxkv[c * 128:(c + 1) * 128, :]), writes=[("ckv", c, 4)], dma=True)
        for r in range(4):
            S.add("sp", lambda e, r=r: e.dma_start(out=krd[:, r * NOWN:(r + 1) * NOWN], in_=xall[4][r * 128:(r + 1) * 128, :]),
                  writes=[("krd", r)], dma=True)
        S.add("sp", lambda e: e.dma_start(out=krd[:, SEQ:NK], in_=ctxkv[512:640, :]), writes=[("krd", 4)], dma=True)
        for i in range(4):
            S.add("pool", lambda e, i=i: e.dma_start(
                out=wkv[:, :, i * 512:(i + 1) * 512], in_=w_ukv[:, i * 512:(i + 1) * 512].rearrange("(kc p) n -> p kc n", p=128)),
                writes=[("wkv", i)], dma=True)
        ckr = [("ckv", c, r) for c in range(4) for r in range(5)]
        krr = [("krd", r) for r in range(5)]
        npT = 0
        nq = 0
        for h in range(8):
            wr = [("wkv", h // 2)]
            for kg in range(17):
                n = 512 if kg < 16 else 256
                pk_ = k.ps[6 + kg % 2]
                for kc in range(4):
                    mm(S, pk_[:, :n], wkv[:, kc, h * 256:h * 256 + 128], ckv[:, kc, kg * 512:kg * 512 + n], kc == 0, kc == 3,
                       wr + ckr, ("ps", 6 + kg % 2))
                S.add("act", lambda e, pk_=pk_, kg=kg, n=n: e.activation(out=knT[:, kg * 512:kg * 512 + n], in_=pk_[:, :n], func=AF.Identity),
                      reads=[("ps", 6 + kg % 2)], writes=["knT"])
            for g4 in range(17):
                nb = 4 if g4 < 16 else 2
                pv_ = k.ps[6 + g4 % 2]
                for bb in range(nb):
                    kb = g4 * 4 + bb
                    for kc in range(4):
                        mm(S, pv_[:, bb * 128:(bb + 1) * 128], ckv[:, kc, kb * 128:(kb + 1) * 128],
                           wkv[:, kc, h * 256 + 128:h * 256 + 256], kc == 0, kc == 3, wr + ckr, ("ps", 6 + g4 % 2))
                S.add("dve", lambda e, pv_=pv_, g4=g4, nb=nb: e.tensor_copy(
                    out=vh[:, g4 * 4:g4 * 4 + nb, :], in_=pv_[:, :nb * 128].rearrange("p (b d) -> p b d", d=128)),
                    reads=[("ps", 6 + g4 % 2)], writes=["vh"])
            e2 = h % 2
            for qgi in range(4):
                i2 = nq % 2
                nq += 1
                q0 = qgi * 512
                S.add("sp", lambda e, i2=i2, q0=q0, h=h: e.dma_start(out=qn[i2][:], in_=qnT[h * 128:(h + 1) * 128, q0:q0 + 512]),
                      writes=[("qn", i2)], dma=True)
                S.add("sp", lambda e, i2=i2, q0=q0, h=h, e2=e2: e.dma_start(
                    out=qr[e2][i2][64 * e2:64 * e2 + 64, :], in_=qrT[(h // 2) * 128 + 64 * e2:(h // 2) * 128 + 64 * e2 + 64, q0:q0 + 512]),
                    writes=[("qr", e2, i2)], dma=True)
                po, pd = k.ps[4], k.ps[5]
                LOOK = 3
                pend = {}
                for kc in range(NKC + LOOK):
                    if kc < NKC:
                        bi = kc % 4
                        ps_ = k.ps[bi]
                        mm(S, ps_[:, :], knT[:, kc * 128:(kc + 1) * 128], qn[i2][:], True, False, ["knT", ("qn", i2)], ("ps", bi))
                        mm(S, ps_[:, :], krd[:, kc * 128:(kc + 1) * 128], qr[e2][i2][:], False, True,
                           krr + [("qr", e2, i2)], ("ps", bi))
                        p = pT[npT % 6]
                        pk = ("pT", npT % 6)
                        npT += 1
                        S.add("act", lambda e, p=p, ps_=ps_: e.activation(out=p[:], in_=ps_[:, :], func=AF.Exp, scale=scale),
                              reads=[("ps", bi)], writes=[pk])
                        pend[kc] = (p, pk)
                    j = kc - LOOK
                    if j >= 0:
                        p, pk = pend.pop(j)
                        mm(S, po[:, :], vh[:, j, :], p[:], j == 0, j == NKC - 1, ["vh", pk], ("ps", 4))
                        if j % 2 == 1:
                            mm(S, pd[:, :], k.ones_b[:], p[:], j == 1, False, ["ones_b", pk], ("ps", 5))
                        elif j == 0:
                            S.add("dve", lambda e, p=p: e.tensor_copy(out=dacc[0][:], in_=p[:]), reads=[pk], writes=[("dacc", 0)])
                        else:
                            S.add("dve", lambda e, p=p: e.tensor_tensor(out=dacc[0][:], in0=dacc[0][:], in1=p[:], op=ALU.add),
                                  reads=[pk, ("dacc", 0)], writes=[("dacc", 0)])
                mm(S, pd[:, :], k.ones_f[:], dacc[0][:], False, True, ["ones_f", ("dacc", 0)], ("ps", 5))
                S.add("dve", lambda e, pd=pd: e.reciprocal(out=rr[:], in_=pd[:, :]), reads=[("ps", 5)], writes=["rr"])
                o_ = oo[i2]
                S.add("dve", lambda e, po=po, o_=o_: e.tensor_tensor(out=o_[:], in0=po[:, :], in1=rr[:], op=ALU.mult),
                      reads=[("ps", 4), "rr"], writes=[("oo", i2)])
                S.add("sp", lambda e, o_=o_, h=h, q0=q0: e.dma_start(out=oT[h * 128:(h + 1) * 128, q0:q0 + 512], in_=o_[:]),
                      reads=[("oo", i2)], writes=[("dr", "oT2", h, q0)], dma=True)
        S.barrier()


def phase_cd_out(k, x_in, x_out):
    nc, S = k.nc, k.S
    l = 1
    w = k.dram("cd_w_out", [1, D, D])[0]
    oT = k.dram("oT2", [1024, NOWN], BF16)
    dbT = k.dram("dbT", [1024, NOWN], BF16)
    zT = k.dram("zT", [1024, NOWN])
    cwD = k.dram("convT", [128, 24])
    with ExitStack() as st:
        al = k.alloc(st)
        xres = al("xres", [128, KC, 512], F32)
        hT = al("hT", [128, KC, 512], BF16)
        ze = al("ze", [128, 8, 514], F32)
        dbs = al("dbs", [128, 8, 512], BF16)
        cw = al("cw", [128, 24], F32)
        ct = [al("ct%d" % i, [128, 512], F32) for i in range(2)]
        wa = [al("wa%d" % i, [128, KC, 256], BF16) for i in range(2)]
        S.add("sp", lambda e: e.dma_start(out=cw[:], in_=cwD), writes=["cw"], dma=True)
        na = 0
        for (t0, T, segs) in OWN_TILES:
            load_x_tile(k, xres, x_in, t0, T)
            for c in range(8):
                S.add("sp", lambda e, c=c, t0=t0: e.dma_start(out=hT[:, c, :], in_=oT[c * 128:(c + 1) * 128, t0:t0 + 512]),
                      writes=[("hT", c)], dma=True)
            S.add("sp", lambda e, t0=t0: e.dma_start(out=ze[:, :, 1:513], in_=zT[:, t0:t0 + 512].rearrange("(c p) t -> p c t", p=128)),
                  writes=[("ze", 1)], dma=True)
            if t0 > 0:
                S.add("sp", lambda e, t0=t0: e.dma_start(out=ze[:, :, 0:1], in_=zT[:, t0 - 1:t0].rearrange("(c p) t -> p c t", p=128),
                                                         allow_slow_non_contiguous=True), writes=[("ze", 0)], dma=True)
            else:
                S.add("dve", lambda e: e.tensor_copy(out=ze[:, :, 0], in_=k.zpn[:, 0, :]), reads=["zpn"], writes=[("ze", 0)])
            if t0 + 512 < NOWN:
                S.add("sp", lambda e, t0=t0: e.dma_start(out=ze[:, :, 513:514], in_=zT[:, t0 + 512:t0 + 513].rearrange("(c p) t -> p c t", p=128),
                                                         allow_slow_non_contiguous=True), writes=[("ze", 2)], dma=True)
            else:
                S.add("dve", lambda e: e.tensor_copy(out=ze[:, :, 513], in_=k.zpn[:, 1, :]), reads=["zpn"], writes=[("ze", 2)])
            S.add("sp", lambda e, t0=t0: e.dma_start(out=dbs[:], in_=dbT[:, t0:t0 + 512].rearrange("(c p) t -> p c t", p=128)),
                  writes=["dbs"], dma=True)
            zr = [("ze", i) for i in range(3)]
            for c in range(8):
                t_ = ct[c % 2]
                tk = ("ct", c % 2)
                S.add("dve", lambda e, c=c, t_=t_: e.tensor_scalar(out=t_[:], in0=ze[:, c, 0:512], scalar1=cw[:, c:c + 1], scalar2=None,
                                                                   op0=ALU.mult), reads=zr + ["cw"], writes=[tk])
                S.add("dve", lambda e, c=c, t_=t_: e.scalar_tensor_tensor(out=t_[:], in0=ze[:, c, 1:513], scalar=cw[:, 8 + c:9 + c],
                                                                          op0=ALU.mult, in1=t_[:], op1=ALU.add), reads=zr + ["cw", tk], writes=[tk])
                S.add("dve", lambda e, c=c, t_=t_: e.scalar_tensor_tensor(out=t_[:], in0=ze[:, c, 2:514], scalar=cw[:, 16 + c:17 + c],
                                                                          op0=ALU.mult, in1=t_[:], op1=ALU.add), reads=zr + ["cw", tk], writes=[tk])
                S.add("dve", lambda e, c=c, t_=t_: e.tensor_tensor(out=hT[:, 8 + c, :], in0=t_[:], in1=dbs[:, c, :], op=ALU.mult),
                      reads=[tk, "dbs"], writes=[("hT", 8 + c)])
            for sl in range(KC // 2):
                a = wa[na % 2]
                ka = "wa%d" % (na % 2)
                na += 1
                S.add("pool", lambda e, a=a, sl=sl: e.dma_start(
                    out=a[:], in_=w[:, sl * 256:(sl + 1) * 256].rearrange("(kc p) n -> p kc n", p=128)), writes=[ka], dma=True)
                for jj in range(2):
                    c = sl * 2 + jj
                    py = k.ps[c % 2]
                    for kc in range(KC):
                        mm(S, py[:, :T], a[:, kc, jj * 128:(jj + 1) * 128], hT[:, kc, :T], kc == 0, kc == KC - 1,
                           [ka, ("hT", kc)], ("ps", c % 2))
                    residual_store(k, py, ("ps", c % 2), xres, c, T, segs, l, 1, x_out, t0)
        S.barrier()


OWN_TILES = [(i * 512, 512, [(0, 512, 0)]) for i in range(4)]
MISC_TILE = (2048, 512, [(0, 256, 0), (256, 256, 1)])
CTX_TILE = (CT0, 256, [(0, 256, 1)])
OWN3_CTX_TILE = ([(1536, 512), (CT0, 256)], [(0, 512, 0), (512, 256, 1)])


def fm(v):
    v = np.asarray(v)
    lead = v.shape[:-1]
    n = v.shape[-1] // 128
    r = v.reshape(lead + (n, 128))
    r = np.moveaxis(r, -1, 0)
    return np.ascontiguousarray(r.reshape(128, -1))


def prep_core(inp, core):
    b, q = core // 4, core % 4
    p0 = q * NOWN
    x = inp["x"]
    xT = np.zeros((D, NT), np.float32)
    xT[:, 0:NOWN] = x[b, p0:p0 + NOWN].T
    if q > 0:
        xT[:, HP0:HP0 + 128] = x[b, p0 - 128:p0].T
    if q < 3:
        xT[:, HN0:HN0 + 128] = x[b, p0 + NOWN:p0 + NOWN + 128].T
    xT[:, CT0:CT0 + NCTX] = inp["ctx"][b].T
    m = {"xT": xT}
    m["cT"] = np.ascontiguousarray(np.stack([inp["c"][b], inp["c_ctx"]], axis=1))
    pos = np.zeros(NT, np.int64)
    pos[0:NOWN] = p0 + np.arange(NOWN)
    pos[HP0:HP0 + 128] = p0 - 128 + np.arange(128)
    pos[HN0:HN0 + 128] = p0 + NOWN + np.arange(128)
    pos = np.clip(pos, 0, SEQ - 1)
    row = (pos // 64).astype(np.float32)
    col = (pos % 64).astype(np.float32)
    inv = (10000.0 ** (-np.arange(0, 32, 2, dtype=np.float32) / 32)).astype(np.float32)
    ang = np.zeros((64, NT), np.float32)
    ang[0:16] = (row[None, :] * inv[:, None]).astype(np.float32)
    ang[16:32] = ang[0:16]
    ang[32:48] = (col[None, :] * inv[:, None]).astype(np.float32)
    ang[48:64] = ang[32:48]
    cosT = np.cos(ang).astype(np.float32)
    sinT = np.sin(ang).astype(np.float32)
    cosT[:, CT0:] = 1.0
    sinT[:, CT0:] = 0.0
    m["cosT"] = np.ascontiguousarray(np.concatenate([cosT, cosT], 0))
    m["sinT"] = np.ascontiguousarray(np.concatenate([sinT, sinT], 0))
    j = np.arange(128)[:, None]
    i = np.arange(128)[None, :]
    mk = np.zeros((128, 4, 128), np.float32)
    mk[:, 0, :] = (j >= i)
    mk[:, 1, :] = (j <= i)
    mk[:, 2, :] = (j >= i) * (1.0 if q > 0 else 0.0)
    mk[:, 3, :] = (j <= i) * (1.0 if q < 3 else 0.0)
    m["masks"] = mk
    sel = np.zeros((128, 2, 8), np.float32)
    if q > 0:
        sel[:, 0, 2 * (q - 1) + 1] = 1.0
    if q < 3:
        sel[:, 1, 2 * (q + 1)] = 1.0
    m["selT"] = sel
    return m


def prep_shared(inp):
    m = {}
    m["mod_w"] = inp["mod_w"]
    m["mod_bT"] = np.stack([fm(inp["mod_b"][l]) for l in range(2)])
    m["norm_gT"] = np.stack([fm(inp["norm_g"][l]) for l in range(2)])
    for n in ("ffn_w1", "ffn_w3", "ffn_w2", "ab_w_in", "ab_w_out", "cd_w_in", "cd_w_out", "c_w_uq", "c_w_ukv"):
        m[n] = inp[n]
    pm = np.zeros((128, 128), np.float32)
    for mm_ in range(128):
        if mm_ % 32 < 16:
            pm[mm_ + 16, mm_] = -1.0
        else:
            pm[mm_ - 16, mm_] = 1.0
    m["permM"] = pm
    sk = inp["a_sink"][0]
    m["sinkT"] = np.ascontiguousarray(np.stack([np.repeat(sk[2 * c:2 * c + 2], 64) for c in range(8)], axis=1))
    m["b_wsT"] = np.ascontiguousarray(np.transpose(inp["b_ws"][0], (2, 0, 1)))
    m["b_biasbc"] = np.ascontiguousarray(np.broadcast_to(inp["b_bias"][0][None], (128, 8, 128)))
    m["c_qnT"] = fm(inp["c_q_norm"][0])
    m["c_kvnT"] = fm(inp["c_kv_norm"][0])
    m["convT"] = fm(inp["d_conv_w"][0])
    m["fnT"] = fm(inp["final_norm"])
    return m


def load_final_consts(k):
    fnD = k.dram("fnT", [128, 16])
    k.fnT = k.sb("fnT_sb", [128, 16], F32)
    k.S.add("sp", lambda e: e.dma_start(out=k.fnT[:], in_=fnD), writes=["fnT"], dma=True)
    selD = k.dram("selT", [128, 2, 8])
    k.selT = k.sb("selT_sb", [128, 2, 8], F32)
    k.zpn = k.sb("zpn_sb", [128, 2, 8], F32)
    k.S.add("sp", lambda e: e.dma_start(out=k.selT[:], in_=selD), writes=["selT"], dma=True)


INS = ["xT", "cT", "mod_w", "mod_bT", "norm_gT", "ffn_w1", "ffn_w3", "ffn_w2", "ab_w_in", "ab_w_out", "cosT", "sinT",
       "permM", "masks", "sinkT", "b_wsT", "b_biasbc", "cd_w_in", "c_w_uq", "c_qnT", "c_kvnT",
       "cd_w_out", "c_w_ukv", "convT", "fnT", "selT"]
OUTS = ["outT"]


def build():
    k = K(INS, OUTS)
    xT = k.dram("xT", [D, NT])
    x1, x2, x3 = k.dram("x1", [D, NT]), k.dram("x2", [D, NT]), k.dram("x3", [D, NT])
    x4, x5 = k.dram("x4", [D, NT]), k.dram("x5", [D, NOWN])
    outT = k.dram("outT", [D, NOWN])
    phase_setup(k)
    load_rope_consts(k)
    load_final_consts(k)
    phase_mod(k, (0, 1))
    phase_ffn(k, 0, 0, xT, x1, OWN_TILES + [MISC_TILE])
    phase_ab_in(k, x1, OWN_TILES + [MISC_TILE])
    phase_ab_mix(k)
    phase_outproj(k, 0, "ab_w_out", k.dram("oT", [2048, NT], BF16), x1, x2, OWN_TILES + [CTX_TILE])
    phase_ffn(k, 0, 1, x2, x3, OWN_TILES[:3] + [OWN3_CTX_TILE])
    phase_ffn(k, 1, 0, x3, x4, OWN_TILES[:3] + [OWN3_CTX_TILE])
    phase_cd_in(k, x4, OWN_TILES + [CTX_TILE])
    phase_exchange(k)
    phase_mla(k)
    phase_cd_out(k, x4, x5)
    phase_ffn(k, 1, 1, x5, None, OWN_TILES, final_out=outT)
    k.S.emit(k.st)
    k.st.close()
    return k


def kernel(**inp):
    inp = {kk: np.asarray(v) for kk, v in inp.items()}
    n = 8
    sh = prep_shared(inp)
    for nm in ("ffn_w1", "ffn_w3", "ffn_w2"):
        a = inp[nm]
        sh[nm] = a.reshape((4,) + a.shape[2:])
    k = build()
    maps = []
    for c in range(n):
        m = dict(sh)
        m.update(prep_core(inp, c))
        maps.append({kk: m[kk] for kk in INS})
    res = run_bass_kernel_spmd(k.nc, maps, core_ids=list(range(n))).results
    out = np.zeros((2, SEQ, D), np.float32)
    for c in range(n):
        b, q = c // 4, c % 4
        out[b, q * NOWN:(q + 1) * NOWN, :] = np.asarray(res[c]["outT"]).T
    return out
```

```python
import math
import types
from contextlib import ExitStack
import numpy as np
import concourse.bass as bass
import concourse.mybir as mybir
from concourse.bass_utils import run_bass_kernel_spmd

F32 = mybir.dt.float32
BF16 = mybir.dt.bfloat16
AF = mybir.ActivationFunctionType
ALU = mybir.AluOpType

D = 2048
FFN = 5632
NFC = FFN // 128
KC = D // 128
SEQ = 8192
NOWN = 2048
NCTX = 256
NT = 2560
HP0, HN0, CT0 = 2048, 2176, 2304
EPS = 1e-6
ENGS = ("pe", "act", "dve", "pool", "sp")
N_SW = 4
DBG_NOPERM = False
ROPE_ADD_ENG = "pool"
DBG_SECT = None
SW_FRESH = False


def _freeze(fn):
    if fn is None or fn.__closure__ is None:
        return fn
    cells = []
    for c in fn.__closure__:
        try:
            cells.append(types.CellType(c.cell_contents))
        except ValueError:
            cells.append(c)
    return types.FunctionType(fn.__code__, fn.__globals__, fn.__name__, fn.__defaults__, tuple(cells))


class Op:
    __slots__ = ("eng", "fn", "deps", "idx", "ms", "dma", "sem", "semval", "msval", "tag")


class Sched:
    def __init__(self, nc, n_dma_sems=24):
        if SW_FRESH:
            n_dma_sems = 8
        self.nc = nc
        self.ops = {e: [] for e in ENGS}
        self.lastw = {}
        self.readers = {}
        self.n_dma_sems = n_dma_sems
        self.dma_rr = 0
        self.dma_count = [0] * n_dma_sems
        self.dma_last = [None] * n_dma_sems
        self.sw_keys = {}
        self.sw_rr = 0
        self.sw_last = {}

    def add(self, eng, fn, reads=(), writes=(), dma=False, tag=None):
        op = Op()
        op.eng, op.fn, op.dma, op.ms, op.tag = eng, _freeze(fn), dma, False, tag
        op.sem = op.semval = op.msval = None
        sw = dma and eng == "pool" and SW_FRESH
        if eng in ("act", "dve", "pool") and not dma:
            extra = [("psr", r[1]) for r in reads if isinstance(r, tuple) and len(r) == 2 and r[0] == "ps"]
            if extra:
                writes = list(writes) + extra
        deps = set()
        for r in reads:
            w = self.lastw.get(r)
            if w is not None:
                deps.add(w)
        for k in writes:
            w = self.lastw.get(k)
            if w is not None:
                deps.add(w)
            for rd in self.readers.get(k, ()):
                deps.add(rd)
        if sw:
            key = len(self.sw_keys)
            self.sw_keys[key] = key
            op.sem, op.semval = ("sw", key), 16
            op.tag = False
            self.sw_last[key] = op
        elif dma:
            if eng == "pool":
                s = self.n_dma_sems - N_SW + self.sw_rr
                self.sw_rr = (self.sw_rr + 1) % N_SW
            else:
                s = self.dma_rr
                self.dma_rr = (self.dma_rr + 1) % (self.n_dma_sems - N_SW)
            if self.dma_last[s] is not None:
                deps.add(self.dma_last[s])
            self.dma_count[s] += 1
            op.sem, op.semval = s, 16 * self.dma_count[s]
            self.dma_last[s] = op
        if eng == "pe":
            deps = {d for d in deps if d.dma or d.eng != "pe"}
        op.deps = deps
        for d in deps:
            if not d.dma:
                d.ms = True
        for r in reads:
            self.readers.setdefault(r, []).append(op)
        for k in writes:
            self.lastw[k] = op
            self.readers[k] = []
        op.idx = len(self.ops[eng])
        self.ops[eng].append(op)
        return op

    def add_cc(self, fn, reads=(), writes=()):
        op = self.add("pool", fn, reads=reads, writes=[("cc_issue",)])
        op.tag = "cc"
        self.n_cc = getattr(self, "n_cc", 0) + 1
        op.semval = self.n_cc
        return op

    def barrier(self):
        lasts = [self.ops[e][-1] for e in ENGS if self.ops[e]]
        lasts = [x for x in lasts if not x.dma and x.fn is not None]
        dmas = [d for d in self.dma_last if d is not None] + list(self.sw_last.values())
        for e in ENGS:
            op = Op()
            op.eng, op.fn, op.dma, op.ms, op.tag = e, None, False, False, "barrier"
            op.sem = op.semval = op.msval = None
            op.deps = set(x for x in lasts if x.eng != e) | set(dmas)
            for d in op.deps:
                if not d.dma:
                    d.ms = True
            op.idx = len(self.ops[e])
            self.ops[e].append(op)
        self.lastw.clear()
        self.readers.clear()

    def emit(self, stack):
        nc = self.nc
        esem = {e: stack.enter_context(nc.semaphore("s_" + e)) for e in ENGS if e != "sp"}
        dsem = [stack.enter_context(nc.semaphore("d_%d" % i)) for i in range(self.n_dma_sems)]
        wsem = [stack.enter_context(nc.semaphore("w_%d" % i)) for i in range(len(self.sw_keys))]
        ccsem = stack.enter_context(nc.semaphore("ccsem"))
        for e in ENGS:
            c = 0
            for op in self.ops[e]:
                if op.ms and not op.dma:
                    c += 1
                    op.msval = c
        ops = self.ops
        final_dma = [(dsem[i], 16 * self.dma_count[i]) for i in range(self.n_dma_sems) if self.dma_count[i]]

        def run(ename, eng):
            known = {}
            for op in ops[ename]:
                for d in op.deps:
                    if d.dma and isinstance(d.sem, tuple):
                        key, sem, val = ("w", d.sem[1], id(d)), wsem[d.sem[1]], 16
                    elif d.dma:
                        key, sem, val = ("d", d.sem), dsem[d.sem], d.semval
                    else:
                        key, sem, val = d.eng, esem[d.eng], d.msval
                    if known.get(key, 0) < val:
                        eng.wait_ge(sem, val)
                        known[key] = val
                if op.fn is None:
                    continue
                if op.dma and isinstance(op.sem, tuple):
                    if op.tag:
                        eng.wait_ge(wsem[op.sem[1]], 16)
                        eng.sem_clear(wsem[op.sem[1]])
                    op.fn(eng).then_inc(wsem[op.sem[1]], 16)
                    continue
                ins = op.fn(eng)
                if op.tag == "cc":
                    ins.then_inc(ccsem, 1)
                    eng.wait_ge(ccsem, op.semval)
                    if op.ms:
                        eng.memset(self.cc_dummy[:], 0.0).then_inc(esem[ename], 1)
                    continue
                if op.dma:
                    ins.then_inc(dsem[op.sem], 16)
                elif op.ms:
                    ins.then_inc(esem[ename], 1)
            if ename == "sp":
                for sem, val in final_dma:
                    eng.wait_ge(sem, val)

        with nc.Block() as block:
            @block.tensor
            def _(e):
                run("pe", e)

            @block.scalar
            def _(e):
                run("act", e)

            @block.vector
            def _(e):
                run("dve", e)

            @block.gpsimd
            def _(e):
                run("pool", e)

            @block.sync
            def _(e):
                run("sp", e)


class K:
    def __init__(self, ext_in, ext_out):
        self.nc = bass.Bass("TRN2", target_bir_lowering=False)
        self.S = Sched(self.nc)
        self.ext_in, self.ext_out = set(ext_in), set(ext_out)
        self.dr = {}
        self.st = ExitStack()
        self.uid = 0
        self.ffn_sel = {(0, 0): 0, (0, 1): 1, (1, 0): 2, (1, 1): 3}

    def dram(self, name, shape, dtype=F32):
        if name in self.dr:
            return self.dr[name]
        kind = "ExternalInput" if name in self.ext_in else ("ExternalOutput" if name in self.ext_out else "Internal")
        t = self.nc.dram_tensor(name, list(shape), dtype, kind=kind).ap()
        self.dr[name] = t
        return t

    def sb(self, name, shape, dtype):
        return self.st.enter_context(self.nc.sbuf_tensor(name, list(shape), dtype))

    def alloc(self, st):
        self.uid += 1
        u = self.uid
        return lambda n, sh, dt: st.enter_context(self.nc.sbuf_tensor("%s_u%d" % (n, u), list(sh), dt))

    def psum(self, name):
        return self.st.enter_context(self.nc.psum_tensor(name, [128, 512], F32))


def mm(S, ps_ap, lhsT, rhs, start, stop, reads, pskey):
    S.add("pe", lambda e: e.matmul(ps_ap, lhsT, rhs, start=start, stop=stop), reads=reads, writes=[pskey])


def phase_setup(k):
    nc, S = k.nc, k.S
    k.ps = [k.psum("ps%d" % i) for i in range(8)]
    k.ones_f = k.sb("ones_f", [128, 128], F32)
    k.ones_b = k.sb("ones_b", [128, 128], BF16)
    S.add("pool", lambda e: e.memset(k.ones_f[:], 1.0), writes=["ones_f"])
    S.add("pool", lambda e: e.memset(k.ones_b[:], 1.0), writes=["ones_b"])
    k.eps_t = k.sb("eps_t", [128, 1], F32)
    S.cc_dummy = k.sb("cc_dummy", [128, 8], F32)
    S.add("pool", lambda e: e.memset(k.eps_t[:], EPS), writes=["eps_t"])


def phase_mod(k, layers=(0, 1)):
    nc, S = k.nc, k.S
    cT = k.dram("cT", [D, 2])
    nl = len(layers)
    mod_w = k.dram("mod_w", [nl, D, 9 * D])
    mod_bT = k.dram("mod_bT", [nl, 128, 144])
    norm_gT = k.dram("norm_gT", [nl, 128, 48])
    k.modT = [k.sb("modT%d" % l, [128, 144, 2], F32) for l in range(2)]
    k.modA = [k.sb("modA%d" % l, [128, 48, 2], F32) for l in range(2)]
    k.modG = [k.sb("modG%d" % l, [128, 48, 2], F32) for l in range(2)]
    with ExitStack() as st:
        cin = st.enter_context(nc.sbuf_tensor("cin_sb", [128, KC, 2], F32))
        scT = st.enter_context(nc.sbuf_tensor("scT", [128, KC, 2], BF16))
        mb = st.enter_context(nc.sbuf_tensor("mb", [128, 144], F32))
        ng = st.enter_context(nc.sbuf_tensor("ng", [128, 48], F32))
        wm = [st.enter_context(nc.sbuf_tensor("wm%d" % i, [128, KC, 512], BF16)) for i in range(2)]
        S.add("sp", lambda e: e.dma_start(out=cin[:], in_=cT.rearrange("(kc p) n -> p kc n", p=128)), writes=["cin"], dma=True)
        S.add("act", lambda e: e.activation(out=scT[:], in_=cin[:], func=AF.Silu), reads=["cin"], writes=["scT"])
        si = 0
        for li, l in enumerate(layers):
            S.add("sp", lambda e, li=li: e.dma_start(out=mb[:], in_=mod_bT[li]), writes=["mb"], dma=True)
            S.add("sp", lambda e, li=li: e.dma_start(out=ng[:], in_=norm_gT[li]), writes=["ng"], dma=True)
            ps = k.ps[l]
            for sl in range(36):
                w = wm[si % 2]
                wkey = "wm%d" % (si % 2)
                si += 1
                S.add("pool", lambda e, w=w, li=li, sl=sl: e.dma_start(
                    out=w[:], in_=mod_w[li, :, sl * 512:(sl + 1) * 512].rearrange("(kc p) n -> p kc n", p=128)),
                    writes=[wkey], dma=True)
                for jj in range(4):
                    j = sl * 4 + jj
                    for kc in range(KC):
                        mm(S, ps[:, 2 * j:2 * j + 2], w[:, kc, jj * 128:(jj + 1) * 128], scT[:, kc, :],
                           kc == 0, kc == KC - 1, [wkey, "scT"], ("ps", l))
            modT = k.modT[l]
            for cnd in range(2):
                S.add("dve", lambda e, modT=modT, ps=ps, cnd=cnd: e.tensor_tensor(
                    out=modT[:, :, cnd], in0=ps[:, 0:288].rearrange("p (j c) -> p j c", c=2)[:, :, cnd], in1=mb[:], op=ALU.add),
                    reads=[("ps", l), "mb"], writes=[("modT", l)])
            for s in range(3):
                for cnd in range(2):
                    S.add("dve", lambda e, l=l, s=s, cnd=cnd, modT=modT: e.scalar_tensor_tensor(
                        out=k.modA[l][:, s * 16:(s + 1) * 16, cnd], in0=modT[:, (3 * s + 1) * 16:(3 * s + 2) * 16, cnd],
                        scalar=1.0, op0=ALU.add, in1=ng[:, s * 16:(s + 1) * 16], op1=ALU.mult),
                        reads=[("modT", l), "ng"], writes=[("modA", l)])
                    S.add("dve", lambda e, l=l, s=s, cnd=cnd, modT=modT: e.tensor_scalar(
                        out=k.modG[l][:, s * 16:(s + 1) * 16, cnd], in0=modT[:, (3 * s + 2) * 16:(3 * s + 3) * 16, cnd],
                        scalar1=(1.0 if s == 1 else 0.5), scalar2=None, op0=ALU.mult),
                        reads=[("modT", l)], writes=[("modG", l)])
        S.barrier()


def load_x_tile(k, xres, x_in, t0, T):
    S = k.S
    for c in range(KC):
        S.add("sp", lambda e, c=c: e.dma_start(out=xres[:, c, :T], in_=x_in[c * 128:(c + 1) * 128, t0:t0 + T]),
              writes=[("xres", c)], dma=True)


def norm_mod(k, xres, hT, T, segs, l, s, tmp, rstd, sqb):
    S = k.S
    ps = k.ps[6]
    for c in range(KC):
        sq = sqb[c % 2]
        S.add("act", lambda e, c=c, sq=sq: e.activation(out=sq[:, :T], in_=xres[:, c, :T], func=AF.Square),
              reads=[("xres", c)], writes=[("sq", c % 2)])
        mm(S, ps[:, :T], k.ones_b[:], sq[:, :T], c == 0, c == KC - 1, [("sq", c % 2), "ones_b"], ("ps", 6))
    S.add("act", lambda e: e.activation(out=rstd[:, :T], in_=ps[:, :T], func=AF.Sqrt, scale=1.0 / D, bias=k.eps_t[:]),
          reads=[("ps", 6), "eps_t"], writes=["rstd"])
    S.add("dve", lambda e: e.reciprocal(out=rstd[:, :T], in_=rstd[:, :T]), reads=["rstd"], writes=["rstd"])
    A, SH = k.modA[l], k.modT[l]
    for c in range(KC):
        tb = tmp[c % 2]
        for (o, n, cnd) in segs:
            S.add("dve", lambda e, c=c, o=o, n=n, cnd=cnd, tb=tb: e.scalar_tensor_tensor(
                out=tb[:, o:o + n], in0=xres[:, c, o:o + n], scalar=A[:, s * 16 + c, cnd:cnd + 1], op0=ALU.mult,
                in1=rstd[:, o:o + n], op1=ALU.mult),
                reads=[("xres", c), "rstd", ("modA", l)], writes=[("tmp", c % 2)])
            S.add("act", lambda e, c=c, o=o, n=n, cnd=cnd, tb=tb: e.activation(
                out=hT[:, c, o:o + n], in_=tb[:, o:o + n], func=AF.Identity,
                bias=SH[:, 3 * s * 16 + c, cnd:cnd + 1], scale=1.0),
                reads=[("tmp", c % 2), ("modT", l)], writes=[("hT", c)])


def residual_store(k, ps_ap, pskey, xres, c, T, segs, l, s, x_out, t0):
    S = k.S
    G = k.modG[l]
    for (o, n, cnd) in segs:
        S.add("dve", lambda e, o=o, n=n, cnd=cnd: e.scalar_tensor_tensor(
            out=xres[:, c, o:o + n], in0=ps_ap[:, o:o + n], scalar=G[:, s * 16 + c, cnd:cnd + 1], op0=ALU.mult,
            in1=xres[:, c, o:o + n], op1=ALU.add),
            reads=[pskey, ("xres", c), ("modG", l)], writes=[("xres", c)])
    if x_out is not None:
        S.add("sp", lambda e: e.dma_start(out=x_out[c * 128:(c + 1) * 128, t0:t0 + T], in_=xres[:, c, :T]),
              reads=[("xres", c)], writes=[("xdram", x_out.tensor.name, t0, c)], dma=True)


def _norm_stats(k, xres, T, rstd, sqb, banks=(6, 7)):
    S = k.S
    groups = [(0, min(T, 512))] + ([(512, T - 512)] if T > 512 else [])
    for c in range(KC):
        sq = sqb[c % 2]
        S.add("act", lambda e, c=c, sq=sq: e.activation(out=sq[:, :T], in_=xres[:, c, :T], func=AF.Square),
              reads=[("xres", c)], writes=[("sq", c % 2)])
        for g, (o, n) in enumerate(groups):
            mm(S, k.ps[banks[g]][:, :n], k.ones_b[:], sq[:, o:o + n], c == 0, c == KC - 1, [("sq", c % 2), "ones_b"], ("ps", banks[g]))
    for g, (o, n) in enumerate(groups):
        S.add("act", lambda e, g=g, o=o, n=n: e.activation(out=rstd[:, o:o + n], in_=k.ps[banks[g]][:, :n], func=AF.Sqrt,
                                                          scale=1.0 / D, bias=k.eps_t[:]),
              reads=[("ps", banks[g]), "eps_t"], writes=["rstd"])
    S.add("dve", lambda e: e.reciprocal(out=rstd[:, :T], in_=rstd[:, :T]), reads=["rstd"], writes=["rstd"])


def phase_ffn(k, l, widx, x_in, x_out, tiles, final_out=None):
    nc, S = k.nc, k.S
    s = 0 if widx == 0 else 2
    tiles = [(([(t[0], t[1])], t[2]) if len(t) == 3 else t) for t in tiles]
    TM = max(sum(n for _, n in parts) for parts, _ in tiles)
    FH = 2 if TM > 512 else 1
    NF = NFC // FH
    SLW = 256
    nsel = len(k.ffn_sel)
    wi_ = k.ffn_sel[(l, widx)]
    w1 = k.dram("ffn_w1", [nsel, D, FFN])[wi_]
    w3 = k.dram("ffn_w3", [nsel, D, FFN])[wi_]
    w2 = k.dram("ffn_w2", [nsel, FFN, D])[wi_]
    UB = ((0, 1), (4, 5))
    VB = ((2, 3), (6, 7))
    YB = ((0, 1), (2, 3))
    with ExitStack() as st:
        al = k.alloc(st)
        xres = al("xres", [128, KC, TM], F32)
        hT = al("hT", [128, KC, TM], BF16)
        gT = al("gT", [128, NF, TM], BF16)
        tmp = [al("tmp%d" % i, [128, TM], F32) for i in range(2)]
        sqb = [al("sq%d" % i, [128, TM], BF16) for i in range(2)]
        rstd = al("rstd", [128, TM], F32)
        su = [al("su%d" % i, [128, TM], BF16) for i in range(2)]
        wa = [al("wa%d" % i, [128, KC, SLW], BF16) for i in range(2)]
        wb = [al("wb%d" % i, [128, KC, SLW], BF16) for i in range(2)]
        wc = [al("wc%d" % i, [128, NF, 256], BF16) for i in range(2)]
        na = nc_ = 0
        A, SH, G = k.modA[l], k.modT[l], k.modG[l]
        for (parts, segs) in tiles:
            T = sum(n for _, n in parts)
            groups = [(0, min(T, 512))] + ([(512, T - 512)] if T > 512 else [])
            offs = []
            o_ = 0
            for (c0, n) in parts:
                offs.append((o_, c0, n))
                o_ += n
            for c in range(KC):
                for (o, c0, n) in offs:
                    S.add("sp", lambda e, c=c, o=o, c0=c0, n=n: e.dma_start(out=xres[:, c, o:o + n], in_=x_in[c * 128:(c + 1) * 128, c0:c0 + n]),
                          writes=[("xres", c)], dma=True)
            _norm_stats(k, xres, T, rstd, sqb)
            for c in range(KC):
                tb = tmp[c % 2]
                for (o, n, cnd) in segs:
                    S.add("dve", lambda e, c=c, o=o, n=n, cnd=cnd, tb=tb: e.scalar_tensor_tensor(
                        out=tb[:, o:o + n], in0=xres[:, c, o:o + n], scalar=A[:, s * 16 + c, cnd:cnd + 1], op0=ALU.mult,
                        in1=rstd[:, o:o + n], op1=ALU.mult),
                        reads=[("xres", c), "rstd", ("modA", l)], writes=[("tmp", c % 2)])
                    S.add("act", lambda e, c=c, o=o, n=n, cnd=cnd, tb=tb: e.activation(
                        out=hT[:, c, o:o + n], in_=tb[:, o:o + n], func=AF.Identity,
                        bias=SH[:, 3 * s * 16 + c, cnd:cnd + 1], scale=1.0),
                        reads=[("tmp", c % 2), ("modT", l)], writes=[("hT", c)])
            for fh in range(FH):
                f0 = fh * NF
                for sl in range(NF * 128 // SLW):
                    a, b = wa[na % 2], wb[na % 2]
                    ka, kb = "wa%d" % (na % 2), "wb%d" % (na % 2)
                    na += 1
                    cc0 = f0 * 128 + sl * SLW
                    S.add("pool", lambda e, a=a, cc0=cc0: e.dma_start(
                        out=a[:], in_=w1[:, cc0:cc0 + SLW].rearrange("(kc p) n -> p kc n", p=128)), writes=[ka], dma=True)
                    S.add("pool", lambda e, b=b, cc0=cc0: e.dma_start(
                        out=b[:], in_=w3[:, cc0:cc0 + SLW].rearrange("(kc p) n -> p kc n", p=128)), writes=[kb], dma=True)
                    for jj in range(SLW // 128):
                        fc = sl * (SLW // 128) + jj
                        ub, vb = UB[fc % 2], VB[fc % 2]
                        for g, (o, n) in enumerate(groups):
                            for kc in range(KC):
                                mm(S, k.ps[ub[g]][:, :n], a[:, kc, jj * 128:(jj + 1) * 128], hT[:, kc, o:o + n], kc == 0, kc == KC - 1,
                                   [ka, ("hT", kc)], ("ps", ub[g]))
                        for g, (o, n) in enumerate(groups):
                            for kc in range(KC):
                                mm(S, k.ps[vb[g]][:, :n], b[:, kc, jj * 128:(jj + 1) * 128], hT[:, kc, o:o + n], kc == 0, kc == KC - 1,
                                   [kb, ("hT", kc)], ("ps", vb[g]))
                        sut = su[fc % 2]
                        for g, (o, n) in enumerate(groups):
                            S.add("act", lambda e, g=g, o=o, n=n, ub=ub, sut=sut: e.activation(
                                out=sut[:, o:o + n], in_=k.ps[ub[g]][:, :n], func=AF.Silu),
                                reads=[("ps", ub[g])], writes=[("su", fc % 2, g)])
                            S.add("dve", lambda e, g=g, o=o, n=n, vb=vb, sut=sut, fc=fc: e.tensor_tensor(
                                out=gT[:, fc, o:o + n], in0=k.ps[vb[g]][:, :n], in1=sut[:, o:o + n], op=ALU.mult),
                                reads=[("ps", vb[g]), ("su", fc % 2, g)], writes=[("gT", fc, g)])
                for ds in range(KC // 2):
                    w = wc[nc_ % 2]
                    kw = "wc%d" % (nc_ % 2)
                    nc_ += 1
                    S.add("pool", lambda e, w=w, ds=ds, f0=f0: e.dma_start(
                        out=w[:], in_=w2[f0 * 128:(f0 + NF) * 128, ds * 256:(ds + 1) * 256].rearrange("(fc p) n -> p fc n", p=128)),
                        writes=[kw], dma=True)
                    for jj in range(2):
                        c = ds * 2 + jj
                        yb = YB[c % 2]
                        for g, (o, n) in enumerate(groups):
                            for fc in range(NF):
                                mm(S, k.ps[yb[g]][:, :n], w[:, fc, jj * 128:(jj + 1) * 128], gT[:, fc, o:o + n], fc == 0, fc == NF - 1,
                                   [kw, ("gT", fc, g)], ("ps", yb[g]))
                        for (o, n, cnd) in segs:
                            g = 0 if o < 512 else 1
                            lo = o - groups[g][0]
                            S.add("dve", lambda e, c=c, o=o, n=n, cnd=cnd, g=g, lo=lo, yb=yb: e.scalar_tensor_tensor(
                                out=xres[:, c, o:o + n], in0=k.ps[yb[g]][:, lo:lo + n], scalar=G[:, s * 16 + c, cnd:cnd + 1], op0=ALU.mult,
                                in1=xres[:, c, o:o + n], op1=ALU.add),
                                reads=[("ps", yb[g]), ("xres", c), ("modG", l)], writes=[("xres", c)])
                        if x_out is not None and fh == FH - 1:
                            for (o, c0, n) in offs:
                                S.add("sp", lambda e, c=c, o=o, c0=c0, n=n: e.dma_start(
                                    out=x_out[c * 128:(c + 1) * 128, c0:c0 + n], in_=xres[:, c, o:o + n]),
                                    reads=[("xres", c)], writes=[("xdram", x_out.tensor.name, c0, c)], dma=True)
            if final_out is not None:
                final_norm_store(k, xres, T, parts[0][0], final_out, tmp, rstd, sqb)
        S.barrier()


def final_norm_store(k, xres, T, t0, out, tmp, rstd, sqb):
    S = k.S
    ps = k.ps[6]
    fn = k.fnT
    for c in range(KC):
        sq = sqb[c % 2]
        S.add("act", lambda e, c=c, sq=sq: e.activation(out=sq[:, :T], in_=xres[:, c, :T], func=AF.Square),
              reads=[("xres", c)], writes=[("sq", c % 2)])
        mm(S, ps[:, :T], k.ones_b[:], sq[:, :T], c == 0, c == KC - 1, [("sq", c % 2), "ones_b"], ("ps", 6))
    S.add("act", lambda e: e.activation(out=rstd[:, :T], in_=ps[:, :T], func=AF.Sqrt, scale=1.0 / D, bias=k.eps_t[:]),
          reads=[("ps", 6), "eps_t"], writes=["rstd"])
    S.add("dve", lambda e: e.reciprocal(out=rstd[:, :T], in_=rstd[:, :T]), reads=["rstd"], writes=["rstd"])
    for c in range(KC):
        S.add("dve", lambda e, c=c: e.scalar_tensor_tensor(
            out=xres[:, c, :T], in0=xres[:, c, :T], scalar=fn[:, c:c + 1], op0=ALU.mult, in1=rstd[:, :T], op1=ALU.mult),
            reads=[("xres", c), "rstd", "fnT"], writes=[("xres", c)])
        S.add("sp", lambda e, c=c: e.dma_start(out=out[c * 128:(c + 1) * 128, t0:t0 + T], in_=xres[:, c, :T]),
              reads=[("xres", c)], writes=[("odram", t0, c)], dma=True)


GC1, GC2 = 0.044715, 1.5957691216057308


def gelu_from_psum(k, ps_ap, pskey, out_ap, outkeys, n, t1, t2, t1k, t2k):
    S = k.S
    S.add("act", lambda e: e.activation(out=t1, in_=ps_ap, func=AF.Square), reads=[pskey], writes=[t1k])
    S.add("dve", lambda e: e.tensor_scalar(out=t1, in0=t1, scalar1=GC1, scalar2=1.0, op0=ALU.mult, op1=ALU.add),
          reads=[t1k], writes=[t1k])
    S.add("dve", lambda e: e.tensor_tensor(out=t1, in0=ps_ap, in1=t1, op=ALU.mult), reads=[pskey, t1k], writes=[t1k])
    S.add("act", lambda e: e.activation(out=t2, in_=t1, func=AF.Sigmoid, scale=GC2), reads=[t1k], writes=[t2k])
    S.add("dve", lambda e: e.tensor_tensor(out=out_ap, in0=ps_ap, in1=t2, op=ALU.mult), reads=[pskey, t2k], writes=outkeys)


def rope_from_psum(k, ps_ap, pskey, ps2, ps2key, cos, sin, T, out_ap, outkeys, qraw, t1, t2, keys):
    S = k.S
    qk, t1k, t2k = keys
    S.add("act", lambda e: e.activation(out=qraw[:, :T], in_=ps_ap, func=AF.Identity), reads=[pskey], writes=[qk])
    S.add("dve", lambda e: e.tensor_tensor(out=t1[:, :T], in0=ps_ap, in1=cos, op=ALU.mult), reads=[pskey, "rope"], writes=[t1k])
    mm(S, ps2[:, :T], (k.ones_b if DBG_NOPERM else k.permM)[:], qraw[:, :T], True, True, [qk, "permM"], ps2key)
    S.add("dve", lambda e: e.tensor_tensor(out=t2[:, :T], in0=ps2[:, :T], in1=sin, op=ALU.mult), reads=[ps2key, "rope"], writes=[t2k])
    S.add(ROPE_ADD_ENG, lambda e: e.tensor_tensor(out=out_ap, in0=t1[:, :T], in1=t2[:, :T], op=ALU.add), reads=[t1k, t2k], writes=outkeys)


def load_rope_consts(k):
    nc, S = k.nc, k.S
    permD = k.dram("permM", [128, 128])
    k.permM = k.sb("permM_sb", [128, 128], BF16)
    S.add("pool", lambda e: e.dma_start(out=k.permM[:], in_=permD), writes=["permM"], dma=True)


def phase_ab_in(k, x_in, tiles):
    nc, S = k.nc, k.S
    l, s = 0, 1
    w_in = k.dram("ab_w_in", [1, D, 3328])[0]
    cosD, sinD = k.dram("cosT", [128, NT]), k.dram("sinT", [128, NT])
    qT = k.dram("qT", [1024, NT], BF16)
    kT = k.dram("kT", [2, 128, NT], BF16)
    vtok = k.dram("vtok", [NT, 128], BF16)
    guT = k.dram("guT", [1024, NT], BF16)
    gvtok = k.dram("gvtok", [NT, 1024], BF16)
    with ExitStack() as st:
        al = k.alloc(st)
        xres = al("xres", [128, KC, 512], F32)
        hT = al("hT", [128, KC, 512], BF16)
        tmp = [al("tmp%d" % i, [128, 512], F32) for i in range(2)]
        sqb = [al("sq%d" % i, [128, 512], BF16) for i in range(2)]
        rstd = al("rstd", [128, 512], F32)
        cos, sin = al("cos", [128, 512], F32), al("sin", [128, 512], F32)
        wtm = al("wtm", [128, KC, 1152], BF16)
        kdw = al("kdw", [128, KC, 256], BF16)
        wa = [al("wa%d" % i, [128, KC, 256], BF16) for i in range(2)]
        qraw = [al("qraw%d" % i, [128, 512], BF16) for i in range(2)]
        r1 = [al("r1_%d" % i, [128, 512], F32) for i in range(2)]
        r2 = [al("r2_%d" % i, [128, 512], F32) for i in range(2)]
        ob = [al("ob%d" % i, [128, 512], BF16) for i in range(2)]
        obt = [al("obt%d" % i, [128, 1152], BF16) for i in range(2)]
        for i, (c0, n) in enumerate([(1152, 128), (2304, 512), (2816, 512)]):
            o = [0, 128, 640][i]
            S.add("pool", lambda e, c0=c0, n=n, o=o: e.dma_start(
                out=wtm[:, :, o:o + n], in_=w_in[:, c0:c0 + n].rearrange("(kc p) n -> p kc n", p=128)),
                writes=[("wtm", i)], dma=True)
        for i in range(4):
            c0 = 1024 + 64 * (i // 2)
            S.add("pool", lambda e, c0=c0, i=i: e.dma_start(
                out=kdw[:, :, i * 64:(i + 1) * 64], in_=w_in[:, c0:c0 + 64].rearrange("(kc p) n -> p kc n", p=128)),
                writes=[("kdw", i)], dma=True)
        na = 0
        cnt = 0
        for (t0, T, segs) in tiles:
            load_x_tile(k, xres, x_in, t0, T)
            S.add("sp", lambda e, t0=t0, T=T: e.dma_start(out=cos[:, :T], in_=cosD[:, t0:t0 + T]), writes=["rope"], dma=True)
            S.add("sp", lambda e, t0=t0, T=T: e.dma_start(out=sin[:, :T], in_=sinD[:, t0:t0 + T]), writes=["rope"], dma=True)
            norm_mod(k, xres, hT, T, segs, l, s, tmp, rstd, sqb)
            hreads = [("hT", c) for c in range(KC)]
            for sl in range(8):
                if DBG_SECT is not None and ("q" if sl < 4 else "gu") not in DBG_SECT:
                    continue
                a = wa[na % 2]
                ka = "wa%d" % (na % 2)
                na += 1
                c0 = sl * 256 if sl < 4 else 1280 + (sl - 4) * 256
                S.add("pool", lambda e, a=a, c0=c0: e.dma_start(
                    out=a[:], in_=w_in[:, c0:c0 + 256].rearrange("(kc p) n -> p kc n", p=128)), writes=[ka], dma=True)
                for jj in range(2):
                    ch = (sl % 4) * 2 + jj
                    i2 = cnt % 2
                    cnt += 1
                    pq = k.ps[i2]
                    for kc in range(KC):
                        mm(S, pq[:, :T], a[:, kc, jj * 128:(jj + 1) * 128], hT[:, kc, :T], kc == 0, kc == KC - 1,
                           [ka, ("hT", kc)], ("ps", i2))
                    o_ = ob[i2]
                    if sl < 4:
                        rope_from_psum(k, pq[:, :T], ("ps", i2), k.ps[2 + i2], ("ps", 2 + i2), cos[:, :T], sin[:, :T], T,
                                       o_[:, :T], [("ob", i2)], qraw[i2], r1[i2], r2[i2], (("qraw", i2), ("r1", i2), ("r2", i2)))
                        dst = qT[ch * 128:(ch + 1) * 128, t0:t0 + T]
                    else:
                        gelu_from_psum(k, pq[:, :T], ("ps", i2), o_[:, :T], [("ob", i2)], T, r1[i2][:, :T], r2[i2][:, :T],
                                       ("r1", i2), ("r2", i2))
                        dst = guT[ch * 128:(ch + 1) * 128, t0:t0 + T]
                    S.add("sp", lambda e, dst=dst, o_=o_, T=T: e.dma_start(out=dst, in_=o_[:, :T]),
                          reads=[("ob", i2)], writes=[("dr", dst.tensor.name, t0, ch)], dma=True)
            for kv in range(2):
                if DBG_SECT is not None and "k" not in DBG_SECT:
                    continue
                i2 = cnt % 2
                cnt += 1
                pq = k.ps[i2]
                for kc in range(KC):
                    mm(S, pq[:, :T], kdw[:, kc, kv * 128:(kv + 1) * 128], hT[:, kc, :T], kc == 0, kc == KC - 1,
                       [("kdw", 2 * kv), ("kdw", 2 * kv + 1), ("hT", kc)], ("ps", i2))
                o_ = ob[i2]
                rope_from_psum(k, pq[:, :T], ("ps", i2), k.ps[2 + i2], ("ps", 2 + i2), cos[:, :T], sin[:, :T], T,
                               o_[:, :T], [("ob", i2)], qraw[i2], r1[i2], r2[i2], (("qraw", i2), ("r1", i2), ("r2", i2)))
                dst = kT[kv, :, t0:t0 + T]
                S.add("sp", lambda e, dst=dst, o_=o_, T=T: e.dma_start(out=dst, in_=o_[:, :T]),
                      reads=[("ob", i2)], writes=[("dr", "kT", t0, kv)], dma=True)
            for tb in range(T // 128):
                if DBG_SECT is not None and "tok" not in DBG_SECT:
                    continue
                ot = obt[tb % 2]
                okey = ("obt", tb % 2)
                pv = k.ps[4 + tb % 2]
                for kc in range(KC):
                    mm(S, pv[:, :128], hT[:, kc, tb * 128:(tb + 1) * 128], wtm[:, kc, 0:128], kc == 0, kc == KC - 1,
                       [("wtm", 0), ("hT", kc)], ("ps", 4 + tb % 2))
                S.add("act", lambda e, pv=pv, ot=ot: e.activation(out=ot[:, 0:128], in_=pv[:, :128], func=AF.Identity),
                      reads=[("ps", 4 + tb % 2)], writes=[okey])
                for hf in range(2):
                    pg = k.ps[6 + hf]
                    for kc in range(KC):
                        mm(S, pg[:, :], hT[:, kc, tb * 128:(tb + 1) * 128], wtm[:, kc, 128 + hf * 512:128 + (hf + 1) * 512],
                           kc == 0, kc == KC - 1, [("wtm", 1 + hf), ("hT", kc)], ("ps", 6 + hf))
                    gelu_from_psum(k, pg[:, :], ("ps", 6 + hf), ot[:, 128 + hf * 512:128 + (hf + 1) * 512], [okey], 512,
                                   r1[hf][:, :], r2[hf][:, :], ("r1", hf), ("r2", hf))
                r0 = t0 + tb * 128
                S.add("sp", lambda e, ot=ot, r0=r0: e.dma_start(out=vtok[r0:r0 + 128, :], in_=ot[:, 0:128]),
                      reads=[okey], writes=[("dr", "vtok", r0)], dma=True)
                S.add("sp", lambda e, ot=ot, r0=r0: e.dma_start(out=gvtok[r0:r0 + 128, :], in_=ot[:, 128:1152]),
                      reads=[okey], writes=[("dr", "gvtok", r0)], dma=True)
        S.barrier()


def phase_ab_mix(k):
    nc, S = k.nc, k.S
    qT = k.dram("qT", [1024, NT], BF16)
    kT = k.dram("kT", [2, 128, NT], BF16)
    vtok = k.dram("vtok", [NT, 128], BF16)
    guT = k.dram("guT", [1024, NT], BF16)
    gvtok = k.dram("gvtok", [NT, 1024], BF16)
    oT = k.dram("oT", [2048, NT], BF16)
    masksD = k.dram("masks", [128, 4, 128])
    sinkD = k.dram("sinkT", [128, 8])
    wsD = k.dram("b_wsT", [128, 8, 128])
    bbD = k.dram("b_biasbc", [128, 8, 128])
    scale = 1.0 / 8.0
    with ExitStack() as st:
        al = k.alloc(st)
        kTs = al("kTs", [128, 2, NT], BF16)
        vts = al("vts", [128, 20, 128], BF16)
        masks = al("masks", [128, 4, 128], BF16)
        sink = al("sink", [128, 8], F32)
        esbc = al("esbc", [128, 8, 128], F32)
        wsT = al("wsT", [128, 8, 128], BF16)
        bbc = al("bbc", [128, 8, 128], F32)
        qb_ = [al("qb%d" % i, [128, 8, 128], BF16) for i in range(2)]
        gub = [al("gub%d" % i, [128, 8, 128], BF16) for i in range(2)]
        gvb = [al("gvb%d" % i, [128, 1024], BF16) for i in range(2)]
        pT = [al("pT%d" % i, [128, 2, 4, 128], BF16) for i in range(3)]
        rr = al("rr", [128, 512], F32)
        gt = al("gt", [128, 512], F32)
        ob = [al("oblk%d" % i, [128, 16, 128], BF16) for i in range(2)]
        S.add("sp", lambda e: e.dma_start(out=kTs[:], in_=kT.rearrange("v p t -> p v t")), writes=["kTs"], dma=True)
        S.add("sp", lambda e: e.dma_start(out=vts[:], in_=vtok.rearrange("(b p) d -> p b d", p=128)), writes=["vts"], dma=True)
        S.add("pool", lambda e: e.dma_start(out=masks[:], in_=masksD), writes=["masks"], dma=True)
        S.add("pool", lambda e: e.dma_start(out=wsT[:], in_=wsD), writes=["wsT"], dma=True)
        S.add("sp", lambda e: e.dma_start(out=bbc[:], in_=bbD), writes=["bbc"], dma=True)
        S.add("sp", lambda e: e.dma_start(out=sink[:], in_=sinkD), writes=["sink"], dma=True)
        S.add("act", lambda e: e.activation(out=sink[:], in_=sink[:], func=AF.Exp), reads=["sink"], writes=["sink"])
        S.add("dve", lambda e: e.tensor_copy(out=esbc[:], in_=sink[:].unsqueeze(2).broadcast_to([128, 8, 128])),
              reads=["sink"], writes=["esbc"])
        blocks = list(range(16)) + [18, 19]
        npT = 0
        for bi, b in enumerate(blocks):
            i2 = bi % 2
            c0 = b * 128
            qb, gu, gv, o_ = qb_[i2], gub[i2], gvb[i2], ob[i2]
            S.add("sp", lambda e, qb=qb, c0=c0: e.dma_start(out=qb[:], in_=qT[:, c0:c0 + 128].rearrange("(c p) t -> p c t", p=128)),
                  writes=[("qb", i2)], dma=True)
            S.add("sp", lambda e, gu=gu, c0=c0: e.dma_start(out=gu[:], in_=guT[:, c0:c0 + 128].rearrange("(c p) t -> p c t", p=128)),
                  writes=[("gub", i2)], dma=True)
            S.add("sp", lambda e, gv=gv, c0=c0: e.dma_start(out=gv[:], in_=gvtok[c0:c0 + 128, :]), writes=[("gvb", i2)], dma=True)
            if b < 16:
                kl = [((b - 1) if b > 0 else 16, 0 if b > 0 else 2), (b, None), ((b + 1) if b < 15 else 17, 1 if b < 15 else 3),
                      (18, None), (19, None)]
            else:
                kl = [(18, None), (19, None)]
            for kv in range(2):
                po, pd = k.ps[4], k.ps[5]
                pend = {}
                for ji in range(len(kl) + 1):
                    if ji < len(kl):
                        kb, mi = kl[ji]
                        pa, pb = k.ps[2 * (ji % 2)], k.ps[2 * (ji % 2) + 1]
                        ka, kb_ = ("ps", 2 * (ji % 2)), ("ps", 2 * (ji % 2) + 1)
                        mm(S, pa[:, :], kTs[0:64, kv, kb * 128:(kb + 1) * 128], qb[0:64, kv * 4:(kv + 1) * 4, :], True, True,
                           ["kTs", ("qb", i2)], ka)
                        mm(S, pb[:, :], kTs[64:128, kv, kb * 128:(kb + 1) * 128], qb[64:128, kv * 4:(kv + 1) * 4, :], True, True,
                           ["kTs", ("qb", i2)], kb_)
                        p = pT[npT % 3]
                        pk = ("pT", npT % 3)
                        npT += 1
                        S.add("act", lambda e, p=p, pa=pa: e.activation(out=p[:, 0, :, :], in_=pa[:, :].rearrange("p (c t) -> p c t", c=4),
                                                                        func=AF.Exp, scale=scale), reads=[ka], writes=[pk])
                        S.add("act", lambda e, p=p, pb=pb: e.activation(out=p[:, 1, :, :], in_=pb[:, :].rearrange("p (c t) -> p c t", c=4),
                                                                        func=AF.Exp, scale=scale), reads=[kb_], writes=[pk])
                        if mi is not None:
                            S.add("dve", lambda e, p=p, mi=mi: e.tensor_tensor(
                                out=p[:].rearrange("p e c t -> p (e c) t"), in0=p[:].rearrange("p e c t -> p (e c) t"),
                                in1=masks[:, mi:mi + 1, :].broadcast_to([128, 8, 128]), op=ALU.mult),
                                reads=[pk, "masks"], writes=[pk])
                        pend[ji] = (p, pk, kb)
                    jj_ = ji - 1
                    if jj_ >= 0:
                        p, pk, kb = pend.pop(jj_)
                        first, last = jj_ == 0, jj_ == len(kl) - 1
                        for e_ in range(2):
                            mm(S, po[64 * e_:64 * e_ + 64, :], vts[:, kb, kv * 64:(kv + 1) * 64], p[:, e_, :, :], first, last,
                               ["vts", pk], ("ps", 4))
                            mm(S, pd[64 * e_:64 * e_ + 64, :], k.ones_b[:, 0:64], p[:, e_, :, :], first, last,
                               ["ones_b", pk], ("ps", 5))
                S.add("dve", lambda e, pd=pd, kv=kv: e.tensor_tensor(
                    out=rr[:].rearrange("p (c t) -> p c t", c=4), in0=pd[:, :].rearrange("p (c t) -> p c t", c=4),
                    in1=esbc[:, kv * 4:(kv + 1) * 4, :], op=ALU.add), reads=[("ps", 5), "esbc"], writes=["rr"])
                S.add("dve", lambda e: e.reciprocal(out=rr[:], in_=rr[:]), reads=["rr"], writes=["rr"])
                S.add("dve", lambda e, po=po, kv=kv, o_=o_: e.tensor_tensor(
                    out=o_[:, kv * 4:(kv + 1) * 4, :], in0=po[:, :].rearrange("p (c t) -> p c t", c=4),
                    in1=rr[:].rearrange("p (c t) -> p c t", c=4), op=ALU.mult), reads=[("ps", 4), "rr"], writes=[("oblk", i2)])
            for hf in range(2):
                pg = k.ps[6 + hf]
                for gg in range(4):
                    g = hf * 4 + gg
                    mm(S, pg[:, gg * 128:(gg + 1) * 128], gv[:, g * 128:(g + 1) * 128], wsT[:, g, :], True, True,
                       [("gvb", i2), "wsT"], ("ps", 6 + hf))
                S.add("dve", lambda e, pg=pg, hf=hf: e.tensor_tensor(
                    out=gt[:].rearrange("p (c t) -> p c t", c=4), in0=pg[:, :].rearrange("p (c t) -> p c t", c=4),
                    in1=bbc[:, hf * 4:(hf + 1) * 4, :], op=ALU.add), reads=[("ps", 6 + hf), "bbc"], writes=["gt"])
                S.add("dve", lambda e, hf=hf, o_=o_, gu=gu: e.tensor_tensor(
                    out=o_[:, 8 + hf * 4:8 + (hf + 1) * 4, :], in0=gt[:].rearrange("p (c t) -> p c t", c=4),
                    in1=gu[:, hf * 4:(hf + 1) * 4, :], op=ALU.mult), reads=["gt", ("gub", i2)], writes=[("oblk", i2)])
            S.add("sp", lambda e, o_=o_, c0=c0: e.dma_start(out=oT[:, c0:c0 + 128].rearrange("(c p) t -> p c t", p=128), in_=o_[:]),
                  reads=[("oblk", i2)], writes=[("dr", "oT", c0)], dma=True)
        S.barrier()


def phase_outproj(k, l, wname, oT, x_in, x_out, tiles):
    nc, S = k.nc, k.S
    w = k.dram(wname, [1, D, D])[0]
    with ExitStack() as st:
        al = k.alloc(st)
        xres = al("xres", [128, KC, 512], F32)
        hT = al("hT", [128, KC, 512], BF16)
        wa = [al("wa%d" % i, [128, KC, 256], BF16) for i in range(2)]
        na = 0
        for (t0, T, segs) in tiles:
            load_x_tile(k, xres, x_in, t0, T)
            for c in range(KC):
                S.add("sp", lambda e, c=c, t0=t0, T=T: e.dma_start(out=hT[:, c, :T], in_=oT[c * 128:(c + 1) * 128, t0:t0 + T]),
                      writes=[("hT", c)], dma=True)
            for sl in range(KC // 2):
                a = wa[na % 2]
                ka = "wa%d" % (na % 2)
                na += 1
                S.add("pool", lambda e, a=a, sl=sl: e.dma_start(
                    out=a[:], in_=w[:, sl * 256:(sl + 1) * 256].rearrange("(kc p) n -> p kc n", p=128)), writes=[ka], dma=True)
                for jj in range(2):
                    c = sl * 2 + jj
                    py = k.ps[c % 2]
                    for kc in range(KC):
                        mm(S, py[:, :T], a[:, kc, jj * 128:(jj + 1) * 128], hT[:, kc, :T], kc == 0, kc == KC - 1,
                           [ka, ("hT", kc)], ("ps", c % 2))
                    residual_store(k, py, ("ps", c % 2), xres, c, T, segs, l, 1, x_out, t0)
        S.barrier()


NK = SEQ + NCTX
NKC = NK // 128


def lat_norm(k, src, nch, T, gain, dst, rstd, sqb, pskey_i):
    S = k.S
    ps = k.ps[pskey_i]
    for c in range(nch):
        sq = sqb[c % 2]
        S.add("act", lambda e, c=c, sq=sq: e.activation(out=sq[:, :T], in_=src[:, c, :T], func=AF.Square),
              reads=[("lsrc", c)], writes=[("sq", c % 2)])
        mm(S, ps[:, :T], k.ones_b[:], sq[:, :T], c == 0, c == nch - 1, [("sq", c % 2), "ones_b"], ("ps", pskey_i))
    S.add("act", lambda e: e.activation(out=rstd[:, :T], in_=ps[:, :T], func=AF.Sqrt, scale=1.0 / (nch * 128), bias=k.eps_t[:]),
          reads=[("ps", pskey_i), "eps_t"], writes=["rstd2"])
    S.add("dve", lambda e: e.reciprocal(out=rstd[:, :T], in_=rstd[:, :T]), reads=["rstd2"], writes=["rstd2"])
    for c in range(nch):
        S.add("dve", lambda e, c=c: e.scalar_tensor_tensor(
            out=dst[:, c, :T], in0=src[:, c, :T], scalar=gain[:, c:c + 1], op0=ALU.mult, in1=rstd[:, :T], op1=ALU.mult),
            reads=[("lsrc", c), "rstd2", "gains"], writes=[("ldst", c)])


def phase_cd_in(k, x_in, tiles):
    nc, S = k.nc, k.S
    l, s = 1, 1
    w_in = k.dram("cd_w_in", [1, D, 4416])[0]
    w_uq = k.dram("c_w_uq", [1, 768, 1536])[0]
    cosD, sinD = k.dram("cosT", [128, NT]), k.dram("sinT", [128, NT])
    qgD, kgD = k.dram("c_qnT", [128, 6]), k.dram("c_kvnT", [128, 4])
    qnT = k.dram("qnT", [1024, NOWN], BF16)
    qrT = k.dram("qrT", [512, NOWN], BF16)
    xin = [k.dram("xch_in%d" % i, [128, NOWN], BF16) for i in range(5)]
    ctxkv = k.dram("ctxkv", [640, NCTX], BF16)
    zb = k.dram("zb_in", [2, 1024])
    dbT = k.dram("dbT", [1024, NOWN], BF16)
    zT = k.dram("zT", [1024, NOWN])
    with ExitStack() as st:
        al = k.alloc(st)
        xres = al("xres", [128, KC, 512], F32)
        hT = al("hT", [128, KC, 512], BF16)
        tmp = [al("tmp%d" % i, [128, 512], F32) for i in range(2)]
        sqb = [al("sq%d" % i, [128, 512], BF16) for i in range(2)]
        rstd = al("rstd", [128, 512], F32)
        rstd2 = al("rstd2", [128, 512], F32)
        cos, sin = al("cos", [128, 512], F32), al("sin", [128, 512], F32)
        lat = al("lat", [128, 6, 512], F32)
        latn = al("latn", [128, 6, 512], BF16)
        qg, kg = al("qg", [128, 6], F32), al("kg", [128, 4], F32)
        wuq = al("wuq", [128, 6, 1536], BF16)
        wqr = al("wqr", [128, 6, 4, 128], BF16)
        krw = al("krw", [128, KC, 128], BF16)
        wa = [al("wa%d" % i, [128, KC, 256], BF16) for i in range(2)]
        qraw = [al("qraw%d" % i, [128, 512], BF16) for i in range(2)]
        r1 = [al("r1_%d" % i, [128, 512], F32) for i in range(2)]
        r2 = [al("r2_%d" % i, [128, 512], F32) for i in range(2)]
        ob = [al("ob%d" % i, [128, 512], BF16) for i in range(2)]
        zo = [al("zo%d" % i, [128, 512], F32) for i in range(2)]
        dcs = [al("dcs%d" % i, [128, 512], F32) for i in range(2)]
        S.add("sp", lambda e: e.dma_start(out=qg[:], in_=qgD), writes=["gains"], dma=True)
        S.add("sp", lambda e: e.dma_start(out=kg[:], in_=kgD), writes=["gains"], dma=True)
        for i in range(3):
            S.add("pool", lambda e, i=i: e.dma_start(
                out=wuq[:, :, i * 512:(i + 1) * 512], in_=w_uq[:, i * 512:(i + 1) * 512].rearrange("(kc p) n -> p kc n", p=128)),
                writes=[("wuq", i)], dma=True)
        for h in range(8):
            S.add("pool", lambda e, h=h: e.dma_start(
                out=wqr[:, :, h // 2, (h % 2) * 64:(h % 2) * 64 + 64],
                in_=w_uq[:, h * 192 + 128:h * 192 + 192].rearrange("(kc p) n -> p kc n", p=128)),
                writes=[("wqr", h)], dma=True)
        for i in range(2):
            S.add("pool", lambda e, i=i: e.dma_start(
                out=krw[:, :, i * 64:(i + 1) * 64], in_=w_in[:, 1280:1344].rearrange("(kc p) n -> p kc n", p=128)),
                writes=[("krw", i)], dma=True)
        wuq_r = [("wuq", i) for i in range(3)]
        wqr_r = [("wqr", h) for h in range(8)]
        na = 0
        cnt = 0
        for (t0, T, segs) in tiles:
            own = t0 < NOWN
            oc0 = t0 if own else 0
            kvd = (lambda i: xin[i]) if own else (lambda i: ctxkv[i * 128:(i + 1) * 128, :])
            load_x_tile(k, xres, x_in, t0, T)
            S.add("sp", lambda e, t0=t0, T=T: e.dma_start(out=cos[:, :T], in_=cosD[:, t0:t0 + T]), writes=["rope"], dma=True)
            S.add("sp", lambda e, t0=t0, T=T: e.dma_start(out=sin[:, :T], in_=sinD[:, t0:t0 + T]), writes=["rope"], dma=True)
            norm_mod(k, xres, hT, T, segs, l, s, tmp, rstd, sqb)

            def proj_chunk(a, ka, jj):
                nonlocal cnt
                i2 = cnt % 2
                cnt += 1
                pq = k.ps[i2]
                for kc in range(KC):
                    mm(S, pq[:, :T], a[:, kc, jj * 128:(jj + 1) * 128], hT[:, kc, :T], kc == 0, kc == KC - 1,
                       [ka, ("hT", kc)] if not isinstance(ka, list) else ka + [("hT", kc)], ("ps", i2))
                return pq, i2

            def slab(c0):
                nonlocal na
                a = wa[na % 2]
                ka = "wa%d" % (na % 2)
                na += 1
                S.add("pool", lambda e: e.dma_start(
                    out=a[:], in_=w_in[:, c0:c0 + 256].rearrange("(kc p) n -> p kc n", p=128)), writes=[ka], dma=True)
                return a, ka

            if own:
                for sl in range(3):
                    a, ka = slab(sl * 256)
                    for jj in range(2):
                        c = sl * 2 + jj
                        pq, i2 = proj_chunk(a, ka, jj)
                        S.add("act", lambda e, pq=pq, c=c: e.activation(out=lat[:, c, :T], in_=pq[:, :T], func=AF.Identity),
                              reads=[("ps", i2)], writes=[("lsrc", c)])
                lat_norm(k, lat, 6, T, qg, latn, rstd2, sqb, 6)
                lr = [("ldst", c) for c in range(6)]
                for h in range(8):
                    i2 = cnt % 2
                    cnt += 1
                    pq = k.ps[i2]
                    for kc in range(6):
                        mm(S, pq[:, :T], wuq[:, kc, h * 192:h * 192 + 128], latn[:, kc, :T], kc == 0, kc == 5,
                           wuq_r + [("ldst", kc)], ("ps", i2))
                    o_ = ob[i2]
                    S.add("act", lambda e, pq=pq, o_=o_: e.activation(out=o_[:, :T], in_=pq[:, :T], func=AF.Identity),
                          reads=[("ps", i2)], writes=[("ob", i2)])
                    S.add("sp", lambda e, o_=o_, h=h: e.dma_start(out=qnT[h * 128:(h + 1) * 128, t0:t0 + T], in_=o_[:, :T]),
                          reads=[("ob", i2)], writes=[("dr", "qnT", t0, h)], dma=True)
                for j in range(4):
                    i2 = cnt % 2
                    cnt += 1
                    pq = k.ps[i2]
                    for kc in range(6):
                        mm(S, pq[:, :T], wqr[:, kc, j, :], latn[:, kc, :T], kc == 0, kc == 5, wqr_r + [("ldst", kc)], ("ps", i2))
                    o_ = ob[i2]
                    rope_from_psum(k, pq[:, :T], ("ps", i2), k.ps[2 + i2], ("ps", 2 + i2), cos[:, :T], sin[:, :T], T,
                                   o_[:, :T], [("ob", i2)], qraw[i2], r1[i2], r2[i2], (("qraw", i2), ("r1", i2), ("r2", i2)))
                    S.add("sp", lambda e, o_=o_, j=j: e.dma_start(out=qrT[j * 128:(j + 1) * 128, t0:t0 + T], in_=o_[:, :T]),
                          reads=[("ob", i2)], writes=[("dr", "qrT", t0, j)], dma=True)
            for sl in range(2):
                a, ka = slab(768 + sl * 256)
                for jj in range(2):
                    c = sl * 2 + jj
                    pq, i2 = proj_chunk(a, ka, jj)
                    S.add("act", lambda e, pq=pq, c=c: e.activation(out=lat[:, c, :T], in_=pq[:, :T], func=AF.Identity),
                          reads=[("ps", i2)], writes=[("lsrc", c)])
            lat_norm(k, lat, 4, T, kg, latn, rstd2, sqb, 6)
            for c in range(4):
                S.add("sp", lambda e, c=c: e.dma_start(out=kvd(c)[:, oc0:oc0 + T], in_=latn[:, c, :T]),
                      reads=[("ldst", c)], writes=[("dr", "ckvn", t0, c)], dma=True)
            pq, i2 = proj_chunk(krw, [("krw", 0), ("krw", 1)], 0)
            o_ = ob[i2]
            rope_from_psum(k, pq[:, :T], ("ps", i2), k.ps[2 + i2], ("ps", 2 + i2), cos[:, :T], sin[:, :T], T,
                           o_[:, :T], [("ob", i2)], qraw[i2], r1[i2], r2[i2], (("qraw", i2), ("r1", i2), ("r2", i2)))
            S.add("sp", lambda e, o_=o_: e.dma_start(out=kvd(4)[:, oc0:oc0 + T], in_=o_[:, :T]),
                  reads=[("ob", i2)], writes=[("dr", "krT", t0)], dma=True)
            if own:
                for sl in range(4):
                    a, ka = slab(1344 + sl * 256)
                    for jj in range(2):
                        c = sl * 2 + jj
                        pq, i2 = proj_chunk(a, ka, jj)
                        o_ = ob[i2]
                        S.add("act", lambda e, pq=pq, o_=o_: e.activation(out=o_[:, :T], in_=pq[:, :T], func=AF.Identity),
                              reads=[("ps", i2)], writes=[("ob", i2)])
                        S.add("sp", lambda e, o_=o_, c=c: e.dma_start(out=dbT[c * 128:(c + 1) * 128, t0:t0 + T], in_=o_[:, :T]),
                              reads=[("ob", i2)], writes=[("dr", "dbT", t0, c)], dma=True)
                for sl in range(4):
                    a, ka = slab(2368 + sl * 256)
                    a2, ka2 = slab(3392 + sl * 256)
                    for jj in range(2):
                        c = sl * 2 + jj
                        pq, i2 = proj_chunk(a, ka, jj)
                        dc_ = dcs[i2]
                        S.add("act", lambda e, pq=pq, dc_=dc_: e.activation(out=dc_[:, :T], in_=pq[:, :T], func=AF.Identity),
                              reads=[("ps", i2)], writes=[("dcs", i2)])
                        pq2, j2 = proj_chunk(a2, ka2, jj)
                        z_ = zo[i2]
                        S.add("dve", lambda e, pq2=pq2, dc_=dc_, z_=z_: e.tensor_tensor(
                            out=z_[:, :T], in0=pq2[:, :T], in1=dc_[:, :T], op=ALU.mult),
                            reads=[("ps", j2), ("dcs", i2)], writes=[("zo", i2)])
                        S.add("sp", lambda e, z_=z_, c=c: e.dma_start(out=zT[c * 128:(c + 1) * 128, t0:t0 + T], in_=z_[:, :T]),
                              reads=[("zo", i2)], writes=[("dr", "zT", t0, c)], dma=True)
                        for (tt, which, col) in ((0, 0, 0), (NOWN - 512, 1, 511)):
                            if t0 == tt:
                                S.add("sp", lambda e, z_=z_, c=c, which=which, col=col: e.dma_start(
                                    out=zb[which:which + 1, :].rearrange("a (p c) -> p (a c)", c=8)[:, c:c + 1],
                                    in_=z_[:, col:col + 1], allow_slow_non_contiguous=True),
                                    reads=[("zo", i2)], writes=[("dr", "zb", which, c)], dma=True)
        S.barrier()


def phase_exchange(k):
    nc, S = k.nc, k.S
    xin = [k.dram("xch_in%d" % i, [128, NOWN], BF16) for i in range(5)]
    xall = [k.dram("xch_all%d" % i, [512, NOWN], BF16) for i in range(5)]
    zb = k.dram("zb_in", [2, 1024])
    zall = k.dram("zb_all", [8, 1024])
    groups = [[0, 1, 2, 3], [4, 5, 6, 7]]
    for i in range(5):
        S.add_cc(lambda e, i=i: e.collective_compute("AllGather", ALU.bypass, replica_groups=groups, ins=[xin[i].opt()], outs=[xall[i].opt()]))
    S.add_cc(lambda e: e.collective_compute("AllGather", ALU.bypass, replica_groups=groups, ins=[zb.opt()], outs=[zall.opt()]))
    S.barrier()
    with ExitStack() as st:
        al = k.alloc(st)
        zsel = al("zsel", [128, 8, 8], F32)
        S.add("sp", lambda e: e.dma_start(out=zsel[:], in_=zall.rearrange("j (p c) -> p j c", c=8)), reads=["zall"], writes=["zsel"], dma=True)
        for w in range(2):
            for j in range(8):
                if j == 0:
                    S.add("dve", lambda e, w=w, j=j: e.tensor_scalar(out=k.zpn[:, w, :], in0=zsel[:, j, :], scalar1=k.selT[:, w, j:j + 1],
                                                                     scalar2=None, op0=ALU.mult), reads=["zsel", "selT"], writes=["zpn"])
                else:
                    S.add("dve", lambda e, w=w, j=j: e.scalar_tensor_tensor(out=k.zpn[:, w, :], in0=zsel[:, j, :], scalar=k.selT[:, w, j:j + 1],
                                                                            op0=ALU.mult, in1=k.zpn[:, w, :], op1=ALU.add),
                          reads=["zsel", "selT", "zpn"], writes=["zpn"])
        S.barrier()


def phase_mla(k):
    nc, S = k.nc, k.S
    xall = [k.dram("xch_all%d" % i, [512, NOWN], BF16) for i in range(5)]
    ctxkv = k.dram("ctxkv", [640, NCTX], BF16)
    w_ukv = k.dram("c_w_ukv", [1, 512, 2048])[0]
    qnT = k.dram("qnT", [1024, NOWN], BF16)
    qrT = k.dram("qrT", [512, NOWN], BF16)
    oT = k.dram("oT2", [1024, NOWN], BF16)
    scale = 192.0 ** -0.5
    with ExitStack() as st:
        al = k.alloc(st)
        ckv = al("ckv", [128, 4, NK], BF16)
        krd = al("krd", [128, NK], BF16)
        wkv = al("wkv", [128, 4, 2048], BF16)
        knT = al("knT", [128, NK], BF16)
        vh = al("vh", [128, NKC, 128], BF16)
        qn = [al("qn%d" % i, [128, 512], BF16) for i in range(2)]
        qr = [[al("qr%d_%d" % (e_, i), [128, 512], BF16) for i in range(2)] for e_ in range(2)]
        for e_ in range(2):
            for i in range(2):
                S.add("pool", lambda e, e_=e_, i=i: e.memset(qr[e_][i][:], 0.0), writes=[("qr", e_, i)])
        pT = [al("pT%d" % i, [128, 512], BF16) for i in range(6)]
        rr = al("rr", [128, 512], F32)
        dacc = [al("dacc%d" % i, [128, 512], F32) for i in range(2)]
        oo = [al("oo%d" % i, [128, 512], BF16) for i in range(2)]
        for c in range(4):
            for r in range(4):
                S.add("sp", lambda e, c=c, r=r: e.dma_start(out=ckv[:, c, r * NOWN:(r + 1) * NOWN],
                                                            in_=xall[c][r * 128:(r + 1) * 128, :]),
                      writes=[("ckv", c, r)], dma=True)
            S.add("sp", lambda e, c=c: e.dma_start(out=ckv[:, c, SEQ:NK], in_=ctxkv[c * 128:(c + 1) * 128, :]), writes=[("ckv", c, 4)], dma=True)
        for r in range(4):
            S.add("sp", lambda e, r=r: e.dma_start(out=krd[:, r * NOWN:(r + 1) * NOWN], in_=xall[4][r * 128:(r + 1) * 128, :]),
                  writes=[("krd", r)], dma=True)
        S.add("sp", lambda e: e.dma_start(out=krd[:, SEQ:NK], in_=ctxkv[512:640, :]), writes=[("krd", 4)], dma=True)
        for i in range(4):
            S.add("pool", lambda e, i=i: e.dma_start(
                out=wkv[:, :, i * 512:(i + 1) * 512], in_=w_ukv[:, i * 512:(i + 1) * 512].rearrange("(kc p) n -> p kc n", p=128)),
                writes=[("wkv", i)], dma=True)
        ckr = [("ckv", c, r) for c in range(4) for r in range(5)]
        krr = [("krd", r) for r in range(5)]
        npT = 0
        nq = 0
        for h in range(8):
            wr = [("wkv", h // 2)]
            for kg in range(17):
                n = 512 if kg < 16 else 256
                pk_ = k.ps[6 + kg % 2]
                for kc in range(4):
                    mm(S, pk_[:, :n], wkv[:, kc, h * 256:h * 256 + 128], ckv[:, kc, kg * 512:kg * 512 + n], kc == 0, kc == 3,
                       wr + ckr, ("ps", 6 + kg % 2))
                S.add("act", lambda e, pk_=pk_, kg=kg, n=n: e.activation(out=knT[:, kg * 512:kg * 512 + n], in_=pk_[:, :n], func=AF.Identity),
                      reads=[("ps", 6 + kg % 2)], writes=["knT"])
            for g4 in range(17):
                nb = 4 if g4 < 16 else 2
                pv_ = k.ps[6 + g4 % 2]
                for bb in range(nb):
                    kb = g4 * 4 + bb
                    for kc in range(4):
                        mm(S, pv_[:, bb * 128:(bb + 1) * 128], ckv[:, kc, kb * 128:(kb + 1) * 128],
                           wkv[:, kc, h * 256 + 128:h * 256 + 256], kc == 0, kc == 3, wr + ckr, ("ps", 6 + g4 % 2))
                S.add("dve", lambda e, pv_=pv_, g4=g4, nb=nb: e.tensor_copy(
                    out=vh[:, g4 * 4:g4 * 4 + nb, :], in_=pv_[:, :nb * 128].rearrange("p (b d) -> p b d", d=128)),
                    reads=[("ps", 6 + g4 % 2)], writes=["vh"])
            e2 = h % 2
            for qgi in range(4):
                i2 = nq % 2
                nq += 1
                q0 = qgi * 512
                S.add("sp", lambda e, i2=i2, q0=q0, h=h: e.dma_start(out=qn[i2][:], in_=qnT[h * 128:(h + 1) * 128, q0:q0 + 512]),
                      writes=[("qn", i2)], dma=True)
                S.add("sp", lambda e, i2=i2, q0=q0, h=h, e2=e2: e.dma_start(
                    out=qr[e2][i2][64 * e2:64 * e2 + 64, :], in_=qrT[(h // 2) * 128 + 64 * e2:(h // 2) * 128 + 64 * e2 + 64, q0:q0 + 512]),
                    writes=[("qr", e2, i2)], dma=True)
                po, pd = k.ps[4], k.ps[5]
                LOOK = 3
                pend = {}
                for kc in range(NKC + LOOK):
                    if kc < NKC:
                        bi = kc % 4
                        ps_ = k.ps[bi]
                        mm(S, ps_[:, :], knT[:, kc * 128:(kc + 1) * 128], qn[i2][:], True, False, ["knT", ("qn", i2)], ("ps", bi))
                        mm(S, ps_[:, :], krd[:, kc * 128:(kc + 1) * 128], qr[e2][i2][:], False, True,
                           krr + [("qr", e2, i2)], ("ps", bi))
                        p = pT[npT % 6]
                        pk = ("pT", npT % 6)
                        npT += 1
                        S.add("act", lambda e, p=p, ps_=ps_: e.activation(out=p[:], in_=ps_[:, :], func=AF.Exp, scale=scale),
                              reads=[("ps", bi)], writes=[pk])
                        pend[kc] = (p, pk)
                    j = kc - LOOK
                    if j >= 0:
                        p, pk = pend.pop(j)
                        mm(S, po[:, :], vh[:, j, :], p[:], j == 0, j == NKC - 1, ["vh", pk], ("ps", 4))
                        if j % 2 == 1:
                            mm(S, pd[:, :], k.ones_b[:], p[:], j == 1, False, ["ones_b", pk], ("ps", 5))
                        elif j == 0:
                            S.add("dve", lambda e, p=p: e.tensor_copy(out=dacc[0][:], in_=p[:]), reads=[pk], writes=[("dacc", 0)])
                        else:
                            S.add("dve", lambda e, p=p: e.tensor_tensor(out=dacc[0][:], in0=dacc[0][:], in1=p[:], op=ALU.add),
                                  reads=[pk, ("dacc", 0)], writes=[("dacc", 0)])
                mm(S, pd[:, :], k.ones_f[:], dacc[0][:], False, True, ["ones_f", ("dacc", 0)], ("ps", 5))
                S.add("dve", lambda e, pd=pd: e.reciprocal(out=rr[:], in_=pd[:, :]), reads=[("ps", 5)], writes=["rr"])
                o_ = oo[i2]
                S.add("dve", lambda e, po=po, o_=o_: e.tensor_tensor(out=o_[:], in0=po[:, :], in1=rr[:], op=ALU.mult),
                      reads=[("ps", 4), "rr"], writes=[("oo", i2)])
                S.add("sp", lambda e, o_=o_, h=h, q0=q0: e.dma_start(out=oT[h * 128:(h + 1) * 128, q0:q0 + 512], in_=o_[:]),
                      reads=[("oo", i2)], writes=[("dr", "oT2", h, q0)], dma=True)
        S.barrier()


def phase_cd_out(k, x_in, x_out):
    nc, S = k.nc, k.S
    l = 1
    w = k.dram("cd_w_out", [1, D, D])[0]
    oT = k.dram("oT2", [1024, NOWN], BF16)
    dbT = k.dram("dbT", [1024, NOWN], BF16)
    zT = k.dram("zT", [1024, NOWN])
    cwD = k.dram("convT", [128, 24])
    with ExitStack() as st:
        al = k.alloc(st)
        xres = al("xres", [128, KC, 512], F32)
        hT = al("hT", [128, KC, 512], BF16)
        ze = al("ze", [128, 8, 514], F32)
        dbs = al("dbs", [128, 8, 512], BF16)
        cw = al("cw", [128, 24], F32)
        ct = [al("ct%d" % i, [128, 512], F32) for i in range(2)]
        wa = [al("wa%d" % i, [128, KC, 256], BF16) for i in range(2)]
        S.add("sp", lambda e: e.dma_start(out=cw[:], in_=cwD), writes=["cw"], dma=True)
        na = 0
        for (t0, T, segs) in OWN_TILES:
            load_x_tile(k, xres, x_in, t0, T)
            for c in range(8):
                S.add("sp", lambda e, c=c, t0=t0: e.dma_start(out=hT[:, c, :], in_=oT[c * 128:(c + 1) * 128, t0:t0 + 512]),
                      writes=[("hT", c)], dma=True)
            S.add("sp", lambda e, t0=t0: e.dma_start(out=ze[:, :, 1:513], in_=zT[:, t0:t0 + 512].rearrange("(c p) t -> p c t", p=128)),
                  writes=[("ze", 1)], dma=True)
            if t0 > 0:
                S.add("sp", lambda e, t0=t0: e.dma_start(out=ze[:, :, 0:1], in_=zT[:, t0 - 1:t0].rearrange("(c p) t -> p c t", p=128),
                                                         allow_slow_non_contiguous=True), writes=[("ze", 0)], dma=True)
            else:
                S.add("dve", lambda e: e.tensor_copy(out=ze[:, :, 0], in_=k.zpn[:, 0, :]), reads=["zpn"], writes=[("ze", 0)])
            if t0 + 512 < NOWN:
                S.add("sp", lambda e, t0=t0: e.dma_start(out=ze[:, :, 513:514], in_=zT[:, t0 + 512:t0 + 513].rearrange("(c p) t -> p c t", p=128),
                                                         allow_slow_non_contiguous=True), writes=[("ze", 2)], dma=True)
            else:
                S.add("dve", lambda e: e.tensor_copy(out=ze[:, :, 513], in_=k.zpn[:, 1, :]), reads=["zpn"], writes=[("ze", 2)])
            S.add("sp", lambda e, t0=t0: e.dma_start(out=dbs[:], in_=dbT[:, t0:t0 + 512].rearrange("(c p) t -> p c t", p=128)),
                  writes=["dbs"], dma=True)
            zr = [("ze", i) for i in range(3)]
            for c in range(8):
                t_ = ct[c % 2]
                tk = ("ct", c % 2)
                S.add("dve", lambda e, c=c, t_=t_: e.tensor_scalar(out=t_[:], in0=ze[:, c, 0:512], scalar1=cw[:, c:c + 1], scalar2=None,
                                                                   op0=ALU.mult), reads=zr + ["cw"], writes=[tk])
                S.add("dve", lambda e, c=c, t_=t_: e.scalar_tensor_tensor(out=t_[:], in0=ze[:, c, 1:513], scalar=cw[:, 8 + c:9 + c],
                                                                          op0=ALU.mult, in1=t_[:], op1=ALU.add), reads=zr + ["cw", tk], writes=[tk])
                S.add("dve", lambda e, c=c, t_=t_: e.scalar_tensor_tensor(out=t_[:], in0=ze[:, c, 2:514], scalar=cw[:, 16 + c:17 + c],
                                                                          op0=ALU.mult, in1=t_[:], op1=ALU.add), reads=zr + ["cw", tk], writes=[tk])
                S.add("dve", lambda e, c=c, t_=t_: e.tensor_tensor(out=hT[:, 8 + c, :], in0=t_[:], in1=dbs[:, c, :], op=ALU.mult),
                      reads=[tk, "dbs"], writes=[("hT", 8 + c)])
            for sl in range(KC // 2):
                a = wa[na % 2]
                ka = "wa%d" % (na % 2)
                na += 1
                S.add("pool", lambda e, a=a, sl=sl: e.dma_start(
                    out=a[:], in_=w[:, sl * 256:(sl + 1) * 256].rearrange("(kc p) n -> p kc n", p=128)), writes=[ka], dma=True)
                for jj in range(2):
                    c = sl * 2 + jj
                    py = k.ps[c % 2]
                    for kc in range(KC):
                        mm(S, py[:, :T], a[:, kc, jj * 128:(jj + 1) * 128], hT[:, kc, :T], kc == 0, kc == KC - 1,
                           [ka, ("hT", kc)], ("ps", c % 2))
                    residual_store(k, py, ("ps", c % 2), xres, c, T, segs, l, 1, x_out, t0)
        S.barrier()


OWN_TILES = [(i * 512, 512, [(0, 512, 0)]) for i in range(4)]
MISC_TILE = (2048, 512, [(0, 256, 0), (256, 256, 1)])
CTX_TILE = (CT0, 256, [(0, 256, 1)])
OWN3_CTX_TILE = ([(1536, 512), (CT0, 256)], [(0, 512, 0), (512, 256, 1)])


def fm(v):
    v = np.asarray(v)
    lead = v.shape[:-1]
    n = v.shape[-1] // 128
    r = v.reshape(lead + (n, 128))
    r = np.moveaxis(r, -1, 0)
    return np.ascontiguousarray(r.reshape(128, -1))


def prep_core(inp, core):
    b, q = core // 4, core % 4
    p0 = q * NOWN
    x = inp["x"]
    xT = np.zeros((D, NT), np.float32)
    xT[:, 0:NOWN] = x[b, p0:p0 + NOWN].T
    if q > 0:
        xT[:, HP0:HP0 + 128] = x[b, p0 - 128:p0].T
    if q < 3:
        xT[:, HN0:HN0 + 128] = x[b, p0 + NOWN:p0 + NOWN + 128].T
    xT[:, CT0:CT0 + NCTX] = inp["ctx"][b].T
    m = {"xT": xT}
    m["cT"] = np.ascontiguousarray(np.stack([inp["c"][b], inp["c_ctx"]], axis=1))
    pos = np.zeros(NT, np.int64)
    pos[0:NOWN] = p0 + np.arange(NOWN)
    pos[HP0:HP0 + 128] = p0 - 128 + np.arange(128)
    pos[HN0:HN0 + 128] = p0 + NOWN + np.arange(128)
    pos = np.clip(pos, 0, SEQ - 1)
    row = (pos // 64).astype(np.float32)
    col = (pos % 64).astype(np.float32)
    inv = (10000.0 ** (-np.arange(0, 32, 2, dtype=np.float32) / 32)).astype(np.float32)
    ang = np.zeros((64, NT), np.float32)
    ang[0:16] = (row[None, :] * inv[:, None]).astype(np.float32)
    ang[16:32] = ang[0:16]
    ang[32:48] = (col[None, :] * inv[:, None]).astype(np.float32)
    ang[48:64] = ang[32:48]
    cosT = np.cos(ang).astype(np.float32)
    sinT = np.sin(ang).astype(np.float32)
    cosT[:, CT0:] = 1.0
    sinT[:, CT0:] = 0.0
    m["cosT"] = np.ascontiguousarray(np.concatenate([cosT, cosT], 0))
    m["sinT"] = np.ascontiguousarray(np.concatenate([sinT, sinT], 0))
    j = np.arange(128)[:, None]
    i = np.arange(128)[None, :]
    mk = np.zeros((128, 4, 128), np.float32)
    mk[:, 0, :] = (j >= i)
    mk[:, 1, :] = (j <= i)
    mk[:, 2, :] = (j >= i) * (1.0 if q > 0 else 0.0)
    mk[:, 3, :] = (j <= i) * (1.0 if q < 3 else 0.0)
    m["masks"] = mk
    sel = np.zeros((128, 2, 8), np.float32)
    if q > 0:
        sel[:, 0, 2 * (q - 1) + 1] = 1.0
    if q < 3:
        sel[:, 1, 2 * (q + 1)] = 1.0
    m["selT"] = sel
    return m


def prep_shared(inp):
    m = {}
    m["mod_w"] = inp["mod_w"]
    m["mod_bT"] = np.stack([fm(inp["mod_b"][l]) for l in range(2)])
    m["norm_gT"] = np.stack([fm(inp["norm_g"][l]) for l in range(2)])
    for n in ("ffn_w1", "ffn_w3", "ffn_w2", "ab_w_in", "ab_w_out", "cd_w_in", "cd_w_out", "c_w_uq", "c_w_ukv"):
        m[n] = inp[n]
    pm = np.zeros((128, 128), np.float32)
    for mm_ in range(128):
        if mm_ % 32 < 16:
            pm[mm_ + 16, mm_] = -1.0
        else:
            pm[mm_ - 16, mm_] = 1.0
    m["permM"] = pm
    sk = inp["a_sink"][0]
    m["sinkT"] = np.ascontiguousarray(np.stack([np.repeat(sk[2 * c:2 * c + 2], 64) for c in range(8)], axis=1))
    m["b_wsT"] = np.ascontiguousarray(np.transpose(inp["b_ws"][0], (2, 0, 1)))
    m["b_biasbc"] = np.ascontiguousarray(np.broadcast_to(inp["b_bias"][0][None], (128, 8, 128)))
    m["c_qnT"] = fm(inp["c_q_norm"][0])
    m["c_kvnT"] = fm(inp["c_kv_norm"][0])
    m["convT"] = fm(inp["d_conv_w"][0])
    m["fnT"] = fm(inp["final_norm"])
    return m


def load_final_consts(k):
    fnD = k.dram("fnT", [128, 16])
    k.fnT = k.sb("fnT_sb", [128, 16], F32)
    k.S.add("sp", lambda e: e.dma_start(out=k.fnT[:], in_=fnD), writes=["fnT"], dma=True)
    selD = k.dram("selT", [128, 2, 8])
    k.selT = k.sb("selT_sb", [128, 2, 8], F32)
    k.zpn = k.sb("zpn_sb", [128, 2, 8], F32)
    k.S.add("sp", lambda e: e.dma_start(out=k.selT[:], in_=selD), writes=["selT"], dma=True)


INS = ["xT", "cT", "mod_w", "mod_bT", "norm_gT", "ffn_w1", "ffn_w3", "ffn_w2", "ab_w_in", "ab_w_out", "cosT", "sinT",
       "permM", "masks", "sinkT", "b_wsT", "b_biasbc", "cd_w_in", "c_w_uq", "c_qnT", "c_kvnT",
       "cd_w_out", "c_w_ukv", "convT", "fnT", "selT"]
OUTS = ["outT"]


def build():
    k = K(INS, OUTS)
    xT = k.dram("xT", [D, NT])
    x1, x2, x3 = k.dram("x1", [D, NT]), k.dram("x2", [D, NT]), k.dram("x3", [D, NT])
    x4, x5 = k.dram("x4", [D, NT]), k.dram("x5", [D, NOWN])
    outT = k.dram("outT", [D, NOWN])
    phase_setup(k)
    load_rope_consts(k)
    load_final_consts(k)
    phase_mod(k, (0, 1))
    phase_ffn(k, 0, 0, xT, x1, OWN_TILES + [MISC_TILE])
    phase_ab_in(k, x1, OWN_TILES + [MISC_TILE])
    phase_ab_mix(k)
    phase_outproj(k, 0, "ab_w_out", k.dram("oT", [2048, NT], BF16), x1, x2, OWN_TILES + [CTX_TILE])
    phase_ffn(k, 0, 1, x2, x3, OWN_TILES[:3])
    phase_ffn(k, 0, 1, x2, x3, [OWN3_CTX_TILE])
    phase_ffn(k, 1, 0, x3, x4, OWN_TILES[:3])
    phase_ffn(k, 1, 0, x3, x4, [OWN3_CTX_TILE])
    phase_cd_in(k, x4, OWN_TILES + [CTX_TILE])
    phase_exchange(k)
    phase_mla(k)
    phase_cd_out(k, x4, x5)
    phase_ffn(k, 1, 1, x5, None, OWN_TILES, final_out=outT)
    k.S.emit(k.st)
    k.st.close()
    return k


def kernel(**inp):
    inp = {kk: np.asarray(v) for kk, v in inp.items()}
    n = 8
    sh = prep_shared(inp)
    for nm in ("ffn_w1", "ffn_w3", "ffn_w2"):
        a = inp[nm]
        sh[nm] = a.reshape((4,) + a.shape[2:])
    k = build()
    maps = []
    for c in range(n):
        m = dict(sh)
        m.update(prep_core(inp, c))
        maps.append({kk: m[kk] for kk in INS})
    res = run_bass_kernel_spmd(k.nc, maps, core_ids=list(range(n))).results
    out = np.zeros((2, SEQ, D), np.float32)
    for c in range(n):
        b, q = c // 4, c % 4
        out[b, q * NOWN:(q + 1) * NOWN, :] = np.asarray(res[c]["outT"]).T
    return out
```

```python
import math
import types
from contextlib import ExitStack
import numpy as np
import concourse.bass as bass
import concourse.mybir as mybir
from concourse.bass_utils import run_bass_kernel_spmd

F32 = mybir.dt.float32
BF16 = mybir.dt.bfloat16
AF = mybir.ActivationFunctionType
ALU = mybir.AluOpType

D = 2048
FFN = 5632
NFC = FFN // 128
KC = D // 128
SEQ = 8192
NOWN = 2048
NCTX = 256
NT = 2560
HP0, HN0, CT0 = 2048, 2176, 2304
EPS = 1e-6
ENGS = ("pe", "act", "dve", "pool", "sp")
N_SW = 4
DBG_NOPERM = False
ROPE_ADD_ENG = "pool"
DBG_SECT = None
SW_FRESH = False


def _freeze(fn):
    if fn is None or fn.__closure__ is None:
        return fn
    cells = []
    for c in fn.__closure__:
        try:
            cells.append(types.CellType(c.cell_contents))
        except ValueError:
            cells.append(c)
    return types.FunctionType(fn.__code__, fn.__globals__, fn.__name__, fn.__defaults__, tuple(cells))


class Op:
    __slots__ = ("eng", "fn", "deps", "idx", "ms", "dma", "sem", "semval", "msval", "tag")


class Sched:
    def __init__(self, nc, n_dma_sems=24):
        if SW_FRESH:
            n_dma_sems = 8
        self.nc = nc
        self.ops = {e: [] for e in ENGS}
        self.lastw = {}
        self.readers = {}
        self.n_dma_sems = n_dma_sems
        self.dma_rr = 0
        self.dma_count = [0] * n_dma_sems
        self.dma_last = [None] * n_dma_sems
        self.sw_keys = {}
        self.sw_rr = 0
        self.sw_last = {}

    def add(self, eng, fn, reads=(), writes=(), dma=False, tag=None):
        op = Op()
        op.eng, op.fn, op.dma, op.ms, op.tag = eng, _freeze(fn), dma, False, tag
        op.sem = op.semval = op.msval = None
        sw = dma and eng == "pool" and SW_FRESH
        if eng in ("act", "dve", "pool") and not dma:
            extra = [("psr", r[1]) for r in reads if isinstance(r, tuple) and len(r) == 2 and r[0] == "ps"]
            if extra:
                writes = list(writes) + extra
        deps = set()
        for r in reads:
            w = self.lastw.get(r)
            if w is not None:
                deps.add(w)
        for k in writes:
            w = self.lastw.get(k)
            if w is not None:
                deps.add(w)
            for rd in self.readers.get(k, ()):
                deps.add(rd)
        if sw:
            key = len(self.sw_keys)
            self.sw_keys[key] = key
            op.sem, op.semval = ("sw", key), 16
            op.tag = False
            self.sw_last[key] = op
        elif dma:
            if eng == "pool":
                s = self.n_dma_sems - N_SW + self.sw_rr
                self.sw_rr = (self.sw_rr + 1) % N_SW
            else:
                s = self.dma_rr
                self.dma_rr = (self.dma_rr + 1) % (self.n_dma_sems - N_SW)
            if self.dma_last[s] is not None:
                deps.add(self.dma_last[s])
            self.dma_count[s] += 1
            op.sem, op.semval = s, 16 * self.dma_count[s]
            self.dma_last[s] = op
        if eng == "pe":
            deps = {d for d in deps if d.dma or d.eng != "pe"}
        op.deps = deps
        for d in deps:
            if not d.dma:
                d.ms = True
        for r in reads:
            self.readers.setdefault(r, []).append(op)
        for k in writes:
            self.lastw[k] = op
            self.readers[k] = []
        op.idx = len(self.ops[eng])
        self.ops[eng].append(op)
        return op

    def add_cc(self, fn, reads=(), writes=()):
        op = self.add("pool", fn, reads=reads, writes=[("cc_issue",)])
        op.tag = "cc"
        self.n_cc = getattr(self, "n_cc", 0) + 1
        op.semval = self.n_cc
        return op

    def barrier(self):
        lasts = [self.ops[e][-1] for e in ENGS if self.ops[e]]
        lasts = [x for x in lasts if not x.dma and x.fn is not None]
        dmas = [d for d in self.dma_last if d is not None] + list(self.sw_last.values())
        for e in ENGS:
            op = Op()
            op.eng, op.fn, op.dma, op.ms, op.tag = e, None, False, False, "barrier"
            op.sem = op.semval = op.msval = None
            op.deps = set(x for x in lasts if x.eng != e) | set(dmas)
            for d in op.deps:
                if not d.dma:
                    d.ms = True
            op.idx = len(self.ops[e])
            self.ops[e].append(op)
        self.lastw.clear()
        self.readers.clear()

    def emit(self, stack):
        nc = self.nc
        esem = {e: stack.enter_context(nc.semaphore("s_" + e)) for e in ENGS if e != "sp"}
        dsem = [stack.enter_context(nc.semaphore("d_%d" % i)) for i in range(self.n_dma_sems)]
        wsem = [stack.enter_context(nc.semaphore("w_%d" % i)) for i in range(len(self.sw_keys))]
        ccsem = stack.enter_context(nc.semaphore("ccsem"))
        for e in ENGS:
            c = 0
            for op in self.ops[e]:
                if op.ms and not op.dma:
                    c += 1
                    op.msval = c
        ops = self.ops
        final_dma = [(dsem[i], 16 * self.dma_count[i]) for i in range(self.n_dma_sems) if self.dma_count[i]]

        def run(ename, eng):
            known = {}
            for op in ops[ename]:
                for d in op.deps:
                    if d.dma and isinstance(d.sem, tuple):
                        key, sem, val = ("w", d.sem[1], id(d)), wsem[d.sem[1]], 16
                    elif d.dma:
                        key, sem, val = ("d", d.sem), dsem[d.sem], d.semval
                    else:
                        key, sem, val = d.eng, esem[d.eng], d.msval
                    if known.get(key, 0) < val:
                        eng.wait_ge(sem, val)
                        known[key] = val
                if op.fn is None:
                    continue
                if op.dma and isinstance(op.sem, tuple):
                    if op.tag:
                        eng.wait_ge(wsem[op.sem[1]], 16)
                        eng.sem_clear(wsem[op.sem[1]])
                    op.fn(eng).then_inc(wsem[op.sem[1]], 16)
                    continue
                ins = op.fn(eng)
                if op.tag == "cc":
                    ins.then_inc(ccsem, 1)
                    eng.wait_ge(ccsem, op.semval)
                    if op.ms:
                        eng.memset(self.cc_dummy[:], 0.0).then_inc(esem[ename], 1)
                    continue
                if op.dma:
                    ins.then_inc(dsem[op.sem], 16)
                elif op.ms:
                    ins.then_inc(esem[ename], 1)
            if ename == "sp":
                for sem, val in final_dma:
                    eng.wait_ge(sem, val)

        with nc.Block() as block:
            @block.tensor
            def _(e):
                run("pe", e)

            @block.scalar
            def _(e):
                run("act", e)

            @block.vector
            def _(e):
                run("dve", e)

            @block.gpsimd
            def _(e):
                run("pool", e)

            @block.sync
            def _(e):
                run("sp", e)


class K:
    def __init__(self, ext_in, ext_out):
        self.nc = bass.Bass("TRN2", target_bir_lowering=False)
        self.S = Sched(self.nc)
        self.ext_in, self.ext_out = set(ext_in), set(ext_out)
        self.dr = {}
        self.st = ExitStack()
        self.uid = 0
        self.ffn_sel = {(0, 0): 0, (0, 1): 1, (1, 0): 2, (1, 1): 3}

    def dram(self, name, shape, dtype=F32):
        if name in self.dr:
            return self.dr[name]
        kind = "ExternalInput" if name in self.ext_in else ("ExternalOutput" if name in self.ext_out else "Internal")
        t = self.nc.dram_tensor(name, list(shape), dtype, kind=kind).ap()
        self.dr[name] = t
        return t

    def sb(self, name, shape, dtype):
        return self.st.enter_context(self.nc.sbuf_tensor(name, list(shape), dtype))

    def alloc(self, st):
        self.uid += 1
        u = self.uid
        return lambda n, sh, dt: st.enter_context(self.nc.sbuf_tensor("%s_u%d" % (n, u), list(sh), dt))

    def psum(self, name):
        return self.st.enter_context(self.nc.psum_tensor(name, [128, 512], F32))


def mm(S, ps_ap, lhsT, rhs, start, stop, reads, pskey):
    S.add("pe", lambda e: e.matmul(ps_ap, lhsT, rhs, start=start, stop=stop), reads=reads, writes=[pskey])


def phase_setup(k):
    nc, S = k.nc, k.S
    k.ps = [k.psum("ps%d" % i) for i in range(8)]
    k.ones_f = k.sb("ones_f", [128, 128], F32)
    k.ones_b = k.sb("ones_b", [128, 128], BF16)
    S.add("pool", lambda e: e.memset(k.ones_f[:], 1.0), writes=["ones_f"])
    S.add("pool", lambda e: e.memset(k.ones_b[:], 1.0), writes=["ones_b"])
    k.eps_t = k.sb("eps_t", [128, 1], F32)
    S.cc_dummy = k.sb("cc_dummy", [128, 8], F32)
    S.add("pool", lambda e: e.memset(k.eps_t[:], EPS), writes=["eps_t"])


def phase_mod(k, layers=(0, 1)):
    nc, S = k.nc, k.S
    cT = k.dram("cT", [D, 2])
    nl = len(layers)
    mod_w = k.dram("mod_w", [nl, D, 9 * D])
    mod_bT = k.dram("mod_bT", [nl, 128, 144])
    norm_gT = k.dram("norm_gT", [nl, 128, 48])
    k.modT = [k.sb("modT%d" % l, [128, 144, 2], F32) for l in range(2)]
    k.modA = [k.sb("modA%d" % l, [128, 48, 2], F32) for l in range(2)]
    k.modG = [k.sb("modG%d" % l, [128, 48, 2], F32) for l in range(2)]
    with ExitStack() as st:
        cin = st.enter_context(nc.sbuf_tensor("cin_sb", [128, KC, 2], F32))
        scT = st.enter_context(nc.sbuf_tensor("scT", [128, KC, 2], BF16))
        mb = st.enter_context(nc.sbuf_tensor("mb", [128, 144], F32))
        ng = st.enter_context(nc.sbuf_tensor("ng", [128, 48], F32))
        wm = [st.enter_context(nc.sbuf_tensor("wm%d" % i, [128, KC, 512], BF16)) for i in range(2)]
        S.add("sp", lambda e: e.dma_start(out=cin[:], in_=cT.rearrange("(kc p) n -> p kc n", p=128)), writes=["cin"], dma=True)
        S.add("act", lambda e: e.activation(out=scT[:], in_=cin[:], func=AF.Silu), reads=["cin"], writes=["scT"])
        si = 0
        for li, l in enumerate(layers):
            S.add("sp", lambda e, li=li: e.dma_start(out=mb[:], in_=mod_bT[li]), writes=["mb"], dma=True)
            S.add("sp", lambda e, li=li: e.dma_start(out=ng[:], in_=norm_gT[li]), writes=["ng"], dma=True)
            ps = k.ps[l]
            for sl in range(36):
                w = wm[si % 2]
                wkey = "wm%d" % (si % 2)
                si += 1
                S.add("pool", lambda e, w=w, li=li, sl=sl: e.dma_start(
                    out=w[:], in_=mod_w[li, :, sl * 512:(sl + 1) * 512].rearrange("(kc p) n -> p kc n", p=128)),
                    writes=[wkey], dma=True)
                for jj in range(4):
                    j = sl * 4 + jj
                    for kc in range(KC):
                        mm(S, ps[:, 2 * j:2 * j + 2], w[:, kc, jj * 128:(jj + 1) * 128], scT[:, kc, :],
                           kc == 0, kc == KC - 1, [wkey, "scT"], ("ps", l))
            modT = k.modT[l]
            for cnd in range(2):
                S.add("dve", lambda e, modT=modT, ps=ps, cnd=cnd: e.tensor_tensor(
                    out=modT[:, :, cnd], in0=ps[:, 0:288].rearrange("p (j c) -> p j c", c=2)[:, :, cnd], in1=mb[:], op=ALU.add),
                    reads=[("ps", l), "mb"], writes=[("modT", l)])
            for s in range(3):
                for cnd in range(2):
                    S.add("dve", lambda e, l=l, s=s, cnd=cnd, modT=modT: e.scalar_tensor_tensor(
                        out=k.modA[l][:, s * 16:(s + 1) * 16, cnd], in0=modT[:, (3 * s + 1) * 16:(3 * s + 2) * 16, cnd],
                        scalar=1.0, op0=ALU.add, in1=ng[:, s * 16:(s + 1) * 16], op1=ALU.mult),
                        reads=[("modT", l), "ng"], writes=[("modA", l)])
                    S.add("dve", lambda e, l=l, s=s, cnd=cnd, modT=modT: e.tensor_scalar(
                        out=k.modG[l][:, s * 16:(s + 1) * 16, cnd], in0=modT[:, (3 * s + 2) * 16:(3 * s + 3) * 16, cnd],
                        scalar1=(1.0 if s == 1 else 0.5), scalar2=None, op0=ALU.mult),
                        reads=[("modT", l)], writes=[("modG", l)])
        S.barrier()


def load_x_tile(k, xres, x_in, t0, T):
    S = k.S
    for c in range(KC):
        S.add("sp", lambda e, c=c: e.dma_start(out=xres[:, c, :T], in_=x_in[c * 128:(c + 1) * 128, t0:t0 + T]),
              writes=[("xres", c)], dma=True)


def norm_mod(k, xres, hT, T, segs, l, s, tmp, rstd, sqb):
    S = k.S
    ps = k.ps[6]
    for c in range(KC):
        sq = sqb[c % 2]
        S.add("act", lambda e, c=c, sq=sq: e.activation(out=sq[:, :T], in_=xres[:, c, :T], func=AF.Square),
              reads=[("xres", c)], writes=[("sq", c % 2)])
        mm(S, ps[:, :T], k.ones_b[:], sq[:, :T], c == 0, c == KC - 1, [("sq", c % 2), "ones_b"], ("ps", 6))
    S.add("act", lambda e: e.activation(out=rstd[:, :T], in_=ps[:, :T], func=AF.Sqrt, scale=1.0 / D, bias=k.eps_t[:]),
          reads=[("ps", 6), "eps_t"], writes=["rstd"])
    S.add("dve", lambda e: e.reciprocal(out=rstd[:, :T], in_=rstd[:, :T]), reads=["rstd"], writes=["rstd"])
    A, SH = k.modA[l], k.modT[l]
    for c in range(KC):
        tb = tmp[c % 2]
        for (o, n, cnd) in segs:
            S.add("dve", lambda e, c=c, o=o, n=n, cnd=cnd, tb=tb: e.scalar_tensor_tensor(
                out=tb[:, o:o + n], in0=xres[:, c, o:o + n], scalar=A[:, s * 16 + c, cnd:cnd + 1], op0=ALU.mult,
                in1=rstd[:, o:o + n], op1=ALU.mult),
                reads=[("xres", c), "rstd", ("modA", l)], writes=[("tmp", c % 2)])
            S.add("act", lambda e, c=c, o=o, n=n, cnd=cnd, tb=tb: e.activation(
                out=hT[:, c, o:o + n], in_=tb[:, o:o + n], func=AF.Identity,
                bias=SH[:, 3 * s * 16 + c, cnd:cnd + 1], scale=1.0),
                reads=[("tmp", c % 2), ("modT", l)], writes=[("hT", c)])


def residual_store(k, ps_ap, pskey, xres, c, T, segs, l, s, x_out, t0):
    S = k.S
    G = k.modG[l]
    for (o, n, cnd) in segs:
        S.add("dve", lambda e, o=o, n=n, cnd=cnd: e.scalar_tensor_tensor(
            out=xres[:, c, o:o + n], in0=ps_ap[:, o:o + n], scalar=G[:, s * 16 + c, cnd:cnd + 1], op0=ALU.mult,
            in1=xres[:, c, o:o + n], op1=ALU.add),
            reads=[pskey, ("xres", c), ("modG", l)], writes=[("xres", c)])
    if x_out is not None:
        S.add("sp", lambda e: e.dma_start(out=x_out[c * 128:(c + 1) * 128, t0:t0 + T], in_=xres[:, c, :T]),
              reads=[("xres", c)], writes=[("xdram", x_out.tensor.name, t0, c)], dma=True)


def _norm_stats(k, xres, T, rstd, sqb, banks=(6, 7)):
    S = k.S
    groups = [(0, min(T, 512))] + ([(512, T - 512)] if T > 512 else [])
    for c in range(KC):
        sq = sqb[c % 2]
        S.add("act", lambda e, c=c, sq=sq: e.activation(out=sq[:, :T], in_=xres[:, c, :T], func=AF.Square),
              reads=[("xres", c)], writes=[("sq", c % 2)])
        for g, (o, n) in enumerate(groups):
            mm(S, k.ps[banks[g]][:, :n], k.ones_b[:], sq[:, o:o + n], c == 0, c == KC - 1, [("sq", c % 2), "ones_b"], ("ps", banks[g]))
    for g, (o, n) in enumerate(groups):
        S.add("act", lambda e, g=g, o=o, n=n: e.activation(out=rstd[:, o:o + n], in_=k.ps[banks[g]][:, :n], func=AF.Sqrt,
                                                          scale=1.0 / D, bias=k.eps_t[:]),
              reads=[("ps", banks[g]), "eps_t"], writes=["rstd"])
    S.add("dve", lambda e: e.reciprocal(out=rstd[:, :T], in_=rstd[:, :T]), reads=["rstd"], writes=["rstd"])


def phase_ffn(k, l, widx, x_in, x_out, tiles, final_out=None):
    nc, S = k.nc, k.S
    s = 0 if widx == 0 else 2
    tiles = [(([(t[0], t[1])], t[2]) if len(t) == 3 else t) for t in tiles]
    TM = max(sum(n for _, n in parts) for parts, _ in tiles)
    FH = 2 if TM > 512 else 1
    NF = NFC // FH
    SLW = 256
    nsel = len(k.ffn_sel)
    wi_ = k.ffn_sel[(l, widx)]
    w1 = k.dram("ffn_w1", [nsel, D, FFN])[wi_]
    w3 = k.dram("ffn_w3", [nsel, D, FFN])[wi_]
    w2 = k.dram("ffn_w2", [nsel, FFN, D])[wi_]
    UB = ((0, 1), (4, 5))
    VB = ((2, 3), (6, 7))
    YB = ((0, 1), (2, 3))
    with ExitStack() as st:
        al = k.alloc(st)
        xres = al("xres", [128, KC, TM], F32)
        hT = al("hT", [128, KC, TM], BF16)
        gT = al("gT", [128, NF, TM], BF16)
        tmp = [al("tmp%d" % i, [128, TM], F32) for i in range(2)]
        sqb = [al("sq%d" % i, [128, TM], BF16) for i in range(2)]
        rstd = al("rstd", [128, TM], F32)
        su = [al("su%d" % i, [128, TM], BF16) for i in range(2)]
        wa = [al("wa%d" % i, [128, KC, SLW], BF16) for i in range(2)]
        wb = [al("wb%d" % i, [128, KC, SLW], BF16) for i in range(2)]
        wc = [al("wc%d" % i, [128, NF, 256], BF16) for i in range(2)]
        na = nc_ = 0
        A, SH, G = k.modA[l], k.modT[l], k.modG[l]
        for (parts, segs) in tiles:
            T = sum(n for _, n in parts)
            groups = [(0, min(T, 512))] + ([(512, T - 512)] if T > 512 else [])
            offs = []
            o_ = 0
            for (c0, n) in parts:
                offs.append((o_, c0, n))
                o_ += n
            for c in range(KC):
                for (o, c0, n) in offs:
                    S.add("sp", lambda e, c=c, o=o, c0=c0, n=n: e.dma_start(out=xres[:, c, o:o + n], in_=x_in[c * 128:(c + 1) * 128, c0:c0 + n]),
                          writes=[("xres", c)], dma=True)
            _norm_stats(k, xres, T, rstd, sqb)
            for c in range(KC):
                tb = tmp[c % 2]
                for (o, n, cnd) in segs:
                    S.add("dve", lambda e, c=c, o=o, n=n, cnd=cnd, tb=tb: e.scalar_tensor_tensor(
                        out=tb[:, o:o + n], in0=xres[:, c, o:o + n], scalar=A[:, s * 16 + c, cnd:cnd + 1], op0=ALU.mult,
                        in1=rstd[:, o:o + n], op1=ALU.mult),
                        reads=[("xres", c), "rstd", ("modA", l)], writes=[("tmp", c % 2)])
                    S.add("act", lambda e, c=c, o=o, n=n, cnd=cnd, tb=tb: e.activation(
                        out=hT[:, c, o:o + n], in_=tb[:, o:o + n], func=AF.Identity,
                        bias=SH[:, 3 * s * 16 + c, cnd:cnd + 1], scale=1.0),
                        reads=[("tmp", c % 2), ("modT", l)], writes=[("hT", c)])
            for fh in range(FH):
                f0 = fh * NF
                for sl in range(NF * 128 // SLW):
                    a, b = wa[na % 2], wb[na % 2]
                    ka, kb = "wa%d" % (na % 2), "wb%d" % (na % 2)
                    na += 1
                    cc0 = f0 * 128 + sl * SLW
                    S.add("pool", lambda e, a=a, cc0=cc0: e.dma_start(
                        out=a[:], in_=w1[:, cc0:cc0 + SLW].rearrange("(kc p) n -> p kc n", p=128)), writes=[ka], dma=True)
                    S.add("pool", lambda e, b=b, cc0=cc0: e.dma_start(
                        out=b[:], in_=w3[:, cc0:cc0 + SLW].rearrange("(kc p) n -> p kc n", p=128)), writes=[kb], dma=True)
                    for jj in range(SLW // 128):
                        fc = sl * (SLW // 128) + jj
                        ub, vb = UB[fc % 2], VB[fc % 2]
                        for g, (o, n) in enumerate(groups):
                            for kc in range(KC):
                                mm(S, k.ps[ub[g]][:, :n], a[:, kc, jj * 128:(jj + 1) * 128], hT[:, kc, o:o + n], kc == 0, kc == KC - 1,
                                   [ka, ("hT", kc)], ("ps", ub[g]))
                        for g, (o, n) in enumerate(groups):
                            for kc in range(KC):
                                mm(S, k.ps[vb[g]][:, :n], b[:, kc, jj * 128:(jj + 1) * 128], hT[:, kc, o:o + n], kc == 0, kc == KC - 1,
                                   [kb, ("hT", kc)], ("ps", vb[g]))
                        sut = su[fc % 2]
                        for g, (o, n) in enumerate(groups):
                            S.add("act", lambda e, g=g, o=o, n=n, ub=ub, sut=sut: e.activation(
                                out=sut[:, o:o + n], in_=k.ps[ub[g]][:, :n], func=AF.Silu),
                                reads=[("ps", ub[g])], writes=[("su", fc % 2, g)])
                            S.add("dve", lambda e, g=g, o=o, n=n, vb=vb, sut=sut, fc=fc: e.tensor_tensor(
                                out=gT[:, fc, o:o + n], in0=k.ps[vb[g]][:, :n], in1=sut[:, o:o + n], op=ALU.mult),
                                reads=[("ps", vb[g]), ("su", fc % 2, g)], writes=[("gT", fc, g)])
                for ds in range(KC // 2):
                    w = wc[nc_ % 2]
                    kw = "wc%d" % (nc_ % 2)
                    nc_ += 1
                    S.add("pool", lambda e, w=w, ds=ds, f0=f0: e.dma_start(
                        out=w[:], in_=w2[f0 * 128:(f0 + NF) * 128, ds * 256:(ds + 1) * 256].rearrange("(fc p) n -> p fc n", p=128)),
                        writes=[kw], dma=True)
                    for jj in range(2):
                        c = ds * 2 + jj
                        yb = YB[c % 2]
                        for g, (o, n) in enumerate(groups):
                            for fc in range(NF):
                                mm(S, k.ps[yb[g]][:, :n], w[:, fc, jj * 128:(jj + 1) * 128], gT[:, fc, o:o + n], fc == 0, fc == NF - 1,
                                   [kw, ("gT", fc, g)], ("ps", yb[g]))
                        for (o, n, cnd) in segs:
                            g = 0 if o < 512 else 1
                            lo = o - groups[g][0]
                            S.add("dve", lambda e, c=c, o=o, n=n, cnd=cnd, g=g, lo=lo, yb=yb: e.scalar_tensor_tensor(
                                out=xres[:, c, o:o + n], in0=k.ps[yb[g]][:, lo:lo + n], scalar=G[:, s * 16 + c, cnd:cnd + 1], op0=ALU.mult,
                                in1=xres[:, c, o:o + n], op1=ALU.add),
                                reads=[("ps", yb[g]), ("xres", c), ("modG", l)], writes=[("xres", c)])
                        if x_out is not None and fh == FH - 1:
                            for (o, c0, n) in offs:
                                S.add("sp", lambda e, c=c, o=o, c0=c0, n=n: e.dma_start(
                                    out=x_out[c * 128:(c + 1) * 128, c0:c0 + n], in_=xres[:, c, o:o + n]),
                                    reads=[("xres", c)], writes=[("xdram", x_out.tensor.name, c0, c)], dma=True)
            if final_out is not None:
                final_norm_store(k, xres, T, parts[0][0], final_out, tmp, rstd, sqb)
        S.barrier()


def final_norm_store(k, xres, T, t0, out, tmp, rstd, sqb):
    S = k.S
    ps = k.ps[6]
    fn = k.fnT
    for c in range(KC):
        sq = sqb[c % 2]
        S.add("act", lambda e, c=c, sq=sq: e.activation(out=sq[:, :T], in_=xres[:, c, :T], func=AF.Square),
              reads=[("xres", c)], writes=[("sq", c % 2)])
        mm(S, ps[:, :T], k.ones_b[:], sq[:, :T], c == 0, c == KC - 1, [("sq", c % 2), "ones_b"], ("ps", 6))
    S.add("act", lambda e: e.activation(out=rstd[:, :T], in_=ps[:, :T], func=AF.Sqrt, scale=1.0 / D, bias=k.eps_t[:]),
          reads=[("ps", 6), "eps_t"], writes=["rstd"])
    S.add("dve", lambda e: e.reciprocal(out=rstd[:, :T], in_=rstd[:, :T]), reads=["rstd"], writes=["rstd"])
    for c in range(KC):
        S.add("dve", lambda e, c=c: e.scalar_tensor_tensor(
            out=xres[:, c, :T], in0=xres[:, c, :T], scalar=fn[:, c:c + 1], op0=ALU.mult, in1=rstd[:, :T], op1=ALU.mult),
            reads=[("xres", c), "rstd", "fnT"], writes=[("xres", c)])
        S.add("sp", lambda e, c=c: e.dma_start(out=out[c * 128:(c + 1) * 128, t0:t0 + T], in_=xres[:, c, :T]),
              reads=[("xres", c)], writes=[("odram", t0, c)], dma=True)


GC1, GC2 = 0.044715, 1.5957691216057308


def gelu_from_psum(k, ps_ap, pskey, out_ap, outkeys, n, t1, t2, t1k, t2k):
    S = k.S
    S.add("act", lambda e: e.activation(out=t1, in_=ps_ap, func=AF.Square), reads=[pskey], writes=[t1k])
    S.add("dve", lambda e: e.tensor_scalar(out=t1, in0=t1, scalar1=GC1, scalar2=1.0, op0=ALU.mult, op1=ALU.add),
          reads=[t1k], writes=[t1k])
    S.add("dve", lambda e: e.tensor_tensor(out=t1, in0=ps_ap, in1=t1, op=ALU.mult), reads=[pskey, t1k], writes=[t1k])
    S.add("act", lambda e: e.activation(out=t2, in_=t1, func=AF.Sigmoid, scale=GC2), reads=[t1k], writes=[t2k])
    S.add("dve", lambda e: e.tensor_tensor(out=out_ap, in0=ps_ap, in1=t2, op=ALU.mult), reads=[pskey, t2k], writes=outkeys)


def rope_from_psum(k, ps_ap, pskey, ps2, ps2key, cos, sin, T, out_ap, outkeys, qraw, t1, t2, keys):
    S = k.S
    qk, t1k, t2k = keys
    S.add("act", lambda e: e.activation(out=qraw[:, :T], in_=ps_ap, func=AF.Identity), reads=[pskey], writes=[qk])
    S.add("dve", lambda e: e.tensor_tensor(out=t1[:, :T], in0=ps_ap, in1=cos, op=ALU.mult), reads=[pskey, "rope"], writes=[t1k])
    mm(S, ps2[:, :T], (k.ones_b if DBG_NOPERM else k.permM)[:], qraw[:, :T], True, True, [qk, "permM"], ps2key)
    S.add("dve", lambda e: e.tensor_tensor(out=t2[:, :T], in0=ps2[:, :T], in1=sin, op=ALU.mult), reads=[ps2key, "rope"], writes=[t2k])
    S.add(ROPE_ADD_ENG, lambda e: e.tensor_tensor(out=out_ap, in0=t1[:, :T], in1=t2[:, :T], op=ALU.add), reads=[t1k, t2k], writes=outkeys)


def load_rope_consts(k):
    nc, S = k.nc, k.S
    permD = k.dram("permM", [128, 128])
    k.permM = k.sb("permM_sb", [128, 128], BF16)
    S.add("pool", lambda e: e.dma_start(out=k.permM[:], in_=permD), writes=["permM"], dma=True)


def phase_ab_in(k, x_in, tiles):
    nc, S = k.nc, k.S
    l, s = 0, 1
    w_in = k.dram("ab_w_in", [1, D, 3328])[0]
    cosD, sinD = k.dram("cosT", [128, NT]), k.dram("sinT", [128, NT])
    qT = k.dram("qT", [1024, NT], BF16)
    kT = k.dram("kT", [2, 128, NT], BF16)
    vtok = k.dram("vtok", [NT, 128], BF16)
    guT = k.dram("guT", [1024, NT], BF16)
    gvtok = k.dram("gvtok", [NT, 1024], BF16)
    with ExitStack() as st:
        al = k.alloc(st)
        xres = al("xres", [128, KC, 512], F32)
        hT = al("hT", [128, KC, 512], BF16)
        tmp = [al("tmp%d" % i, [128, 512], F32) for i in range(2)]
        sqb = [al("sq%d" % i, [128, 512], BF16) for i in range(2)]
        rstd = al("rstd", [128, 512], F32)
        cos, sin = al("cos", [128, 512], F32), al("sin", [128, 512], F32)
        wtm = al("wtm", [128, KC, 1152], BF16)
        kdw = al("kdw", [128, KC, 256], BF16)
        wa = [al("wa%d" % i, [128, KC, 256], BF16) for i in range(2)]
        qraw = [al("qraw%d" % i, [128, 512], BF16) for i in range(2)]
        r1 = [al("r1_%d" % i, [128, 512], F32) for i in range(2)]
        r2 = [al("r2_%d" % i, [128, 512], F32) for i in range(2)]
        ob = [al("ob%d" % i, [128, 512], BF16) for i in range(2)]
        obt = [al("obt%d" % i, [128, 1152], BF16) for i in range(2)]
        for i, (c0, n) in enumerate([(1152, 128), (2304, 512), (2816, 512)]):
            o = [0, 128, 640][i]
            S.add("pool", lambda e, c0=c0, n=n, o=o: e.dma_start(
                out=wtm[:, :, o:o + n], in_=w_in[:, c0:c0 + n].rearrange("(kc p) n -> p kc n", p=128)),
                writes=[("wtm", i)], dma=True)
        for i in range(4):
            c0 = 1024 + 64 * (i // 2)
            S.add("pool", lambda e, c0=c0, i=i: e.dma_start(
                out=kdw[:, :, i * 64:(i + 1) * 64], in_=w_in[:, c0:c0 + 64].rearrange("(kc p) n -> p kc n", p=128)),
                writes=[("kdw", i)], dma=True)
        na = 0
        cnt = 0
        for (t0, T, segs) in tiles:
            load_x_tile(k, xres, x_in, t0, T)
            S.add("sp", lambda e, t0=t0, T=T: e.dma_start(out=cos[:, :T], in_=cosD[:, t0:t0 + T]), writes=["rope"], dma=True)
            S.add("sp", lambda e, t0=t0, T=T: e.dma_start(out=sin[:, :T], in_=sinD[:, t0:t0 + T]), writes=["rope"], dma=True)
            norm_mod(k, xres, hT, T, segs, l, s, tmp, rstd, sqb)
            hreads = [("hT", c) for c in range(KC)]
            for sl in range(8):
                if DBG_SECT is not None and ("q" if sl < 4 else "gu") not in DBG_SECT:
                    continue
                a = wa[na % 2]
                ka = "wa%d" % (na % 2)
                na += 1
                c0 = sl * 256 if sl < 4 else 1280 + (sl - 4) * 256
                S.add("pool", lambda e, a=a, c0=c0: e.dma_start(
                    out=a[:], in_=w_in[:, c0:c0 + 256].rearrange("(kc p) n -> p kc n", p=128)), writes=[ka], dma=True)
                for jj in range(2):
                    ch = (sl % 4) * 2 + jj
                    i2 = cnt % 2
                    cnt += 1
                    pq = k.ps[i2]
                    for kc in range(KC):
                        mm(S, pq[:, :T], a[:, kc, jj * 128:(jj + 1) * 128], hT[:, kc, :T], kc == 0, kc == KC - 1,
                           [ka, ("hT", kc)], ("ps", i2))
                    o_ = ob[i2]
                    if sl < 4:
                        rope_from_psum(k, pq[:, :T], ("ps", i2), k.ps[2 + i2], ("ps", 2 + i2), cos[:, :T], sin[:, :T], T,
                                       o_[:, :T], [("ob", i2)], qraw[i2], r1[i2], r2[i2], (("qraw", i2), ("r1", i2), ("r2", i2)))
                        dst = qT[ch * 128:(ch + 1) * 128, t0:t0 + T]
                    else:
                        gelu_from_psum(k, pq[:, :T], ("ps", i2), o_[:, :T], [("ob", i2)], T, r1[i2][:, :T], r2[i2][:, :T],
                                       ("r1", i2), ("r2", i2))
                        dst = guT[ch * 128:(ch + 1) * 128, t0:t0 + T]
                    S.add("sp", lambda e, dst=dst, o_=o_, T=T: e.dma_start(out=dst, in_=o_[:, :T]),
                          reads=[("ob", i2)], writes=[("dr", dst.tensor.name, t0, ch)], dma=True)
            for kv in range(2):
                if DBG_SECT is not None and "k" not in DBG_SECT:
                    continue
                i2 = cnt % 2
                cnt += 1
                pq = k.ps[i2]
                for kc in range(KC):
                    mm(S, pq[:, :T], kdw[:, kc, kv * 128:(kv + 1) * 128], hT[:, kc, :T], kc == 0, kc == KC - 1,
                       [("kdw", 2 * kv), ("kdw", 2 * kv + 1), ("hT", kc)], ("ps", i2))
                o_ = ob[i2]
                rope_from_psum(k, pq[:, :T], ("ps", i2), k.ps[2 + i2], ("ps", 2 + i2), cos[:, :T], sin[:, :T], T,
                               o_[:, :T], [("ob", i2)], qraw[i2], r1[i2], r2[i2], (("qraw", i2), ("r1", i2), ("r2", i2)))
                dst = kT[kv, :, t0:t0 + T]
                S.add("sp", lambda e, dst=dst, o_=o_, T=T: e.dma_start(out=dst, in_=o_[:, :T]),
                      reads=[("ob", i2)], writes=[("dr", "kT", t0, kv)], dma=True)
            for tb in range(T // 128):
                if DBG_SECT is not None and "tok" not in DBG_SECT:
                    continue
                ot = obt[tb % 2]
                okey = ("obt", tb % 2)
                pv = k.ps[4 + tb % 2]
                for kc in range(KC):
                    mm(S, pv[:, :128], hT[:, kc, tb * 128:(tb + 1) * 128], wtm[:, kc, 0:128], kc == 0, kc == KC - 1,
                       [("wtm", 0), ("hT", kc)], ("ps", 4 + tb % 2))
                S.add("act", lambda e, pv=pv, ot=ot: e.activation(out=ot[:, 0:128], in_=pv[:, :128], func=AF.Identity),
                      reads=[("ps", 4 + tb % 2)], writes=[okey])
                for hf in range(2):
                    pg = k.ps[6 + hf]
                    for kc in range(KC):
                        mm(S, pg[:, :], hT[:, kc, tb * 128:(tb + 1) * 128], wtm[:, kc, 128 + hf * 512:128 + (hf + 1) * 512],
                           kc == 0, kc == KC - 1, [("wtm", 1 + hf), ("hT", kc)], ("ps", 6 + hf))
                    gelu_from_psum(k, pg[:, :], ("ps", 6 + hf), ot[:, 128 + hf * 512:128 + (hf + 1) * 512], [okey], 512,
                                   r1[hf][:, :], r2[hf][:, :], ("r1", hf), ("r2", hf))
                r0 = t0 + tb * 128
                S.add("sp", lambda e, ot=ot, r0=r0: e.dma_start(out=vtok[r0:r0 + 128, :], in_=ot[:, 0:128]),
                      reads=[okey], writes=[("dr", "vtok", r0)], dma=True)
                S.add("sp", lambda e, ot=ot, r0=r0: e.dma_start(out=gvtok[r0:r0 + 128, :], in_=ot[:, 128:1152]),
                      reads=[okey], writes=[("dr", "gvtok", r0)], dma=True)
        S.barrier()


def phase_ab_mix(k):
    nc, S = k.nc, k.S
    qT = k.dram("qT", [1024, NT], BF16)
    kT = k.dram("kT", [2, 128, NT], BF16)
    vtok = k.dram("vtok", [NT, 128], BF16)
    guT = k.dram("guT", [1024, NT], BF16)
    gvtok = k.dram("gvtok", [NT, 1024], BF16)
    oT = k.dram("oT", [2048, NT], BF16)
    masksD = k.dram("masks", [128, 4, 128])
    sinkD = k.dram("sinkT", [128, 8])
    wsD = k.dram("b_wsT", [128, 8, 128])
    bbD = k.dram("b_biasbc", [128, 8, 128])
    scale = 1.0 / 8.0
    with ExitStack() as st:
        al = k.alloc(st)
        kTs = al("kTs", [128, 2, NT], BF16)
        vts = al("vts", [128, 20, 128], BF16)
        masks = al("masks", [128, 4, 128], BF16)
        sink = al("sink", [128, 8], F32)
        esbc = al("esbc", [128, 8, 128], F32)
        wsT = al("wsT", [128, 8, 128], BF16)
        bbc = al("bbc", [128, 8, 128], F32)
        qb_ = [al("qb%d" % i, [128, 8, 128], BF16) for i in range(2)]
        gub = [al("gub%d" % i, [128, 8, 128], BF16) for i in range(2)]
        gvb = [al("gvb%d" % i, [128, 1024], BF16) for i in range(2)]
        pT = [al("pT%d" % i, [128, 2, 4, 128], BF16) for i in range(3)]
        rr = al("rr", [128, 512], F32)
        gt = al("gt", [128, 512], F32)
        ob = [al("oblk%d" % i, [128, 16, 128], BF16) for i in range(2)]
        S.add("sp", lambda e: e.dma_start(out=kTs[:], in_=kT.rearrange("v p t -> p v t")), writes=["kTs"], dma=True)
        S.add("sp", lambda e: e.dma_start(out=vts[:], in_=vtok.rearrange("(b p) d -> p b d", p=128)), writes=["vts"], dma=True)
        S.add("pool", lambda e: e.dma_start(out=masks[:], in_=masksD), writes=["masks"], dma=True)
        S.add("pool", lambda e: e.dma_start(out=wsT[:], in_=wsD), writes=["wsT"], dma=True)
        S.add("sp", lambda e: e.dma_start(out=bbc[:], in_=bbD), writes=["bbc"], dma=True)
        S.add("sp", lambda e: e.dma_start(out=sink[:], in_=sinkD), writes=["sink"], dma=True)
        S.add("act", lambda e: e.activation(out=sink[:], in_=sink[:], func=AF.Exp), reads=["sink"], writes=["sink"])
        S.add("dve", lambda e: e.tensor_copy(out=esbc[:], in_=sink[:].unsqueeze(2).broadcast_to([128, 8, 128])),
              reads=["sink"], writes=["esbc"])
        blocks = list(range(16)) + [18, 19]
        npT = 0
        for bi, b in enumerate(blocks):
            i2 = bi % 2
            c0 = b * 128
            qb, gu, gv, o_ = qb_[i2], gub[i2], gvb[i2], ob[i2]
            S.add("sp", lambda e, qb=qb, c0=c0: e.dma_start(out=qb[:], in_=qT[:, c0:c0 + 128].rearrange("(c p) t -> p c t", p=128)),
                  writes=[("qb", i2)], dma=True)
            S.add("sp", lambda e, gu=gu, c0=c0: e.dma_start(out=gu[:], in_=guT[:, c0:c0 + 128].rearrange("(c p) t -> p c t", p=128)),
                  writes=[("gub", i2)], dma=True)
            S.add("sp", lambda e, gv=gv, c0=c0: e.dma_start(out=gv[:], in_=gvtok[c0:c0 + 128, :]), writes=[("gvb", i2)], dma=True)
            if b < 16:
                kl = [((b - 1) if b > 0 else 16, 0 if b > 0 else 2), (b, None), ((b + 1) if b < 15 else 17, 1 if b < 15 else 3),
                      (18, None), (19, None)]
            else:
                kl = [(18, None), (19, None)]
            for kv in range(2):
                po, pd = k.ps[4], k.ps[5]
                pend = {}
                for ji in range(len(kl) + 1):
                    if ji < len(kl):
                        kb, mi = kl[ji]
                        pa, pb = k.ps[2 * (ji % 2)], k.ps[2 * (ji % 2) + 1]
                        ka, kb_ = ("ps", 2 * (ji % 2)), ("ps", 2 * (ji % 2) + 1)
                        mm(S, pa[:, :], kTs[0:64, kv, kb * 128:(kb + 1) * 128], qb[0:64, kv * 4:(kv + 1) * 4, :], True, True,
                           ["kTs", ("qb", i2)], ka)
                        mm(S, pb[:, :], kTs[64:128, kv, kb * 128:(kb + 1) * 128], qb[64:128, kv * 4:(kv + 1) * 4, :], True, True,
                           ["kTs", ("qb", i2)], kb_)
                        p = pT[npT % 3]
                        pk = ("pT", npT % 3)
                        npT += 1
                        S.add("act", lambda e, p=p, pa=pa: e.activation(out=p[:, 0, :, :], in_=pa[:, :].rearrange("p (c t) -> p c t", c=4),
                                                                        func=AF.Exp, scale=scale), reads=[ka], writes=[pk])
                        S.add("act", lambda e, p=p, pb=pb: e.activation(out=p[:, 1, :, :], in_=pb[:, :].rearrange("p (c t) -> p c t", c=4),
                                                                        func=AF.Exp, scale=scale), reads=[kb_], writes=[pk])
                        if mi is not None:
                            S.add("dve", lambda e, p=p, mi=mi: e.tensor_tensor(
                                out=p[:].rearrange("p e c t -> p (e c) t"), in0=p[:].rearrange("p e c t -> p (e c) t"),
                                in1=masks[:, mi:mi + 1, :].broadcast_to([128, 8, 128]), op=ALU.mult),
                                reads=[pk, "masks"], writes=[pk])
                        pend[ji] = (p, pk, kb)
                    jj_ = ji - 1
                    if jj_ >= 0:
                        p, pk, kb = pend.pop(jj_)
                        first, last = jj_ == 0, jj_ == len(kl) - 1
                        for e_ in range(2):
                            mm(S, po[64 * e_:64 * e_ + 64, :], vts[:, kb, kv * 64:(kv + 1) * 64], p[:, e_, :, :], first, last,
                               ["vts", pk], ("ps", 4))
                            mm(S, pd[64 * e_:64 * e_ + 64, :], k.ones_b[:, 0:64], p[:, e_, :, :], first, last,
                               ["ones_b", pk], ("ps", 5))
                S.add("dve", lambda e, pd=pd, kv=kv: e.tensor_tensor(
                    out=rr[:].rearrange("p (c t) -> p c t", c=4), in0=pd[:, :].rearrange("p (c t) -> p c t", c=4),
                    in1=esbc[:, kv * 4:(kv + 1) * 4, :], op=ALU.add), reads=[("ps", 5), "esbc"], writes=["rr"])
                S.add("dve", lambda e: e.reciprocal(out=rr[:], in_=rr[:]), reads=["rr"], writes=["rr"])
                S.add("dve", lambda e, po=po, kv=kv, o_=o_: e.tensor_tensor(
                    out=o_[:, kv * 4:(kv + 1) * 4, :], in0=po[:, :].rearrange("p (c t) -> p c t", c=4),
                    in1=rr[:].rearrange("p (c t) -> p c t", c=4), op=ALU.mult), reads=[("ps", 4), "rr"], writes=[("oblk", i2)])
            for hf in range(2):
                pg = k.ps[6 + hf]
                for gg in range(4):
                    g = hf * 4 + gg
                    mm(S, pg[:, gg * 128:(gg + 1) * 128], gv[:, g * 128:(g + 1) * 128], wsT[:, g, :], True, True,
                       [("gvb", i2), "wsT"], ("ps", 6 + hf))
                S.add("dve", lambda e, pg=pg, hf=hf: e.tensor_tensor(
                    out=gt[:].rearrange("p (c t) -> p c t", c=4), in0=pg[:, :].rearrange("p (c t) -> p c t", c=4),
                    in1=bbc[:, hf * 4:(hf + 1) * 4, :], op=ALU.add), reads=[("ps", 6 + hf), "bbc"], writes=["gt"])
                S.add("dve", lambda e, hf=hf, o_=o_, gu=gu: e.tensor_tensor(
                    out=o_[:, 8 + hf * 4:8 + (hf + 1) * 4, :], in0=gt[:].rearrange("p (c t) -> p c t", c=4),
                    in1=gu[:, hf * 4:(hf + 1) * 4, :], op=ALU.mult), reads=["gt", ("gub", i2)], writes=[("oblk", i2)])
            S.add("sp", lambda e, o_=o_, c0=c0: e.dma_start(out=oT[:, c0:c0 + 128].rearrange("(c p) t -> p c t", p=128), in_=o_[:]),
                  reads=[("oblk", i2)], writes=[("dr", "oT", c0)], dma=True)
        S.barrier()


def phase_outproj(k, l, wname, oT, x_in, x_out, tiles):
    nc, S = k.nc, k.S
    w = k.dram(wname, [1, D, D])[0]
    with ExitStack() as st:
        al = k.alloc(st)
        xres = al("xres", [128, KC, 512], F32)
        hT = al("hT", [128, KC, 512], BF16)
        wa = [al("wa%d" % i, [128, KC, 256], BF16) for i in range(2)]
        na = 0
        for (t0, T, segs) in tiles:
            load_x_tile(k, xres, x_in, t0, T)
            for c in range(KC):
                S.add("sp", lambda e, c=c, t0=t0, T=T: e.dma_start(out=hT[:, c, :T], in_=oT[c * 128:(c + 1) * 128, t0:t0 + T]),
                      writes=[("hT", c)], dma=True)
            for sl in range(KC // 2):
                a = wa[na % 2]
                ka = "wa%d" % (na % 2)
                na += 1
                S.add("pool", lambda e, a=a, sl=sl: e.dma_start(
                    out=a[:], in_=w[:, sl * 256:(sl + 1) * 256].rearrange("(kc p) n -> p kc n", p=128)), writes=[ka], dma=True)
                for jj in range(2):
                    c = sl * 2 + jj
                    py = k.ps[c % 2]
                    for kc in range(KC):
                        mm(S, py[:, :T], a[:, kc, jj * 128:(jj + 1) * 128], hT[:, kc, :T], kc == 0, kc == KC - 1,
                           [ka, ("hT", kc)], ("ps", c % 2))
                    residual_store(k, py, ("ps", c % 2), xres, c, T, segs, l, 1, x_out, t0)
        S.barrier()


NK = SEQ + NCTX
NKC = NK // 128


def lat_norm(k, src, nch, T, gain, dst, rstd, sqb, pskey_i):
    S = k.S
    ps = k.ps[pskey_i]
    for c in range(nch):
        sq = sqb[c % 2]
        S.add("act", lambda e, c=c, sq=sq: e.activation(out=sq[:, :T], in_=src[:, c, :T], func=AF.Square),
              reads=[("lsrc", c)], writes=[("sq", c % 2)])
        mm(S, ps[:, :T], k.ones_b[:], sq[:, :T], c == 0, c == nch - 1, [("sq", c % 2), "ones_b"], ("ps", pskey_i))
    S.add("act", lambda e: e.activation(out=rstd[:, :T], in_=ps[:, :T], func=AF.Sqrt, scale=1.0 / (nch * 128), bias=k.eps_t[:]),
          reads=[("ps", pskey_i), "eps_t"], writes=["rstd2"])
    S.add("dve", lambda e: e.reciprocal(out=rstd[:, :T], in_=rstd[:, :T]), reads=["rstd2"], writes=["rstd2"])
    for c in range(nch):
        S.add("dve", lambda e, c=c: e.scalar_tensor_tensor(
            out=dst[:, c, :T], in0=src[:, c, :T], scalar=gain[:, c:c + 1], op0=ALU.mult, in1=rstd[:, :T], op1=ALU.mult),
            reads=[("lsrc", c), "rstd2", "gains"], writes=[("ldst", c)])


def phase_cd_in(k, x_in, tiles):
    nc, S = k.nc, k.S
    l, s = 1, 1
    w_in = k.dram("cd_w_in", [1, D, 4416])[0]
    w_uq = k.dram("c_w_uq", [1, 768, 1536])[0]
    cosD, sinD = k.dram("cosT", [128, NT]), k.dram("sinT", [128, NT])
    qgD, kgD = k.dram("c_qnT", [128, 6]), k.dram("c_kvnT", [128, 4])
    qnT = k.dram("qnT", [1024, NOWN], BF16)
    qrT = k.dram("qrT", [512, NOWN], BF16)
    xin = [k.dram("xch_in%d" % i, [128, NOWN], BF16) for i in range(5)]
    ctxkv = k.dram("ctxkv", [640, NCTX], BF16)
    zb = k.dram("zb_in", [2, 1024])
    dbT = k.dram("dbT", [1024, NOWN], BF16)
    zT = k.dram("zT", [1024, NOWN])
    with ExitStack() as st:
        al = k.alloc(st)
        xres = al("xres", [128, KC, 512], F32)
        hT = al("hT", [128, KC, 512], BF16)
        tmp = [al("tmp%d" % i, [128, 512], F32) for i in range(2)]
        sqb = [al("sq%d" % i, [128, 512], BF16) for i in range(2)]
        rstd = al("rstd", [128, 512], F32)
        rstd2 = al("rstd2", [128, 512], F32)
        cos, sin = al("cos", [128, 512], F32), al("sin", [128, 512], F32)
        lat = al("lat", [128, 6, 512], F32)
        latn = al("latn", [128, 6, 512], BF16)
        qg, kg = al("qg", [128, 6], F32), al("kg", [128, 4], F32)
        wuq = al("wuq", [128, 6, 1536], BF16)
        wqr = al("wqr", [128, 6, 4, 128], BF16)
        krw = al("krw", [128, KC, 128], BF16)
        wa = [al("wa%d" % i, [128, KC, 256], BF16) for i in range(2)]
        qraw = [al("qraw%d" % i, [128, 512], BF16) for i in range(2)]
        r1 = [al("r1_%d" % i, [128, 512], F32) for i in range(2)]
        r2 = [al("r2_%d" % i, [128, 512], F32) for i in range(2)]
        ob = [al("ob%d" % i, [128, 512], BF16) for i in range(2)]
        zo = [al("zo%d" % i, [128, 512], F32) for i in range(2)]
        dcs = [al("dcs%d" % i, [128, 512], F32) for i in range(2)]
        S.add("sp", lambda e: e.dma_start(out=qg[:], in_=qgD), writes=["gains"], dma=True)
        S.add("sp", lambda e: e.dma_start(out=kg[:], in_=kgD), writes=["gains"], dma=True)
        for i in range(3):
            S.add("pool", lambda e, i=i: e.dma_start(
                out=wuq[:, :, i * 512:(i + 1) * 512], in_=w_uq[:, i * 512:(i + 1) * 512].rearrange("(kc p) n -> p kc n", p=128)),
                writes=[("wuq", i)], dma=True)
        for h in range(8):
            S.add("pool", lambda e, h=h: e.dma_start(
                out=wqr[:, :, h // 2, (h % 2) * 64:(h % 2) * 64 + 64],
                in_=w_uq[:, h * 192 + 128:h * 192 + 192].rearrange("(kc p) n -> p kc n", p=128)),
                writes=[("wqr", h)], dma=True)
        for i in range(2):
            S.add("pool", lambda e, i=i: e.dma_start(
                out=krw[:, :, i * 64:(i + 1) * 64], in_=w_in[:, 1280:1344].rearrange("(kc p) n -> p kc n", p=128)),
                writes=[("krw", i)], dma=True)
        wuq_r = [("wuq", i) for i in range(3)]
        wqr_r = [("wqr", h) for h in range(8)]
        na = 0
        cnt = 0
        for (t0, T, segs) in tiles:
            own = t0 < NOWN
            oc0 = t0 if own else 0
            kvd = (lambda i: xin[i]) if own else (lambda i: ctxkv[i * 128:(i + 1) * 128, :])
            load_x_tile(k, xres, x_in, t0, T)
            S.add("sp", lambda e, t0=t0, T=T: e.dma_start(out=cos[:, :T], in_=cosD[:, t0:t0 + T]), writes=["rope"], dma=True)
            S.add("sp", lambda e, t0=t0, T=T: e.dma_start(out=sin[:, :T], in_=sinD[:, t0:t0 + T]), writes=["rope"], dma=True)
            norm_mod(k, xres, hT, T, segs, l, s, tmp, rstd, sqb)

            def proj_chunk(a, ka, jj):
                nonlocal cnt
                i2 = cnt % 2
                cnt += 1
                pq = k.ps[i2]
                for kc in range(KC):
                    mm(S, pq[:, :T], a[:, kc, jj * 128:(jj + 1) * 128], hT[:, kc, :T], kc == 0, kc == KC - 1,
                       [ka, ("hT", kc)] if not isinstance(ka, list) else ka + [("hT", kc)], ("ps", i2))
                return pq, i2

            def slab(c0):
                nonlocal na
                a = wa[na % 2]
                ka = "wa%d" % (na % 2)
                na += 1
                S.add("pool", lambda e: e.dma_start(
                    out=a[:], in_=w_in[:, c0:c0 + 256].rearrange("(kc p) n -> p kc n", p=128)), writes=[ka], dma=True)
                return a, ka

            if own:
                for sl in range(3):
                    a, ka = slab(sl * 256)
                    for jj in range(2):
                        c = sl * 2 + jj
                        pq, i2 = proj_chunk(a, ka, jj)
                        S.add("act", lambda e, pq=pq, c=c: e.activation(out=lat[:, c, :T], in_=pq[:, :T], func=AF.Identity),
                              reads=[("ps", i2)], writes=[("lsrc", c)])
                lat_norm(k, lat, 6, T, qg, latn, rstd2, sqb, 6)
                lr = [("ldst", c) for c in range(6)]
                for h in range(8):
                    i2 = cnt % 2
                    cnt += 1
                    pq = k.ps[i2]
                    for kc in range(6):
                        mm(S, pq[:, :T], wuq[:, kc, h * 192:h * 192 + 128], latn[:, kc, :T], kc == 0, kc == 5,
                           wuq_r + [("ldst", kc)], ("ps", i2))
                    o_ = ob[i2]
                    S.add("act", lambda e, pq=pq, o_=o_: e.activation(out=o_[:, :T], in_=pq[:, :T], func=AF.Identity),
                          reads=[("ps", i2)], writes=[("ob", i2)])
                    S.add("sp", lambda e, o_=o_, h=h: e.dma_start(out=qnT[h * 128:(h + 1) * 128, t0:t0 + T], in_=o_[:, :T]),
                          reads=[("ob", i2)], writes=[("dr", "qnT", t0, h)], dma=True)
                for j in range(4):
                    i2 = cnt % 2
                    cnt += 1
                    pq = k.ps[i2]
                    for kc in range(6):
                        mm(S, pq[:, :T], wqr[:, kc, j, :], latn[:, kc, :T], kc == 0, kc == 5, wqr_r + [("ldst", kc)], ("ps", i2))
                    o_ = ob[i2]
                    rope_from_psum(k, pq[:, :T], ("ps", i2), k.ps[2 + i2], ("ps", 2 + i2), cos[:, :T], sin[:, :T], T,
                                   o_[:, :T], [("ob", i2)], qraw[i2], r1[i2], r2[i2], (("qraw", i2), ("r1", i2), ("r2", i2)))
                    S.add("sp", lambda e, o_=o_, j=j: e.dma_start(out=qrT[j * 128:(j + 1) * 128, t0:t0 + T], in_=o_[:, :T]),
                          reads=[("ob", i2)], writes=[("dr", "qrT", t0, j)], dma=True)
            for sl in range(2):
                a, ka = slab(768 + sl * 256)
                for jj in range(2):
                    c = sl * 2 + jj
                    pq, i2 = proj_chunk(a, ka, jj)
                    S.add("act", lambda e, pq=pq, c=c: e.activation(out=lat[:, c, :T], in_=pq[:, :T], func=AF.Identity),
                          reads=[("ps", i2)], writes=[("lsrc", c)])
            lat_norm(k, lat, 4, T, kg, latn, rstd2, sqb, 6)
            for c in range(4):
                S.add("sp", lambda e, c=c: e.dma_start(out=kvd(c)[:, oc0:oc0 + T], in_=latn[:, c, :T]),
                      reads=[("ldst", c)], writes=[("dr", "ckvn", t0, c)], dma=True)
            pq, i2 = proj_chunk(krw, [("krw", 0), ("krw", 1)], 0)
            o_ = ob[i2]
            rope_from_psum(k, pq[:, :T], ("ps", i2), k.ps[2 + i2], ("ps", 2 + i2), cos[:, :T], sin[:, :T], T,
                           o_[:, :T], [("ob", i2)], qraw[i2], r1[i2], r2[i2], (("qraw", i2), ("r1", i2), ("r2", i2)))
            S.add("sp", lambda e, o_=o_: e.dma_start(out=kvd(4)[:, oc0:oc0 + T], in_=o_[:, :T]),
                  reads=[("ob", i2)], writes=[("dr", "krT", t0)], dma=True)
            if own:
                for sl in range(4):
                    a, ka = slab(1344 + sl * 256)
                    for jj in range(2):
                        c = sl * 2 + jj
                        pq, i2 = proj_chunk(a, ka, jj)
                        o_ = ob[i2]
                        S.add("act", lambda e, pq=pq, o_=o_: e.activation(out=o_[:, :T], in_=pq[:, :T], func=AF.Identity),
                              reads=[("ps", i2)], writes=[("ob", i2)])
                        S.add("sp", lambda e, o_=o_, c=c: e.dma_start(out=dbT[c * 128:(c + 1) * 128, t0:t0 + T], in_=o_[:, :T]),
                              reads=[("ob", i2)], writes=[("dr", "dbT", t0, c)], dma=True)
                for sl in range(4):
                    a, ka = slab(2368 + sl * 256)
                    a2, ka2 = slab(3392 + sl * 256)
                    for jj in range(2):
                        c = sl * 2 + jj
                        pq, i2 = proj_chunk(a, ka, jj)
                        dc_ = dcs[i2]
                        S.add("act", lambda e, pq=pq, dc_=dc_: e.activation(out=dc_[:, :T], in_=pq[:, :T], func=AF.Identity),
                              reads=[("ps", i2)], writes=[("dcs", i2)])
                        pq2, j2 = proj_chunk(a2, ka2, jj)
                        z_ = zo[i2]
                        S.add("dve", lambda e, pq2=pq2, dc_=dc_, z_=z_: e.tensor_tensor(
                            out=z_[:, :T], in0=pq2[:, :T], in1=dc_[:, :T], op=ALU.mult),
                            reads=[("ps", j2), ("dcs", i2)], writes=[("zo", i2)])
                        S.add("sp", lambda e, z_=z_, c=c: e.dma_start(out=zT[c * 128:(c + 1) * 128, t0:t0 + T], in_=z_[:, :T]),
                              reads=[("zo", i2)], writes=[("dr", "zT", t0, c)], dma=True)
                        for (tt, which, col) in ((0, 0, 0), (NOWN - 512, 1, 511)):
                            if t0 == tt:
                                S.add("sp", lambda e, z_=z_, c=c, which=which, col=col: e.dma_start(
                                    out=zb[which:which + 1, :].rearrange("a (p c) -> p (a c)", c=8)[:, c:c + 1],
                                    in_=z_[:, col:col + 1], allow_slow_non_contiguous=True),
                                    reads=[("zo", i2)], writes=[("dr", "zb", which, c)], dma=True)
        S.barrier()


def phase_exchange(k):
    nc, S = k.nc, k.S
    xin = [k.dram("xch_in%d" % i, [128, NOWN], BF16) for i in range(5)]
    xall = [k.dram("xch_all%d" % i, [512, NOWN], BF16) for i in range(5)]
    zb = k.dram("zb_in", [2, 1024])
    zall = k.dram("zb_all", [8, 1024])
    groups = [[0, 1, 2, 3], [4, 5, 6, 7]]
    for i in range(5):
        S.add_cc(lambda e, i=i: e.collective_compute("AllGather", ALU.bypass, replica_groups=groups, ins=[xin[i].opt()], outs=[xall[i].opt()]))
    S.add_cc(lambda e: e.collective_compute("AllGather", ALU.bypass, replica_groups=groups, ins=[zb.opt()], outs=[zall.opt()]))
    S.barrier()
    with ExitStack() as st:
        al = k.alloc(st)
        zsel = al("zsel", [128, 8, 8], F32)
        S.add("sp", lambda e: e.dma_start(out=zsel[:], in_=zall.rearrange("j (p c) -> p j c", c=8)), reads=["zall"], writes=["zsel"], dma=True)
        for w in range(2):
            for j in range(8):
                if j == 0:
                    S.add("dve", lambda e, w=w, j=j: e.tensor_scalar(out=k.zpn[:, w, :], in0=zsel[:, j, :], scalar1=k.selT[:, w, j:j + 1],
                                                                     scalar2=None, op0=ALU.mult), reads=["zsel", "selT"], writes=["zpn"])
                else:
                    S.add("dve", lambda e, w=w, j=j: e.scalar_tensor_tensor(out=k.zpn[:, w, :], in0=zsel[:, j, :], scalar=k.selT[:, w, j:j + 1],
                                                                            op0=ALU.mult, in1=k.zpn[:, w, :], op1=ALU.add),
                          reads=["zsel", "selT", "zpn"], writes=["zpn"])
        S.barrier()


def phase_mla(k):
    nc, S = k.nc, k.S
    xall = [k.dram("xch_all%d" % i, [512, NOWN], BF16) for i in range(5)]
    ctxkv = k.dram("ctxkv", [640, NCTX], BF16)
    w_ukv = k.dram("c_w_ukv", [1, 512, 2048])[0]
    qnT = k.dram("qnT", [1024, NOWN], BF16)
    qrT = k.dram("qrT", [512, NOWN], BF16)
    oT = k.dram("oT2", [1024, NOWN], BF16)
    scale = 192.0 ** -0.5
    with ExitStack() as st:
        al = k.alloc(st)
        ckv = al("ckv", [128, 4, NK], BF16)
        krd = al("krd", [128, NK], BF16)
        wkv = al("wkv", [128, 4, 2048], BF16)
        knT = al("knT", [128, NK], BF16)
        vh = al("vh", [128, NKC, 128], BF16)
        qn = [al("qn%d" % i, [128, 512], BF16) for i in range(2)]
        qr = [[al("qr%d_%d" % (e_, i), [128, 512], BF16) for i in range(2)] for e_ in range(2)]
        for e_ in range(2):
            for i in range(2):
                S.add("pool", lambda e, e_=e_, i=i: e.memset(qr[e_][i][:], 0.0), writes=[("qr", e_, i)])
        pT = [al("pT%d" % i, [128, 512], BF16) for i in range(6)]
        rr = al("rr", [128, 512], F32)
        dacc = [al("dacc%d" % i, [128, 512], F32) for i in range(2)]
        oo = [al("oo%d" % i, [128, 512], BF16) for i in range(2)]
        for c in range(4):
            for r in range(4):
                S.add("sp", lambda e, c=c, r=r: e.dma_start(out=ckv[:, c, r * NOWN:(r + 1) * NOWN],
                                                            in_=xall[c][r * 128:(r + 1) * 128, :]),
                      writes=[("ckv", c, r)], dma=True)
            S.add("sp", lambda e, c=c: e.dma_start(out=ckv[:, c, SEQ:NK], in_=ctxkv[c * 128:(c + 1) * 128, :]), writes=[("ckv", c, 4)], dma=True)
        for r in range(4):
            S.add("sp", lambda e, r=r: e.dma_start(out=krd[:, r * NOWN:(r + 1) * NOWN], in_=xall[4][r * 128:(r + 1) * 128, :]),
                  writes=[("krd", r)], dma=True)
        S.add("sp", lambda e: e.dma_start(out=krd[:, SEQ:NK], in_=ctxkv[512:640, :]), writes=[("krd", 4)], dma=True)
        for i in range(4):
            S.add("pool", lambda e, i=i: e.dma_start(
                out=wkv[:, :, i * 512:(i + 1) * 512], in_=w_ukv[:, i * 512:(i + 1) * 512].rearrange("(kc p) n -> p kc n", p=128)),
                writes=[("wkv", i)], dma=True)
        ckr = [("ckv", c, r) for c in range(4) for r in range(5)]
        krr = [("krd", r) for r in range(5)]
        npT = 0
        nq = 0
        for h in range(8):
            wr = [("wkv", h // 2)]
            for kg in range(17):
                n = 512 if kg < 16 else 256
                pk_ = k.ps[6 + kg % 2]
                for kc in range(4):
                    mm(S, pk_[:, :n], wkv[:, kc, h * 256:h * 256 + 128], ckv[:, kc, kg * 512:kg * 512 + n], kc == 0, kc == 3,
                       wr + ckr, ("ps", 6 + kg % 2))
                S.add("act", lambda e, pk_=pk_, kg=kg, n=n: e.activation(out=knT[:, kg * 512:kg * 512 + n], in_=pk_[:, :n], func=AF.Identity),
                      reads=[("ps", 6 + kg % 2)], writes=["knT"])
            for g4 in range(17):
                nb = 4 if g4 < 16 else 2
                pv_ = k.ps[6 + g4 % 2]
                for bb in range(nb):
                    kb = g4 * 4 + bb
                    for kc in range(4):
                        mm(S, pv_[:, bb * 128:(bb + 1) * 128], ckv[:, kc, kb * 128:(kb + 1) * 128],
                           wkv[:, kc, h * 256 + 128:h * 256 + 256], kc == 0, kc == 3, wr + ckr, ("ps", 6 + g4 % 2))
                S.add("dve", lambda e, pv_=pv_, g4=g4, nb=nb: e.tensor_copy(
                    out=vh[:, g4 * 4:g4 * 4 + nb, :], in_=pv_[:, :nb * 128].rearrange("p (b d) -> p b d", d=128)),
                    reads=[("ps", 6 + g4 % 2)], writes=["vh"])
            e2 = h % 2
            for qgi in range(4):
                i2 = nq % 2
                nq += 1
                q0 = qgi * 512
                S.add("sp", lambda e, i2=i2, q0=q0, h=h: e.dma_start(out=qn[i2][:], in_=qnT[h * 128:(h + 1) * 128, q0:q0 + 512]),
                      writes=[("qn", i2)], dma=True)
                S.add("sp", lambda e, i2=i2, q0=q0, h=h, e2=e2: e.dma_start(
                    out=qr[e2][i2][64 * e2:64 * e2 + 64, :], in_=qrT[(h // 2) * 128 + 64 * e2:(h // 2) * 128 + 64 * e2 + 64, q0:q0 + 512]),
                    writes=[("qr", e2, i2)], dma=True)
                po, pd = k.ps[4], k.ps[5]
                LOOK = 3
                pend = {}
                for kc in range(NKC + LOOK):
                    if kc < NKC:
                        bi = kc % 4
                        ps_ = k.ps[bi]
                        mm(S, ps_[:, :], knT[:, kc * 128:(kc + 1) * 128], qn[i2][:], True, False, ["knT", ("qn", i2)], ("ps", bi))
                        mm(S, ps_[:, :], krd[:, kc * 128:(kc + 1) * 128], qr[e2][i2][:], False, True,
                           krr + [("qr", e2, i2)], ("ps", bi))
                        p = pT[npT % 6]
                        pk = ("pT", npT % 6)
                        npT += 1
                        S.add("act", lambda e, p=p, ps_=ps_: e.activation(out=p[:], in_=ps_[:, :], func=AF.Exp, scale=scale),
                              reads=[("ps", bi)], writes=[pk])
                        pend[kc] = (p, pk)
                    j = kc - LOOK
                    if j >= 0:
                        p, pk = pend.pop(j)
                        mm(S, po[:, :], vh[:, j, :], p[:], j == 0, j == NKC - 1, ["vh", pk], ("ps", 4))
                        if j % 3 == 1:
                            mm(S, pd[:, :], k.ones_b[:], p[:], j == 1, False, ["ones_b", pk], ("ps", 5))
                        elif j == 0:
                            S.add("dve", lambda e, p=p: e.tensor_copy(out=dacc[0][:], in_=p[:]), reads=[pk], writes=[("dacc", 0)])
                        else:
                            S.add("dve", lambda e, p=p: e.tensor_tensor(out=dacc[0][:], in0=dacc[0][:], in1=p[:], op=ALU.add),
                                  reads=[pk, ("dacc", 0)], writes=[("dacc", 0)])
                mm(S, pd[:, :], k.ones_f[:], dacc[0][:], False, True, ["ones_f", ("dacc", 0)], ("ps", 5))
                S.add("dve", lambda e, pd=pd: e.reciprocal(out=rr[:], in_=pd[:, :]), reads=[("ps", 5)], writes=["rr"])
                o_ = oo[i2]
                S.add("dve", lambda e, po=po, o_=o_: e.tensor_tensor(out=o_[:], in0=po[:, :], in1=rr[:], op=ALU.mult),
                      reads=[("ps", 4), "rr"], writes=[("oo", i2)])
                S.add("sp", lambda e, o_=o_, h=h, q0=q0: e.dma_start(out=oT[h * 128:(h + 1) * 128, q0:q0 + 512], in_=o_[:]),
                      reads=[("oo", i2)], writes=[("dr", "oT2", h, q0)], dma=True)
        S.barrier()


def phase_cd_out(k, x_in, x_out):
    nc, S = k.nc, k.S
    l = 1
    w = k.dram("cd_w_out", [1, D, D])[0]
    oT = k.dram("oT2", [1024, NOWN], BF16)
    dbT = k.dram("dbT", [1024, NOWN], BF16)
    zT = k.dram("zT", [1024, NOWN])
    cwD = k.dram("convT", [128, 24])
    with ExitStack() as st:
        al = k.alloc(st)
        xres = al("xres", [128, KC, 512], F32)
        hT = al("hT", [128, KC, 512], BF16)
        ze = al("ze", [128, 8, 514], F32)
        dbs = al("dbs", [128, 8, 512], BF16)
        cw = al("cw", [128, 24], F32)
        ct = [al("ct%d" % i, [128, 512], F32) for i in range(2)]
        wa = [al("wa%d" % i, [128, KC, 256], BF16) for i in range(2)]
        S.add("sp", lambda e: e.dma_start(out=cw[:], in_=cwD), writes=["cw"], dma=True)
        na = 0
        for (t0, T, segs) in OWN_TILES:
            load_x_tile(k, xres, x_in, t0, T)
            for c in range(8):
                S.add("sp", lambda e, c=c, t0=t0: e.dma_start(out=hT[:, c, :], in_=oT[c * 128:(c + 1) * 128, t0:t0 + 512]),
                      writes=[("hT", c)], dma=True)
            S.add("sp", lambda e, t0=t0: e.dma_start(out=ze[:, :, 1:513], in_=zT[:, t0:t0 + 512].rearrange("(c p) t -> p c t", p=128)),
                  writes=[("ze", 1)], dma=True)
            if t0 > 0:
                S.add("sp", lambda e, t0=t0: e.dma_start(out=ze[:, :, 0:1], in_=zT[:, t0 - 1:t0].rearrange("(c p) t -> p c t", p=128),
                                                         allow_slow_non_contiguous=True), writes=[("ze", 0)], dma=True)
            else:
                S.add("dve", lambda e: e.tensor_copy(out=ze[:, :, 0], in_=k.zpn[:, 0, :]), reads=["zpn"], writes=[("ze", 0)])
            if t0 + 512 < NOWN:
                S.add("sp", lambda e, t0=t0: e.dma_start(out=ze[:, :, 513:514], in_=zT[:, t0 + 512:t0 + 513].rearrange("(c p) t -> p c t", p=128),
                                                         allow_slow_non_contiguous=True), writes=[("ze", 2)], dma=True)
            else:
                S.add("dve", lambda e: e.tensor_copy(out=ze[:, :, 513], in_=k.zpn[:, 1, :]), reads=["zpn"], writes=[("ze", 2)])
            S.add("sp", lambda e, t0=t0: e.dma_start(out=dbs[:], in_=dbT[:, t0:t0 + 512].rearrange("(c p) t -> p c t", p=128)),
                  writes=["dbs"], dma=True)
            zr = [("ze", i) for i in range(3)]
            for c in range(8):
                t_ = ct[c % 2]
                tk = ("ct", c % 2)
                S.add("dve", lambda e, c=c, t_=t_: e.tensor_scalar(out=t_[:], in0=ze[:, c, 0:512], scalar1=cw[:, c:c + 1], scalar2=None,
                                                                   op0=ALU.mult), reads=zr + ["cw"], writes=[tk])
                S.add("dve", lambda e, c=c, t_=t_: e.scalar_tensor_tensor(out=t_[:], in0=ze[:, c, 1:513], scalar=cw[:, 8 + c:9 + c],
                                                                          op0=ALU.mult, in1=t_[:], op1=ALU.add), reads=zr + ["cw", tk], writes=[tk])
                S.add("dve", lambda e, c=c, t_=t_: e.scalar_tensor_tensor(out=t_[:], in0=ze[:, c, 2:514], scalar=cw[:, 16 + c:17 + c],
                                                                          op0=ALU.mult, in1=t_[:], op1=ALU.add), reads=zr + ["cw", tk], writes=[tk])
                S.add("dve", lambda e, c=c, t_=t_: e.tensor_tensor(out=hT[:, 8 + c, :], in0=t_[:], in1=dbs[:, c, :], op=ALU.mult),
                      reads=[tk, "dbs"], writes=[("hT", 8 + c)])
            for sl in range(KC // 2):
                a = wa[na % 2]
                ka = "wa%d" % (na % 2)
                na += 1
                S.add("pool", lambda e, a=a, sl=sl: e.dma_start(
                    out=a[:], in_=w[:, sl * 256:(sl + 1) * 256].rearrange("(kc p) n -> p kc n", p=128)), writes=[ka], dma=True)
                for jj in range(2):
                    c = sl * 2 + jj
                    py = k.ps[c % 2]
                    for kc in range(KC):
                        mm(S, py[:, :T], a[:, kc, jj * 128:(jj + 1) * 128], hT[:, kc, :T], kc == 0, kc == KC - 1,
                           [ka, ("hT", kc)], ("ps", c % 2))
                    residual_store(k, py, ("ps", c % 2), xres, c, T, segs, l, 1, x_out, t0)
        S.barrier()


OWN_TILES = [(i * 512, 512, [(0, 512, 0)]) for i in range(4)]
MISC_TILE = (2048, 512, [(0, 256, 0), (256, 256, 1)])
CTX_TILE = (CT0, 256, [(0, 256, 1)])
OWN3_CTX_TILE = ([(1536, 512), (CT0, 256)], [(0, 512, 0), (512, 256, 1)])


def fm(v):
    v = np.asarray(v)
    lead = v.shape[:-1]
    n = v.shape[-1] // 128
    r = v.reshape(lead + (n, 128))
    r = np.moveaxis(r, -1, 0)
    return np.ascontiguousarray(r.reshape(128, -1))


def prep_core(inp, core):
    b, q = core // 4, core % 4
    p0 = q * NOWN
    x = inp["x"]
    xT = np.zeros((D, NT), np.float32)
    xT[:, 0:NOWN] = x[b, p0:p0 + NOWN].T
    if q > 0:
        xT[:, HP0:HP0 + 128] = x[b, p0 - 128:p0].T
    if q < 3:
        xT[:, HN0:HN0 + 128] = x[b, p0 + NOWN:p0 + NOWN + 128].T
    xT[:, CT0:CT0 + NCTX] = inp["ctx"][b].T
    m = {"xT": xT}
    m["cT"] = np.ascontiguousarray(np.stack([inp["c"][b], inp["c_ctx"]], axis=1))
    pos = np.zeros(NT, np.int64)
    pos[0:NOWN] = p0 + np.arange(NOWN)
    pos[HP0:HP0 + 128] = p0 - 128 + np.arange(128)
    pos[HN0:HN0 + 128] = p0 + NOWN + np.arange(128)
    pos = np.clip(pos, 0, SEQ - 1)
    row = (pos // 64).astype(np.float32)
    col = (pos % 64).astype(np.float32)
    inv = (10000.0 ** (-np.arange(0, 32, 2, dtype=np.float32) / 32)).astype(np.float32)
    ang = np.zeros((64, NT), np.float32)
    ang[0:16] = (row[None, :] * inv[:, None]).astype(np.float32)
    ang[16:32] = ang[0:16]
    ang[32:48] = (col[None, :] * inv[:, None]).astype(np.float32)
    ang[48:64] = ang[32:48]
    cosT = np.cos(ang).astype(np.float32)
    sinT = np.sin(ang).astype(np.float32)
    cosT[:, CT0:] = 1.0
    sinT[:, CT0:] = 0.0
    m["cosT"] = np.ascontiguousarray(np.concatenate([cosT, cosT], 0))
    m["sinT"] = np.ascontiguousarray(np.concatenate([sinT, sinT], 0))
    j = np.arange(128)[:, None]
    i = np.arange(128)[None, :]
    mk = np.zeros((128, 4, 128), np.float32)
    mk[:, 0, :] = (j >= i)
    mk[:, 1, :] = (j <= i)
    mk[:, 2, :] = (j >= i) * (1.0 if q > 0 else 0.0)
    mk[:, 3, :] = (j <= i) * (1.0 if q < 3 else 0.0)
    m["masks"] = mk
    sel = np.zeros((128, 2, 8), np.float32)
    if q > 0:
        sel[:, 0, 2 * (q - 1) + 1] = 1.0
    if q < 3:
        sel[:, 1, 2 * (q + 1)] = 1.0
    m["selT"] = sel
    return m


def prep_shared(inp):
    m = {}
    m["mod_w"] = inp["mod_w"]
    m["mod_bT"] = np.stack([fm(inp["mod_b"][l]) for l in range(2)])
    m["norm_gT"] = np.stack([fm(inp["norm_g"][l]) for l in range(2)])
    for n in ("ffn_w1", "ffn_w3", "ffn_w2", "ab_w_in", "ab_w_out", "cd_w_in", "cd_w_out", "c_w_uq", "c_w_ukv"):
        m[n] = inp[n]
    pm = np.zeros((128, 128), np.float32)
    for mm_ in range(128):
        if mm_ % 32 < 16:
            pm[mm_ + 16, mm_] = -1.0
        else:
            pm[mm_ - 16, mm_] = 1.0
    m["permM"] = pm
    sk = inp["a_sink"][0]
    m["sinkT"] = np.ascontiguousarray(np.stack([np.repeat(sk[2 * c:2 * c + 2], 64) for c in range(8)], axis=1))
    m["b_wsT"] = np.ascontiguousarray(np.transpose(inp["b_ws"][0], (2, 0, 1)))
    m["b_biasbc"] = np.ascontiguousarray(np.broadcast_to(inp["b_bias"][0][None], (128, 8, 128)))
    m["c_qnT"] = fm(inp["c_q_norm"][0])
    m["c_kvnT"] = fm(inp["c_kv_norm"][0])
    m["convT"] = fm(inp["d_conv_w"][0])
    m["fnT"] = fm(inp["final_norm"])
    return m


def load_final_consts(k):
    fnD = k.dram("fnT", [128, 16])
    k.fnT = k.sb("fnT_sb", [128, 16], F32)
    k.S.add("sp", lambda e: e.dma_start(out=k.fnT[:], in_=fnD), writes=["fnT"], dma=True)
    selD = k.dram("selT", [128, 2, 8])
    k.selT = k.sb("selT_sb", [128, 2, 8], F32)
    k.zpn = k.sb("zpn_sb", [128, 2, 8], F32)
    k.S.add("sp", lambda e: e.dma_start(out=k.selT[:], in_=selD), writes=["selT"], dma=True)


INS = ["xT", "cT", "mod_w", "mod_bT", "norm_gT", "ffn_w1", "ffn_w3", "ffn_w2", "ab_w_in", "ab_w_out", "cosT", "sinT",
       "permM", "masks", "sinkT", "b_wsT", "b_biasbc", "cd_w_in", "c_w_uq", "c_qnT", "c_kvnT",
       "cd_w_out", "c_w_ukv", "convT", "fnT", "selT"]
OUTS = ["outT"]


def build():
    k = K(INS, OUTS)
    xT = k.dram("xT", [D, NT])
    x1, x2, x3 = k.dram("x1", [D, NT]), k.dram("x2", [D, NT]), k.dram("x3", [D, NT])
    x4, x5 = k.dram("x4", [D, NT]), k.dram("x5", [D, NOWN])
    outT = k.dram("outT", [D, NOWN])
    phase_setup(k)
    load_rope_consts(k)
    load_final_consts(k)
    phase_mod(k, (0, 1))
    phase_ffn(k, 0, 0, xT, x1, OWN_TILES + [MISC_TILE])
    phase_ab_in(k, x1, OWN_TILES + [MISC_TILE])
    phase_ab_mix(k)
    phase_outproj(k, 0, "ab_w_out", k.dram("oT", [2048, NT], BF16), x1, x2, OWN_TILES + [CTX_TILE])
    phase_ffn(k, 0, 1, x2, x3, OWN_TILES[:3])
    phase_ffn(k, 0, 1, x2, x3, [OWN3_CTX_TILE])
    phase_ffn(k, 1, 0, x3, x4, OWN_TILES[:3])
    phase_ffn(k, 1, 0, x3, x4, [OWN3_CTX_TILE])
    phase_cd_in(k, x4, OWN_TILES + [CTX_TILE])
    phase_exchange(k)
    phase_mla(k)
    phase_cd_out(k, x4, x5)
    phase_ffn(k, 1, 1, x5, None, OWN_TILES, final_out=outT)
    k.S.emit(k.st)
    k.st.close()
    return k


def kernel(**inp):
    inp = {kk: np.asarray(v) for kk, v in inp.items()}
    n = 8
    sh = prep_shared(inp)
    for nm in ("ffn_w1", "ffn_w3", "ffn_w2"):
        a = inp[nm]
        sh[nm] = a.reshape((4,) + a.shape[2:])
    k = build()
    maps = []
    for c in range(n):
        m = dict(sh)
        m.update(prep_core(inp, c))
        maps.append({kk: m[kk] for kk in INS})
    res = run_bass_kernel_spmd(k.nc, maps, core_ids=list(range(n))).results
    out = np.zeros((2, SEQ, D), np.float32)
    for c in range(n):
        b, q = c // 4, c % 4
        out[b, q * NOWN:(q + 1) * NOWN, :] = np.asarray(res[c]["outT"]).T
    return out
```
